# Optimizing a Trainium2 kernel written in Bass

```python
import jax
import jax.numpy as jnp
from jax import lax
import numpy as np

D_MODEL = 1024
BATCH = 16
SEQ = 2048
DEPTH = 2
DEC_BATCH = 32
DEC_SEQ = 16
PAST_LEN = 4096

CHUNK = 64
Q_BLOCK = 128
N_EVEN = (DEPTH + 1) // 2
N_ODD = DEPTH // 2
EPS = 1e-6
ROPE_THETA = 10000.0

H_A = 4
DK_A = 64
DV_A = 128
GATE_RANK = 16
GATE_TAU = 16.0
H_B = 4
DK_B = 64
DV_B = 128
IN_AB = 2 * H_A * DK_A + 2 * H_A * DV_A + GATE_RANK + 2 * H_B * DK_B + 2 * H_B * DV_B
MIX_AB = H_A * DV_A + H_B * DV_B
H_C = 8
Q_LORA = 384
KV_LORA = 512
NOPE = 128
ROPE = 64
V_C = 128
IN_C = Q_LORA + KV_LORA + ROPE
D_FF = 2816
CONV_W = 3

kernel_name = "hybrid_gla_retnet_mla_convffn_stream_step"


def split_cols(x, sizes):
    out, start = [], 0
    for s in sizes:
        out.append(x[..., start:start + s])
        start += s
    return out


def rms_norm(x, g):
    xf = x.astype(jnp.float32)
    y = xf * lax.rsqrt(jnp.mean(xf * xf, axis=-1, keepdims=True) + EPS)
    return (y * g.astype(jnp.float32)).astype(x.dtype)


def head_rms_norm(o, g):
    return o * lax.rsqrt(jnp.mean(o * o, axis=-1, keepdims=True) + EPS) * g.astype(jnp.float32)


def head_group_norm(o, g):
    return head_rms_norm(o - jnp.mean(o, axis=-1, keepdims=True), g)


def rope(x, pos):
    half = x.shape[-1] // 2
    freqs = ROPE_THETA ** (-jnp.arange(half, dtype=jnp.float32) / half)
    ang = pos.astype(jnp.float32)[:, None] * freqs[None, :]
    ang = ang.reshape((1, ang.shape[0]) + (1,) * (x.ndim - 3) + (half,))
    cos, sin = jnp.cos(ang), jnp.sin(ang)
    x1 = x[..., :half].astype(jnp.float32)
    x2 = x[..., half:].astype(jnp.float32)
    return jnp.concatenate([x1 * cos - x2 * sin, x1 * sin + x2 * cos], axis=-1).astype(x.dtype)


def decay_linear_attention(q, k, v, log_a, s0):
    B, T, H, K = q.shape
    V = v.shape[-1]
    c = min(CHUNK, T)
    n = T // c
    f32 = jnp.float32
    q = q.astype(f32).reshape(B, n, c, H, K)
    k = k.astype(f32).reshape(B, n, c, H, K)
    v = v.astype(f32).reshape(B, n, c, H, V)
    b = jnp.cumsum(log_a.astype(f32).reshape(B, n, c, H, K), axis=2)
    b_ref = b[:, :, (c - 1) // 2][:, :, None]
    b_last = b[:, :, -1][:, :, None]
    q_rel = q * jnp.exp(b - b_ref)
    k_rel = k * jnp.exp(b_ref - b)
    causal = jnp.tril(jnp.ones((c, c), dtype=bool))
    scores = jnp.where(causal, jnp.einsum('bnthk,bnshk->bnhts', q_rel, k_rel), 0.0)
    o = jnp.einsum('bnhts,bnshv->bnthv', scores, v)
    chunk_kv = jnp.einsum('bnshk,bnshv->bnhkv', k * jnp.exp(b_last - b), v)
    chunk_decay = jnp.exp(b_last[:, :, 0])

    def step(s, inp):
        d, kv_c = inp
        return d[..., None] * s + kv_c, s

    s_final, s_start = lax.scan(step, s0.astype(f32),
                                (jnp.moveaxis(chunk_decay, 1, 0), jnp.moveaxis(chunk_kv, 1, 0)))
    s_start = jnp.moveaxis(s_start, 0, 1)
    o = o + jnp.einsum('bnthk,bnhkv->bnthv', q * jnp.exp(b), s_start)
    return o.reshape(B, T, H, V), s_final


def even_mixer(h, pos, s_gla, s_ret, w_in, w_gate_up, b_gate, g_gla, g_ret, w_out):
    B, T, _ = h.shape
    q_a, k_a, v_a, r_a, lo_a, q_b, k_b, v_b, g_b = split_cols(
        h @ w_in, [H_A * DK_A, H_A * DK_A, H_A * DV_A, H_A * DV_A, GATE_RANK,
                   H_B * DK_B, H_B * DK_B, H_B * DV_B, H_B * DV_B])
    log_a = jax.nn.log_sigmoid((lo_a @ w_gate_up + b_gate).astype(jnp.float32)) / GATE_TAU
    o_a, s_gla_new = decay_linear_attention(
        q_a.reshape(B, T, H_A, DK_A) * DK_A ** -0.5, k_a.reshape(B, T, H_A, DK_A),
        v_a.reshape(B, T, H_A, DV_A), log_a.reshape(B, T, H_A, DK_A), s_gla)
    o_a = head_rms_norm(o_a, g_gla.reshape(H_A, DV_A)) * \
        jax.nn.silu(r_a.astype(jnp.float32)).reshape(B, T, H_A, DV_A)
    qr = rope(q_b.reshape(B, T, H_B, DK_B), pos)
    kr = rope(k_b.reshape(B, T, H_B, DK_B), pos) * DK_B ** -0.5
    log_gamma = jnp.log1p(-jnp.exp2(-5.0 - jnp.arange(H_B, dtype=jnp.float32)))
    log_g = jnp.broadcast_to(log_gamma[:, None], (B, T, H_B, DK_B))
    o_b, s_ret_new = decay_linear_attention(qr, kr, v_b.reshape(B, T, H_B, DV_B), log_g, s_ret)
    o_b = head_group_norm(o_b, g_ret.reshape(H_B, DV_B)) * \
        jax.nn.silu(g_b.astype(jnp.float32)).reshape(B, T, H_B, DV_B)
    mix = jnp.concatenate([o_a.reshape(B, T, -1), o_b.reshape(B, T, -1)], axis=-1).astype(h.dtype)
    return mix @ w_out, s_gla_new, s_ret_new


def mla_attention(q_nope, q_rope, ckv, krope, q_pos, k_pos, w_uk, w_uv):
    B, T, H, _ = q_nope.shape
    qb = min(Q_BLOCK, T)
    nb = T // qb
    k_chunk = k_pos // CHUNK
    scale = (NOPE + ROPE) ** -0.5

    def block(args):
        qn, qr, qp = args
        q_lat = jnp.einsum('bqhn,lhn->bqhl', qn, w_uk)
        s = jnp.einsum('bqhl,bsl->bhqs', q_lat, ckv) + jnp.einsum('bqhr,bsr->bhqs', qr, krope)
        s = s.astype(jnp.float32) * scale
        mask = k_chunk[None, :] <= (qp // CHUNK)[:, None]
        p = jax.nn.softmax(jnp.where(mask, s, -jnp.inf), axis=-1).astype(ckv.dtype)
        o_lat = jnp.einsum('bhqs,bsl->bqhl', p, ckv)
        return jnp.einsum('bqhl,lhv->bqhv', o_lat, w_uv)

    def to_blocks(a):
        return jnp.moveaxis(a.reshape((B, nb, qb) + a.shape[2:]), 1, 0)

    o = lax.map(block, (to_blocks(q_nope), to_blocks(q_rope), q_pos.reshape(nb, qb)))
    return jnp.moveaxis(o, 0, 1).reshape(B, T, H, V_C)


def odd_mixer(h, pos, ckv_past, kr_past, w_in, g_q, g_kv, w_uq, w_uk, w_uv, w_out):
    B, T, _ = h.shape
    cq, ckv, kr = split_cols(h @ w_in, [Q_LORA, KV_LORA, ROPE])
    cq = rms_norm(cq, g_q)
    ckv = rms_norm(ckv, g_kv)
    kr = rope(kr, pos)
    q = (cq @ w_uq).reshape(B, T, H_C, NOPE + ROPE)
    q_nope = q[..., :NOPE]
    q_rope = rope(q[..., NOPE:], pos)
    if ckv_past is None:
        ckv_all, kr_all, k_pos = ckv, kr, pos
    else:
        ckv_all = jnp.concatenate([ckv_past.astype(ckv.dtype), ckv], axis=1)
        kr_all = jnp.concatenate([kr_past.astype(kr.dtype), kr], axis=1)
        k_pos = jnp.concatenate([jnp.arange(ckv_past.shape[1], dtype=jnp.int32), pos])
    o = mla_attention(q_nope, q_rope, ckv_all, kr_all, pos, k_pos, w_uk, w_uv)
    return o.reshape(B, T, H_C * V_C) @ w_out, ckv, kr


def conv_ffn(h, w_in, w_dw, b_dw, w_out, conv_prev):
    T = h.shape[1]
    a, u = split_cols(h @ w_in, [D_FF, D_FF])
    a_full = jnp.concatenate([conv_prev.astype(a.dtype), a], axis=1)
    c = b_dw + a_full[:, 0:T] * w_dw[0]
    for j in range(1, CONV_W):
        c = c + a_full[:, j:j + T] * w_dw[j]
    act = jax.nn.gelu(c.astype(jnp.float32), approximate=False) * u.astype(jnp.float32)
    return act.astype(h.dtype) @ w_out, a_full[:, T:]


def run_trunk(x, pos0, gla0, ret0, ckv_past, kr_past, conv0, w):
    B, T, _ = x.shape
    pos = pos0 + jnp.arange(T, dtype=jnp.int32)
    gla_new, ret_new, ckv_new, kr_new, conv_new = [], [], [], [], []
    for layer in range(DEPTH):
        i = layer // 2
        h = rms_norm(x, w['norm_mix'][layer])
        if layer % 2 == 0:
            s_gla = jnp.zeros((B, H_A, DK_A, DV_A), jnp.float32) if gla0 is None else gla0[i]
            s_ret = jnp.zeros((B, H_B, DK_B, DV_B), jnp.float32) if ret0 is None else ret0[i]
            out, s_gla, s_ret = even_mixer(h, pos, s_gla, s_ret, w['w_in_ab'][i], w['w_gate_up'][i],
                                           w['b_gate'][i], w['g_gla'][i], w['g_ret'][i], w['w_out_ab'][i])
            gla_new.append(s_gla.astype(x.dtype))
            ret_new.append(s_ret.astype(x.dtype))
        else:
            past_c = None if ckv_past is None else ckv_past[i]
            past_r = None if kr_past is None else kr_past[i]
            out, ckv, kr = odd_mixer(h, pos, past_c, past_r, w['w_in_c'][i], w['g_q'][i], w['g_kv'][i],
                                     w['w_uq'][i], w['w_uk'][i], w['w_uv'][i], w['w_out_c'][i])
            ckv_new.append(ckv)
            kr_new.append(kr)
        x = x + out.astype(x.dtype)
        h = rms_norm(x, w['norm_ffn'][layer])
        prev = jnp.zeros((B, CONV_W - 1, D_FF), x.dtype) if conv0 is None else conv0[layer]
        out, conv_rows = conv_ffn(h, w['w_ffn_in'][layer], w['w_dwconv'][layer], w['b_dwconv'][layer],
                                  w['w_ffn_out'][layer], prev)
        x = x + out.astype(x.dtype)
        conv_new.append(conv_rows)
    y = rms_norm(x, w['norm_final'])
    return (y, jnp.stack(gla_new), jnp.stack(ret_new), jnp.stack(ckv_new),
            jnp.stack(kr_new), jnp.stack(conv_new))


def setup_inputs(seed: int = 0) -> dict:
    key = jax.random.key(seed)
    ks = iter(jax.random.split(key, 40))

    def nrm(shape, scale):
        return jax.random.normal(next(ks), shape, jnp.float32) * scale

    def gain(shape):
        return 1.0 + 0.1 * jax.random.normal(next(ks), shape, jnp.float32)

    return {
        'x_prompt': nrm((BATCH, SEQ, D_MODEL), 1.0),
        'x_sample': nrm((DEC_BATCH, DEC_SEQ, D_MODEL), 1.0),
        'state_gla': nrm((N_EVEN, DEC_BATCH, H_A, DK_A, DV_A), 0.5),
        'state_ret': nrm((N_EVEN, DEC_BATCH, H_B, DK_B, DV_B), 1.0),
        'cache_ckv': nrm((N_ODD, DEC_BATCH, PAST_LEN, KV_LORA), 1.0),
        'cache_krope': nrm((N_ODD, DEC_BATCH, PAST_LEN, ROPE), 1.0),
        'state_conv': nrm((DEPTH, DEC_BATCH, CONV_W - 1, D_FF), 1.0),
        'norm_mix': gain((DEPTH, D_MODEL)),
        'norm_ffn': gain((DEPTH, D_MODEL)),
        'norm_final': gain((D_MODEL,)),
        'w_in_ab': nrm((N_EVEN, D_MODEL, IN_AB), D_MODEL ** -0.5),
        'w_gate_up': nrm((N_EVEN, GATE_RANK, H_A * DK_A), GATE_RANK ** -0.5),
        'b_gate': nrm((N_EVEN, H_A * DK_A), 0.1),
        'g_gla': gain((N_EVEN, H_A * DV_A)),
        'g_ret': gain((N_EVEN, H_B * DV_B)),
        'w_out_ab': nrm((N_EVEN, MIX_AB, D_MODEL), MIX_AB ** -0.5),
        'w_in_c': nrm((N_ODD, D_MODEL, IN_C), D_MODEL ** -0.5),
        'g_q': gain((N_ODD, Q_LORA)),
        'g_kv': gain((N_ODD, KV_LORA)),
        'w_uq': nrm((N_ODD, Q_LORA, H_C * (NOPE + ROPE)), Q_LORA ** -0.5),
        'w_uk': nrm((N_ODD, KV_LORA, H_C, NOPE), KV_LORA ** -0.5),
        'w_uv': nrm((N_ODD, KV_LORA, H_C, V_C), KV_LORA ** -0.5),
        'w_out_c': nrm((N_ODD, H_C * V_C, D_MODEL), (H_C * V_C) ** -0.5),
        'w_ffn_in': nrm((DEPTH, D_MODEL, 2 * D_FF), D_MODEL ** -0.5),
        'w_dwconv': nrm((DEPTH, CONV_W, D_FF), CONV_W ** -0.5),
        'b_dwconv': nrm((DEPTH, D_FF), 0.02),
        'w_ffn_out': nrm((DEPTH, D_FF, D_MODEL), D_FF ** -0.5),
    }


def reference(x_prompt, x_sample, state_gla, state_ret, cache_ckv, cache_krope, state_conv,
              norm_mix, norm_ffn, norm_final, w_in_ab, w_gate_up, b_gate, g_gla, g_ret, w_out_ab,
              w_in_c, g_q, g_kv, w_uq, w_uk, w_uv, w_out_c, w_ffn_in, w_dwconv, b_dwconv, w_ffn_out):
    w = {'norm_mix': norm_mix, 'norm_ffn': norm_ffn, 'norm_final': norm_final,
         'w_in_ab': w_in_ab, 'w_gate_up': w_gate_up, 'b_gate': b_gate, 'g_gla': g_gla,
         'g_ret': g_ret, 'w_out_ab': w_out_ab, 'w_in_c': w_in_c, 'g_q': g_q, 'g_kv': g_kv,
         'w_uq': w_uq, 'w_uk': w_uk, 'w_uv': w_uv, 'w_out_c': w_out_c, 'w_ffn_in': w_ffn_in,
         'w_dwconv': w_dwconv, 'b_dwconv': b_dwconv, 'w_ffn_out': w_ffn_out}
    y_prompt, gla_p, ret_p, ckv_p, kr_p, conv_p = run_trunk(
        x_prompt, 0, None, None, None, None, None, w)
    y_sample, gla_s, ret_s, ckv_s, kr_s, conv_s = run_trunk(
        x_sample, PAST_LEN, state_gla, state_ret, cache_ckv, cache_krope, state_conv, w)
    return (y_prompt, y_sample, gla_p, gla_s, ret_p, ret_s, ckv_p, ckv_s, kr_p, kr_s, conv_p, conv_s)
```

```python
import contextlib
import math
import os

import numpy as np
import concourse.bass as bass
import concourse.mybir as mybir
from concourse.bass_utils import run_bass_kernel_spmd

F32 = mybir.dt.float32
BF16 = mybir.dt.bfloat16
AF = mybir.ActivationFunctionType
ALU = mybir.AluOpType
AX = mybir.AxisListType

EPS = 1e-6
D_FF = 2816
NJ = 22
STAGE = int(os.environ.get("MK_STAGE", "99"))
NSEQ_P = int(os.environ.get("MK_NSEQ", "2"))
STOPAT = int(os.environ.get("MK_STOP", "0"))


class _Stop(Exception):
    pass


def ck(k):
    if STOPAT == k:
        raise _Stop()

COMPUTE = ("pe", "act", "dve", "pool")
DMA_K = 8
EPOCH = 6000


class Tok:
    __slots__ = ("w", "rs", "excl")

    def __init__(self, excl=False):
        self.w = None
        self.rs = []
        self.excl = excl


class Ins:
    __slots__ = ("eng", "fn", "deps", "dma", "signal", "ev", "prev_ev")

    def __init__(self, eng, fn, dma):
        self.eng = eng
        self.fn = fn
        self.dma = dma
        self.deps = []
        self.signal = False
        self.ev = None
        self.prev_ev = None


class Prog:
    def __init__(self, nc):
        self.nc = nc
        self.es = contextlib.ExitStack()
        self.streams = {e: [] for e in ("pe", "act", "dve", "pool", "sp")}
        self.pending = {e: [] for e in self.streams}
        self.dmas = []
        self.n = 0

    def sb(self, shape, dt, name="t"):
        self.n += 1
        return self.es.enter_context(self.nc.sbuf_tensor(f"{name}{self.n}", list(shape), dt))

    def ps(self, shape, dt, name="p"):
        self.n += 1
        return self.es.enter_context(self.nc.psum_tensor(f"{name}{self.n}", list(shape), dt))

    def op(self, eng, fn, R=(), W=(), dma=False, force=()):
        ins = Ins(eng, fn, dma)
        deps = {id(d): d for d in force}

        def same(d):
            return (not d.dma) and (not dma) and d.eng == eng

        def readers(t):
            seen = set()
            for r in reversed(t.rs):
                if r.dma:
                    yield r
                elif r.eng not in seen:
                    seen.add(r.eng)
                    if not (r.eng == eng and eng == "pe" and not dma):
                        yield r

        for t in R:
            d = t.w
            if d is not None and not (same(d) and eng == "pe"):
                deps[id(d)] = d
            if t.excl:
                for r in readers(t):
                    if not same(r):
                        deps[id(r)] = r
        for t in W:
            d = t.w
            if d is not None and not (same(d) and eng == "pe"):
                deps[id(d)] = d
            for r in readers(t):
                deps[id(r)] = r
        for t in R:
            t.rs.append(ins)
        for t in W:
            t.w = ins
            t.rs = []
        if self.pending[eng]:
            for d in self.pending[eng]:
                deps[id(d)] = d
            self.pending[eng] = []
        ins.deps = list(deps.values())
        for d in ins.deps:
            d.signal = True
        if dma:
            ins.signal = True
            self.dmas.append(ins)
        self.streams[eng].append(ins)
        return ins

    def dma(self, q, out, in_, R=(), W=(), **kw):
        return self.op(q, lambda e: e.dma_start(out=out, in_=in_, **kw), R=R, W=W, dma=True)

    def barrier(self):
        deps = [st[-1] for st in self.streams.values() if st] + self.dmas
        for d in deps:
            d.signal = True
        for e in self.pending:
            self.pending[e] = self.pending[e] + deps
        self.dmas = []

    def emit(self):
        nc = self.nc
        es = self.es
        sem_dma = {q: [es.enter_context(nc.semaphore(f"semd_{q}{i}")) for i in range(DMA_K)]
                   for q in ("sp", "act", "pool")}
        for e, st in self.streams.items():
            cnt = 0
            nd = 0
            sem = None
            for ins in st:
                if ins.dma:
                    j = nd % DMA_K
                    rnd = nd // DMA_K
                    ins.ev = (sem_dma[e][j], 16 * (rnd + 1))
                    ins.prev_ev = (sem_dma[e][j], 16 * rnd) if rnd > 0 else None
                    nd += 1
                elif ins.signal:
                    if cnt % EPOCH == 0:
                        sem = es.enter_context(nc.semaphore(f"sem_{e}{cnt // EPOCH}"))
                    cnt += 1
                    ins.ev = (sem, (cnt - 1) % EPOCH + 1)
        self.sigcount = {e: sum(1 for i in st if (not i.dma) and i.signal) for e, st in self.streams.items()}
        final_dma = []
        for q in ("sp", "act", "pool"):
            last = {}
            for ins in self.streams[q]:
                if ins.dma:
                    last[id(ins.ev[0])] = ins.ev
            final_dma += list(last.values())

        def run(engobj, st, is_sp):
            waited = {}

            def wait(ev):
                sem, val = ev
                k = id(sem)
                if waited.get(k, 0) < val:
                    engobj.wait_ge(sem, val)
                    waited[k] = val

            for ins in st:
                for d in ins.deps:
                    wait(d.ev)
                if ins.dma and ins.prev_ev is not None:
                    wait(ins.prev_ev)
                bi = ins.fn(engobj)
                if ins.dma:
                    bi.then_inc(ins.ev[0], 16)
                elif ins.signal:
                    bi.then_inc(ins.ev[0], 1)
            if is_sp:
                for ev in final_dma:
                    wait(ev)

        block = es.enter_context(nc.Block())
        S = self.streams

        @block.tensor
        def _(e):
            run(e, S["pe"], False)

        @block.scalar
        def _(e):
            run(e, S["act"], False)

        @block.vector
        def _(e):
            run(e, S["dve"], False)

        @block.gpsimd
        def _(e):
            run(e, S["pool"], False)

        @block.sync
        def _(e):
            run(e, S["sp"], True)

    def close(self):
        self.es.close()


class Arena:
    def __init__(self, P, nwords):
        self.t = P.sb([128, nwords], F32, "arena")
        self.n = nwords
        self.off = 0
        self.peak = 0

    def _take(self, words):
        o = self.off
        self.off += words
        self.peak = max(self.peak, self.off)
        assert self.off <= self.n, f"arena overflow {self.off} > {self.n}"
        return o

    def f32(self, *shape):
        n = int(np.prod(shape))
        o = self._take(n)
        ap = self.t[:, o:o + n]
        return self._shape(ap, shape)

    def bf16(self, *shape):
        n = int(np.prod(shape))
        w = (n + 1) // 2
        o = self._take(w)
        ap = self.t[:, o:o + w].bitcast(BF16)
        if 2 * w != n:
            ap = ap[:, 0:n]
        return self._shape(ap, shape)

    @staticmethod
    def _shape(ap, shape):
        if len(shape) == 1:
            return ap
        if len(shape) == 2:
            return ap.rearrange("p (a b) -> p a b", b=shape[1])
        if len(shape) == 3:
            return ap.rearrange("p (a b c) -> p a b c", b=shape[1], c=shape[2])
        raise ValueError(shape)

    def mark(self):
        return self.off

    def reset(self, m):
        self.off = m


class Rot:
    def __init__(self, items):
        self.items = [(it, Tok()) for it in items]
        self.i = 0

    def next(self):
        r = self.items[self.i % len(self.items)]
        self.i += 1
        return r


class Cfg:
    pass


def make_cfgs():
    p = Cfg()
    p.T, p.TT, p.NT, p.C, p.NCH, p.nseq, p.L, p.tab0, p.XB, p.krblk0 = 2048, 512, 4, 128, 4, 1, 512, 0, 128, 0
    p.prompt = True
    p.x0, p.xt0 = 0, 0
    s = Cfg()
    s.T, s.TT, s.NT, s.C, s.NCH, s.nseq, s.L, s.tab0, s.XB, s.krblk0 = 64, 64, 1, 16, 4, 4, 16, 2048, 64, 16
    s.prompt = False
    s.x0, s.xt0 = 2048, 4
    return p, s


def const_tables():
    half = 32
    freqs = (10000.0 ** (-np.arange(half, dtype=np.float32) / half)).astype(np.float32)
    pos = np.concatenate([np.arange(2048), 4096 + (np.arange(64) % 16)]).astype(np.float32)
    ang = pos[None, :] * freqs[:, None]
    ang = ang.astype(np.float32)
    cos = np.cos(ang).astype(np.float32)
    sin = np.sin(ang).astype(np.float32)
    p = np.arange(128)
    d = p % 64
    cosF = cos[d % 32, :]
    sinF = np.where((d < 32)[:, None], -sin[d % 32, :], sin[d % 32, :]).astype(np.float32)
    kc = np.zeros((128, 17, 64), np.float32)
    ks = np.zeros((128, 17, 64), np.float32)
    for blk in range(17):
        if blk < 16:
            pp = (blk * 128 + p).astype(np.float32)
        else:
            pp = (4096 + (p % 16)).astype(np.float32)
        a = (pp[:, None] * freqs[None, :]).astype(np.float32)
        c, s = np.cos(a).astype(np.float32), np.sin(a).astype(np.float32)
        kc[:, blk, :32] = c
        kc[:, blk, 32:] = c
        ks[:, blk, :32] = -s
        ks[:, blk, 32:] = s
    def ret_tabs(C, TT):
        nref = (C - 1) // 2 + 1
        E = np.zeros((128, 2, 2, TT), np.float32)
        cst = np.zeros((128, 2, 3), np.float32)
        i = np.arange(TT) % C
        for kcx in range(2):
            h = 2 * kcx + p // 64
            lg = np.log1p(-np.exp2(-5.0 - h.astype(np.float64)))
            E[:, kcx, 0, :] = np.exp((i[None, :] + 1 - nref) * lg[:, None])
            E[:, kcx, 1, :] = np.exp((nref - i[None, :] - 1) * lg[:, None]) * (64 ** -0.5)
            cst[:, kcx, 0] = np.exp(nref * lg)
            cst[:, kcx, 1] = np.exp((C - nref) * lg)
            cst[:, kcx, 2] = np.exp(C * lg)
        return E, cst
    Ep, cp = ret_tabs(128, 512)
    Es, cs = ret_tabs(16, 64)
    mask = (np.arange(128)[:, None] <= np.arange(128)[None, :]).astype(np.float32)
    mask4 = np.tile(mask, (1, 4))
    resetp = np.ones((128, 512), np.float32)
    resetp[:, ::128] = 0.0
    resets = np.ones((128, 64), np.float32)
    resets[:, ::16] = 0.0
    ident = np.eye(128, dtype=np.float32)
    return dict(cosF=cosF, sinF=sinF, krc=kc, krs_=ks, retEp=Ep, retcp=cp, retEs=Es, retcs=cs,
                mask4=mask4, resetp=resetp, resets=resets, ident=ident)


_IN_SHAPES = dict(
    xp=[2, 2048, 1024], xs=[64, 1024], sgla=[4, 256, 128], sret=[4, 256, 128],
    cckv=[4, 4096, 512], ckr=[4, 4096, 64], sconv=[16, 2816], nrm=[5, 1024],
    w_in_ab=[1024, 3088], w_ab_sw=[1024, 512], wgu=[16, 256], bgate=[1, 256], ggla=[1, 512],
    gret=[1, 512], w_out_ab=[1024, 1024], w_in_c=[1024, 960], gq=[1, 384], gkv=[1, 512],
    wuq_n=[384, 1024], wuq_r=[384, 512], wuq_rs=[384, 512], wuk=[512, 1024], wukT=[128, 8, 512],
    wuv=[512, 1024], w_out_c=[1024, 1024], wffi=[2, 1024, 5632], dwc=[8, 2816],
    wffo=[2, 2816, 1024],
    cosF=[128, 2112], sinF=[128, 2112], krc=[128, 17, 64], krs_=[128, 17, 64],
    retEp=[128, 2, 2, 512], retcp=[128, 2, 3], retEs=[128, 2, 2, 64], retcs=[128, 2, 3],
    mask4=[128, 512], resetp=[128, 512], resets=[128, 64], ident=[128, 128],
)
_OUT_SHAPES = dict(
    yp=[2, 2048, 1024], ys=[64, 1024], glap=[2, 256, 128], glas=[4, 256, 128],
    retp=[2, 256, 128], rets=[4, 256, 128], ckvp=[2, 2048, 512], ckvs=[64, 512],
    krp=[2, 2048, 64], krs=[64, 64], convp=[2, 2, 2, 2816], convs=[2, 4, 2, 2816],
)


def build_program():
    nc = bass.Bass("TRN2", target_bir_lowering=False)
    D = {k: nc.dram_tensor(k, list(v), F32, kind="ExternalInput").ap() for k, v in _IN_SHAPES.items()}
    O = {k: nc.dram_tensor(k, list(v), F32, kind="ExternalOutput").ap() for k, v in _OUT_SHAPES.items()}
    P = Prog(nc)
    A = Arena(P, 52900)
    PB = [P.ps([128, 512], F32, "bank") for _ in range(8)]
    TPB = [Tok(excl=True) for _ in range(8)]
    gen = {"banks": [0, 1, 2, 3, 4, 5], "i": 0}

    def gb():
        b = gen["banks"][gen["i"] % len(gen["banks"])]
        gen["i"] += 1
        return PB[b], TPB[b]

    def bfv(bank):
        return bank[:].bitcast(BF16)

    def mm(out, lhsT, rhs, R, W, st=True, sp=True, force=()):
        return P.op("pe", lambda e: e.matmul(out, lhsT, rhs, start=st, stop=sp), R, W, force=force)

    def tr(out, in_, idn, R, W):
        P.op("pe", lambda e: e.transpose(out, in_, idn), R, W)

    def act(out, in_, func, R, W, bias=None, scale=None, accum=None):
        kw = {}
        if bias is not None:
            kw["bias"] = bias
        if scale is not None:
            kw["scale"] = scale
        if accum is not None:
            kw["accum_out"] = accum
        P.op("act", lambda e: e.activation(out=out, in_=in_, func=func, **kw), R, W)

    def tt(out, a, b, op, R, W, eng="dve"):
        P.op(eng, lambda e: e.tensor_tensor(out=out, in0=a, in1=b, op=op), R, W)

    def ts(out, a, s1, op0, R, W, s2=None, op1=None, eng="dve"):
        if op1 is None:
            P.op(eng, lambda e: e.tensor_scalar(out, a, s1, None, op0=op0), R, W)
        else:
            P.op(eng, lambda e: e.tensor_scalar(out, a, s1, s2, op0=op0, op1=op1), R, W)

    def stt(out, in0, scalar, in1, op0, op1, R, W, eng="dve"):
        P.op(eng, lambda e: e.scalar_tensor_tensor(out=out, in0=in0, scalar=scalar, in1=in1,
                                                    op0=op0, op1=op1), R, W)

    def cp(out, in_, R, W, eng="dve"):
        if eng == "act":
            act(out, in_, AF.Copy, R, W)
        else:
            P.op(eng, lambda e: e.tensor_copy(out=out, in_=in_), R, W)

    def memset(ap, val, W, eng="dve"):
        P.op(eng, lambda e: e.memset(ap, val), (), W)

    def load(dst, src, q="sp"):
        t = Tok()
        P.dma(q, dst, src, W=[t])
        return t

    MUL, ADD, SUB = ALU.mult, ALU.add, ALU.subtract

    xT = A.f32(8, 2048 + 64)
    t_x = [Tok() for _ in range(5)]
    ident_f = A.f32(128)
    ident_b = A.bf16(128)
    ones_b = A.bf16(128)
    gn = A.f32(8, 5)
    dw = A.f32(NJ, 8)
    gqc = A.f32(3)
    nbg = A.f32(2)
    t_idf = load(ident_f, D["ident"])
    t_idb = load(ident_b, D["ident"], "pool")
    t_one = Tok()
    memset(ones_b, 1.0, [t_one])
    t_const = [t_idf, t_idb, t_one]
    base_mark = A.mark()

    def rows_to_cols(src_rows, R_, n, dst, t_dst):
        m = A.mark()
        stage = A.f32(n * 128)
        t_s = load(stage[:R_, :], src_rows)
        bank, tb = gb()
        for c in range(n):
            tr(bank[:, c * R_:(c + 1) * R_], stage[:R_, c * 128:(c + 1) * 128], ident_f[:R_, :R_],
               [t_s, t_idf], [tb])
        if len(dst.shape) == 3:
            cp(dst, bank[:, 0:n * R_].rearrange("p (a b) -> p a b", b=R_), [tb], [t_dst])
        else:
            cp(dst, bank[:, 0:n * R_], [tb], [t_dst])
        P.barrier()
        A.reset(m)

    t_gn, t_dw, t_gq, t_nbg = Tok(), Tok(), Tok(), Tok()
    rows_to_cols(D["nrm"], 5, 8, gn, t_gn)
    rows_to_cols(D["dwc"], 8, NJ, dw, t_dw)
    rows_to_cols(D["gq"], 1, 3, gqc, t_gq)
    rows_to_cols(D["bgate"], 1, 2, nbg, t_nbg)
    ts(nbg, nbg, -1.0, MUL, [t_nbg], [t_nbg])

    def load_x(cfg, xd):
        m = A.mark()
        xin = Rot([A.f32(1024) for _ in range(2)])
        n = cfg.XB
        for blk in range(cfg.T // n):
            xi, txi = xin.next()
            P.dma("sp", xi[:n, :], xd[blk * n:(blk + 1) * n, :], W=[txi])
            ttile = (blk * n) // cfg.TT
            for half in range(2):
                bank, tb = gb()
                for c4 in range(4):
                    c = half * 4 + c4
                    tr(bank[:, c4 * n:(c4 + 1) * n], xi[:n, c * 128:(c + 1) * 128], ident_f[:n, :n],
                       [txi, t_idf], [tb])
                src = bank[:, 0:4 * n].rearrange("p (a b) -> p a b", b=n)
                dst = xT[:, half * 4:(half + 1) * 4, cfg.x0 + blk * n:cfg.x0 + (blk + 1) * n]
                cp(dst, src, [tb], [t_x[cfg.xt0 + ttile]], eng=("act" if half == 0 else "dve"))
        P.barrier()
        A.reset(m)

    def rmsnorm(cfg, tti, grow, hdst, t_h, sqrot, _unused, rs, t_rs):
        n = cfg.TT
        cols = slice(cfg.x0 + tti * n, cfg.x0 + (tti + 1) * n)
        t_xt = t_x[cfg.xt0 + tti]
        bank, tb = gb()
        for c in range(8):
            sqb, t_sq = sqrot.next()
            act(sqb[:, :n], xT[:, c, cols], AF.Square, [t_xt], [t_sq])
            mm(bank[:, :n], ones_b, sqb[:, :n], [t_sq, t_one], [tb], st=(c == 0), sp=(c == 7))
        act(rs[:, :n], bank[:, :n], AF.Ln, [tb], [t_rs], scale=1.0 / 1024, bias=EPS)
        act(rs[:, :n], rs[:, :n], AF.Exp, [t_rs], [t_rs], scale=-0.5)
        for c in range(8):
            stt(hdst[:, c, :n], xT[:, c, cols], gn[:, c, grow:grow + 1], rs[:, :n], MUL, MUL,
                [t_xt, t_rs, t_gn], [t_h])

    def rstd_small(dst, src, inv_n, R, W):
        act(dst, src, AF.Ln, R, W, scale=inv_n, bias=EPS)
        act(dst, dst, AF.Exp, W, W, scale=-0.5)

    def phase_mixer_ab(cfg, seq, s0_gla, s0_ret, out_gla, out_ret):
        m = A.mark()
        n, C, NCH = cfg.TT, cfg.C, cfg.NCH
        refi = (C - 1) // 2
        retE = A.f32(2, 2, n)
        retc = A.f32(2, 3)
        mask4 = A.f32(512)
        reset = A.f32(n)
        gglab = A.f32(512)
        gretb = A.f32(512)
        wgu = A.f32(256)
        Wlo = A.bf16(8, 16)
        t_tab = [load(retE, D["retEp"] if cfg.prompt else D["retEs"]),
                 load(retc, D["retcp"] if cfg.prompt else D["retcs"]),
                 load(mask4, D["mask4"]),
                 load(reset, D["resetp"] if cfg.prompt else D["resets"]),
                 load(gglab, D["ggla"].to_broadcast([128, 512])),
                 load(gretb, D["gret"].to_broadcast([128, 512])),
                 load(wgu[:16, :], D["wgu"]),
                 load(Wlo, D["w_in_ab"][:, 1536:1552].rearrange("(kc p) c -> p kc c", p=128), "pool")]
        cosr = Rot([A.f32(n) for _ in range(1)])
        sinr = Rot([A.f32(n) for _ in range(1)])
        sq, t_sq = Rot([A.bf16(n) for _ in range(2)]), None
        rs, t_rs = A.f32(n), Tok()
        hT2, t_h2 = [A.bf16(8, n), A.bf16(8, n)], [Tok(), Tok()]
        WA = Rot([A.bf16(8, 512) for _ in range(3 if cfg.prompt else 6)])
        vtok = [A.bf16(NCH, 512), A.bf16(NCH, 512)]
        t_v = [[Tok() for _ in range(NCH)] for _ in range(2)]
        gs = [A.bf16(NCH, 512), A.bf16(NCH, 512)]
        t_gs = [[Tok() for _ in range(NCH)] for _ in range(2)]
        srt = Rot([A.f32(512) for _ in range(1)])
        loT, t_lo = A.f32(n), Tok()
        ebuf, t_e = A.f32(n), Tok()
        spb, t_sp = A.f32(2, n), Tok()
        gate_region = (loT, ebuf, spb)
        cum, t_cum = A.f32(2, n), Tok()
        E1, E2 = A.f32(2, n), A.f32(2, n)
        t_E = [Tok(), Tok()]
        pbias, nbias, dd = A.f32(2, NCH), A.f32(2, NCH), A.f32(2, NCH)
        eref, elr, dec = A.f32(2, NCH), A.f32(2, NCH), A.f32(2, NCH)
        t_col = Tok()
        qrel, krel = A.bf16(4, n), A.bf16(4, n)
        t_q = [Tok() for _ in range(4)]
        t_k = [Tok() for _ in range(4)]
        r1, r2, r3 = ebuf, spb[:, 0, :], spb[:, 1, :]
        t_r = [t_e, t_sp, t_sp]
        kreltok, t_kt = A.bf16(4, 128), Tok()
        Sm2, t_sm2 = [A.bf16(8, 128), A.bf16(8, 128)], [[Tok(), Tok()], [Tok(), Tok()]]
        S, t_S = A.f32(4, 128), [Tok() for _ in range(4)]
        Sp2, t_Sp2 = [A.bf16(4, 128), A.bf16(4, 128)], [[Tok() for _ in range(4)] for _ in range(2)]
        tmpkv, t_tmp = A.f32(4, 128), [Tok() for _ in range(4)]
        mix2, t_mix2 = [A.bf16(1024), A.bf16(1024)], [Tok(), Tok()]
        tmpo, t_tmpo = A.f32(512), Tok()
        mixT, t_mixT = A.bf16(8, n), Tok()
        gen["banks"] = [0, 1, 2, 3]
        obank2 = [[(PB[6], TPB[6]), (PB[7], TPB[7])], [(PB[4], TPB[4]), (PB[5], TPB[5])]]
        stat2, t_ss2, t_st2 = [A.f32(16), A.f32(16)], [[Tok() for _ in range(12)] for _ in range(2)], [Tok(), Tok()]
        junk8 = A.bf16(8, 128)

        def wpiece(src):
            w, tw = WA.next()
            ncol = src.shape[1]
            P.dma("pool", w[:, :, :ncol], src.rearrange("(kc p) c -> p kc c", p=128), W=[tw])
            return w, tw

        if cfg.prompt:
            for u in range(4):
                memset(S[:, u, :], 0.0, [t_S[u]])

        for tti in range(cfg.NT):
            cols = slice(tti * n, (tti + 1) * n)
            tcol = slice(cfg.tab0 + tti * n, cfg.tab0 + (tti + 1) * n)
            cos_t, t_cos = cosr.next()
            sin_t, t_sin = sinr.next()
            P.dma("sp", cos_t, D["cosF"][:, tcol], W=[t_cos])
            P.dma("sp", sin_t, D["sinF"][:, tcol], W=[t_sin])
            hT, t_h = hT2[tti % 2], t_h2[tti % 2]
            if tti == 0:
                rmsnorm(cfg, 0, 0, hT, t_h, sq, t_sq, rs, t_rs)
            ck(1)
            bank, tb = gb()
            for kc in range(8):
                mm(bank[:16, :n], Wlo[:, kc, :], hT[:, kc, :n], [t_h, t_tab[7]], [tb], st=(kc == 0), sp=(kc == 7))
            cp(loT[:16, :], bank[:16, :n], [tb], [t_lo], eng="act")
            for kc in range(2):
                bank, tb = gb()
                mm(bank[:, :n], wgu[:16, kc * 128:(kc + 1) * 128], loT[:16, :], [t_lo, t_tab[6]], [tb])
                act(ebuf, bank[:, :n], AF.Exp, [tb, t_nbg], [t_e], scale=-1.0, bias=nbg[:, kc:kc + 1])
                act(spb[:, kc, :], ebuf, AF.Ln, [t_e], [t_sp], bias=1.0)
                P.op("dve", lambda e, kc=kc: e.tensor_tensor_scan(out=cum[:, kc, :], data0=reset, data1=spb[:, kc, :],
                                                                   initial=0.0, op0=MUL, op1=ADD),
                     [t_sp, t_tab[3]], [t_cum])
            ck(3)
            for pi, c0 in enumerate((512, 1024, 2064, 2576)):
                w, tw = wpiece(D["w_in_ab"][:, c0:c0 + 512])
                grp = pi // 2
                for ci in range(NCH):
                    bank, tb = gb()
                    for kc in range(8):
                        mm(bank[:C, :], hT[:, kc, ci * C:(ci + 1) * C], w[:, kc, :], [t_h, tw], [tb],
                           st=(kc == 0), sp=(kc == 7))
                    if pi % 2 == 0:
                        cp(vtok[grp][:C, ci, :], bank[:C, :], [tb], [t_v[grp][ci]], eng="act")
                    else:
                        sr, tsr = srt.next()
                        act(sr[:C, :], bank[:C, :], AF.Silu, [tb], [tsr])
                        tt(gs[grp][:C, ci, :], sr[:C, :], (gglab if grp == 0 else gretb)[:C, :], MUL,
                           [tsr, t_tab[4 + grp]], [t_gs[grp][ci]])
            ck(2)
            cview = cum.rearrange("p k (c i) -> p k c i", i=C)
            cref = cview[:, :, :, refi]
            clast = cview[:, :, :, C - 1]
            ts(pbias, cref, 1.0 / 16, MUL, [t_cum], [t_col])
            ts(nbias, cref, -1.0 / 16, MUL, [t_cum], [t_col])
            tt(dd, clast, cref, SUB, [t_cum], [t_col])
            act(eref, cref, AF.Exp, [t_cum], [t_col], scale=-1.0 / 16)
            act(elr, dd, AF.Exp, [t_col], [t_col], scale=-1.0 / 16)
            act(dec, clast, AF.Exp, [t_cum], [t_col], scale=-1.0 / 16)
            for kc in range(2):
                for ci in range(NCH):
                    cs_ = slice(ci * C, (ci + 1) * C)
                    act(E1[:, kc, cs_], cum[:, kc, cs_], AF.Exp, [t_cum, t_col], [t_E[0]],
                        scale=-1.0 / 16, bias=pbias[:, kc, ci:ci + 1])
                    act(E2[:, kc, cs_], cum[:, kc, cs_], AF.Exp, [t_cum, t_col], [t_E[1]],
                        scale=1.0 / 16, bias=nbias[:, kc, ci:ci + 1])
            ck(4)
            w, tw = wpiece(D["w_in_ab"][:, 0:512])
            for j in range(4):
                bank, tb = gb()
                for kc in range(8):
                    mm(bank[:, :n], w[:, kc, j * 128:(j + 1) * 128], hT[:, kc, :n], [t_h, tw], [tb],
                       st=(kc == 0), sp=(kc == 7))
                kcx = j % 2
                if j < 2:
                    stt(qrel[:, kcx, :], bank[:, :n], 0.125, E1[:, kcx, :], MUL, MUL, [tb, t_E[0]], [t_q[kcx]])
                else:
                    tt(krel[:, kcx, :], bank[:, :n], E2[:, kcx, :], MUL, [tb, t_E[1]], [t_k[kcx]])
            w1, tw1 = wpiece(D["w_in_ab"][:, 1552:2064])
            w2, tw2 = wpiece(D["w_ab_sw"])
            for j in range(4):
                bank1, tb1 = gb()
                for kc in range(8):
                    mm(bank1[:, :n], w1[:, kc, j * 128:(j + 1) * 128], hT[:, kc, :n], [t_h, tw1], [tb1],
                       st=(kc == 0), sp=(kc == 7))
                bank2, tb2 = gb()
                for kc in range(8):
                    mm(bank2[:, :n], w2[:, kc, j * 128:(j + 1) * 128], hT[:, kc, :n], [t_h, tw2], [tb2],
                       st=(kc == 0), sp=(kc == 7))
                kcx = j % 2
                tt(r1, bank1[:, :n], cos_t, MUL, [tb1, t_cos], [t_r[0]])
                tt(r2, bank2[:, :n], sin_t, MUL, [tb2, t_sin], [t_r[1]])
                tt(r3, r1, r2, ADD, [t_r[0], t_r[1]], [t_r[2]])
                if j < 2:
                    tt(qrel[:, 2 + kcx, :], r3, retE[:, kcx, 0, :], MUL, [t_r[2], t_tab[0]], [t_q[2 + kcx]])
                else:
                    tt(krel[:, 2 + kcx, :], r3, retE[:, kcx, 1, :], MUL, [t_r[2], t_tab[0]], [t_k[2 + kcx]])
            ck(5)
            if tti + 1 < cfg.NT:
                rmsnorm(cfg, tti + 1, 0, hT2[(tti + 1) % 2], t_h2[(tti + 1) % 2], sq, t_sq, rs, t_rs)

            def chunk_front(ci):
                    Sm, t_sm, Sp, t_Sp = Sm2[ci % 2], t_sm2[ci % 2], Sp2[ci % 2], t_Sp2[ci % 2]
                    g = tti * NCH + ci
                    cs_ = slice(ci * C, (ci + 1) * C)
                    first = (not cfg.prompt) or g == 0
                    last = (not cfg.prompt) or g == cfg.NT * NCH - 1
                    if not cfg.prompt:
                        for u in range(4):
                            src = (s0_gla if u < 2 else s0_ret)[ci, (u % 2) * 128:(u % 2 + 1) * 128, :]
                            P.dma("sp", S[:, u, :], src, W=[t_S[u]])
                    bank, tb = gb()
                    bv = bfv(bank)
                    for u in range(4):
                        tr(bv[:C, u * 128:(u + 1) * 128], krel[:, u, cs_], ident_b, [t_k[u], t_idb], [tb])
                    cp(kreltok[:C, :, :], bv[:C, 0:512].rearrange("p (a b) -> p a b", b=128), [tb], [t_kt], eng="act")
                    for u in range(4):
                        sc = eref[:, u, ci:ci + 1] if u < 2 else retc[:, u - 2, 0:1]
                        act(Sp[:, u, :], S[:, u, :], AF.Copy, [t_S[u], t_col, t_tab[1]], [t_Sp[u]], scale=sc)
                    for half in range(2):
                        bank, tb = gb()
                        for uu in range(2):
                            u = half * 2 + uu
                            grp, kcx = u // 2, u % 2
                            mm(bank[:, uu * 256:(uu + 1) * 256], kreltok[:C, u, :],
                               vtok[grp][:C, ci, kcx * 256:(kcx + 1) * 256], [t_kt, t_v[grp][ci]], [tb])
                        for uu in range(2):
                            u = half * 2 + uu
                            kcx = u % 2
                            e_lr = elr[:, kcx, ci:ci + 1] if u < 2 else retc[:, kcx, 1:2]
                            e_dc = dec[:, kcx, ci:ci + 1] if u < 2 else retc[:, kcx, 2:3]
                            ts(tmpkv[0:64, u, :], bank[0:64, uu * 256:uu * 256 + 128], e_lr[0:64, :], MUL,
                               [tb, t_col, t_tab[1]], [t_tmp[u]])
                            ts(tmpkv[64:128, u, :], bank[64:128, uu * 256 + 128:uu * 256 + 256], e_lr[64:128, :], MUL,
                               [tb, t_col, t_tab[1]], [t_tmp[u]])
                            stt(S[:, u, :], S[:, u, :], e_dc, tmpkv[:, u, :], MUL, ADD,
                                [t_S[u], t_tmp[u], t_col, t_tab[1]], [t_S[u]])
                            if last:
                                dst = (out_gla if u < 2 else out_ret)
                                dsti = seq if cfg.prompt else ci
                                P.dma("sp", dst[dsti, kcx * 128:(kcx + 1) * 128, :], S[:, u, :], R=[t_S[u]])
                    v3 = lambda ap: ap.rearrange("p (h c) -> p h c", c=128)[:C, :, :C]
                    for par in range(2):
                        bank, tb = gb()
                        pr = slice(par * 64, par * 64 + 64)
                        for slot in range(4):
                            grp = slot // 2
                            u = grp * 2 + slot % 2
                            mm(bank[:C, slot * 128:slot * 128 + C], krel[pr, u, cs_], qrel[pr, u, cs_],
                               [t_k[u], t_q[u]], [tb])
                        tt(Sm[:C, par * 4:(par + 1) * 4, :C], v3(bank[:, :]), v3(mask4[:, :]), MUL,
                           [tb, t_tab[2]], [t_sm[par]])
            def chunk_back(ci):
                    Sm, t_sm, Sp, t_Sp = Sm2[ci % 2], t_sm2[ci % 2], Sp2[ci % 2], t_Sp2[ci % 2]
                    obank = obank2[ci % 2]
                    (oA, t_oA), (oB, t_oB) = obank
                    mix, t_mix = mix2[ci % 2], t_mix2[ci % 2]
                    cs_ = slice(ci * C, (ci + 1) * C)
                    for grp in range(2):
                        ob, tob = obank[grp]
                        for hh in range(4):
                            u = grp * 2 + hh // 2
                            par = hh % 2
                            pr = slice(par * 64, par * 64 + 64)
                            smi = par * 4 + grp * 2 + hh // 2
                            i1 = mm(ob[:C, hh * 128:(hh + 1) * 128], Sm[:C, smi, :C],
                                    vtok[grp][:C, ci, hh * 128:(hh + 1) * 128], [t_sm[par], t_v[grp][ci]], [tob],
                                    st=True, sp=False)
                            mm(ob[:C, hh * 128:(hh + 1) * 128], qrel[pr, u, cs_], Sp[pr, u, :],
                               [t_q[u], t_Sp[u]], [tob], st=False, sp=True, force=([i1] if C < 64 else ()))
                    stat, t_ss, t_st = stat2[ci % 2], t_ss2[ci % 2], t_st2[ci % 2]
                    for grp in range(2):
                        ob, tob = obank[grp]
                        for hh in range(4):
                            k8 = grp * 4 + hh
                            act(junk8[:C, k8, :], ob[:C, hh * 128:(hh + 1) * 128], AF.Square, [tob], [t_ss[k8]],
                                accum=stat[:C, k8:k8 + 1])
                    for hh in range(4):
                        act(junk8[:C, hh, :], oB[:C, hh * 128:(hh + 1) * 128], AF.Copy, [t_oB, t_ss[hh]], [t_ss[hh], t_ss[8 + hh]],
                            accum=stat[:C, 8 + hh:9 + hh])
                    stt(stat[:C, 12:16], stat[:C, 8:12], -1.0 / 128, stat[:C, 8:12], MUL, MUL, [t_st] + t_ss[8:12], [t_st])
                    tt(stat[:C, 4:8], stat[:C, 4:8], stat[:C, 12:16], ADD, [t_st] + t_ss[4:8], [t_st])
                    ts(stat[:C, 8:12], stat[:C, 8:12], 1.0 / 128, MUL, [t_st] + t_ss[8:12], [t_st] + t_ss[8:12])
            def chunk_back2(ci):
                    obank = obank2[ci % 2]
                    (oA, t_oA), (oB, t_oB) = obank
                    mix, t_mix = mix2[ci % 2], t_mix2[ci % 2]
                    stat, t_ss, t_st = stat2[ci % 2], t_ss2[ci % 2], t_st2[ci % 2]
                    rstd_small(stat[:C, 0:8], stat[:C, 0:8], 1.0 / 128, [t_st] + t_ss[0:4], [t_st])
                    for hh in range(4):
                        hs = slice(hh * 128, (hh + 1) * 128)
                        stt(mix[:C, hs], oA[:C, hs], stat[:C, hh:hh + 1], gs[0][:C, ci, hs], MUL, MUL,
                            [t_oA, t_st, t_gs[0][ci]], [t_mix])
                        ts(tmpo[:C, hs], oB[:C, hs], stat[:C, 8 + hh:9 + hh], SUB, [t_oB, t_st], [t_tmpo],
                           s2=stat[:C, 4 + hh:5 + hh], op1=MUL)
                    tt(mix[:C, 512:1024], tmpo[:C, :], gs[1][:C, ci, :], MUL, [t_tmpo, t_gs[1][ci]], [t_mix])
            def chunk_trans(ci):
                    mix, t_mix = mix2[ci % 2], t_mix2[ci % 2]
                    cs_ = slice(ci * C, (ci + 1) * C)
                    bank, tb = gb()
                    bv = bfv(bank)
                    for c in range(8):
                        tr(bv[:, c * 128:c * 128 + C], mix[:C, c * 128:(c + 1) * 128], ident_b[:C, :C],
                           [t_mix, t_idb], [tb])
                    cp(mixT[:, :, cs_], bv[:, :].rearrange("p (a b) -> p a b", b=128)[:, :, :C], [tb], [t_mixT], eng="act")
            for st_ in range(NCH + 3):
                if st_ < NCH:
                    chunk_front(st_)
                if 1 <= st_ <= NCH:
                    chunk_back(st_ - 1)
                if 2 <= st_ <= NCH + 1:
                    chunk_back2(st_ - 2)
                if st_ >= 3:
                    chunk_trans(st_ - 3)
            ck(10)
            for half in range(2):
                w, tw = wpiece(D["w_out_ab"][:, half * 512:(half + 1) * 512])
                for o4 in range(4):
                    oc = half * 4 + o4
                    bank, tb = gb()
                    for kc in range(8):
                        mm(bank[:, :n], w[:, kc, o4 * 128:(o4 + 1) * 128], mixT[:, kc, :n], [tw, t_mixT], [tb],
                           st=(kc == 0), sp=(kc == 7))
                    xc = slice(cfg.x0 + tti * n, cfg.x0 + (tti + 1) * n)
                    tt(xT[:, oc, xc], xT[:, oc, xc], bank[:, :n], ADD, [t_x[cfg.xt0 + tti], tb], [t_x[cfg.xt0 + tti]])
        P.barrier()
        A.reset(m)

    def phase_ffn(groups, layer):
        m = A.mark()
        NH = NJ // 2
        Ttot = sum(g[0].T for g in groups)
        nmax = max(g[0].TT for g in groups)
        deep = not any(g[0].prompt for g in groups)
        hT = A.bf16(8, Ttot)
        sq, t_sq = Rot([A.bf16(nmax) for _ in range(2)]), None
        rs, t_rs = A.f32(nmax), Tok()
        actb = A.bf16(NH, Ttot)
        WI = Rot([A.bf16(8, 256) for _ in range(8 if deep else 3)])
        WO = Rot([A.bf16(NH, 128) for _ in range(6 if deep else 2)])
        cbuf = Rot([A.f32(nmax) for _ in range(6 if deep else 2)])
        gbuf = Rot([A.f32(nmax) for _ in range(6 if deep else 2)])
        wi_tok = {}
        abc_tok = {}
        gen["banks"] = [0, 1, 2, 3, 4, 5, 6, 7]
        units = []
        G = []
        col = 0
        stg, t_stg = A.f32(NJ * 128), Tok()
        for cfg, carry_in_rows, conv_out in groups:
            nseq, L, n = cfg.nseq, cfg.L, cfg.TT
            g = Cfg()
            g.cfg, g.conv_out, g.c0 = cfg, conv_out, col
            g.carry, g.t_carry = A.f32(NJ, nseq * 2), [Tok() for _ in range(NJ)]
            g.abuf = Rot([A.f32(nseq * (L + 2)) for _ in range(6 if deep else 3)])
            g.t_h = [Tok() for _ in range(cfg.NT)]
            g.t_act = [[Tok() for _ in range(cfg.NT)] for _ in range(NH)]
            if carry_in_rows is None:
                for j in range(NJ):
                    memset(g.carry[:, j, :], 0.0, [g.t_carry[j]])
            else:
                R_ = nseq * 2
                stage, t_s = stg, t_stg
                P.dma("sp", stage[:R_, :], carry_in_rows, W=[t_s])
                bank, tb = gb()
                for j in range(NJ):
                    tr(bank[:, j * R_:(j + 1) * R_], stage[:R_, j * 128:(j + 1) * 128], ident_f[:R_, :R_],
                       [t_s, t_idf], [tb])
                cp(g.carry, bank[:, 0:NJ * R_].rearrange("p (a b) -> p a b", b=R_), [tb], g.t_carry)
            for tti in range(cfg.NT):
                rmsnorm(cfg, tti, 2 + layer, hT[:, :, col + tti * n:col + (tti + 1) * n], g.t_h[tti], sq, t_sq, rs, t_rs)
                units.append((g, tti))
            col += cfg.T
            G.append(g)
        wd = lambda k, j: dw[:, j, layer * 3 + k:layer * 3 + k + 1]
        bd = lambda j: dw[:, j, 6 + layer:7 + layer]
        for jh in range(2):
            for jj in range(NH):
                j = jh * NH + jj
                w, tw = WI.next()
                tw = wi_tok.setdefault(id(tw), (tw, Tok()))
                for g_ in range(2):
                    c0 = g_ * D_FF + j * 128
                    P.dma("pool", w[:, :, g_ * 128:(g_ + 1) * 128],
                          D["wffi"][layer, :, c0:c0 + 128].rearrange("(kc p) c -> p kc c", p=128), W=[tw[g_]])
                for g, tti in units:
                    cfg = g.cfg
                    nseq, L, n = cfg.nseq, cfg.L, cfg.TT
                    cols = slice(g.c0 + tti * n, g.c0 + (tti + 1) * n)
                    ba, tba = gb()
                    for kc in range(8):
                        mm(ba[:, :n], w[:, kc, 0:128], hT[:, kc, cols], [tw[0], g.t_h[tti]], [tba], st=(kc == 0), sp=(kc == 7))
                    bu, tbu = gb()
                    for kc in range(8):
                        mm(bu[:, :n], w[:, kc, 128:256], hT[:, kc, cols], [tw[1], g.t_h[tti]], [tbu], st=(kc == 0), sp=(kc == 7))
                    ab, tab_ = g.abuf.next()
                    tabc = abc_tok.setdefault(id(tab_), Tok())
                    ab3 = ab.rearrange("p (s l) -> p s l", l=L + 2)
                    cr = g.carry[:, j, :].rearrange("p (s r) -> p s r", r=2)
                    cp(ab3[:, :, 0:2], cr, [g.t_carry[j]], [tabc])
                    act(ab3[:, :, 2:L + 2], ba[:, :n].rearrange("p (s l) -> p s l", l=L), AF.Copy, [tba], [tab_])
                    cb, tcb = cbuf.next()
                    cb3 = cb[:, :n].rearrange("p (s l) -> p s l", l=L)
                    act(cb[:, :n], ba[:, :n], AF.Identity, [tba, t_dw], [tcb], scale=wd(2, j), bias=bd(j))
                    stt(cb3, ab3[:, :, 1:L + 1], wd(1, j), cb3, MUL, ADD, [tab_, tabc, tcb, t_dw], [tcb])
                    stt(cb3, ab3[:, :, 0:L], wd(0, j), cb3, MUL, ADD, [tab_, tabc, tcb, t_dw], [tcb])
                    cp(cr, ab3[:, :, L:L + 2], [tab_], [g.t_carry[j]])
                    ge, tge = gbuf.next()
                    act(ge[:, :n], cb[:, :n], AF.Gelu, [tcb], [tge])
                    tt(actb[:, jj, cols], ge[:, :n], bu[:, :n], MUL, [tge, tbu], [g.t_act[jj][tti]])
            for oc in range(8):
                w, tw = WO.next()
                P.dma("pool", w, D["wffo"][layer, jh * NH * 128:(jh + 1) * NH * 128, oc * 128:(oc + 1) * 128]
                      .rearrange("(j p) c -> p j c", p=128), W=[tw])
                for g, tti in units:
                    cfg = g.cfg
                    n = cfg.TT
                    cols = slice(g.c0 + tti * n, g.c0 + (tti + 1) * n)
                    xc = slice(cfg.x0 + tti * n, cfg.x0 + (tti + 1) * n)
                    t_xt = t_x[cfg.xt0 + tti]
                    bank, tb = gb()
                    for jj in range(NH):
                        mm(bank[:, :n], w[:, jj, :], actb[:, jj, cols], [tw, g.t_act[jj][tti]], [tb],
                           st=(jj == 0), sp=(jj == NH - 1))
                    tt(xT[:, oc, xc], xT[:, oc, xc], bank[:, :n], ADD, [t_xt, tb], [t_xt])
        for g in G:
            R_ = g.cfg.nseq * 2
            cstage, t_cs = stg, t_stg
            for q4 in range((NJ + 3) // 4):
                j0, j1 = q4 * 4, min(NJ, q4 * 4 + 4)
                bank, tb = gb()
                for j in range(j0, j1):
                    tr(bank[:R_, (j - j0) * 128:(j - j0 + 1) * 128], g.carry[:, j, :], ident_f, [g.t_carry[j], t_idf], [tb])
                cp(cstage[:R_, j0 * 128:j1 * 128], bank[:R_, 0:(j1 - j0) * 128], [tb], [t_cs])
            P.dma("sp", g.conv_out.rearrange("s r c -> (s r) c"), cstage[:R_, :], R=[t_cs])
        P.barrier()
        A.reset(m)

    SCALE = 192.0 ** -0.5

    def phase_mla(cfg, seq, ckv_out, kr_out, cache_ckv=None, cache_kr=None):
        m = A.mark()
        n, C, NCH, T = cfg.TT, cfg.C, cfg.NCH, cfg.T
        KB = 128 if cfg.prompt else 16
        NB = T // KB
        cqnT, t_cqn = A.bf16(3, T), [Tok() for _ in range(cfg.NT)]
        ckvT, t_ckvT = A.bf16(4, T), [Tok() for _ in range(cfg.NT)]
        krT2, t_krT = A.bf16(T), [Tok() for _ in range(cfg.NT)]
        ckvn_b = None
        if not cfg.prompt:
            ckvn_b, t_cnb = A.bf16(NB, 512), [Tok() for _ in range(NB)]
        m1 = A.mark()
        Wc = A.bf16(8, 960)
        t_wc = [load(Wc[:, :, c0:c1], D["w_in_c"][:, c0:c1].rearrange("(kc p) c -> p kc c", p=128), "pool")
                for c0, c1 in ((0, 384), (384, 896), (896, 960))]
        krc, krs_ = A.f32(17, 64), A.f32(17, 64)
        gkvb = A.f32(512)
        t_t = [load(krc, D["krc"]), load(krs_, D["krs_"]), load(gkvb, D["gkv"].to_broadcast([128, 512]))]
        sq, t_sq = Rot([A.bf16(n) for _ in range(2)]), None
        rs, t_rs = A.f32(n), Tok()
        hT2c, t_h2c = [A.bf16(8, n), A.bf16(8, n)], [Tok(), Tok()]
        sq3, t_sq3 = A.bf16(3, n), Tok()
        rq, t_rq = A.f32(n), Tok()
        ckvn = Rot([A.f32(512) for _ in range(3)])
        cb16 = Rot([A.bf16(512) for _ in range(3)])
        krr = Rot([A.f32(64) for _ in range(3)])
        kt1, kt2, t_kt = A.f32(64), A.f32(64), Tok()
        kb16 = Rot([A.bf16(128) for _ in range(3)])
        st1, t_st1 = A.f32(4), Tok()
        junk, t_junk = A.bf16(512), Tok()
        gen["banks"] = [0, 1, 2, 3, 4, 5, 6, 7]
        for tti in range(cfg.NT):
            cols = slice(tti * n, (tti + 1) * n)
            hT, t_h = hT2c[tti % 2], t_h2c[tti % 2]
            if tti == 0:
                rmsnorm(cfg, 0, 1, hT, t_h, sq, t_sq, rs, t_rs)
            cqb = []
            for j in range(3):
                bank, tb = gb()
                for kc in range(8):
                    mm(bank[:, :n], Wc[:, kc, j * 128:(j + 1) * 128], hT[:, kc, :n], [t_wc[0], t_h], [tb],
                       st=(kc == 0), sp=(kc == 7))
                act(sq3[:, j, :], bank[:, :n], AF.Square, [tb], [t_sq3])
                cqb.append((bank, tb))
            bank, tb = gb()
            for j in range(3):
                mm(bank[:, :n], ones_b, sq3[:, j, :], [t_sq3, t_one], [tb], st=(j == 0), sp=(j == 2))
            act(rq, bank[:, :n], AF.Ln, [tb], [t_rq], scale=1.0 / 384, bias=EPS)
            act(rq, rq, AF.Exp, [t_rq], [t_rq], scale=-0.5)
            for j in range(3):
                stt(cqnT[:, j, cols], cqb[j][0][:, :n], gqc[:, j:j + 1], rq, MUL, MUL,
                    [cqb[j][1], t_rq, t_gq], [t_cqn[tti]])
            if tti + 1 < cfg.NT:
                rmsnorm(cfg, tti + 1, 1, hT2c[(tti + 1) % 2], t_h2c[(tti + 1) % 2], sq, t_sq, rs, t_rs)
            def c1_proj(bi):
                blk = tti * (n // KB) + bi
                tcs = slice(bi * KB, (bi + 1) * KB)
                gcs = slice(blk * KB, (blk + 1) * KB)
                bank, tb = gb()
                for kc in range(8):
                    mm(bank[:KB, :], hT[:, kc, tcs], Wc[:, kc, 384:896], [t_h, t_wc[1]], [tb], st=(kc == 0), sp=(kc == 7))
                act(junk[:KB, :], bank[:KB, :], AF.Square, [tb, t_st1], [t_junk, t_st1], accum=st1[:KB, 0:1])
                rstd_small(st1[:KB, 0:1], st1[:KB, 0:1], 1.0 / 512, [t_st1], [t_st1])
                cn, tcn = ckvn.next()
                stt(cn[:KB, :], bank[:KB, :], st1[:KB, 0:1], gkvb[:KB, :], MUL, MUL, [tb, t_st1, t_t[2]], [tcn])
                P.dma("sp", ckv_out[gcs, :], cn[:KB, :], R=[tcn])
                if cfg.prompt:
                    c16, tc16 = cb16.next()
                else:
                    c16, tc16 = ckvn_b[:, blk, :], t_cnb[blk]
                cp(c16[:KB, :], cn[:KB, :], [tcn], [tc16], eng="act")
                bank, tb = gb()
                for kc in range(8):
                    mm(bank[:KB, 0:64], hT[:, kc, tcs], Wc[:, kc, 896:960], [t_h, t_wc[2]], [tb], st=(kc == 0), sp=(kc == 7))
                tblk = blk if cfg.prompt else 16
                tt(kt1[:KB, :], bank[:KB, 0:64], krc[:KB, tblk, :], MUL, [tb, t_t[0]], [t_kt])
                tt(kt2[:KB, 0:32], bank[:KB, 32:64], krs_[:KB, tblk, 0:32], MUL, [tb, t_t[1]], [t_kt])
                tt(kt2[:KB, 32:64], bank[:KB, 0:32], krs_[:KB, tblk, 32:64], MUL, [tb, t_t[1]], [t_kt])
                kr_, tkr = krr.next()
                tt(kr_[:KB, :], kt1[:KB, :], kt2[:KB, :], ADD, [t_kt], [tkr])
                P.dma("sp", kr_out[gcs, :], kr_[:KB, :], R=[tkr])
                k16, tk16 = kb16.next()
                cp(k16[:KB, 0:64], kr_[:KB, :], [tkr], [tk16], eng="act")
                cp(k16[:KB, 64:128], kr_[:KB, :], [tkr], [tk16], eng="act")
                return gcs, c16, tc16, k16, tk16

            def c1_trans(item):
                gcs, c16, tc16, k16, tk16 = item
                bank2, tb2 = gb()
                bv = bfv(bank2)
                for kc in range(4):
                    tr(bv[:, kc * 128:kc * 128 + KB], c16[:KB, kc * 128:(kc + 1) * 128], ident_b[:KB, :KB],
                       [tc16, t_idb], [tb2])
                tr(bv[:, 512:512 + KB], k16[:KB, :], ident_b[:KB, :KB], [tk16, t_idb], [tb2])
                cp(ckvT[:, :, gcs], bv[:, 0:512].rearrange("p (a b) -> p a b", b=128)[:, :, :KB], [tb2], [t_ckvT[tti]])
                cp(krT2[:, gcs], bv[:, 512:512 + KB], [tb2], [t_krT[tti]])

            pend = []
            for bi in range(n // KB):
                pend.append(c1_proj(bi))
                if len(pend) > 1:
                    c1_trans(pend.pop(0))
            while pend:
                c1_trans(pend.pop(0))
        P.barrier()
        A.reset(m1)
        if cfg.prompt:
            mla_prompt_c2(cfg, cqnT, t_cqn, ckvT, t_ckvT, krT2, t_krT)
        else:
            mla_sample_c2(cfg, cqnT, t_cqn, ckvT, t_ckvT, krT2, t_krT, ckvn_b, t_cnb, cache_ckv, cache_kr)
        P.barrier()
        A.reset(m)

    def mla_prompt_c2(cfg, cqnT, t_cqn, ckvT, t_ckvT, krT2, t_krT):
        n, T, NT = cfg.TT, cfg.T, cfg.NT
        qn, t_qn = [A.bf16(T), A.bf16(T)], [[Tok() for _ in range(NT)] for _ in range(2)]
        qr, t_qr = A.bf16(T), [Tok() for _ in range(NT)]
        kn, t_kn = [A.bf16(T), A.bf16(T)], [[Tok() for _ in range(NT)] for _ in range(2)]
        Vp, t_V = A.bf16(16, 256), [Tok() for _ in range(16)]
        ao2, t_ao2 = [A.bf16(2, n), A.bf16(2, n)], [Tok(), Tok()]
        pending_out = [None]
        WQ = Rot([A.bf16(3, 256) for _ in range(2)])
        WQR = Rot([A.bf16(3, 128) for _ in range(2)])
        WQS = Rot([A.bf16(3, 128) for _ in range(2)])
        WK = Rot([A.bf16(4, 256) for _ in range(2)])
        WV = Rot([A.bf16(4, 256) for _ in range(2)])
        WOo = Rot([A.bf16(2, 1024) for _ in range(2)])
        cosr = Rot([A.f32(n) for _ in range(2)])
        sinr = Rot([A.f32(n) for _ in range(2)])
        r1, r2, t_r = A.f32(n), A.f32(n), [Tok(), Tok()]
        PT = Rot([A.bf16(512) for _ in range(5)])
        rden, t_rden = A.f32(n), Tok()
        gen["banks"] = [0, 1, 2, 3]
        obk = Rot([PB[4], PB[5]])
        obk.items = [(PB[4], TPB[4]), (PB[5], TPB[5])]
        dbk = Rot([PB[6], PB[7]])
        dbk.items = [(PB[6], TPB[6]), (PB[7], TPB[7])]
        r3 = lambda src: src.rearrange("(kc p) c -> p kc c", p=128)
        for pr in range(4):
            wq, twq = WQ.next()
            P.dma("pool", wq, r3(D["wuq_n"][:, pr * 256:(pr + 1) * 256]), W=[twq])
            wqr, twqr = WQR.next()
            P.dma("pool", wqr, r3(D["wuq_r"][:, pr * 128:(pr + 1) * 128]), W=[twqr])
            wqs, twqs = WQS.next()
            P.dma("pool", wqs, r3(D["wuq_rs"][:, pr * 128:(pr + 1) * 128]), W=[twqs])
            wk, twk = WK.next()
            P.dma("pool", wk, r3(D["wuk"][:, pr * 256:(pr + 1) * 256]), W=[twk])
            wv, twv = WV.next()
            P.dma("pool", wv, r3(D["wuv"][:, pr * 256:(pr + 1) * 256]), W=[twv])
            wo, two = WOo.next()
            P.dma("pool", wo, D["w_out_c"][pr * 256:(pr + 1) * 256, :].rearrange("(h p) c -> p h c", p=128), W=[two])
            for tti in range(NT):
                cols = slice(tti * n, (tti + 1) * n)
                for hh in range(2):
                    bank, tb = gb()
                    for kc in range(3):
                        mm(bank[:, :n], wq[:, kc, hh * 128:(hh + 1) * 128], cqnT[:, kc, cols], [twq, t_cqn[tti]], [tb],
                           st=(kc == 0), sp=(kc == 2))
                    cp(qn[hh][:, cols], bank[:, :n], [tb], [t_qn[hh][tti]], eng="act")
                    bank, tb = gb()
                    for kc in range(4):
                        mm(bank[:, :n], wk[:, kc, hh * 128:(hh + 1) * 128], ckvT[:, kc, cols], [twk, t_ckvT[tti]], [tb],
                           st=(kc == 0), sp=(kc == 3))
                    cp(kn[hh][:, cols], bank[:, :n], [tb], [t_kn[hh][tti]], eng="dve")
                cos_t, t_cos = cosr.next()
                sin_t, t_sin = sinr.next()
                P.dma("sp", cos_t, D["cosF"][:, cols], W=[t_cos])
                P.dma("sp", sin_t, D["sinF"][:, cols], W=[t_sin])
                bank1, tb1 = gb()
                for kc in range(3):
                    mm(bank1[:, :n], wqr[:, kc, :], cqnT[:, kc, cols], [twqr, t_cqn[tti]], [tb1], st=(kc == 0), sp=(kc == 2))
                bank2, tb2 = gb()
                for kc in range(3):
                    mm(bank2[:, :n], wqs[:, kc, :], cqnT[:, kc, cols], [twqs, t_cqn[tti]], [tb2], st=(kc == 0), sp=(kc == 2))
                tt(r1, bank1[:, :n], cos_t, MUL, [tb1, t_cos], [t_r[0]])
                tt(r2, bank2[:, :n], sin_t, MUL, [tb2, t_sin], [t_r[1]])
                tt(qr[:, cols], r1, r2, ADD, t_r, [t_qr[tti]])
                for b4 in range(4):
                    blk = tti * 4 + b4
                    bank, tb = gb()
                    for kc in range(4):
                        mm(bank[:, 0:256], ckvT[:, kc, blk * 128:(blk + 1) * 128], wv[:, kc, :], [twv, t_ckvT[tti]], [tb],
                           st=(kc == 0), sp=(kc == 3))
                    cp(Vp[:, blk, :], bank[:, 0:256], [tb], [t_V[blk]], eng=("act" if b4 % 2 == 0 else "dve"))
            for qt in range(NT):
                qcols0 = qt * n
                ao, t_ao = ao2[qt % 2], t_ao2[qt % 2]
                for hh in range(2):
                    prs = slice(hh * 64, hh * 64 + 64)
                    ob, tob = obk.next()
                    db, tdb = dbk.next()
                    nkb = 4 * qt + 4

                    def scores(kb):
                        i = kb - 4 * qt
                        q0 = 0 if i <= 0 else i * 128
                        N = n - q0
                        qs = slice(qcols0 + q0, qcols0 + n)
                        ks = slice(kb * 128, (kb + 1) * 128)
                        bank, tb = gb()
                        mm(bank[:, :N], kn[hh][:, ks], qn[hh][:, qs], [t_kn[hh][kb // 4], t_qn[hh][qt]], [tb], st=True, sp=False)
                        mm(bank[:, :N], krT2[prs, ks], qr[prs, qs], [t_krT[kb // 4], t_qr[qt]], [tb], st=False, sp=True)
                        pt, tpt = PT.next()
                        act(pt[:, :N], bank[:, :N], AF.Exp, [tb], [tpt], scale=SCALE)
                        if i >= 0:
                            memset(pt[64:128, 0:64], 0.0, [tpt])
                        return kb, q0, N, pt, tpt

                    def pv(item):
                        kb, q0, N, pt, tpt = item
                        mm(ob[:, q0:n], Vp[:, kb, hh * 128:(hh + 1) * 128], pt[:, :N], [t_V[kb], tpt], [tob],
                           st=(kb == 0), sp=(kb == nkb - 1))
                        mm(db[:, q0:n], ones_b, pt[:, :N], [t_one, tpt], [tdb], st=(kb == 0), sp=(kb == nkb - 1))

                    pend = []
                    for kb in range(nkb):
                        pend.append(scores(kb))
                        if len(pend) > 2:
                            pv(pend.pop(0))
                        if hh == 0 and kb == 2 and pending_out[0] is not None:
                            pending_out[0]()
                            pending_out[0] = None
                    while pend:
                        pv(pend.pop(0))
                    P.op("dve", lambda e, db=db: e.reciprocal(out=rden, in_=db[:, :n]), [tdb], [t_rden])
                    tt(ao[:, hh, :], ob[:, :n], rden, MUL, [tob, t_rden], [t_ao])

                def outproj(qt=qt, ao=ao, t_ao=t_ao, wo=wo, two=two):
                    qc = slice(qt * n, qt * n + n)
                    for oc in range(8):
                        bank, tb = gb()
                        for hh in range(2):
                            mm(bank[:, :n], wo[:, hh, oc * 128:(oc + 1) * 128], ao[:, hh, :], [two, t_ao], [tb],
                               st=(hh == 0), sp=(hh == 1))
                        tt(xT[:, oc, qc], xT[:, oc, qc], bank[:, :n], ADD, [t_x[qt], tb], [t_x[qt]])
                pending_out[0] = outproj
        if pending_out[0] is not None:
            pending_out[0]()
            pending_out[0] = None

    def mla_sample_c2(cfg, cqnT, t_cqn, ckvT, t_ckvT, krT2, t_krT, ckvn_b, t_cnb, cache_ckv, cache_kr):
        T = cfg.T
        r3 = lambda src: src.rearrange("(kc p) c -> p kc c", p=128)
        WQ, WQR, WQS = A.bf16(3, 1024), A.bf16(3, 512), A.bf16(3, 512)
        WUKT, WV = A.bf16(8, 512), A.bf16(4, 1024)
        t_w = [load(WQ, r3(D["wuq_n"]), "pool"), load(WQR, r3(D["wuq_r"]), "pool"), load(WQS, r3(D["wuq_rs"]), "pool"),
               load(WUKT, D["wukT"], "pool"), load(WV, r3(D["wuv"]), "pool")]
        WOo = Rot([A.bf16(8, 128) for _ in range(2)])
        cos_t, sin_t = A.f32(T), A.f32(T)
        t_cs = [load(cos_t, D["cosF"][:, 2048:2048 + T]), load(sin_t, D["sinF"][:, 2048:2048 + T])]
        qnS, t_qnS = A.bf16(8, T), Tok()
        qrS, t_qrS = A.bf16(8, T), Tok()
        r1, r2, t_r = A.f32(T), A.f32(T), [Tok(), Tok()]
        qlat, t_ql = [A.bf16(4, 128) for _ in range(4)], [Tok() for _ in range(4)]
        CQ = Rot([A.bf16(8, 512) for _ in range(3)])
        KQ = Rot([A.bf16(8, 128) for _ in range(3)])
        kq_tok = {}
        CT = Rot([A.bf16(4, 1024) for _ in range(2)])
        KT = Rot([A.bf16(1024) for _ in range(2)])
        PT = Rot([A.bf16(128) for _ in range(5)])
        rden, t_rden = A.f32(1), Tok()
        olatn, t_on = A.bf16(512), Tok()
        olatT, t_oT = A.bf16(4, 128), Tok()
        aoS, t_ao = A.bf16(8, T), Tok()
        gen["banks"] = [0, 1, 2, 3, 4, 5]
        olb, t_olb, dnb, t_dnb = PB[6], TPB[6], PB[7], TPB[7]
        for h in range(8):
            bank, tb = gb()
            for kc in range(3):
                mm(bank[:, :T], WQ[:, kc, h * 128:(h + 1) * 128], cqnT[:, kc, :T], [t_w[0], t_cqn[0]], [tb], st=(kc == 0), sp=(kc == 2))
            cp(qnS[:, h, :], bank[:, :T], [tb], [t_qnS], eng=("act" if h % 2 else "dve"))
        for h in range(8):
            b1, tb1 = gb()
            for kc in range(3):
                mm(b1[:64, :T], WQR[:, kc, h * 64:(h + 1) * 64], cqnT[:, kc, :T], [t_w[1], t_cqn[0]], [tb1], st=(kc == 0), sp=(kc == 2))
            b2, tb2 = gb()
            for kc in range(3):
                mm(b2[:64, :T], WQS[:, kc, h * 64:(h + 1) * 64], cqnT[:, kc, :T], [t_w[2], t_cqn[0]], [tb2], st=(kc == 0), sp=(kc == 2))
            tt(r1[:64, :], b1[:64, :T], cos_t[:64, :], MUL, [tb1, t_cs[0]], [t_r[0]])
            tt(r2[:64, :], b2[:64, :T], sin_t[:64, :], MUL, [tb2, t_cs[1]], [t_r[1]])
            tt(qrS[:64, h, :], r1[:64, :], r2[:64, :], ADD, t_r, [t_qrS])
        for b in range(4):
            bank, tb = gb()
            for kc in range(4):
                for h in range(8):
                    mm(bank[:, kc * 128 + h * 16:kc * 128 + (h + 1) * 16], WUKT[:, h, kc * 128:(kc + 1) * 128],
                       qnS[:, h, b * 16:(b + 1) * 16], [t_w[3], t_qnS], [tb])
            cp(qlat[b], bank[:, :].rearrange("p (a b) -> p a b", b=128), [tb], [t_ql[b]], eng="act")
        for b in range(4):
            pend = []

            def scores(K_, lc, lr, vrows, Rk, blk):
                bank, tb = gb()
                for kc in range(4):
                    mm(bank[:K_, 0:128], lc(kc), qlat[b][:, kc, :], Rk + [t_ql[b]], [tb], st=(kc == 0), sp=False)
                mm(bank[:K_, 0:128], lr, qrS[:64, :, b * 16:(b + 1) * 16], Rk + [t_qrS], [tb], st=False, sp=True)
                pt, tpt = PT.next()
                act(pt[:K_, :], bank[:K_, 0:128], AF.Exp, [tb], [tpt], scale=SCALE)
                return K_, pt, tpt, vrows, Rk, blk

            def pv(item):
                K_, pt, tpt, vrows, Rk, blk = item
                mm(olb[:, :], pt[:K_, :], vrows, [tpt] + Rk, [t_olb], st=(blk == 0), sp=(blk == 32))
                mm(dnb[:, 0:1], pt[:K_, :], ones_b[:K_, 0:1], [tpt, t_one], [t_dnb], st=(blk == 0), sp=(blk == 32))

            def push(item):
                pend.append(item)
                if len(pend) > 2:
                    pv(pend.pop(0))

            for q4 in range(4):
                cq, tcq = CQ.next()
                kq, tkq = KQ.next()
                tkq = kq_tok.setdefault(id(tkq), (tkq, Tok()))
                P.dma("pool", cq, cache_ckv[b, q4 * 1024:(q4 + 1) * 1024, :].rearrange("(k p) l -> p k l", p=128), W=[tcq])
                for dup in range(2):
                    P.dma("pool", kq[:, :, dup * 64:(dup + 1) * 64],
                          cache_kr[b, q4 * 1024:(q4 + 1) * 1024, :].rearrange("(k p) r -> p k r", p=128), W=[tkq[dup]])
                ct, tct = CT.next()
                kt, tkt = KT.next()
                for k8 in range(8):
                    bs = slice(k8 * 128, (k8 + 1) * 128)
                    bank, tb = gb()
                    bv = bfv(bank)
                    for kc in range(4):
                        tr(bv[:, kc * 128:(kc + 1) * 128], cq[:, k8, kc * 128:(kc + 1) * 128], ident_b, [tcq, t_idb], [tb])
                    tr(bv[:, 512:640], kq[:, k8, :], ident_b, [tkq[0], tkq[1], t_idb], [tb])
                    cp(ct[:, :, bs], bv[:, 0:512].rearrange("p (a b) -> p a b", b=128), [tb], [tct],
                       eng=("act" if k8 % 2 else "dve"))
                    cp(kt[:, bs], bv[:, 512:640], [tb], [tkt], eng=("dve" if k8 % 2 else "act"))
                for k8 in range(8):
                    bs = slice(k8 * 128, (k8 + 1) * 128)
                    push(scores(128, (lambda kc, bs=bs, ct=ct: ct[:, kc, bs]), kt[0:64, bs], cq[:, k8, :],
                                [tct, tkt, tcq], q4 * 8 + k8))
            bs = slice(b * 16, (b + 1) * 16)
            push(scores(16, (lambda kc, bs=bs: ckvT[:, kc, bs]), krT2[0:64, bs], ckvn_b[:16, b, :],
                        [t_ckvT[0], t_krT[0], t_cnb[b]], 32))
            while pend:
                pv(pend.pop(0))
            P.op("dve", lambda e: e.reciprocal(out=rden, in_=dnb[:, 0:1]), [t_dnb], [t_rden])
            ts(olatn, olb[:, :], rden[:, 0:1], MUL, [t_olb, t_rden], [t_on])
            bank, tb = gb()
            bv = bfv(bank)
            for kc in range(4):
                tr(bv[:, kc * 128:(kc + 1) * 128], olatn[:, kc * 128:(kc + 1) * 128], ident_b, [t_on, t_idb], [tb])
            cp(olatT, bv[:, 0:512].rearrange("p (a b) -> p a b", b=128), [tb], [t_oT], eng="act")
            bank, tb = gb()
            for h in range(8):
                for kc in range(4):
                    mm(bank[:, h * 16:(h + 1) * 16], WV[:, kc, h * 128:(h + 1) * 128], olatT[:, kc, h * 16:(h + 1) * 16],
                       [t_w[4], t_oT], [tb], st=(kc == 0), sp=(kc == 3))
            cp(aoS[:, :, b * 16:(b + 1) * 16], bank[:, 0:128].rearrange("p (h i) -> p h i", i=16), [tb], [t_ao])
        for oc in range(8):
            wo, two = WOo.next()
            P.dma("pool", wo, D["w_out_c"][:, oc * 128:(oc + 1) * 128].rearrange("(h p) c -> p h c", p=128), W=[two])
            bank, tb = gb()
            for h in range(8):
                mm(bank[:, :T], wo[:, h, :], aoS[:, h, :], [two, t_ao], [tb], st=(h == 0), sp=(h == 7))
            tt(xT[:, oc, cfg.x0:cfg.x0 + T], xT[:, oc, cfg.x0:cfg.x0 + T], bank[:, :T], ADD, [t_x[cfg.xt0], tb], [t_x[cfg.xt0]])

    def phase_final(cfg, y_out):
        m = A.mark()
        n = cfg.XB
        gfb = A.f32(1024)
        t_g = load(gfb, D["nrm"][4:5, :].to_broadcast([128, 1024]))
        ybuf = Rot([A.f32(1024) for _ in range(2)])
        st, t_st = A.f32(4), Tok()
        junk, t_junk = A.bf16(512), Tok()
        gen["banks"] = [0, 1, 2, 3, 4, 5, 6, 7]
        for blk in range(cfg.T // n):
            cs_ = slice(blk * n, (blk + 1) * n)
            xs_ = slice(cfg.x0 + blk * n, cfg.x0 + (blk + 1) * n)
            tti = cfg.xt0 + (blk * n) // cfg.TT
            banks = []
            for half in range(2):
                bank, tb = gb()
                for c4 in range(4):
                    tr(bank[:n, c4 * 128:(c4 + 1) * 128], xT[:, half * 4 + c4, xs_], ident_f, [t_x[tti], t_idf], [tb])
                act(junk[:n, :], bank[:n, :], AF.Square, [tb, t_st], [t_junk, t_st], accum=st[:n, half:half + 1])
                banks.append((bank, tb))
            tt(st[:n, 2:3], st[:n, 0:1], st[:n, 1:2], ADD, [t_st], [t_st])
            rstd_small(st[:n, 2:3], st[:n, 2:3], 1.0 / 1024, [t_st], [t_st])
            yb, tyb = ybuf.next()
            for half in range(2):
                stt(yb[:n, half * 512:(half + 1) * 512], banks[half][0][:n, :], st[:n, 2:3],
                    gfb[:n, half * 512:(half + 1) * 512], MUL, MUL, [banks[half][1], t_st, t_g], [tyb])
            P.dma("sp", y_out[cs_, :], yb[:n, :], R=[tyb])
        P.barrier()
        A.reset(m)

    pc, sc = make_cfgs()
    MERGE = STAGE >= 99 and (NSEQ_P == 2 or os.environ.get("MK_MERGE") == "1")
    for seq in range(NSEQ_P):
        merged = MERGE and seq == NSEQ_P - 1
        load_x(pc, D["xp"][seq])
        if merged:
            load_x(sc, D["xs"])
        if STAGE >= 1:
            try:
                phase_mixer_ab(pc, seq, None, None, O["glap"], O["retp"])
            except _Stop:
                P.barrier()
                A.reset(base_mark)
            if merged:
                phase_mixer_ab(sc, 0, D["sgla"], D["sret"], O["glas"], O["rets"])
        if STAGE >= 2:
            grp = [(pc, None, O["convp"][0, seq:seq + 1])]
            if merged:
                grp.append((sc, D["sconv"][0:8, :], O["convs"][0]))
            phase_ffn(grp, 0)
        if STAGE >= 3:
            phase_mla(pc, seq, O["ckvp"][seq], O["krp"][seq])
            if merged:
                phase_mla(sc, 0, O["ckvs"], O["krs"], D["cckv"], D["ckr"])
        if STAGE >= 4:
            grp = [(pc, None, O["convp"][1, seq:seq + 1])]
            if merged:
                grp.append((sc, D["sconv"][8:16, :], O["convs"][1]))
            phase_ffn(grp, 1)
        phase_final(pc, O["yp"][seq])
        if merged:
            phase_final(sc, O["ys"])
    if STAGE >= 5 and not MERGE:
        load_x(sc, D["xs"])
        phase_mixer_ab(sc, 0, D["sgla"], D["sret"], O["glas"], O["rets"])
        phase_ffn([(sc, D["sconv"][0:8, :], O["convs"][0])], 0)
        phase_mla(sc, 0, O["ckvs"], O["krs"], D["cckv"], D["ckr"])
        phase_ffn([(sc, D["sconv"][8:16, :], O["convs"][1])], 1)
        phase_final(sc, O["ys"])
    P.emit()
    P.close()
    print("arena peak words", A.peak, "instrs", {k: len(v) for k, v in P.streams.items()}, "signals", P.sigcount)
    return nc


_CACHE = {}


def kernel(x_prompt, x_sample, state_gla, state_ret, cache_ckv, cache_krope, state_conv,
           norm_mix, norm_ffn, norm_final, w_in_ab, w_gate_up, b_gate, g_gla, g_ret, w_out_ab,
           w_in_c, g_q, g_kv, w_uq, w_uk, w_uv, w_out_c, w_ffn_in, w_dwconv, b_dwconv, w_ffn_out):
    f = lambda a: np.ascontiguousarray(np.asarray(a, dtype=np.float32))
    ncores = int(os.environ.get('MK_CORES', '8'))
    if "nc" not in _CACHE:
        _CACHE["nc"] = build_program()
        _CACHE["tabs"] = const_tables()
    nc = _CACHE["nc"]
    tabs = _CACHE["tabs"]
    w_in_ab0 = f(w_in_ab)[0]
    sw = w_in_ab0[:, 1552:2064].reshape(1024, 8, 2, 32)[:, :, ::-1, :].reshape(1024, 512)
    wuq = f(w_uq)[0].reshape(384, 8, 192)
    wuk0 = f(w_uk)[0]
    shared = dict(
        nrm=np.stack([f(norm_mix)[0], f(norm_mix)[1], f(norm_ffn)[0], f(norm_ffn)[1], f(norm_final)]),
        w_in_ab=w_in_ab0, w_ab_sw=f(sw), wgu=f(w_gate_up)[0], bgate=f(b_gate)[0][None, :],
        ggla=f(g_gla)[0][None, :], gret=f(g_ret)[0][None, :], w_out_ab=f(w_out_ab)[0], w_in_c=f(w_in_c)[0],
        gq=f(g_q)[0][None, :], gkv=f(g_kv)[0][None, :],
        wuq_n=f(wuq[:, :, :128].reshape(384, 1024)), wuq_r=f(wuq[:, :, 128:].reshape(384, 512)),
        wuq_rs=f(wuq[:, :, 128:].reshape(384, 8, 2, 32)[:, :, ::-1, :].reshape(384, 512)),
        wuk=f(wuk0.reshape(512, 1024)), wukT=f(wuk0.transpose(2, 1, 0)), wuv=f(w_uv)[0].reshape(512, 1024),
        w_out_c=f(w_out_c)[0], wffi=f(w_ffn_in),
        dwc=f(np.concatenate([f(w_dwconv).reshape(6, 2816), f(b_dwconv)], axis=0)), wffo=f(w_ffn_out),
    )
    shared.update(tabs)
    xpv, xsv = f(x_prompt), f(x_sample)
    sg, sr = f(state_gla)[0], f(state_ret)[0]
    cc, ck, scv = f(cache_ckv)[0], f(cache_krope)[0], f(state_conv)
    in_maps = []
    for c in range(ncores):
        d = dict(shared)
        d["xp"] = xpv[2 * c:2 * c + 2]
        d["xs"] = xsv[4 * c:4 * c + 4].reshape(64, 1024)
        d["sgla"] = sg[4 * c:4 * c + 4].reshape(4, 256, 128)
        d["sret"] = sr[4 * c:4 * c + 4].reshape(4, 256, 128)
        d["cckv"] = cc[4 * c:4 * c + 4]
        d["ckr"] = ck[4 * c:4 * c + 4]
        d["sconv"] = f(scv[:, 4 * c:4 * c + 4].reshape(16, 2816))
        in_maps.append({k: np.ascontiguousarray(v) for k, v in d.items()})
    res = run_bass_kernel_spmd(nc, in_maps, core_ids=list(range(ncores)))
    R = res.results
    cat = lambda k, shp: np.concatenate([R[c][k].reshape(shp) for c in range(ncores)], axis=0)
    y_prompt = cat("yp", (2, 2048, 1024))
    y_sample = cat("ys", (4, 16, 1024))
    gla_p = cat("glap", (2, 4, 64, 128))[None]
    gla_s = cat("glas", (4, 4, 64, 128))[None]
    ret_p = cat("retp", (2, 4, 64, 128))[None]
    ret_s = cat("rets", (4, 4, 64, 128))[None]
    ckv_p = cat("ckvp", (2, 2048, 512))[None]
    ckv_s = cat("ckvs", (4, 16, 512))[None]
    kr_p = cat("krp", (2, 2048, 64))[None]
    kr_s = cat("krs", (4, 16, 64))[None]
    conv_p = np.concatenate([R[c]["convp"] for c in range(ncores)], axis=1)
    conv_s = np.concatenate([R[c]["convs"] for c in range(ncores)], axis=1)
    outs = (y_prompt, y_sample, gla_p, gla_s, ret_p, ret_s, ckv_p, ckv_s, kr_p, kr_s, conv_p, conv_s)
    return tuple(np.ascontiguousarray(o, dtype=np.float32) for o in outs)
```

```python
import contextlib
import math
import os

import numpy as np
import concourse.bass as bass
import concourse.mybir as mybir
from concourse.bass_utils import run_bass_kernel_spmd

F32 = mybir.dt.float32
BF16 = mybir.dt.bfloat16
AF = mybir.ActivationFunctionType
ALU = mybir.AluOpType
AX = mybir.AxisListType

EPS = 1e-6
D_FF = 2816
NJ = 22
STAGE = int(os.environ.get("MK_STAGE", "99"))
NSEQ_P = int(os.environ.get("MK_NSEQ", "2"))
STOPAT = int(os.environ.get("MK_STOP", "0"))


class _Stop(Exception):
    pass


def ck(k):
    if STOPAT == k:
        raise _Stop()

COMPUTE = ("pe", "act", "dve", "pool")
DMA_K = 8
EPOCH = 6000


class Tok:
    __slots__ = ("w", "rs", "excl")

    def __init__(self, excl=False):
        self.w = None
        self.rs = []
        self.excl = excl


class Ins:
    __slots__ = ("eng", "fn", "deps", "dma", "signal", "ev", "prev_ev")

    def __init__(self, eng, fn, dma):
        self.eng = eng
        self.fn = fn
        self.dma = dma
        self.deps = []
        self.signal = False
        self.ev = None
        self.prev_ev = None


class Prog:
    def __init__(self, nc):
        self.nc = nc
        self.es = contextlib.ExitStack()
        self.streams = {e: [] for e in ("pe", "act", "dve", "pool", "sp")}
        self.pending = {e: [] for e in self.streams}
        self.dmas = []
        self.n = 0
        self.nobar = False

    def sb(self, shape, dt, name="t"):
        self.n += 1
        return self.es.enter_context(self.nc.sbuf_tensor(f"{name}{self.n}", list(shape), dt))

    def ps(self, shape, dt, name="p"):
        self.n += 1
        return self.es.enter_context(self.nc.psum_tensor(f"{name}{self.n}", list(shape), dt))

    def op(self, eng, fn, R=(), W=(), dma=False, force=()):
        ins = Ins(eng, fn, dma)
        deps = {id(d): d for d in force}

        def same(d):
            return (not d.dma) and (not dma) and d.eng == eng

        def readers(t):
            seen = set()
            for r in reversed(t.rs):
                if r.dma:
                    yield r
                elif r.eng not in seen:
                    seen.add(r.eng)
                    if not (r.eng == eng and eng == "pe" and not dma):
                        yield r

        for t in R:
            d = t.w
            if d is not None and not (same(d) and eng == "pe"):
                deps[id(d)] = d
            if t.excl:
                for r in readers(t):
                    if not same(r):
                        deps[id(r)] = r
        for t in W:
            d = t.w
            if d is not None and not (same(d) and eng == "pe"):
                deps[id(d)] = d
            for r in readers(t):
                deps[id(r)] = r
        for t in R:
            t.rs.append(ins)
        for t in W:
            t.w = ins
            t.rs = []
        if self.pending[eng] and not self.nobar:
            for d in self.pending[eng]:
                deps[id(d)] = d
            self.pending[eng] = []
        ins.deps = list(deps.values())
        for d in ins.deps:
            d.signal = True
        if dma:
            ins.signal = True
            self.dmas.append(ins)
        self.streams[eng].append(ins)
        return ins

    def dma(self, q, out, in_, R=(), W=(), **kw):
        return self.op(q, lambda e: e.dma_start(out=out, in_=in_, **kw), R=R, W=W, dma=True)

    def barrier(self):
        deps = [st[-1] for st in self.streams.values() if st] + self.dmas
        for d in deps:
            d.signal = True
        for e in self.pending:
            self.pending[e] = self.pending[e] + deps
        self.dmas = []

    def emit(self):
        nc = self.nc
        es = self.es
        sem_dma = {q: [es.enter_context(nc.semaphore(f"semd_{q}{i}")) for i in range(DMA_K)]
                   for q in ("sp", "act", "pool")}
        for e, st in self.streams.items():
            cnt = 0
            nd = 0
            sem = None
            for ins in st:
                if ins.dma:
                    j = nd % DMA_K
                    rnd = nd // DMA_K
                    ins.ev = (sem_dma[e][j], 16 * (rnd + 1))
                    ins.prev_ev = (sem_dma[e][j], 16 * rnd) if rnd > 0 else None
                    nd += 1
                elif ins.signal:
                    if cnt % EPOCH == 0:
                        sem = es.enter_context(nc.semaphore(f"sem_{e}{cnt // EPOCH}"))
                    cnt += 1
                    ins.ev = (sem, (cnt - 1) % EPOCH + 1)
        self.sigcount = {e: sum(1 for i in st if (not i.dma) and i.signal) for e, st in self.streams.items()}
        final_dma = []
        for q in ("sp", "act", "pool"):
            last = {}
            for ins in self.streams[q]:
                if ins.dma:
                    last[id(ins.ev[0])] = ins.ev
            final_dma += list(last.values())

        def run(engobj, st, is_sp):
            waited = {}

            def wait(ev):
                sem, val = ev
                k = id(sem)
                if waited.get(k, 0) < val:
                    engobj.wait_ge(sem, val)
                    waited[k] = val

            for ins in st:
                for d in ins.deps:
                    wait(d.ev)
                if ins.dma and ins.prev_ev is not None:
                    wait(ins.prev_ev)
                bi = ins.fn(engobj)
                if ins.dma:
                    bi.then_inc(ins.ev[0], 16)
                elif ins.signal:
                    bi.then_inc(ins.ev[0], 1)
            if is_sp:
                for ev in final_dma:
                    wait(ev)

        block = es.enter_context(nc.Block())
        S = self.streams

        @block.tensor
        def _(e):
            run(e, S["pe"], False)

        @block.scalar
        def _(e):
            run(e, S["act"], False)

        @block.vector
        def _(e):
            run(e, S["dve"], False)

        @block.gpsimd
        def _(e):
            run(e, S["pool"], False)

        @block.sync
        def _(e):
            run(e, S["sp"], True)

    def close(self):
        self.es.close()


class Arena:
    def __init__(self, P, nwords):
        self.t = P.sb([128, nwords], F32, "arena")
        self.n = nwords
        self.off = 0
        self.peak = 0

    def _take(self, words):
        o = self.off
        self.off += words
        self.peak = max(self.peak, self.off)
        assert self.off <= self.n, f"arena overflow {self.off} > {self.n}"
        return o

    def f32(self, *shape):
        n = int(np.prod(shape))
        o = self._take(n)
        ap = self.t[:, o:o + n]
        return self._shape(ap, shape)

    def bf16(self, *shape):
        n = int(np.prod(shape))
        w = (n + 1) // 2
        o = self._take(w)
        ap = self.t[:, o:o + w].bitcast(BF16)
        if 2 * w != n:
            ap = ap[:, 0:n]
        return self._shape(ap, shape)

    @staticmethod
    def _shape(ap, shape):
        if len(shape) == 1:
            return ap
        if len(shape) == 2:
            return ap.rearrange("p (a b) -> p a b", b=shape[1])
        if len(shape) == 3:
            return ap.rearrange("p (a b c) -> p a b c", b=shape[1], c=shape[2])
        raise ValueError(shape)

    def mark(self):
        return self.off

    def reset(self, m):
        self.off = m


class Rot:
    def __init__(self, items):
        self.items = [(it, Tok()) for it in items]
        self.i = 0

    def next(self):
        r = self.items[self.i % len(self.items)]
        self.i += 1
        return r


class Cfg:
    pass


def make_cfgs():
    p = Cfg()
    p.T, p.TT, p.NT, p.C, p.NCH, p.nseq, p.L, p.tab0, p.XB, p.krblk0 = 2048, 512, 4, 128, 4, 1, 512, 0, 128, 0
    p.prompt = True
    p.x0, p.xt0 = 0, 0
    s = Cfg()
    s.T, s.TT, s.NT, s.C, s.NCH, s.nseq, s.L, s.tab0, s.XB, s.krblk0 = 64, 64, 1, 16, 4, 4, 16, 2048, 64, 16
    s.prompt = False
    s.x0, s.xt0 = 2048, 4
    return p, s


def const_tables():
    half = 32
    freqs = (10000.0 ** (-np.arange(half, dtype=np.float32) / half)).astype(np.float32)
    pos = np.concatenate([np.arange(2048), 4096 + (np.arange(64) % 16)]).astype(np.float32)
    ang = pos[None, :] * freqs[:, None]
    ang = ang.astype(np.float32)
    cos = np.cos(ang).astype(np.float32)
    sin = np.sin(ang).astype(np.float32)
    p = np.arange(128)
    d = p % 64
    cosF = cos[d % 32, :]
    sinF = np.where((d < 32)[:, None], -sin[d % 32, :], sin[d % 32, :]).astype(np.float32)
    kc = np.zeros((128, 17, 64), np.float32)
    ks = np.zeros((128, 17, 64), np.float32)
    for blk in range(17):
        if blk < 16:
            pp = (blk * 128 + p).astype(np.float32)
        else:
            pp = (4096 + (p % 16)).astype(np.float32)
        a = (pp[:, None] * freqs[None, :]).astype(np.float32)
        c, s = np.cos(a).astype(np.float32), np.sin(a).astype(np.float32)
        kc[:, blk, :32] = c
        kc[:, blk, 32:] = c
        ks[:, blk, :32] = -s
        ks[:, blk, 32:] = s
    def ret_tabs(C, TT):
        nref = (C - 1) // 2 + 1
        E = np.zeros((128, 2, 2, TT), np.float32)
        cst = np.zeros((128, 2, 3), np.float32)
        i = np.arange(TT) % C
        for kcx in range(2):
            h = 2 * kcx + p // 64
            lg = np.log1p(-np.exp2(-5.0 - h.astype(np.float64)))
            E[:, kcx, 0, :] = np.exp((i[None, :] + 1 - nref) * lg[:, None])
            E[:, kcx, 1, :] = np.exp((nref - i[None, :] - 1) * lg[:, None]) * (64 ** -0.5)
            cst[:, kcx, 0] = np.exp(nref * lg)
            cst[:, kcx, 1] = np.exp((C - nref) * lg)
            cst[:, kcx, 2] = np.exp(C * lg)
        return E, cst
    Ep, cp = ret_tabs(128, 512)
    Es, cs = ret_tabs(16, 64)
    mask = (np.arange(128)[:, None] <= np.arange(128)[None, :]).astype(np.float32)
    mask4 = np.tile(mask, (1, 4))
    resetp = np.ones((128, 512), np.float32)
    resetp[:, ::128] = 0.0
    resets = np.ones((128, 64), np.float32)
    resets[:, ::16] = 0.0
    ident = np.eye(128, dtype=np.float32)
    return dict(cosF=cosF, sinF=sinF, krc=kc, krs_=ks, retEp=Ep, retcp=cp, retEs=Es, retcs=cs,
                mask4=mask4, resetp=resetp, resets=resets, ident=ident)


_IN_SHAPES = dict(
    xp=[2, 2048, 1024], xs=[64, 1024], sgla=[4, 256, 128], sret=[4, 256, 128],
    cckv=[4, 4096, 512], ckr=[4, 4096, 64], sconv=[16, 2816], nrm=[5, 1024],
    w_in_ab=[1024, 3088], w_ab_sw=[1024, 512], wgu=[16, 256], bgate=[1, 256], ggla=[1, 512],
    gret=[1, 512], w_out_ab=[1024, 1024], w_in_c=[1024, 960], gq=[1, 384], gkv=[1, 512],
    wuq_n=[384, 1024], wuq_r=[384, 512], wuq_rs=[384, 512], wuk=[512, 1024], wukT=[128, 8, 512],
    wuv=[512, 1024], w_out_c=[1024, 1024], wffi=[2, 1024, 5632], dwc=[8, 2816],
    wffo=[2, 2816, 1024],
    cosF=[128, 2112], sinF=[128, 2112], krc=[128, 17, 64], krs_=[128, 17, 64],
    retEp=[128, 2, 2, 512], retcp=[128, 2, 3], retEs=[128, 2, 2, 64], retcs=[128, 2, 3],
    mask4=[128, 512], resetp=[128, 512], resets=[128, 64], ident=[128, 128],
)
_OUT_SHAPES = dict(
    yp=[2, 2048, 1024], ys=[64, 1024], glap=[2, 256, 128], glas=[4, 256, 128],
    retp=[2, 256, 128], rets=[4, 256, 128], ckvp=[2, 2048, 512], ckvs=[64, 512],
    krp=[2, 2048, 64], krs=[64, 64], convp=[2, 2, 2, 2816], convs=[2, 4, 2, 2816],
)


def build_program():
    nc = bass.Bass("TRN2", target_bir_lowering=False)
    D = {k: nc.dram_tensor(k, list(v), F32, kind="ExternalInput").ap() for k, v in _IN_SHAPES.items()}
    O = {k: nc.dram_tensor(k, list(v), F32, kind="ExternalOutput").ap() for k, v in _OUT_SHAPES.items()}
    P = Prog(nc)
    A = Arena(P, 52900)
    PB = [P.ps([128, 512], F32, "bank") for _ in range(8)]
    TPB = [Tok(excl=True) for _ in range(8)]
    gen = {"banks": [0, 1, 2, 3, 4, 5], "i": 0}

    def gb():
        b = gen["banks"][gen["i"] % len(gen["banks"])]
        gen["i"] += 1
        return PB[b], TPB[b]

    def bfv(bank):
        return bank[:].bitcast(BF16)

    def mm(out, lhsT, rhs, R, W, st=True, sp=True, force=()):
        return P.op("pe", lambda e: e.matmul(out, lhsT, rhs, start=st, stop=sp), R, W, force=force)

    def tr(out, in_, idn, R, W):
        P.op("pe", lambda e: e.transpose(out, in_, idn), R, W)

    def act(out, in_, func, R, W, bias=None, scale=None, accum=None):
        kw = {}
        if bias is not None:
            kw["bias"] = bias
        if scale is not None:
            kw["scale"] = scale
        if accum is not None:
            kw["accum_out"] = accum
        P.op("act", lambda e: e.activation(out=out, in_=in_, func=func, **kw), R, W)

    def tt(out, a, b, op, R, W, eng="dve"):
        P.op(eng, lambda e: e.tensor_tensor(out=out, in0=a, in1=b, op=op), R, W)

    def ts(out, a, s1, op0, R, W, s2=None, op1=None, eng="dve"):
        if op1 is None:
            P.op(eng, lambda e: e.tensor_scalar(out, a, s1, None, op0=op0), R, W)
        else:
            P.op(eng, lambda e: e.tensor_scalar(out, a, s1, s2, op0=op0, op1=op1), R, W)

    def stt(out, in0, scalar, in1, op0, op1, R, W, eng="dve"):
        P.op(eng, lambda e: e.scalar_tensor_tensor(out=out, in0=in0, scalar=scalar, in1=in1,
                                                    op0=op0, op1=op1), R, W)

    def cp(out, in_, R, W, eng="dve"):
        if eng == "act":
            act(out, in_, AF.Copy, R, W)
        else:
            P.op(eng, lambda e: e.tensor_copy(out=out, in_=in_), R, W)

    def memset(ap, val, W, eng="dve"):
        P.op(eng, lambda e: e.memset(ap, val), (), W)

    def load(dst, src, q="sp"):
        t = Tok()
        P.dma(q, dst, src, W=[t])
        return t

    MUL, ADD, SUB = ALU.mult, ALU.add, ALU.subtract

    xT = A.f32(8, 2048 + 64)
    t_x = [Tok() for _ in range(5)]
    ident_f = A.f32(128)
    ident_b = A.bf16(128)
    ones_b = A.bf16(128)
    gn = A.f32(8, 5)
    dw = A.f32(NJ, 8)
    gqc = A.f32(3)
    nbg = A.f32(2)
    t_idf = load(ident_f, D["ident"])
    t_idb = load(ident_b, D["ident"], "pool")
    t_one = Tok()
    memset(ones_b, 1.0, [t_one])
    t_const = [t_idf, t_idb, t_one]
    PSQ = Rot([A.bf16(512) for _ in range(2)])
    PRS, t_PRS = A.f32(512), Tok()
    PHT, t_PHT = A.bf16(8, 512), Tok()
    PWA, t_PWA, t_PWA2 = A.bf16(8, 512), Tok(), Tok()
    base_mark = A.mark()

    def rows_to_cols(src_rows, R_, n, dst, t_dst):
        m = A.mark()
        stage = A.f32(n * 128)
        t_s = load(stage[:R_, :], src_rows)
        bank, tb = gb()
        for c in range(n):
            tr(bank[:, c * R_:(c + 1) * R_], stage[:R_, c * 128:(c + 1) * 128], ident_f[:R_, :R_],
               [t_s, t_idf], [tb])
        if len(dst.shape) == 3:
            cp(dst, bank[:, 0:n * R_].rearrange("p (a b) -> p a b", b=R_), [tb], [t_dst])
        else:
            cp(dst, bank[:, 0:n * R_], [tb], [t_dst])
        P.barrier()
        A.reset(m)

    t_gn, t_dw, t_gq, t_nbg = Tok(), Tok(), Tok(), Tok()
    rows_to_cols(D["nrm"], 5, 8, gn, t_gn)
    rows_to_cols(D["dwc"], 8, NJ, dw, t_dw)
    rows_to_cols(D["gq"], 1, 3, gqc, t_gq)
    rows_to_cols(D["bgate"], 1, 2, nbg, t_nbg)
    ts(nbg, nbg, -1.0, MUL, [t_nbg], [t_nbg])

    def load_x(cfg, xd):
        m = A.mark()
        xin = Rot([A.f32(1024) for _ in range(2)])
        n = cfg.XB
        for blk in range(cfg.T // n):
            xi, txi = xin.next()
            P.dma("sp", xi[:n, :], xd[blk * n:(blk + 1) * n, :], W=[txi])
            ttile = (blk * n) // cfg.TT
            for half in range(2):
                bank, tb = gb()
                for c4 in range(4):
                    c = half * 4 + c4
                    tr(bank[:, c4 * n:(c4 + 1) * n], xi[:n, c * 128:(c + 1) * 128], ident_f[:n, :n],
                       [txi, t_idf], [tb])
                src = bank[:, 0:4 * n].rearrange("p (a b) -> p a b", b=n)
                dst = xT[:, half * 4:(half + 1) * 4, cfg.x0 + blk * n:cfg.x0 + (blk + 1) * n]
                cp(dst, src, [tb], [t_x[cfg.xt0 + ttile]], eng=("act" if half == 0 else "dve"))
        P.barrier()
        A.reset(m)

    def rmsnorm(cfg, tti, grow, hdst, t_h, sqrot, _unused, rs, t_rs):
        n = cfg.TT
        cols = slice(cfg.x0 + tti * n, cfg.x0 + (tti + 1) * n)
        t_xt = t_x[cfg.xt0 + tti]
        bank, tb = gb()
        for c in range(8):
            sqb, t_sq = sqrot.next()
            act(sqb[:, :n], xT[:, c, cols], AF.Square, [t_xt], [t_sq])
            mm(bank[:, :n], ones_b, sqb[:, :n], [t_sq, t_one], [tb], st=(c == 0), sp=(c == 7))
        act(rs[:, :n], bank[:, :n], AF.Ln, [tb], [t_rs], scale=1.0 / 1024, bias=EPS)
        act(rs[:, :n], rs[:, :n], AF.Exp, [t_rs], [t_rs], scale=-0.5)
        for c in range(8):
            stt(hdst[:, c, :n], xT[:, c, cols], gn[:, c, grow:grow + 1], rs[:, :n], MUL, MUL,
                [t_xt, t_rs, t_gn], [t_h])

    def rstd_small(dst, src, inv_n, R, W):
        act(dst, src, AF.Ln, R, W, scale=inv_n, bias=EPS)
        act(dst, dst, AF.Exp, W, W, scale=-0.5)

    def phase_mixer_ab(cfg, seq, s0_gla, s0_ret, out_gla, out_ret):
        m = A.mark()
        n, C, NCH = cfg.TT, cfg.C, cfg.NCH
        refi = (C - 1) // 2
        P.nobar = True
        rmsnorm(cfg, 0, 0, PHT, t_PHT, PSQ, None, PRS, t_PRS)
        P.dma("pool", PWA, D["w_in_ab"][:, 512:1024].rearrange("(kc p) c -> p kc c", p=128), W=[t_PWA, t_PWA2])
        P.nobar = False
        retE = A.f32(2, 2, n)
        retc = A.f32(2, 3)
        mask4 = A.f32(512)
        reset = A.f32(n)
        gglab = A.f32(512)
        gretb = A.f32(512)
        wgu = A.f32(256)
        Wlo = A.bf16(8, 16)
        t_tab = [load(retE, D["retEp"] if cfg.prompt else D["retEs"]),
                 load(retc, D["retcp"] if cfg.prompt else D["retcs"]),
                 load(mask4, D["mask4"]),
                 load(reset, D["resetp"] if cfg.prompt else D["resets"]),
                 load(gglab, D["ggla"].to_broadcast([128, 512])),
                 load(gretb, D["gret"].to_broadcast([128, 512])),
                 load(wgu[:16, :], D["wgu"]),
                 load(Wlo, D["w_in_ab"][:, 1536:1552].rearrange("(kc p) c -> p kc c", p=128), "pool")]
        cosr = Rot([A.f32(n) for _ in range(1)])
        sinr = Rot([A.f32(n) for _ in range(1)])
        sq, t_sq = PSQ, None
        rs, t_rs = PRS, t_PRS
        hT2, t_h2 = [PHT, A.bf16(8, n)], [t_PHT, Tok()]
        WA = Rot([A.bf16(8, 512) for _ in range(2 if cfg.prompt else 5)])
        WA.items.insert(0, (PWA, t_PWA))
        vtok = [A.bf16(NCH, 512), A.bf16(NCH, 512)]
        t_v = [[Tok() for _ in range(NCH)] for _ in range(2)]
        gs = [A.bf16(NCH, 512), A.bf16(NCH, 512)]
        t_gs = [[Tok() for _ in range(NCH)] for _ in range(2)]
        srt = Rot([A.f32(512) for _ in range(1)])
        loT, t_lo = A.f32(n), Tok()
        ebuf, t_e = A.f32(n), Tok()
        spb, t_sp = A.f32(2, n), Tok()
        gate_region = (loT, ebuf, spb)
        cum, t_cum = A.f32(2, n), Tok()
        E1, E2 = A.f32(2, n), A.f32(2, n)
        t_E = [Tok(), Tok()]
        pbias, nbias, dd = A.f32(2, NCH), A.f32(2, NCH), A.f32(2, NCH)
        eref, elr, dec = A.f32(2, NCH), A.f32(2, NCH), A.f32(2, NCH)
        t_col = Tok()
        qrel, krel = A.bf16(4, n), A.bf16(4, n)
        t_q = [Tok() for _ in range(4)]
        t_k = [Tok() for _ in range(4)]
        r1, r2, r3 = ebuf, spb[:, 0, :], spb[:, 1, :]
        t_r = [t_e, t_sp, t_sp]
        kreltok, t_kt = A.bf16(4, 128), Tok()
        Sm2, t_sm2 = [A.bf16(8, 128), A.bf16(8, 128)], [[Tok(), Tok()], [Tok(), Tok()]]
        S, t_S = A.f32(4, 128), [Tok() for _ in range(4)]
        Sp2, t_Sp2 = [A.bf16(4, 128), A.bf16(4, 128)], [[Tok() for _ in range(4)] for _ in range(2)]
        tmpkv, t_tmp = A.f32(4, 128), [Tok() for _ in range(4)]
        mix2, t_mix2 = [A.bf16(1024), A.bf16(1024)], [Tok(), Tok()]
        tmpo, t_tmpo = A.f32(512), Tok()
        mixT, t_mixT = A.bf16(8, n), Tok()
        gen["banks"] = [0, 1, 2, 3]
        obank2 = [[(PB[6], TPB[6]), (PB[7], TPB[7])], [(PB[4], TPB[4]), (PB[5], TPB[5])]]
        stat2, t_ss2, t_st2 = [A.f32(16), A.f32(16)], [[Tok() for _ in range(12)] for _ in range(2)], [Tok(), Tok()]
        junk8 = A.bf16(8, 128)

        def wpiece(src):
            w, tw = WA.next()
            ncol = src.shape[1]
            P.dma("pool", w[:, :, :ncol], src.rearrange("(kc p) c -> p kc c", p=128),
                  W=([tw, t_PWA2] if tw is t_PWA else [tw]))
            return w, tw

        if cfg.prompt:
            for u in range(4):
                memset(S[:, u, :], 0.0, [t_S[u]])

        for tti in range(cfg.NT):
            cols = slice(tti * n, (tti + 1) * n)
            tcol = slice(cfg.tab0 + tti * n, cfg.tab0 + (tti + 1) * n)
            cos_t, t_cos = cosr.next()
            sin_t, t_sin = sinr.next()
            P.dma("sp", cos_t, D["cosF"][:, tcol], W=[t_cos])
            P.dma("sp", sin_t, D["sinF"][:, tcol], W=[t_sin])
            hT, t_h = hT2[tti % 2], t_h2[tti % 2]
            ck(1)
            bank, tb = gb()
            for kc in range(8):
                mm(bank[:16, :n], Wlo[:, kc, :], hT[:, kc, :n], [t_h, t_tab[7]], [tb], st=(kc == 0), sp=(kc == 7))
            cp(loT[:16, :], bank[:16, :n], [tb], [t_lo], eng="act")
            for kc in range(2):
                bank, tb = gb()
                mm(bank[:, :n], wgu[:16, kc * 128:(kc + 1) * 128], loT[:16, :], [t_lo, t_tab[6]], [tb])
                act(ebuf, bank[:, :n], AF.Exp, [tb, t_nbg], [t_e], scale=-1.0, bias=nbg[:, kc:kc + 1])
                act(spb[:, kc, :], ebuf, AF.Ln, [t_e], [t_sp], bias=1.0)
                P.op("dve", lambda e, kc=kc: e.tensor_tensor_scan(out=cum[:, kc, :], data0=reset, data1=spb[:, kc, :],
                                                                   initial=0.0, op0=MUL, op1=ADD),
                     [t_sp, t_tab[3]], [t_cum])
            ck(3)
            for pi, c0 in enumerate((512, 1024, 2064, 2576)):
                if tti == 0 and pi == 0:
                    w, tw = WA.next()
                else:
                    w, tw = wpiece(D["w_in_ab"][:, c0:c0 + 512])
                grp = pi // 2
                for ci in range(NCH):
                    bank, tb = gb()
                    for kc in range(8):
                        mm(bank[:C, :], hT[:, kc, ci * C:(ci + 1) * C], w[:, kc, :], [t_h, tw], [tb],
                           st=(kc == 0), sp=(kc == 7))
                    if pi % 2 == 0:
                        cp(vtok[grp][:C, ci, :], bank[:C, :], [tb], [t_v[grp][ci]], eng="act")
                    else:
                        sr, tsr = srt.next()
                        act(sr[:C, :], bank[:C, :], AF.Silu, [tb], [tsr])
                        tt(gs[grp][:C, ci, :], sr[:C, :], (gglab if grp == 0 else gretb)[:C, :], MUL,
                           [tsr, t_tab[4 + grp]], [t_gs[grp][ci]])
            ck(2)
            cview = cum.rearrange("p k (c i) -> p k c i", i=C)
            cref = cview[:, :, :, refi]
            clast = cview[:, :, :, C - 1]
            ts(pbias, cref, 1.0 / 16, MUL, [t_cum], [t_col])
            ts(nbias, cref, -1.0 / 16, MUL, [t_cum], [t_col])
            tt(dd, clast, cref, SUB, [t_cum], [t_col])
            act(eref, cref, AF.Exp, [t_cum], [t_col], scale=-1.0 / 16)
            act(elr, dd, AF.Exp, [t_col], [t_col], scale=-1.0 / 16)
            act(dec, clast, AF.Exp, [t_cum], [t_col], scale=-1.0 / 16)
            for kc in range(2):
                for ci in range(NCH):
                    cs_ = slice(ci * C, (ci + 1) * C)
                    act(E1[:, kc, cs_], cum[:, kc, cs_], AF.Exp, [t_cum, t_col], [t_E[0]],
                        scale=-1.0 / 16, bias=pbias[:, kc, ci:ci + 1])
                    act(E2[:, kc, cs_], cum[:, kc, cs_], AF.Exp, [t_cum, t_col], [t_E[1]],
                        scale=1.0 / 16, bias=nbias[:, kc, ci:ci + 1])
            ck(4)
            w, tw = wpiece(D["w_in_ab"][:, 0:512])
            for j in range(4):
                bank, tb = gb()
                for kc in range(8):
                    mm(bank[:, :n], w[:, kc, j * 128:(j + 1) * 128], hT[:, kc, :n], [t_h, tw], [tb],
                       st=(kc == 0), sp=(kc == 7))
                kcx = j % 2
                if j < 2:
                    stt(qrel[:, kcx, :], bank[:, :n], 0.125, E1[:, kcx, :], MUL, MUL, [tb, t_E[0]], [t_q[kcx]])
                else:
                    tt(krel[:, kcx, :], bank[:, :n], E2[:, kcx, :], MUL, [tb, t_E[1]], [t_k[kcx]])
            w1, tw1 = wpiece(D["w_in_ab"][:, 1552:2064])
            w2, tw2 = wpiece(D["w_ab_sw"])
            for j in range(4):
                bank1, tb1 = gb()
                for kc in range(8):
                    mm(bank1[:, :n], w1[:, kc, j * 128:(j + 1) * 128], hT[:, kc, :n], [t_h, tw1], [tb1],
                       st=(kc == 0), sp=(kc == 7))
                bank2, tb2 = gb()
                for kc in range(8):
                    mm(bank2[:, :n], w2[:, kc, j * 128:(j + 1) * 128], hT[:, kc, :n], [t_h, tw2], [tb2],
                       st=(kc == 0), sp=(kc == 7))
                kcx = j % 2
                tt(r1, bank1[:, :n], cos_t, MUL, [tb1, t_cos], [t_r[0]])
                tt(r2, bank2[:, :n], sin_t, MUL, [tb2, t_sin], [t_r[1]])
                tt(r3, r1, r2, ADD, [t_r[0], t_r[1]], [t_r[2]])
                if j < 2:
                    tt(qrel[:, 2 + kcx, :], r3, retE[:, kcx, 0, :], MUL, [t_r[2], t_tab[0]], [t_q[2 + kcx]])
                else:
                    tt(krel[:, 2 + kcx, :], r3, retE[:, kcx, 1, :], MUL, [t_r[2], t_tab[0]], [t_k[2 + kcx]])
            ck(5)
            if tti + 1 < cfg.NT:
                rmsnorm(cfg, tti + 1, 0, hT2[(tti + 1) % 2], t_h2[(tti + 1) % 2], sq, t_sq, rs, t_rs)

            def chunk_front(ci):
                    Sm, t_sm, Sp, t_Sp = Sm2[ci % 2], t_sm2[ci % 2], Sp2[ci % 2], t_Sp2[ci % 2]
                    g = tti * NCH + ci
                    cs_ = slice(ci * C, (ci + 1) * C)
                    first = (not cfg.prompt) or g == 0
                    last = (not cfg.prompt) or g == cfg.NT * NCH - 1
                    if not cfg.prompt:
                        for u in range(4):
                            src = (s0_gla if u < 2 else s0_ret)[ci, (u % 2) * 128:(u % 2 + 1) * 128, :]
                            P.dma("sp", S[:, u, :], src, W=[t_S[u]])
                    bank, tb = gb()
                    bv = bfv(bank)
                    for u in range(4):
                        tr(bv[:C, u * 128:(u + 1) * 128], krel[:, u, cs_], ident_b, [t_k[u], t_idb], [tb])
                    cp(kreltok[:C, :, :], bv[:C, 0:512].rearrange("p (a b) -> p a b", b=128), [tb], [t_kt], eng="act")
                    for u in range(4):
                        sc = eref[:, u, ci:ci + 1] if u < 2 else retc[:, u - 2, 0:1]
                        act(Sp[:, u, :], S[:, u, :], AF.Copy, [t_S[u], t_col, t_tab[1]], [t_Sp[u]], scale=sc)
                    for half in range(2):
                        bank, tb = gb()
                        for uu in range(2):
                            u = half * 2 + uu
                            grp, kcx = u // 2, u % 2
                            mm(bank[:, uu * 256:(uu + 1) * 256], kreltok[:C, u, :],
                               vtok[grp][:C, ci, kcx * 256:(kcx + 1) * 256], [t_kt, t_v[grp][ci]], [tb])
                        for uu in range(2):
                            u = half * 2 + uu
                            kcx = u % 2
                            e_lr = elr[:, kcx, ci:ci + 1] if u < 2 else retc[:, kcx, 1:2]
                            e_dc = dec[:, kcx, ci:ci + 1] if u < 2 else retc[:, kcx, 2:3]
                            ts(tmpkv[0:64, u, :], bank[0:64, uu * 256:uu * 256 + 128], e_lr[0:64, :], MUL,
                               [tb, t_col, t_tab[1]], [t_tmp[u]])
                            ts(tmpkv[64:128, u, :], bank[64:128, uu * 256 + 128:uu * 256 + 256], e_lr[64:128, :], MUL,
                               [tb, t_col, t_tab[1]], [t_tmp[u]])
                            stt(S[:, u, :], S[:, u, :], e_dc, tmpkv[:, u, :], MUL, ADD,
                                [t_S[u], t_tmp[u], t_col, t_tab[1]], [t_S[u]])
                            if last:
                                dst = (out_gla if u < 2 else out_ret)
                                dsti = seq if cfg.prompt else ci
                                P.dma("sp", dst[dsti, kcx * 128:(kcx + 1) * 128, :], S[:, u, :], R=[t_S[u]])
                    v3 = lambda ap: ap.rearrange("p (h c) -> p h c", c=128)[:C, :, :C]
                    for par in range(2):
                        bank, tb = gb()
                        pr = slice(par * 64, par * 64 + 64)
                        for slot in range(4):
                            grp = slot // 2
                            u = grp * 2 + slot % 2
                            mm(bank[:C, slot * 128:slot * 128 + C], krel[pr, u, cs_], qrel[pr, u, cs_],
                               [t_k[u], t_q[u]], [tb])
                        tt(Sm[:C, par * 4:(par + 1) * 4, :C], v3(bank[:, :]), v3(mask4[:, :]), MUL,
                           [tb, t_tab[2]], [t_sm[par]])
            def chunk_back(ci):
                    Sm, t_sm, Sp, t_Sp = Sm2[ci % 2], t_sm2[ci % 2], Sp2[ci % 2], t_Sp2[ci % 2]
                    obank = obank2[ci % 2]
                    (oA, t_oA), (oB, t_oB) = obank
                    mix, t_mix = mix2[ci % 2], t_mix2[ci % 2]
                    cs_ = slice(ci * C, (ci + 1) * C)
                    for grp in range(2):
                        ob, tob = obank[grp]
                        for hh in range(4):
                            u = grp * 2 + hh // 2
                            par = hh % 2
                            pr = slice(par * 64, par * 64 + 64)
                            smi = par * 4 + grp * 2 + hh // 2
                            i1 = mm(ob[:C, hh * 128:(hh + 1) * 128], Sm[:C, smi, :C],
                                    vtok[grp][:C, ci, hh * 128:(hh + 1) * 128], [t_sm[par], t_v[grp][ci]], [tob],
                                    st=True, sp=False)
                            mm(ob[:C, hh * 128:(hh + 1) * 128], qrel[pr, u, cs_], Sp[pr, u, :],
                               [t_q[u], t_Sp[u]], [tob], st=False, sp=True, force=([i1] if C < 64 else ()))
                    stat, t_ss, t_st = stat2[ci % 2], t_ss2[ci % 2], t_st2[ci % 2]
                    for grp in range(2):
                        ob, tob = obank[grp]
                        for hh in range(4):
                            k8 = grp * 4 + hh
                            act(junk8[:C, k8, :], ob[:C, hh * 128:(hh + 1) * 128], AF.Square, [tob], [t_ss[k8]],
                                accum=stat[:C, k8:k8 + 1])
                    for hh in range(4):
                        act(junk8[:C, hh, :], oB[:C, hh * 128:(hh + 1) * 128], AF.Copy, [t_oB, t_ss[hh]], [t_ss[hh], t_ss[8 + hh]],
                            accum=stat[:C, 8 + hh:9 + hh])
                    stt(stat[:C, 12:16], stat[:C, 8:12], -1.0 / 128, stat[:C, 8:12], MUL, MUL, [t_st] + t_ss[8:12], [t_st])
                    tt(stat[:C, 4:8], stat[:C, 4:8], stat[:C, 12:16], ADD, [t_st] + t_ss[4:8], [t_st])
                    ts(stat[:C, 8:12], stat[:C, 8:12], 1.0 / 128, MUL, [t_st] + t_ss[8:12], [t_st] + t_ss[8:12])
            def chunk_back2(ci):
                    obank = obank2[ci % 2]
                    (oA, t_oA), (oB, t_oB) = obank
                    mix, t_mix = mix2[ci % 2], t_mix2[ci % 2]
                    stat, t_ss, t_st = stat2[ci % 2], t_ss2[ci % 2], t_st2[ci % 2]
                    rstd_small(stat[:C, 0:8], stat[:C, 0:8], 1.0 / 128, [t_st] + t_ss[0:4], [t_st])
                    for hh in range(4):
                        hs = slice(hh * 128, (hh + 1) * 128)
                        stt(mix[:C, hs], oA[:C, hs], stat[:C, hh:hh + 1], gs[0][:C, ci, hs], MUL, MUL,
                            [t_oA, t_st, t_gs[0][ci]], [t_mix])
                        ts(tmpo[:C, hs], oB[:C, hs], stat[:C, 8 + hh:9 + hh], SUB, [t_oB, t_st], [t_tmpo],
                           s2=stat[:C, 4 + hh:5 + hh], op1=MUL)
                    tt(mix[:C, 512:1024], tmpo[:C, :], gs[1][:C, ci, :], MUL, [t_tmpo, t_gs[1][ci]], [t_mix])
            def chunk_trans(ci):
                    mix, t_mix = mix2[ci % 2], t_mix2[ci % 2]
                    cs_ = slice(ci * C, (ci + 1) * C)
                    bank, tb = gb()
                    bv = bfv(bank)
                    for c in range(8):
                        tr(bv[:, c * 128:c * 128 + C], mix[:C, c * 128:(c + 1) * 128], ident_b[:C, :C],
                           [t_mix, t_idb], [tb])
                    cp(mixT[:, :, cs_], bv[:, :].rearrange("p (a b) -> p a b", b=128)[:, :, :C], [tb], [t_mixT], eng="act")
            for st_ in range(NCH + 3):
                if st_ < NCH:
                    chunk_front(st_)
                if 1 <= st_ <= NCH:
                    chunk_back(st_ - 1)
                if 2 <= st_ <= NCH + 1:
                    chunk_back2(st_ - 2)
                if st_ >= 3:
                    chunk_trans(st_ - 3)
            ck(10)
            for half in range(2):
                w, tw = wpiece(D["w_out_ab"][:, half * 512:(half + 1) * 512])
                for o4 in range(4):
                    oc = half * 4 + o4
                    bank, tb = gb()
                    for kc in range(8):
                        mm(bank[:, :n], w[:, kc, o4 * 128:(o4 + 1) * 128], mixT[:, kc, :n], [tw, t_mixT], [tb],
                           st=(kc == 0), sp=(kc == 7))
                    xc = slice(cfg.x0 + tti * n, cfg.x0 + (tti + 1) * n)
                    tt(xT[:, oc, xc], xT[:, oc, xc], bank[:, :n], ADD, [t_x[cfg.xt0 + tti], tb], [t_x[cfg.xt0 + tti]])
        P.barrier()
        A.reset(m)

    def phase_ffn(groups, layer):
        m = A.mark()
        NH = NJ // 2
        Ttot = sum(g[0].T for g in groups)
        nmax = max(g[0].TT for g in groups)
        deep = not any(g[0].prompt for g in groups)
        hT = A.bf16(8, Ttot)
        sq, t_sq = PSQ, None
        rs, t_rs = PRS, t_PRS
        actb = A.bf16(NH, Ttot)
        WI = Rot([A.bf16(8, 256) for _ in range(7 if deep else 2)])
        WI.items.insert(0, (PWA[:, :, 0:256], t_PWA))
        ffn_pro = {}
        P.nobar = True
        cfg0 = groups[0][0]
        rmsnorm(cfg0, 0, 2 + layer, PHT[:, :, :cfg0.TT], t_PHT, PSQ, None, PRS, t_PRS)
        w0, tw0 = WI.next()
        ffn_pro["tw"] = (tw0, t_PWA2)
        for g_ in range(2):
            c0_ = g_ * D_FF
            P.dma("pool", w0[:, :, g_ * 128:(g_ + 1) * 128],
                  D["wffi"][layer, :, c0_:c0_ + 128].rearrange("(kc p) c -> p kc c", p=128), W=[ffn_pro["tw"][g_]])
        P.nobar = False
        WO = Rot([A.bf16(NH, 128) for _ in range(6 if deep else 2)])
        cbuf = Rot([A.f32(nmax) for _ in range(6 if deep else 2)])
        gbuf = Rot([A.f32(nmax) for _ in range(6 if deep else 2)])
        wi_tok = {}
        abc_tok = {}
        gen["banks"] = [0, 1, 2, 3, 4, 5, 6, 7]
        units = []
        G = []
        col = 0
        stg, t_stg = A.f32(NJ * 128), Tok()
        for cfg, carry_in_rows, conv_out in groups:
            nseq, L, n = cfg.nseq, cfg.L, cfg.TT
            g = Cfg()
            g.cfg, g.conv_out, g.c0 = cfg, conv_out, col
            g.carry, g.t_carry = A.f32(NJ, nseq * 2), [Tok() for _ in range(NJ)]
            g.abuf = Rot([A.f32(nseq * (L + 2)) for _ in range(6 if deep else 3)])
            g.t_h = [Tok() for _ in range(cfg.NT)]
            g.t_act = [[Tok() for _ in range(cfg.NT)] for _ in range(NH)]
            if carry_in_rows is None:
                for j in range(NJ):
                    memset(g.carry[:, j, :], 0.0, [g.t_carry[j]])
            else:
                R_ = nseq * 2
                stage, t_s = stg, t_stg
                P.dma("sp", stage[:R_, :], carry_in_rows, W=[t_s])
                bank, tb = gb()
                for j in range(NJ):
                    tr(bank[:, j * R_:(j + 1) * R_], stage[:R_, j * 128:(j + 1) * 128], ident_f[:R_, :R_],
                       [t_s, t_idf], [tb])
                cp(g.carry, bank[:, 0:NJ * R_].rearrange("p (a b) -> p a b", b=R_), [tb], g.t_carry)
            g.hT = []
            for tti in range(cfg.NT):
                if not units:
                    g.hT.append(PHT[:, :, :n])
                    g.t_h[tti] = t_PHT
                else:
                    g.hT.append(hT[:, :, col + tti * n:col + (tti + 1) * n])
                    rmsnorm(cfg, tti, 2 + layer, g.hT[tti], g.t_h[tti], sq, t_sq, rs, t_rs)
                units.append((g, tti))
            col += cfg.T
            G.append(g)
        wd = lambda k, j: dw[:, j, layer * 3 + k:layer * 3 + k + 1]
        bd = lambda j: dw[:, j, 6 + layer:7 + layer]
        for jh in range(2):
            for jj in range(NH):
                j = jh * NH + jj
                if j == 0:
                    w, tw = w0, ffn_pro["tw"]
                    wi_tok[id(tw0)] = tw
                else:
                    w, tw = WI.next()
                    tw = wi_tok.setdefault(id(tw), (tw, Tok()))
                    for g_ in range(2):
                        c0 = g_ * D_FF + j * 128
                        P.dma("pool", w[:, :, g_ * 128:(g_ + 1) * 128],
                              D["wffi"][layer, :, c0:c0 + 128].rearrange("(kc p) c -> p kc c", p=128), W=[tw[g_]])
                for g, tti in units:
                    cfg = g.cfg
                    nseq, L, n = cfg.nseq, cfg.L, cfg.TT
                    cols = slice(g.c0 + tti * n, g.c0 + (tti + 1) * n)
                    ba, tba = gb()
                    for kc in range(8):
                        mm(ba[:, :n], w[:, kc, 0:128], g.hT[tti][:, kc, :], [tw[0], g.t_h[tti]], [tba], st=(kc == 0), sp=(kc == 7))
                    bu, tbu = gb()
                    for kc in range(8):
                        mm(bu[:, :n], w[:, kc, 128:256], g.hT[tti][:, kc, :], [tw[1], g.t_h[tti]], [tbu], st=(kc == 0), sp=(kc == 7))
                    ab, tab_ = g.abuf.next()
                    tabc = abc_tok.setdefault(id(tab_), Tok())
                    ab3 = ab.rearrange("p (s l) -> p s l", l=L + 2)
                    cr = g.carry[:, j, :].rearrange("p (s r) -> p s r", r=2)
                    cp(ab3[:, :, 0:2], cr, [g.t_carry[j]], [tabc])
                    act(ab3[:, :, 2:L + 2], ba[:, :n].rearrange("p (s l) -> p s l", l=L), AF.Copy, [tba], [tab_])
                    cb, tcb = cbuf.next()
                    cb3 = cb[:, :n].rearrange("p (s l) -> p s l", l=L)
                    act(cb[:, :n], ba[:, :n], AF.Identity, [tba, t_dw], [tcb], scale=wd(2, j), bias=bd(j))
                    stt(cb3, ab3[:, :, 1:L + 1], wd(1, j), cb3, MUL, ADD, [tab_, tabc, tcb, t_dw], [tcb])
                    stt(cb3, ab3[:, :, 0:L], wd(0, j), cb3, MUL, ADD, [tab_, tabc, tcb, t_dw], [tcb])
                    cp(cr, ab3[:, :, L:L + 2], [tab_], [g.t_carry[j]])
                    ge, tge = gbuf.next()
                    act(ge[:, :n], cb[:, :n], AF.Gelu, [tcb], [tge])
                    tt(actb[:, jj, cols], ge[:, :n], bu[:, :n], MUL, [tge, tbu], [g.t_act[jj][tti]])
            for oc in range(8):
                w, tw = WO.next()
                P.dma("pool", w, D["wffo"][layer, jh * NH * 128:(jh + 1) * NH * 128, oc * 128:(oc + 1) * 128]
                      .rearrange("(j p) c -> p j c", p=128), W=[tw])
                for g, tti in units:
                    cfg = g.cfg
                    n = cfg.TT
                    cols = slice(g.c0 + tti * n, g.c0 + (tti + 1) * n)
                    xc = slice(cfg.x0 + tti * n, cfg.x0 + (tti + 1) * n)
                    t_xt = t_x[cfg.xt0 + tti]
                    bank, tb = gb()
                    for jj in range(NH):
                        mm(bank[:, :n], w[:, jj, :], actb[:, jj, cols], [tw, g.t_act[jj][tti]], [tb],
                           st=(jj == 0), sp=(jj == NH - 1))
                    tt(xT[:, oc, xc], xT[:, oc, xc], bank[:, :n], ADD, [t_xt, tb], [t_xt])
        for g in G:
            R_ = g.cfg.nseq * 2
            cstage, t_cs = stg, t_stg
            for q4 in range((NJ + 3) // 4):
                j0, j1 = q4 * 4, min(NJ, q4 * 4 + 4)
                bank, tb = gb()
                for j in range(j0, j1):
                    tr(bank[:R_, (j - j0) * 128:(j - j0 + 1) * 128], g.carry[:, j, :], ident_f, [g.t_carry[j], t_idf], [tb])
                cp(cstage[:R_, j0 * 128:j1 * 128], bank[:R_, 0:(j1 - j0) * 128], [tb], [t_cs])
            P.dma("sp", g.conv_out.rearrange("s r c -> (s r) c"), cstage[:R_, :], R=[t_cs])
        P.barrier()
        A.reset(m)

    SCALE = 192.0 ** -0.5

    def phase_mla(cfg, seq, ckv_out, kr_out, cache_ckv=None, cache_kr=None):
        m = A.mark()
        n, C, NCH, T = cfg.TT, cfg.C, cfg.NCH, cfg.T
        KB = 128 if cfg.prompt else 16
        NB = T // KB
        cqnT, t_cqn = A.bf16(3, T), [Tok() for _ in range(cfg.NT)]
        ckvT, t_ckvT = A.bf16(4, T), [Tok() for _ in range(cfg.NT)]
        krT2, t_krT = A.bf16(T), [Tok() for _ in range(cfg.NT)]
        ckvn_b = None
        if not cfg.prompt:
            ckvn_b, t_cnb = A.bf16(NB, 512), [Tok() for _ in range(NB)]
        m1 = A.mark()
        P.nobar = True
        rmsnorm(cfg, 0, 1, PHT, t_PHT, PSQ, None, PRS, t_PRS)
        P.dma("pool", PWA[:, :, 0:384], D["w_in_c"][:, 0:384].rearrange("(kc p) c -> p kc c", p=128), W=[t_PWA, t_PWA2])
        P.nobar = False
        Wc = A.bf16(8, 960)
        t_wc = [t_PWA] + [load(Wc[:, :, c0:c1], D["w_in_c"][:, c0:c1].rearrange("(kc p) c -> p kc c", p=128), "pool")
                          for c0, c1 in ((384, 896), (896, 960))]
        krc, krs_ = A.f32(17, 64), A.f32(17, 64)
        gkvb = A.f32(512)
        t_t = [load(krc, D["krc"]), load(krs_, D["krs_"]), load(gkvb, D["gkv"].to_broadcast([128, 512]))]
        sq, t_sq = PSQ, None
        rs, t_rs = PRS, t_PRS
        hT2c, t_h2c = [PHT, A.bf16(8, n)], [t_PHT, Tok()]
        sq3, t_sq3 = A.bf16(3, n), Tok()
        rq, t_rq = A.f32(n), Tok()
        ckvn = Rot([A.f32(512) for _ in range(3)])
        cb16 = Rot([A.bf16(512) for _ in range(3)])
        krr = Rot([A.f32(64) for _ in range(3)])
        kt1, kt2, t_kt = A.f32(64), A.f32(64), Tok()
        kb16 = Rot([A.bf16(128) for _ in range(3)])
        st1, t_st1 = A.f32(4), Tok()
        junk, t_junk = A.bf16(512), Tok()
        gen["banks"] = [0, 1, 2, 3, 4, 5, 6, 7]
        for tti in range(cfg.NT):
            cols = slice(tti * n, (tti + 1) * n)
            hT, t_h = hT2c[tti % 2], t_h2c[tti % 2]
            cqb = []
            for j in range(3):
                bank, tb = gb()
                for kc in range(8):
                    mm(bank[:, :n], PWA[:, kc, j * 128:(j + 1) * 128], hT[:, kc, :n], [t_wc[0], t_h], [tb],
                       st=(kc == 0), sp=(kc == 7))
                act(sq3[:, j, :], bank[:, :n], AF.Square, [tb], [t_sq3])
                cqb.append((bank, tb))
            bank, tb = gb()
            for j in range(3):
                mm(bank[:, :n], ones_b, sq3[:, j, :], [t_sq3, t_one], [tb], st=(j == 0), sp=(j == 2))
            act(rq, bank[:, :n], AF.Ln, [tb], [t_rq], scale=1.0 / 384, bias=EPS)
            act(rq, rq, AF.Exp, [t_rq], [t_rq], scale=-0.5)
            for j in range(3):
                stt(cqnT[:, j, cols], cqb[j][0][:, :n], gqc[:, j:j + 1], rq, MUL, MUL,
                    [cqb[j][1], t_rq, t_gq], [t_cqn[tti]])
            if tti + 1 < cfg.NT:
                rmsnorm(cfg, tti + 1, 1, hT2c[(tti + 1) % 2], t_h2c[(tti + 1) % 2], sq, t_sq, rs, t_rs)
            def c1_proj(bi):
                blk = tti * (n // KB) + bi
                tcs = slice(bi * KB, (bi + 1) * KB)
                gcs = slice(blk * KB, (blk + 1) * KB)
                bank, tb = gb()
                for kc in range(8):
                    mm(bank[:KB, :], hT[:, kc, tcs], Wc[:, kc, 384:896], [t_h, t_wc[1]], [tb], st=(kc == 0), sp=(kc == 7))
                act(junk[:KB, :], bank[:KB, :], AF.Square, [tb, t_st1], [t_junk, t_st1], accum=st1[:KB, 0:1])
                rstd_small(st1[:KB, 0:1], st1[:KB, 0:1], 1.0 / 512, [t_st1], [t_st1])
                cn, tcn = ckvn.next()
                stt(cn[:KB, :], bank[:KB, :], st1[:KB, 0:1], gkvb[:KB, :], MUL, MUL, [tb, t_st1, t_t[2]], [tcn])
                P.dma("sp", ckv_out[gcs, :], cn[:KB, :], R=[tcn])
                if cfg.prompt:
                    c16, tc16 = cb16.next()
                else:
                    c16, tc16 = ckvn_b[:, blk, :], t_cnb[blk]
                cp(c16[:KB, :], cn[:KB, :], [tcn], [tc16], eng="act")
                bank, tb = gb()
                for kc in range(8):
                    mm(bank[:KB, 0:64], hT[:, kc, tcs], Wc[:, kc, 896:960], [t_h, t_wc[2]], [tb], st=(kc == 0), sp=(kc == 7))
                tblk = blk if cfg.prompt else 16
                tt(kt1[:KB, :], bank[:KB, 0:64], krc[:KB, tblk, :], MUL, [tb, t_t[0]], [t_kt])
                tt(kt2[:KB, 0:32], bank[:KB, 32:64], krs_[:KB, tblk, 0:32], MUL, [tb, t_t[1]], [t_kt])
                tt(kt2[:KB, 32:64], bank[:KB, 0:32], krs_[:KB, tblk, 32:64], MUL, [tb, t_t[1]], [t_kt])
                kr_, tkr = krr.next()
                tt(kr_[:KB, :], kt1[:KB, :], kt2[:KB, :], ADD, [t_kt], [tkr])
                P.dma("sp", kr_out[gcs, :], kr_[:KB, :], R=[tkr])
                k16, tk16 = kb16.next()
                cp(k16[:KB, 0:64], kr_[:KB, :], [tkr], [tk16], eng="act")
                cp(k16[:KB, 64:128], kr_[:KB, :], [tkr], [tk16], eng="act")
                return gcs, c16, tc16, k16, tk16

            def c1_trans(item):
                gcs, c16, tc16, k16, tk16 = item
                bank2, tb2 = gb()
                bv = bfv(bank2)
                for kc in range(4):
                    tr(bv[:, kc * 128:kc * 128 + KB], c16[:KB, kc * 128:(kc + 1) * 128], ident_b[:KB, :KB],
                       [tc16, t_idb], [tb2])
                tr(bv[:, 512:512 + KB], k16[:KB, :], ident_b[:KB, :KB], [tk16, t_idb], [tb2])
                cp(ckvT[:, :, gcs], bv[:, 0:512].rearrange("p (a b) -> p a b", b=128)[:, :, :KB], [tb2], [t_ckvT[tti]])
                cp(krT2[:, gcs], bv[:, 512:512 + KB], [tb2], [t_krT[tti]])

            pend = []
            for bi in range(n // KB):
                pend.append(c1_proj(bi))
                if len(pend) > 1:
                    c1_trans(pend.pop(0))
            while pend:
                c1_trans(pend.pop(0))
        P.barrier()
        A.reset(m1)
        if cfg.prompt:
            mla_prompt_c2(cfg, cqnT, t_cqn, ckvT, t_ckvT, krT2, t_krT)
        else:
            mla_sample_c2(cfg, cqnT, t_cqn, ckvT, t_ckvT, krT2, t_krT, ckvn_b, t_cnb, cache_ckv, cache_kr)
        P.barrier()
        A.reset(m)

    def mla_prompt_c2(cfg, cqnT, t_cqn, ckvT, t_ckvT, krT2, t_krT):
        n, T, NT = cfg.TT, cfg.T, cfg.NT
        qn, t_qn = [A.bf16(T), A.bf16(T)], [[Tok() for _ in range(NT)] for _ in range(2)]
        qr, t_qr = A.bf16(T), [Tok() for _ in range(NT)]
        kn, t_kn = [A.bf16(T), A.bf16(T)], [[Tok() for _ in range(NT)] for _ in range(2)]
        Vp, t_V = A.bf16(16, 256), [Tok() for _ in range(16)]
        ao2, t_ao2 = [A.bf16(2, n), A.bf16(2, n)], [Tok(), Tok()]
        pending_out = [None]
        WQ = Rot([A.bf16(3, 256) for _ in range(2)])
        WQR = Rot([A.bf16(3, 128) for _ in range(2)])
        WQS = Rot([A.bf16(3, 128) for _ in range(2)])
        WK = Rot([A.bf16(4, 256) for _ in range(2)])
        WV = Rot([A.bf16(4, 256) for _ in range(2)])
        WOo = Rot([A.bf16(2, 1024) for _ in range(2)])
        cosr = Rot([A.f32(n) for _ in range(2)])
        sinr = Rot([A.f32(n) for _ in range(2)])
        r1, r2, t_r = A.f32(n), A.f32(n), [Tok(), Tok()]
        PT = Rot([A.bf16(512) for _ in range(5)])
        rden, t_rden = A.f32(n), Tok()
        gen["banks"] = [0, 1, 2, 3]
        obk = Rot([PB[4], PB[5]])
        obk.items = [(PB[4], TPB[4]), (PB[5], TPB[5])]
        dbk = Rot([PB[6], PB[7]])
        dbk.items = [(PB[6], TPB[6]), (PB[7], TPB[7])]
        r3 = lambda src: src.rearrange("(kc p) c -> p kc c", p=128)
        for pr in range(4):
            wq, twq = WQ.next()
            P.dma("pool", wq, r3(D["wuq_n"][:, pr * 256:(pr + 1) * 256]), W=[twq])
            wqr, twqr = WQR.next()
            P.dma("pool", wqr, r3(D["wuq_r"][:, pr * 128:(pr + 1) * 128]), W=[twqr])
            wqs, twqs = WQS.next()
            P.dma("pool", wqs, r3(D["wuq_rs"][:, pr * 128:(pr + 1) * 128]), W=[twqs])
            wk, twk = WK.next()
            P.dma("pool", wk, r3(D["wuk"][:, pr * 256:(pr + 1) * 256]), W=[twk])
            wv, twv = WV.next()
            P.dma("pool", wv, r3(D["wuv"][:, pr * 256:(pr + 1) * 256]), W=[twv])
            wo, two = WOo.next()
            P.dma("pool", wo, D["w_out_c"][pr * 256:(pr + 1) * 256, :].rearrange("(h p) c -> p h c", p=128), W=[two])
            for tti in range(NT):
                cols = slice(tti * n, (tti + 1) * n)
                for hh in range(2):
                    bank, tb = gb()
                    for kc in range(3):
                        mm(bank[:, :n], wq[:, kc, hh * 128:(hh + 1) * 128], cqnT[:, kc, cols], [twq, t_cqn[tti]], [tb],
                           st=(kc == 0), sp=(kc == 2))
                    cp(qn[hh][:, cols], bank[:, :n], [tb], [t_qn[hh][tti]], eng="act")
                    bank, tb = gb()
                    for kc in range(4):
                        mm(bank[:, :n], wk[:, kc, hh * 128:(hh + 1) * 128], ckvT[:, kc, cols], [twk, t_ckvT[tti]], [tb],
                           st=(kc == 0), sp=(kc == 3))
                    cp(kn[hh][:, cols], bank[:, :n], [tb], [t_kn[hh][tti]], eng="dve")
                cos_t, t_cos = cosr.next()
                sin_t, t_sin = sinr.next()
                P.dma("sp", cos_t, D["cosF"][:, cols], W=[t_cos])
                P.dma("sp", sin_t, D["sinF"][:, cols], W=[t_sin])
                bank1, tb1 = gb()
                for kc in range(3):
                    mm(bank1[:, :n], wqr[:, kc, :], cqnT[:, kc, cols], [twqr, t_cqn[tti]], [tb1], st=(kc == 0), sp=(kc == 2))
                bank2, tb2 = gb()
                for kc in range(3):
                    mm(bank2[:, :n], wqs[:, kc, :], cqnT[:, kc, cols], [twqs, t_cqn[tti]], [tb2], st=(kc == 0), sp=(kc == 2))
                tt(r1, bank1[:, :n], cos_t, MUL, [tb1, t_cos], [t_r[0]])
                tt(r2, bank2[:, :n], sin_t, MUL, [tb2, t_sin], [t_r[1]])
                tt(qr[:, cols], r1, r2, ADD, t_r, [t_qr[tti]])
                for b4 in range(4):
                    blk = tti * 4 + b4
                    bank, tb = gb()
                    for kc in range(4):
                        mm(bank[:, 0:256], ckvT[:, kc, blk * 128:(blk + 1) * 128], wv[:, kc, :], [twv, t_ckvT[tti]], [tb],
                           st=(kc == 0), sp=(kc == 3))
                    cp(Vp[:, blk, :], bank[:, 0:256], [tb], [t_V[blk]], eng=("act" if b4 % 2 == 0 else "dve"))
            for qt in range(NT):
                qcols0 = qt * n
                ao, t_ao = ao2[qt % 2], t_ao2[qt % 2]
                for hh in range(2):
                    prs = slice(hh * 64, hh * 64 + 64)
                    ob, tob = obk.next()
                    db, tdb = dbk.next()
                    nkb = 4 * qt + 4

                    def scores(kb):
                        i = kb - 4 * qt
                        q0 = 0 if i <= 0 else i * 128
                        N = n - q0
                        qs = slice(qcols0 + q0, qcols0 + n)
                        ks = slice(kb * 128, (kb + 1) * 128)
                        bank, tb = gb()
                        mm(bank[:, :N], kn[hh][:, ks], qn[hh][:, qs], [t_kn[hh][kb // 4], t_qn[hh][qt]], [tb], st=True, sp=False)
                        mm(bank[:, :N], krT2[prs, ks], qr[prs, qs], [t_krT[kb // 4], t_qr[qt]], [tb], st=False, sp=True)
                        pt, tpt = PT.next()
                        act(pt[:, :N], bank[:, :N], AF.Exp, [tb], [tpt], scale=SCALE)
                        if i >= 0:
                            memset(pt[64:128, 0:64], 0.0, [tpt])
                        return kb, q0, N, pt, tpt

                    def pv(item):
                        kb, q0, N, pt, tpt = item
                        mm(ob[:, q0:n], Vp[:, kb, hh * 128:(hh + 1) * 128], pt[:, :N], [t_V[kb], tpt], [tob],
                           st=(kb == 0), sp=(kb == nkb - 1))
                        mm(db[:, q0:n], ones_b, pt[:, :N], [t_one, tpt], [tdb], st=(kb == 0), sp=(kb == nkb - 1))

                    pend = []
                    for kb in range(nkb):
                        pend.append(scores(kb))
                        if len(pend) > 2:
                            pv(pend.pop(0))
                        if hh == 0 and kb == 2 and pending_out[0] is not None:
                            pending_out[0]()
                            pending_out[0] = None
                    while pend:
                        pv(pend.pop(0))
                    P.op("dve", lambda e, db=db: e.reciprocal(out=rden, in_=db[:, :n]), [tdb], [t_rden])
                    tt(ao[:, hh, :], ob[:, :n], rden, MUL, [tob, t_rden], [t_ao])

                def outproj(qt=qt, ao=ao, t_ao=t_ao, wo=wo, two=two):
                    qc = slice(qt * n, qt * n + n)
                    for oc in range(8):
                        bank, tb = gb()
                        for hh in range(2):
                            mm(bank[:, :n], wo[:, hh, oc * 128:(oc + 1) * 128], ao[:, hh, :], [two, t_ao], [tb],
                               st=(hh == 0), sp=(hh == 1))
                        tt(xT[:, oc, qc], xT[:, oc, qc], bank[:, :n], ADD, [t_x[qt], tb], [t_x[qt]])
                pending_out[0] = outproj
        if pending_out[0] is not None:
            pending_out[0]()
            pending_out[0] = None

    def mla_sample_c2(cfg, cqnT, t_cqn, ckvT, t_ckvT, krT2, t_krT, ckvn_b, t_cnb, cache_ckv, cache_kr):
        T = cfg.T
        r3 = lambda src: src.rearrange("(kc p) c -> p kc c", p=128)
        WQ, WQR, WQS = A.bf16(3, 1024), A.bf16(3, 512), A.bf16(3, 512)
        WUKT, WV = A.bf16(8, 512), A.bf16(4, 1024)
        t_w = [load(WQ, r3(D["wuq_n"]), "pool"), load(WQR, r3(D["wuq_r"]), "pool"), load(WQS, r3(D["wuq_rs"]), "pool"),
               load(WUKT, D["wukT"], "pool"), load(WV, r3(D["wuv"]), "pool")]
        WOo = Rot([A.bf16(8, 128) for _ in range(2)])
        cos_t, sin_t = A.f32(T), A.f32(T)
        t_cs = [load(cos_t, D["cosF"][:, 2048:2048 + T]), load(sin_t, D["sinF"][:, 2048:2048 + T])]
        qnS, t_qnS = A.bf16(8, T), Tok()
        qrS, t_qrS = A.bf16(8, T), Tok()
        r1, r2, t_r = A.f32(T), A.f32(T), [Tok(), Tok()]
        qlat, t_ql = [A.bf16(4, 128) for _ in range(4)], [Tok() for _ in range(4)]
        CQ = Rot([A.bf16(8, 512) for _ in range(3)])
        KQ = Rot([A.bf16(8, 128) for _ in range(3)])
        kq_tok = {}
        CT = Rot([A.bf16(4, 1024) for _ in range(2)])
        KT = Rot([A.bf16(1024) for _ in range(2)])
        PT = Rot([A.bf16(128) for _ in range(5)])
        rden, t_rden = A.f32(1), Tok()
        olatn, t_on = A.bf16(512), Tok()
        olatT, t_oT = A.bf16(4, 128), Tok()
        aoS, t_ao = A.bf16(8, T), Tok()
        gen["banks"] = [0, 1, 2, 3, 4, 5]
        olb, t_olb, dnb, t_dnb = PB[6], TPB[6], PB[7], TPB[7]
        for h in range(8):
            bank, tb = gb()
            for kc in range(3):
                mm(bank[:, :T], WQ[:, kc, h * 128:(h + 1) * 128], cqnT[:, kc, :T], [t_w[0], t_cqn[0]], [tb], st=(kc == 0), sp=(kc == 2))
            cp(qnS[:, h, :], bank[:, :T], [tb], [t_qnS], eng=("act" if h % 2 else "dve"))
        for h in range(8):
            b1, tb1 = gb()
            for kc in range(3):
                mm(b1[:64, :T], WQR[:, kc, h * 64:(h + 1) * 64], cqnT[:, kc, :T], [t_w[1], t_cqn[0]], [tb1], st=(kc == 0), sp=(kc == 2))
            b2, tb2 = gb()
            for kc in range(3):
                mm(b2[:64, :T], WQS[:, kc, h * 64:(h + 1) * 64], cqnT[:, kc, :T], [t_w[2], t_cqn[0]], [tb2], st=(kc == 0), sp=(kc == 2))
            tt(r1[:64, :], b1[:64, :T], cos_t[:64, :], MUL, [tb1, t_cs[0]], [t_r[0]])
            tt(r2[:64, :], b2[:64, :T], sin_t[:64, :], MUL, [tb2, t_cs[1]], [t_r[1]])
            tt(qrS[:64, h, :], r1[:64, :], r2[:64, :], ADD, t_r, [t_qrS])
        for b in range(4):
            bank, tb = gb()
            for kc in range(4):
                for h in range(8):
                    mm(bank[:, kc * 128 + h * 16:kc * 128 + (h + 1) * 16], WUKT[:, h, kc * 128:(kc + 1) * 128],
                       qnS[:, h, b * 16:(b + 1) * 16], [t_w[3], t_qnS], [tb])
            cp(qlat[b], bank[:, :].rearrange("p (a b) -> p a b", b=128), [tb], [t_ql[b]], eng="act")
        for b in range(4):
            pend = []

            def scores(K_, lc, lr, vrows, Rk, blk):
                bank, tb = gb()
                for kc in range(4):
                    mm(bank[:K_, 0:128], lc(kc), qlat[b][:, kc, :], Rk + [t_ql[b]], [tb], st=(kc == 0), sp=False)
                mm(bank[:K_, 0:128], lr, qrS[:64, :, b * 16:(b + 1) * 16], Rk + [t_qrS], [tb], st=False, sp=True)
                pt, tpt = PT.next()
                act(pt[:K_, :], bank[:K_, 0:128], AF.Exp, [tb], [tpt], scale=SCALE)
                return K_, pt, tpt, vrows, Rk, blk

            def pv(item):
                K_, pt, tpt, vrows, Rk, blk = item
                mm(olb[:, :], pt[:K_, :], vrows, [tpt] + Rk, [t_olb], st=(blk == 0), sp=(blk == 32))
                mm(dnb[:, 0:1], pt[:K_, :], ones_b[:K_, 0:1], [tpt, t_one], [t_dnb], st=(blk == 0), sp=(blk == 32))

            def push(item):
                pend.append(item)
                if len(pend) > 2:
                    pv(pend.pop(0))

            for q4 in range(4):
                cq, tcq = CQ.next()
                kq, tkq = KQ.next()
                tkq = kq_tok.setdefault(id(tkq), (tkq, Tok()))
                P.dma("pool", cq, cache_ckv[b, q4 * 1024:(q4 + 1) * 1024, :].rearrange("(k p) l -> p k l", p=128), W=[tcq])
                for dup in range(2):
                    P.dma("pool", kq[:, :, dup * 64:(dup + 1) * 64],
                          cache_kr[b, q4 * 1024:(q4 + 1) * 1024, :].rearrange("(k p) r -> p k r", p=128), W=[tkq[dup]])
                ct, tct = CT.next()
                kt, tkt = KT.next()
                for k8 in range(8):
                    bs = slice(k8 * 128, (k8 + 1) * 128)
                    bank, tb = gb()
                    bv = bfv(bank)
                    for kc in range(4):
                        tr(bv[:, kc * 128:(kc + 1) * 128], cq[:, k8, kc * 128:(kc + 1) * 128], ident_b, [tcq, t_idb], [tb])
                    tr(bv[:, 512:640], kq[:, k8, :], ident_b, [tkq[0], tkq[1], t_idb], [tb])
                    cp(ct[:, :, bs], bv[:, 0:512].rearrange("p (a b) -> p a b", b=128), [tb], [tct],
                       eng=("act" if k8 % 2 else "dve"))
                    cp(kt[:, bs], bv[:, 512:640], [tb], [tkt], eng=("dve" if k8 % 2 else "act"))
                for k8 in range(8):
                    bs = slice(k8 * 128, (k8 + 1) * 128)
                    push(scores(128, (lambda kc, bs=bs, ct=ct: ct[:, kc, bs]), kt[0:64, bs], cq[:, k8, :],
                                [tct, tkt, tcq], q4 * 8 + k8))
            bs = slice(b * 16, (b + 1) * 16)
            push(scores(16, (lambda kc, bs=bs: ckvT[:, kc, bs]), krT2[0:64, bs], ckvn_b[:16, b, :],
                        [t_ckvT[0], t_krT[0], t_cnb[b]], 32))
            while pend:
                pv(pend.pop(0))
            P.op("dve", lambda e: e.reciprocal(out=rden, in_=dnb[:, 0:1]), [t_dnb], [t_rden])
            ts(olatn, olb[:, :], rden[:, 0:1], MUL, [t_olb, t_rden], [t_on])
            bank, tb = gb()
            bv = bfv(bank)
            for kc in range(4):
                tr(bv[:, kc * 128:(kc + 1) * 128], olatn[:, kc * 128:(kc + 1) * 128], ident_b, [t_on, t_idb], [tb])
            cp(olatT, bv[:, 0:512].rearrange("p (a b) -> p a b", b=128), [tb], [t_oT], eng="act")
            bank, tb = gb()
            for h in range(8):
                for kc in range(4):
                    mm(bank[:, h * 16:(h + 1) * 16], WV[:, kc, h * 128:(h + 1) * 128], olatT[:, kc, h * 16:(h + 1) * 16],
                       [t_w[4], t_oT], [tb], st=(kc == 0), sp=(kc == 3))
            cp(aoS[:, :, b * 16:(b + 1) * 16], bank[:, 0:128].rearrange("p (h i) -> p h i", i=16), [tb], [t_ao])
        for oc in range(8):
            wo, two = WOo.next()
            P.dma("pool", wo, D["w_out_c"][:, oc * 128:(oc + 1) * 128].rearrange("(h p) c -> p h c", p=128), W=[two])
            bank, tb = gb()
            for h in range(8):
                mm(bank[:, :T], wo[:, h, :], aoS[:, h, :], [two, t_ao], [tb], st=(h == 0), sp=(h == 7))
            tt(xT[:, oc, cfg.x0:cfg.x0 + T], xT[:, oc, cfg.x0:cfg.x0 + T], bank[:, :T], ADD, [t_x[cfg.xt0], tb], [t_x[cfg.xt0]])

    def phase_final(cfg, y_out):
        m = A.mark()
        n = cfg.XB
        gfb = A.f32(1024)
        t_g = load(gfb, D["nrm"][4:5, :].to_broadcast([128, 1024]))
        ybuf = Rot([A.f32(1024) for _ in range(2)])
        st, t_st = A.f32(4), Tok()
        junk, t_junk = A.bf16(512), Tok()
        gen["banks"] = [0, 1, 2, 3, 4, 5, 6, 7]
        for blk in range(cfg.T // n):
            cs_ = slice(blk * n, (blk + 1) * n)
            xs_ = slice(cfg.x0 + blk * n, cfg.x0 + (blk + 1) * n)
            tti = cfg.xt0 + (blk * n) // cfg.TT
            banks = []
            for half in range(2):
                bank, tb = gb()
                for c4 in range(4):
                    tr(bank[:n, c4 * 128:(c4 + 1) * 128], xT[:, half * 4 + c4, xs_], ident_f, [t_x[tti], t_idf], [tb])
                act(junk[:n, :], bank[:n, :], AF.Square, [tb, t_st], [t_junk, t_st], accum=st[:n, half:half + 1])
                banks.append((bank, tb))
            tt(st[:n, 2:3], st[:n, 0:1], st[:n, 1:2], ADD, [t_st], [t_st])
            rstd_small(st[:n, 2:3], st[:n, 2:3], 1.0 / 1024, [t_st], [t_st])
            yb, tyb = ybuf.next()
            for half in range(2):
                stt(yb[:n, half * 512:(half + 1) * 512], banks[half][0][:n, :], st[:n, 2:3],
                    gfb[:n, half * 512:(half + 1) * 512], MUL, MUL, [banks[half][1], t_st, t_g], [tyb])
            P.dma("sp", y_out[cs_, :], yb[:n, :], R=[tyb])
        P.barrier()
        A.reset(m)

    pc, sc = make_cfgs()
    MERGE = STAGE >= 99 and (NSEQ_P == 2 or os.environ.get("MK_MERGE") == "1")
    for seq in range(NSEQ_P):
        merged = MERGE and seq == NSEQ_P - 1
        load_x(pc, D["xp"][seq])
        if merged:
            load_x(sc, D["xs"])
        if STAGE >= 1:
            try:
                phase_mixer_ab(pc, seq, None, None, O["glap"], O["retp"])
            except _Stop:
                P.barrier()
                A.reset(base_mark)
            if merged:
                phase_mixer_ab(sc, 0, D["sgla"], D["sret"], O["glas"], O["rets"])
        if STAGE >= 2:
            grp = [(pc, None, O["convp"][0, seq:seq + 1])]
            if merged:
                grp.append((sc, D["sconv"][0:8, :], O["convs"][0]))
            phase_ffn(grp, 0)
        if STAGE >= 3:
            phase_mla(pc, seq, O["ckvp"][seq], O["krp"][seq])
            if merged:
                phase_mla(sc, 0, O["ckvs"], O["krs"], D["cckv"], D["ckr"])
        if STAGE >= 4:
            grp = [(pc, None, O["convp"][1, seq:seq + 1])]
            if merged:
                grp.append((sc, D["sconv"][8:16, :], O["convs"][1]))
            phase_ffn(grp, 1)
        phase_final(pc, O["yp"][seq])
        if merged:
            phase_final(sc, O["ys"])
    if STAGE >= 5 and not MERGE:
        load_x(sc, D["xs"])
        phase_mixer_ab(sc, 0, D["sgla"], D["sret"], O["glas"], O["rets"])
        phase_ffn([(sc, D["sconv"][0:8, :], O["convs"][0])], 0)
        phase_mla(sc, 0, O["ckvs"], O["krs"], D["cckv"], D["ckr"])
        phase_ffn([(sc, D["sconv"][8:16, :], O["convs"][1])], 1)
        phase_final(sc, O["ys"])
    P.emit()
    P.close()
    print("arena peak words", A.peak, "instrs", {k: len(v) for k, v in P.streams.items()}, "signals", P.sigcount)
    return nc


_CACHE = {}


def kernel(x_prompt, x_sample, state_gla, state_ret, cache_ckv, cache_krope, state_conv,
           norm_mix, norm_ffn, norm_final, w_in_ab, w_gate_up, b_gate, g_gla, g_ret, w_out_ab,
           w_in_c, g_q, g_kv, w_uq, w_uk, w_uv, w_out_c, w_ffn_in, w_dwconv, b_dwconv, w_ffn_out):
    f = lambda a: np.ascontiguousarray(np.asarray(a, dtype=np.float32))
    ncores = int(os.environ.get('MK_CORES', '8'))
    if "nc" not in _CACHE:
        _CACHE["nc"] = build_program()
        _CACHE["tabs"] = const_tables()
    nc = _CACHE["nc"]
    tabs = _CACHE["tabs"]
    w_in_ab0 = f(w_in_ab)[0]
    sw = w_in_ab0[:, 1552:2064].reshape(1024, 8, 2, 32)[:, :, ::-1, :].reshape(1024, 512)
    wuq = f(w_uq)[0].reshape(384, 8, 192)
    wuk0 = f(w_uk)[0]
    shared = dict(
        nrm=np.stack([f(norm_mix)[0], f(norm_mix)[1], f(norm_ffn)[0], f(norm_ffn)[1], f(norm_final)]),
        w_in_ab=w_in_ab0, w_ab_sw=f(sw), wgu=f(w_gate_up)[0], bgate=f(b_gate)[0][None, :],
        ggla=f(g_gla)[0][None, :], gret=f(g_ret)[0][None, :], w_out_ab=f(w_out_ab)[0], w_in_c=f(w_in_c)[0],
        gq=f(g_q)[0][None, :], gkv=f(g_kv)[0][None, :],
        wuq_n=f(wuq[:, :, :128].reshape(384, 1024)), wuq_r=f(wuq[:, :, 128:].reshape(384, 512)),
        wuq_rs=f(wuq[:, :, 128:].reshape(384, 8, 2, 32)[:, :, ::-1, :].reshape(384, 512)),
        wuk=f(wuk0.reshape(512, 1024)), wukT=f(wuk0.transpose(2, 1, 0)), wuv=f(w_uv)[0].reshape(512, 1024),
        w_out_c=f(w_out_c)[0], wffi=f(w_ffn_in),
        dwc=f(np.concatenate([f(w_dwconv).reshape(6, 2816), f(b_dwconv)], axis=0)), wffo=f(w_ffn_out),
    )
    shared.update(tabs)
    xpv, xsv = f(x_prompt), f(x_sample)
    sg, sr = f(state_gla)[0], f(state_ret)[0]
    cc, ck, scv = f(cache_ckv)[0], f(cache_krope)[0], f(state_conv)
    in_maps = []
    for c in range(ncores):
        d = dict(shared)
        d["xp"] = xpv[2 * c:2 * c + 2]
        d["xs"] = xsv[4 * c:4 * c + 4].reshape(64, 1024)
        d["sgla"] = sg[4 * c:4 * c + 4].reshape(4, 256, 128)
        d["sret"] = sr[4 * c:4 * c + 4].reshape(4, 256, 128)
        d["cckv"] = cc[4 * c:4 * c + 4]
        d["ckr"] = ck[4 * c:4 * c + 4]
        d["sconv"] = f(scv[:, 4 * c:4 * c + 4].reshape(16, 2816))
        in_maps.append({k: np.ascontiguousarray(v) for k, v in d.items()})
    res = run_bass_kernel_spmd(nc, in_maps, core_ids=list(range(ncores)))
    R = res.results
    cat = lambda k, shp: np.concatenate([R[c][k].reshape(shp) for c in range(ncores)], axis=0)
    y_prompt = cat("yp", (2, 2048, 1024))
    y_sample = cat("ys", (4, 16, 1024))
    gla_p = cat("glap", (2, 4, 64, 128))[None]
    gla_s = cat("glas", (4, 4, 64, 128))[None]
    ret_p = cat("retp", (2, 4, 64, 128))[None]
    ret_s = cat("rets", (4, 4, 64, 128))[None]
    ckv_p = cat("ckvp", (2, 2048, 512))[None]
    ckv_s = cat("ckvs", (4, 16, 512))[None]
    kr_p = cat("krp", (2, 2048, 64))[None]
    kr_s = cat("krs", (4, 16, 64))[None]
    conv_p = np.concatenate([R[c]["convp"] for c in range(ncores)], axis=1)
    conv_s = np.concatenate([R[c]["convs"] for c in range(ncores)], axis=1)
    outs = (y_prompt, y_sample, gla_p, gla_s, ret_p, ret_s, ckv_p, ckv_s, kr_p, kr_s, conv_p, conv_s)
    return tuple(np.ascontiguousarray(o, dtype=np.float32) for o in outs)
```

```python
import contextlib
import math
import os

import numpy as np
import concourse.bass as bass
import concourse.mybir as mybir
from concourse.bass_utils import run_bass_kernel_spmd

F32 = mybir.dt.float32
BF16 = mybir.dt.bfloat16
AF = mybir.ActivationFunctionType
ALU = mybir.AluOpType
AX = mybir.AxisListType

EPS = 1e-6
D_FF = 2816
NJ = 22
STAGE = int(os.environ.get("MK_STAGE", "99"))
NSEQ_P = int(os.environ.get("MK_NSEQ", "2"))
STOPAT = int(os.environ.get("MK_STOP", "0"))


class _Stop(Exception):
    pass


def ck(k):
    if STOPAT == k:
        raise _Stop()

COMPUTE = ("pe", "act", "dve", "pool")
DMA_K = 8
EPOCH = 6000


class Tok:
    __slots__ = ("w", "rs", "excl")

    def __init__(self, excl=False):
        self.w = None
        self.rs = []
        self.excl = excl


class Ins:
    __slots__ = ("eng", "fn", "deps", "dma", "signal", "ev", "prev_ev")

    def __init__(self, eng, fn, dma):
        self.eng = eng
        self.fn = fn
        self.dma = dma
        self.deps = []
        self.signal = False
        self.ev = None
        self.prev_ev = None


class Prog:
    def __init__(self, nc):
        self.nc = nc
        self.es = contextlib.ExitStack()
        self.streams = {e: [] for e in ("pe", "act", "dve", "pool", "sp")}
        self.pending = {e: [] for e in self.streams}
        self.dmas = []
        self.n = 0
        self.nobar = False

    def sb(self, shape, dt, name="t"):
        self.n += 1
        return self.es.enter_context(self.nc.sbuf_tensor(f"{name}{self.n}", list(shape), dt))

    def ps(self, shape, dt, name="p"):
        self.n += 1
        return self.es.enter_context(self.nc.psum_tensor(f"{name}{self.n}", list(shape), dt))

    def op(self, eng, fn, R=(), W=(), dma=False, force=()):
        ins = Ins(eng, fn, dma)
        deps = {id(d): d for d in force}

        def same(d):
            return (not d.dma) and (not dma) and d.eng == eng

        def readers(t):
            seen = set()
            for r in reversed(t.rs):
                if r.dma:
                    yield r
                elif r.eng not in seen:
                    seen.add(r.eng)
                    if not (r.eng == eng and eng == "pe" and not dma):
                        yield r

        for t in R:
            d = t.w
            if d is not None and not (same(d) and eng == "pe"):
                deps[id(d)] = d
            if t.excl:
                for r in readers(t):
                    if not same(r):
                        deps[id(r)] = r
        for t in W:
            d = t.w
            if d is not None and not (same(d) and eng == "pe"):
                deps[id(d)] = d
            for r in readers(t):
                deps[id(r)] = r
        for t in R:
            t.rs.append(ins)
        for t in W:
            t.w = ins
            t.rs = []
        if self.pending[eng] and not self.nobar:
            for d in self.pending[eng]:
                deps[id(d)] = d
            self.pending[eng] = []
        ins.deps = list(deps.values())
        for d in ins.deps:
            d.signal = True
        if dma:
            ins.signal = True
            self.dmas.append(ins)
        self.streams[eng].append(ins)
        return ins

    def dma(self, q, out, in_, R=(), W=(), **kw):
        return self.op(q, lambda e: e.dma_start(out=out, in_=in_, **kw), R=R, W=W, dma=True)

    def barrier(self):
        deps = [st[-1] for st in self.streams.values() if st] + self.dmas
        for d in deps:
            d.signal = True
        for e in self.pending:
            self.pending[e] = self.pending[e] + deps
        self.dmas = []

    def emit(self):
        nc = self.nc
        es = self.es
        sem_dma = {q: [es.enter_context(nc.semaphore(f"semd_{q}{i}")) for i in range(DMA_K)]
                   for q in ("sp", "act", "pool")}
        for e, st in self.streams.items():
            cnt = 0
            nd = 0
            sem = None
            for ins in st:
                if ins.dma:
                    j = nd % DMA_K
                    rnd = nd // DMA_K
                    ins.ev = (sem_dma[e][j], 16 * (rnd + 1))
                    ins.prev_ev = (sem_dma[e][j], 16 * rnd) if rnd > 0 else None
                    nd += 1
                elif ins.signal:
                    if cnt % EPOCH == 0:
                        sem = es.enter_context(nc.semaphore(f"sem_{e}{cnt // EPOCH}"))
                    cnt += 1
                    ins.ev = (sem, (cnt - 1) % EPOCH + 1)
        self.sigcount = {e: sum(1 for i in st if (not i.dma) and i.signal) for e, st in self.streams.items()}
        final_dma = []
        for q in ("sp", "act", "pool"):
            last = {}
            for ins in self.streams[q]:
                if ins.dma:
                    last[id(ins.ev[0])] = ins.ev
            final_dma += list(last.values())

        def run(engobj, st, is_sp):
            waited = {}

            def wait(ev):
                sem, val = ev
                k = id(sem)
                if waited.get(k, 0) < val:
                    engobj.wait_ge(sem, val)
                    waited[k] = val

            for ins in st:
                for d in ins.deps:
                    wait(d.ev)
                if ins.dma and ins.prev_ev is not None:
                    wait(ins.prev_ev)
                bi = ins.fn(engobj)
                if ins.dma:
                    bi.then_inc(ins.ev[0], 16)
                elif ins.signal:
                    bi.then_inc(ins.ev[0], 1)
            if is_sp:
                for ev in final_dma:
                    wait(ev)

        block = es.enter_context(nc.Block())
        S = self.streams

        @block.tensor
        def _(e):
            run(e, S["pe"], False)

        @block.scalar
        def _(e):
            run(e, S["act"], False)

        @block.vector
        def _(e):
            run(e, S["dve"], False)

        @block.gpsimd
        def _(e):
            run(e, S["pool"], False)

        @block.sync
        def _(e):
            run(e, S["sp"], True)

    def close(self):
        self.es.close()


class Arena:
    def __init__(self, P, nwords):
        self.t = P.sb([128, nwords], F32, "arena")
        self.n = nwords
        self.off = 0
        self.peak = 0

    def _take(self, words):
        o = self.off
        self.off += words
        self.peak = max(self.peak, self.off)
        assert self.off <= self.n, f"arena overflow {self.off} > {self.n}"
        return o

    def f32(self, *shape):
        n = int(np.prod(shape))
        o = self._take(n)
        ap = self.t[:, o:o + n]
        return self._shape(ap, shape)

    def bf16(self, *shape):
        n = int(np.prod(shape))
        w = (n + 1) // 2
        o = self._take(w)
        ap = self.t[:, o:o + w].bitcast(BF16)
        if 2 * w != n:
            ap = ap[:, 0:n]
        return self._shape(ap, shape)

    @staticmethod
    def _shape(ap, shape):
        if len(shape) == 1:
            return ap
        if len(shape) == 2:
            return ap.rearrange("p (a b) -> p a b", b=shape[1])
        if len(shape) == 3:
            return ap.rearrange("p (a b c) -> p a b c", b=shape[1], c=shape[2])
        raise ValueError(shape)

    def mark(self):
        return self.off

    def reset(self, m):
        self.off = m


class Rot:
    def __init__(self, items):
        self.items = [(it, Tok()) for it in items]
        self.i = 0

    def next(self):
        r = self.items[self.i % len(self.items)]
        self.i += 1
        return r


class Cfg:
    pass


def make_cfgs():
    p = Cfg()
    p.T, p.TT, p.NT, p.C, p.NCH, p.nseq, p.L, p.tab0, p.XB, p.krblk0 = 2048, 512, 4, 128, 4, 1, 512, 0, 128, 0
    p.prompt = True
    p.x0, p.xt0 = 0, 0
    s = Cfg()
    s.T, s.TT, s.NT, s.C, s.NCH, s.nseq, s.L, s.tab0, s.XB, s.krblk0 = 64, 64, 1, 16, 4, 4, 16, 2048, 64, 16
    s.prompt = False
    s.x0, s.xt0 = 2048, 4
    return p, s


def const_tables():
    half = 32
    freqs = (10000.0 ** (-np.arange(half, dtype=np.float32) / half)).astype(np.float32)
    pos = np.concatenate([np.arange(2048), 4096 + (np.arange(64) % 16)]).astype(np.float32)
    ang = pos[None, :] * freqs[:, None]
    ang = ang.astype(np.float32)
    cos = np.cos(ang).astype(np.float32)
    sin = np.sin(ang).astype(np.float32)
    p = np.arange(128)
    d = p % 64
    cosF = cos[d % 32, :]
    sinF = np.where((d < 32)[:, None], -sin[d % 32, :], sin[d % 32, :]).astype(np.float32)
    kc = np.zeros((128, 17, 64), np.float32)
    ks = np.zeros((128, 17, 64), np.float32)
    for blk in range(17):
        if blk < 16:
            pp = (blk * 128 + p).astype(np.float32)
        else:
            pp = (4096 + (p % 16)).astype(np.float32)
        a = (pp[:, None] * freqs[None, :]).astype(np.float32)
        c, s = np.cos(a).astype(np.float32), np.sin(a).astype(np.float32)
        kc[:, blk, :32] = c
        kc[:, blk, 32:] = c
        ks[:, blk, :32] = -s
        ks[:, blk, 32:] = s
    def ret_tabs(C, TT):
        nref = (C - 1) // 2 + 1
        E = np.zeros((128, 2, 2, TT), np.float32)
        cst = np.zeros((128, 2, 3), np.float32)
        i = np.arange(TT) % C
        for kcx in range(2):
            h = 2 * kcx + p // 64
            lg = np.log1p(-np.exp2(-5.0 - h.astype(np.float64)))
            E[:, kcx, 0, :] = np.exp((i[None, :] + 1 - nref) * lg[:, None])
            E[:, kcx, 1, :] = np.exp((nref - i[None, :] - 1) * lg[:, None]) * (64 ** -0.5)
            cst[:, kcx, 0] = np.exp(nref * lg)
            cst[:, kcx, 1] = np.exp((C - nref) * lg)
            cst[:, kcx, 2] = np.exp(C * lg)
        return E, cst
    Ep, cp = ret_tabs(128, 512)
    Es, cs = ret_tabs(16, 64)
    mask = (np.arange(128)[:, None] <= np.arange(128)[None, :]).astype(np.float32)
    mask4 = np.tile(mask, (1, 4))
    resetp = np.ones((128, 512), np.float32)
    resetp[:, ::128] = 0.0
    resets = np.ones((128, 64), np.float32)
    resets[:, ::16] = 0.0
    ident = np.eye(128, dtype=np.float32)
    return dict(cosF=cosF, sinF=sinF, krc=kc, krs_=ks, retEp=Ep, retcp=cp, retEs=Es, retcs=cs,
                mask4=mask4, resetp=resetp, resets=resets, ident=ident)


_IN_SHAPES = dict(
    xp=[2, 2048, 1024], xs=[64, 1024], sgla=[4, 256, 128], sret=[4, 256, 128],
    cckv=[4, 4096, 512], ckr=[4, 4096, 64], sconv=[16, 2816], nrm=[5, 1024],
    w_in_ab=[1024, 3088], w_ab_sw=[1024, 512], wgu=[16, 256], bgate=[1, 256], ggla=[1, 512],
    gret=[1, 512], w_out_ab=[1024, 1024], w_in_c=[1024, 960], gq=[1, 384], gkv=[1, 512],
    wuq_n=[384, 1024], wuq_r=[384, 512], wuq_rs=[384, 512], wuk=[512, 1024], wukT=[128, 8, 512],
    wuv=[512, 1024], w_out_c=[1024, 1024], wffi=[2, 1024, 5632], dwc=[8, 2816],
    wffo=[2, 2816, 1024],
    cosF=[128, 2112], sinF=[128, 2112], krc=[128, 17, 64], krs_=[128, 17, 64],
    retEp=[128, 2, 2, 512], retcp=[128, 2, 3], retEs=[128, 2, 2, 64], retcs=[128, 2, 3],
    mask4=[128, 512], resetp=[128, 512], resets=[128, 64], ident=[128, 128],
)
_OUT_SHAPES = dict(
    yp=[2, 2048, 1024], ys=[64, 1024], glap=[2, 256, 128], glas=[4, 256, 128],
    retp=[2, 256, 128], rets=[4, 256, 128], ckvp=[2, 2048, 512], ckvs=[64, 512],
    krp=[2, 2048, 64], krs=[64, 64], convp=[2, 2, 2, 2816], convs=[2, 4, 2, 2816],
)


def build_program():
    nc = bass.Bass("TRN2", target_bir_lowering=False)
    D = {k: nc.dram_tensor(k, list(v), F32, kind="ExternalInput").ap() for k, v in _IN_SHAPES.items()}
    O = {k: nc.dram_tensor(k, list(v), F32, kind="ExternalOutput").ap() for k, v in _OUT_SHAPES.items()}
    P = Prog(nc)
    A = Arena(P, 52900)
    PB = [P.ps([128, 512], F32, "bank") for _ in range(8)]
    TPB = [Tok(excl=True) for _ in range(8)]
    gen = {"banks": [0, 1, 2, 3, 4, 5], "i": 0}

    def gb():
        b = gen["banks"][gen["i"] % len(gen["banks"])]
        gen["i"] += 1
        return PB[b], TPB[b]

    def bfv(bank):
        return bank[:].bitcast(BF16)

    def mm(out, lhsT, rhs, R, W, st=True, sp=True, force=()):
        return P.op("pe", lambda e: e.matmul(out, lhsT, rhs, start=st, stop=sp), R, W, force=force)

    def tr(out, in_, idn, R, W):
        P.op("pe", lambda e: e.transpose(out, in_, idn), R, W)

    def act(out, in_, func, R, W, bias=None, scale=None, accum=None):
        kw = {}
        if bias is not None:
            kw["bias"] = bias
        if scale is not None:
            kw["scale"] = scale
        if accum is not None:
            kw["accum_out"] = accum
        P.op("act", lambda e: e.activation(out=out, in_=in_, func=func, **kw), R, W)

    def tt(out, a, b, op, R, W, eng="dve"):
        P.op(eng, lambda e: e.tensor_tensor(out=out, in0=a, in1=b, op=op), R, W)

    def ts(out, a, s1, op0, R, W, s2=None, op1=None, eng="dve"):
        if op1 is None:
            P.op(eng, lambda e: e.tensor_scalar(out, a, s1, None, op0=op0), R, W)
        else:
            P.op(eng, lambda e: e.tensor_scalar(out, a, s1, s2, op0=op0, op1=op1), R, W)

    def stt(out, in0, scalar, in1, op0, op1, R, W, eng="dve"):
        P.op(eng, lambda e: e.scalar_tensor_tensor(out=out, in0=in0, scalar=scalar, in1=in1,
                                                    op0=op0, op1=op1), R, W)

    def cp(out, in_, R, W, eng="dve"):
        if eng == "act":
            act(out, in_, AF.Copy, R, W)
        else:
            P.op(eng, lambda e: e.tensor_copy(out=out, in_=in_), R, W)

    def memset(ap, val, W, eng="dve"):
        P.op(eng, lambda e: e.memset(ap, val), (), W)

    def load(dst, src, q="sp"):
        t = Tok()
        P.dma(q, dst, src, W=[t])
        return t

    MUL, ADD, SUB = ALU.mult, ALU.add, ALU.subtract

    xT = A.f32(8, 2048 + 64)
    t_x = [Tok() for _ in range(5)]
    ident_f = A.f32(128)
    ident_b = A.bf16(128)
    ones_b = A.bf16(128)
    gn = A.f32(8, 5)
    dw = A.f32(NJ, 8)
    gqc = A.f32(3)
    nbg = A.f32(2)
    t_idf = load(ident_f, D["ident"])
    t_idb = load(ident_b, D["ident"], "pool")
    t_one = Tok()
    memset(ones_b, 1.0, [t_one])
    t_const = [t_idf, t_idb, t_one]
    PSQ = Rot([A.bf16(512) for _ in range(2)])
    PRS, t_PRS = A.f32(512), Tok()
    PHT, t_PHT = A.bf16(8, 512), Tok()
    PWA, t_PWA, t_PWA2 = A.bf16(8, 512), Tok(), Tok()
    base_mark = A.mark()

    def rows_to_cols(src_rows, R_, n, dst, t_dst):
        m = A.mark()
        stage = A.f32(n * 128)
        t_s = load(stage[:R_, :], src_rows)
        bank, tb = gb()
        for c in range(n):
            tr(bank[:, c * R_:(c + 1) * R_], stage[:R_, c * 128:(c + 1) * 128], ident_f[:R_, :R_],
               [t_s, t_idf], [tb])
        if len(dst.shape) == 3:
            cp(dst, bank[:, 0:n * R_].rearrange("p (a b) -> p a b", b=R_), [tb], [t_dst])
        else:
            cp(dst, bank[:, 0:n * R_], [tb], [t_dst])
        P.barrier()
        A.reset(m)

    t_gn, t_dw, t_gq, t_nbg = Tok(), Tok(), Tok(), Tok()
    rows_to_cols(D["nrm"], 5, 8, gn, t_gn)
    rows_to_cols(D["dwc"], 8, NJ, dw, t_dw)
    rows_to_cols(D["gq"], 1, 3, gqc, t_gq)
    rows_to_cols(D["bgate"], 1, 2, nbg, t_nbg)
    ts(nbg, nbg, -1.0, MUL, [t_nbg], [t_nbg])

    def load_x(cfg, xd):
        P.nobar = True
        flat = lambda ap: ap.rearrange("p a b -> p (a b)").bitcast(F32)
        xin = [(flat(PHT), [t_PHT]), (flat(PWA), [t_PWA, t_PWA2])]
        n = cfg.XB
        for blk in range(cfg.T // n):
            xi, txi = xin[blk % 2]
            P.dma("sp", xi[:n, 0:1024], xd[blk * n:(blk + 1) * n, :], W=txi)
            ttile = (blk * n) // cfg.TT
            for half in range(2):
                bank, tb = gb()
                for c4 in range(4):
                    c = half * 4 + c4
                    tr(bank[:, c4 * n:(c4 + 1) * n], xi[:n, c * 128:(c + 1) * 128], ident_f[:n, :n],
                       txi + [t_idf], [tb])
                src = bank[:, 0:4 * n].rearrange("p (a b) -> p a b", b=n)
                dst = xT[:, half * 4:(half + 1) * 4, cfg.x0 + blk * n:cfg.x0 + (blk + 1) * n]
                cp(dst, src, [tb], [t_x[cfg.xt0 + ttile]], eng=("act" if half == 0 else "dve"))
        P.nobar = False

    def rmsnorm(cfg, tti, grow, hdst, t_h, sqrot, _unused, rs, t_rs):
        n = cfg.TT
        cols = slice(cfg.x0 + tti * n, cfg.x0 + (tti + 1) * n)
        t_xt = t_x[cfg.xt0 + tti]
        bank, tb = gb()
        for c in range(8):
            sqb, t_sq = sqrot.next()
            act(sqb[:, :n], xT[:, c, cols], AF.Square, [t_xt], [t_sq])
            mm(bank[:, :n], ones_b, sqb[:, :n], [t_sq, t_one], [tb], st=(c == 0), sp=(c == 7))
        act(rs[:, :n], bank[:, :n], AF.Ln, [tb], [t_rs], scale=1.0 / 1024, bias=EPS)
        act(rs[:, :n], rs[:, :n], AF.Exp, [t_rs], [t_rs], scale=-0.5)
        for c in range(8):
            stt(hdst[:, c, :n], xT[:, c, cols], gn[:, c, grow:grow + 1], rs[:, :n], MUL, MUL,
                [t_xt, t_rs, t_gn], [t_h])

    def rstd_small(dst, src, inv_n, R, W):
        act(dst, src, AF.Ln, R, W, scale=inv_n, bias=EPS)
        act(dst, dst, AF.Exp, W, W, scale=-0.5)

    def phase_mixer_ab(cfg, seq, s0_gla, s0_ret, out_gla, out_ret):
        m = A.mark()
        n, C, NCH = cfg.TT, cfg.C, cfg.NCH
        refi = (C - 1) // 2
        P.nobar = True
        rmsnorm(cfg, 0, 0, PHT, t_PHT, PSQ, None, PRS, t_PRS)
        P.dma("pool", PWA, D["w_in_ab"][:, 1552:2064].rearrange("(kc p) c -> p kc c", p=128), W=[t_PWA, t_PWA2])
        P.nobar = False
        retE = A.f32(2, 2, n)
        retc = A.f32(2, 3)
        mask4 = A.f32(512)
        reset = A.f32(n)
        gglab = A.f32(512)
        gretb = A.f32(512)
        wgu = A.f32(256)
        Wlo = A.bf16(8, 16)
        t_tab = [load(retE, D["retEp"] if cfg.prompt else D["retEs"]),
                 load(retc, D["retcp"] if cfg.prompt else D["retcs"]),
                 load(mask4, D["mask4"]),
                 load(reset, D["resetp"] if cfg.prompt else D["resets"]),
                 load(gglab, D["ggla"].to_broadcast([128, 512])),
                 load(gretb, D["gret"].to_broadcast([128, 512])),
                 load(wgu[:16, :], D["wgu"]),
                 load(Wlo, D["w_in_ab"][:, 1536:1552].rearrange("(kc p) c -> p kc c", p=128), "pool")]
        cosr = Rot([A.f32(n) for _ in range(1)])
        sinr = Rot([A.f32(n) for _ in range(1)])
        sq, t_sq = PSQ, None
        rs, t_rs = PRS, t_PRS
        hT2, t_h2 = [PHT, A.bf16(8, n)], [t_PHT, Tok()]
        WA = Rot([A.bf16(8, 512) for _ in range(2 if cfg.prompt else 5)])
        WA.items.insert(0, (PWA, t_PWA))
        vtok = [A.bf16(NCH, 512), A.bf16(NCH, 512)]
        t_v = [[Tok() for _ in range(NCH)] for _ in range(2)]
        gs = [A.bf16(NCH, 512), A.bf16(NCH, 512)]
        t_gs = [[Tok() for _ in range(NCH)] for _ in range(2)]
        srt = Rot([A.f32(512) for _ in range(1)])
        loT, t_lo = A.f32(n), Tok()
        ebuf, t_e = A.f32(n), Tok()
        spb, t_sp = A.f32(2, n), Tok()
        gate_region = (loT, ebuf, spb)
        cum, t_cum = A.f32(2, n), Tok()
        E1, E2 = A.f32(2, n), A.f32(2, n)
        t_E = [Tok(), Tok()]
        pbias, nbias, dd = A.f32(2, NCH), A.f32(2, NCH), A.f32(2, NCH)
        eref, elr, dec = A.f32(2, NCH), A.f32(2, NCH), A.f32(2, NCH)
        t_col = Tok()
        qrel, krel = A.bf16(4, n), A.bf16(4, n)
        t_q = [Tok() for _ in range(4)]
        t_k = [Tok() for _ in range(4)]
        r1, r2, r3 = ebuf, spb[:, 0, :], spb[:, 1, :]
        t_r = [t_e, t_sp, t_sp]
        kreltok, t_kt = A.bf16(4, 128), Tok()
        Sm2, t_sm2 = [A.bf16(8, 128), A.bf16(8, 128)], [[Tok(), Tok()], [Tok(), Tok()]]
        S, t_S = A.f32(4, 128), [Tok() for _ in range(4)]
        Sp2, t_Sp2 = [A.bf16(4, 128), A.bf16(4, 128)], [[Tok() for _ in range(4)] for _ in range(2)]
        tmpkv, t_tmp = A.f32(4, 128), [Tok() for _ in range(4)]
        mix2, t_mix2 = [A.bf16(1024), A.bf16(1024)], [Tok(), Tok()]
        tmpo, t_tmpo = A.f32(512), Tok()
        mixT, t_mixT = A.bf16(8, n), Tok()
        gen["banks"] = [0, 1, 2, 3]
        obank2 = [[(PB[6], TPB[6]), (PB[7], TPB[7])], [(PB[4], TPB[4]), (PB[5], TPB[5])]]
        stat2, t_ss2, t_st2 = [A.f32(16), A.f32(16)], [[Tok() for _ in range(12)] for _ in range(2)], [Tok(), Tok()]
        junk8 = A.bf16(8, 128)

        def wpiece(src):
            w, tw = WA.next()
            ncol = src.shape[1]
            P.dma("pool", w[:, :, :ncol], src.rearrange("(kc p) c -> p kc c", p=128),
                  W=([tw, t_PWA2] if tw is t_PWA else [tw]))
            return w, tw

        if cfg.prompt:
            for u in range(4):
                memset(S[:, u, :], 0.0, [t_S[u]])

        for tti in range(cfg.NT):
            cols = slice(tti * n, (tti + 1) * n)
            tcol = slice(cfg.tab0 + tti * n, cfg.tab0 + (tti + 1) * n)
            cos_t, t_cos = cosr.next()
            sin_t, t_sin = sinr.next()
            P.dma("sp", cos_t, D["cosF"][:, tcol], W=[t_cos])
            P.dma("sp", sin_t, D["sinF"][:, tcol], W=[t_sin])
            hT, t_h = hT2[tti % 2], t_h2[tti % 2]
            ck(1)
            bank, tb = gb()
            for kc in range(8):
                mm(bank[:16, :n], Wlo[:, kc, :], hT[:, kc, :n], [t_h, t_tab[7]], [tb], st=(kc == 0), sp=(kc == 7))
            cp(loT[:16, :], bank[:16, :n], [tb], [t_lo], eng="act")
            for kc in range(2):
                bank, tb = gb()
                mm(bank[:, :n], wgu[:16, kc * 128:(kc + 1) * 128], loT[:16, :], [t_lo, t_tab[6]], [tb])
                act(ebuf, bank[:, :n], AF.Exp, [tb, t_nbg], [t_e], scale=-1.0, bias=nbg[:, kc:kc + 1])
                act(spb[:, kc, :], ebuf, AF.Ln, [t_e], [t_sp], bias=1.0)
                P.op("dve", lambda e, kc=kc: e.tensor_tensor_scan(out=cum[:, kc, :], data0=reset, data1=spb[:, kc, :],
                                                                   initial=0.0, op0=MUL, op1=ADD),
                     [t_sp, t_tab[3]], [t_cum])
            if tti == 0:
                w1, tw1 = WA.next()
            else:
                w1, tw1 = wpiece(D["w_in_ab"][:, 1552:2064])
            w2, tw2 = wpiece(D["w_ab_sw"])
            for j in range(4):
                bank1, tb1 = gb()
                for kc in range(8):
                    mm(bank1[:, :n], w1[:, kc, j * 128:(j + 1) * 128], hT[:, kc, :n], [t_h, tw1], [tb1],
                       st=(kc == 0), sp=(kc == 7))
                bank2, tb2 = gb()
                for kc in range(8):
                    mm(bank2[:, :n], w2[:, kc, j * 128:(j + 1) * 128], hT[:, kc, :n], [t_h, tw2], [tb2],
                       st=(kc == 0), sp=(kc == 7))
                kcx = j % 2
                tt(r1, bank1[:, :n], cos_t, MUL, [tb1, t_cos], [t_r[0]])
                tt(r2, bank2[:, :n], sin_t, MUL, [tb2, t_sin], [t_r[1]])
                tt(r3, r1, r2, ADD, [t_r[0], t_r[1]], [t_r[2]])
                if j < 2:
                    tt(qrel[:, 2 + kcx, :], r3, retE[:, kcx, 0, :], MUL, [t_r[2], t_tab[0]], [t_q[2 + kcx]])
                else:
                    tt(krel[:, 2 + kcx, :], r3, retE[:, kcx, 1, :], MUL, [t_r[2], t_tab[0]], [t_k[2 + kcx]])
            ck(5)
            ck(3)
            for pi, c0 in enumerate((512, 1024, 2064, 2576)):
                w, tw = wpiece(D["w_in_ab"][:, c0:c0 + 512])
                grp = pi // 2
                for ci in range(NCH):
                    bank, tb = gb()
                    for kc in range(8):
                        mm(bank[:C, :], hT[:, kc, ci * C:(ci + 1) * C], w[:, kc, :], [t_h, tw], [tb],
                           st=(kc == 0), sp=(kc == 7))
                    if pi % 2 == 0:
                        cp(vtok[grp][:C, ci, :], bank[:C, :], [tb], [t_v[grp][ci]], eng="act")
                    else:
                        sr, tsr = srt.next()
                        act(sr[:C, :], bank[:C, :], AF.Silu, [tb], [tsr])
                        tt(gs[grp][:C, ci, :], sr[:C, :], (gglab if grp == 0 else gretb)[:C, :], MUL,
                           [tsr, t_tab[4 + grp]], [t_gs[grp][ci]])
            ck(2)
            cview = cum.rearrange("p k (c i) -> p k c i", i=C)
            cref = cview[:, :, :, refi]
            clast = cview[:, :, :, C - 1]
            ts(pbias, cref, 1.0 / 16, MUL, [t_cum], [t_col])
            ts(nbias, cref, -1.0 / 16, MUL, [t_cum], [t_col])
            tt(dd, clast, cref, SUB, [t_cum], [t_col])
            act(eref, cref, AF.Exp, [t_cum], [t_col], scale=-1.0 / 16)
            act(elr, dd, AF.Exp, [t_col], [t_col], scale=-1.0 / 16)
            act(dec, clast, AF.Exp, [t_cum], [t_col], scale=-1.0 / 16)
            for kc in range(2):
                for ci in range(NCH):
                    cs_ = slice(ci * C, (ci + 1) * C)
                    act(E1[:, kc, cs_], cum[:, kc, cs_], AF.Exp, [t_cum, t_col], [t_E[0]],
                        scale=-1.0 / 16, bias=pbias[:, kc, ci:ci + 1])
                    act(E2[:, kc, cs_], cum[:, kc, cs_], AF.Exp, [t_cum, t_col], [t_E[1]],
                        scale=1.0 / 16, bias=nbias[:, kc, ci:ci + 1])
            ck(4)
            w, tw = wpiece(D["w_in_ab"][:, 0:512])
            for j in range(4):
                bank, tb = gb()
                for kc in range(8):
                    mm(bank[:, :n], w[:, kc, j * 128:(j + 1) * 128], hT[:, kc, :n], [t_h, tw], [tb],
                       st=(kc == 0), sp=(kc == 7))
                kcx = j % 2
                if j < 2:
                    stt(qrel[:, kcx, :], bank[:, :n], 0.125, E1[:, kcx, :], MUL, MUL, [tb, t_E[0]], [t_q[kcx]])
                else:
                    tt(krel[:, kcx, :], bank[:, :n], E2[:, kcx, :], MUL, [tb, t_E[1]], [t_k[kcx]])
            if tti + 1 < cfg.NT:
                rmsnorm(cfg, tti + 1, 0, hT2[(tti + 1) % 2], t_h2[(tti + 1) % 2], sq, t_sq, rs, t_rs)

            def chunk_front(ci):
                    Sm, t_sm, Sp, t_Sp = Sm2[ci % 2], t_sm2[ci % 2], Sp2[ci % 2], t_Sp2[ci % 2]
                    g = tti * NCH + ci
                    cs_ = slice(ci * C, (ci + 1) * C)
                    first = (not cfg.prompt) or g == 0
                    last = (not cfg.prompt) or g == cfg.NT * NCH - 1
                    if not cfg.prompt:
                        for u in range(4):
                            src = (s0_gla if u < 2 else s0_ret)[ci, (u % 2) * 128:(u % 2 + 1) * 128, :]
                            P.dma("sp", S[:, u, :], src, W=[t_S[u]])
                    bank, tb = gb()
                    bv = bfv(bank)
                    for u in range(4):
                        tr(bv[:C, u * 128:(u + 1) * 128], krel[:, u, cs_], ident_b, [t_k[u], t_idb], [tb])
                    cp(kreltok[:C, :, :], bv[:C, 0:512].rearrange("p (a b) -> p a b", b=128), [tb], [t_kt], eng="act")
                    for u in range(4):
                        sc = eref[:, u, ci:ci + 1] if u < 2 else retc[:, u - 2, 0:1]
                        act(Sp[:, u, :], S[:, u, :], AF.Copy, [t_S[u], t_col, t_tab[1]], [t_Sp[u]], scale=sc)
                    for half in range(2):
                        bank, tb = gb()
                        for uu in range(2):
                            u = half * 2 + uu
                            grp, kcx = u // 2, u % 2
                            mm(bank[:, uu * 256:(uu + 1) * 256], kreltok[:C, u, :],
                               vtok[grp][:C, ci, kcx * 256:(kcx + 1) * 256], [t_kt, t_v[grp][ci]], [tb])
                        for uu in range(2):
                            u = half * 2 + uu
                            kcx = u % 2
                            e_lr = elr[:, kcx, ci:ci + 1] if u < 2 else retc[:, kcx, 1:2]
                            e_dc = dec[:, kcx, ci:ci + 1] if u < 2 else retc[:, kcx, 2:3]
                            ts(tmpkv[0:64, u, :], bank[0:64, uu * 256:uu * 256 + 128], e_lr[0:64, :], MUL,
                               [tb, t_col, t_tab[1]], [t_tmp[u]])
                            ts(tmpkv[64:128, u, :], bank[64:128, uu * 256 + 128:uu * 256 + 256], e_lr[64:128, :], MUL,
                               [tb, t_col, t_tab[1]], [t_tmp[u]])
                            stt(S[:, u, :], S[:, u, :], e_dc, tmpkv[:, u, :], MUL, ADD,
                                [t_S[u], t_tmp[u], t_col, t_tab[1]], [t_S[u]])
                            if last:
                                dst = (out_gla if u < 2 else out_ret)
                                dsti = seq if cfg.prompt else ci
                                P.dma("sp", dst[dsti, kcx * 128:(kcx + 1) * 128, :], S[:, u, :], R=[t_S[u]])
                    v3 = lambda ap: ap.rearrange("p (h c) -> p h c", c=128)[:C, :, :C]
                    for par in range(2):
                        bank, tb = gb()
                        pr = slice(par * 64, par * 64 + 64)
                        for slot in range(4):
                            grp = slot // 2
                            u = grp * 2 + slot % 2
                            mm(bank[:C, slot * 128:slot * 128 + C], krel[pr, u, cs_], qrel[pr, u, cs_],
                               [t_k[u], t_q[u]], [tb])
                        tt(Sm[:C, par * 4:(par + 1) * 4, :C], v3(bank[:, :]), v3(mask4[:, :]), MUL,
                           [tb, t_tab[2]], [t_sm[par]])
            def chunk_back(ci):
                    Sm, t_sm, Sp, t_Sp = Sm2[ci % 2], t_sm2[ci % 2], Sp2[ci % 2], t_Sp2[ci % 2]
                    obank = obank2[ci % 2]
                    (oA, t_oA), (oB, t_oB) = obank
                    mix, t_mix = mix2[ci % 2], t_mix2[ci % 2]
                    cs_ = slice(ci * C, (ci + 1) * C)
                    for grp in range(2):
                        ob, tob = obank[grp]
                        for hh in range(4):
                            u = grp * 2 + hh // 2
                            par = hh % 2
                            pr = slice(par * 64, par * 64 + 64)
                            smi = par * 4 + grp * 2 + hh // 2
                            i1 = mm(ob[:C, hh * 128:(hh + 1) * 128], Sm[:C, smi, :C],
                                    vtok[grp][:C, ci, hh * 128:(hh + 1) * 128], [t_sm[par], t_v[grp][ci]], [tob],
                                    st=True, sp=False)
                            mm(ob[:C, hh * 128:(hh + 1) * 128], qrel[pr, u, cs_], Sp[pr, u, :],
                               [t_q[u], t_Sp[u]], [tob], st=False, sp=True, force=([i1] if C < 64 else ()))
                    stat, t_ss, t_st = stat2[ci % 2], t_ss2[ci % 2], t_st2[ci % 2]
                    for grp in range(2):
                        ob, tob = obank[grp]
                        for hh in range(4):
                            k8 = grp * 4 + hh
                            act(junk8[:C, k8, :], ob[:C, hh * 128:(hh + 1) * 128], AF.Square, [tob], [t_ss[k8]],
                                accum=stat[:C, k8:k8 + 1])
                    for hh in range(4):
                        act(junk8[:C, hh, :], oB[:C, hh * 128:(hh + 1) * 128], AF.Copy, [t_oB, t_ss[hh]], [t_ss[hh], t_ss[8 + hh]],
                            accum=stat[:C, 8 + hh:9 + hh])
                    stt(stat[:C, 12:16], stat[:C, 8:12], -1.0 / 128, stat[:C, 8:12], MUL, MUL, [t_st] + t_ss[8:12], [t_st])
                    tt(stat[:C, 4:8], stat[:C, 4:8], stat[:C, 12:16], ADD, [t_st] + t_ss[4:8], [t_st])
                    ts(stat[:C, 8:12], stat[:C, 8:12], 1.0 / 128, MUL, [t_st] + t_ss[8:12], [t_st] + t_ss[8:12])
            def chunk_back2(ci):
                    obank = obank2[ci % 2]
                    (oA, t_oA), (oB, t_oB) = obank
                    mix, t_mix = mix2[ci % 2], t_mix2[ci % 2]
                    stat, t_ss, t_st = stat2[ci % 2], t_ss2[ci % 2], t_st2[ci % 2]
                    rstd_small(stat[:C, 0:8], stat[:C, 0:8], 1.0 / 128, [t_st] + t_ss[0:4], [t_st])
                    for hh in range(4):
                        hs = slice(hh * 128, (hh + 1) * 128)
                        stt(mix[:C, hs], oA[:C, hs], stat[:C, hh:hh + 1], gs[0][:C, ci, hs], MUL, MUL,
                            [t_oA, t_st, t_gs[0][ci]], [t_mix])
                        ts(tmpo[:C, hs], oB[:C, hs], stat[:C, 8 + hh:9 + hh], SUB, [t_oB, t_st], [t_tmpo],
                           s2=stat[:C, 4 + hh:5 + hh], op1=MUL)
                    tt(mix[:C, 512:1024], tmpo[:C, :], gs[1][:C, ci, :], MUL, [t_tmpo, t_gs[1][ci]], [t_mix])
            def chunk_trans(ci):
                    mix, t_mix = mix2[ci % 2], t_mix2[ci % 2]
                    cs_ = slice(ci * C, (ci + 1) * C)
                    bank, tb = gb()
                    bv = bfv(bank)
                    for c in range(8):
                        tr(bv[:, c * 128:c * 128 + C], mix[:C, c * 128:(c + 1) * 128], ident_b[:C, :C],
                           [t_mix, t_idb], [tb])
                    cp(mixT[:, :, cs_], bv[:, :].rearrange("p (a b) -> p a b", b=128)[:, :, :C], [tb], [t_mixT], eng="act")
            for st_ in range(NCH + 3):
                if st_ < NCH:
                    chunk_front(st_)
                if 1 <= st_ <= NCH:
                    chunk_back(st_ - 1)
                if 2 <= st_ <= NCH + 1:
                    chunk_back2(st_ - 2)
                if st_ >= 3:
                    chunk_trans(st_ - 3)
            ck(10)
            for half in range(2):
                w, tw = wpiece(D["w_out_ab"][:, half * 512:(half + 1) * 512])
                for o4 in range(4):
                    oc = half * 4 + o4
                    bank, tb = gb()
                    for kc in range(8):
                        mm(bank[:, :n], w[:, kc, o4 * 128:(o4 + 1) * 128], mixT[:, kc, :n], [tw, t_mixT], [tb],
                           st=(kc == 0), sp=(kc == 7))
                    xc = slice(cfg.x0 + tti * n, cfg.x0 + (tti + 1) * n)
                    tt(xT[:, oc, xc], xT[:, oc, xc], bank[:, :n], ADD, [t_x[cfg.xt0 + tti], tb], [t_x[cfg.xt0 + tti]])
        P.barrier()
        A.reset(m)

    def phase_ffn(groups, layer):
        m = A.mark()
        NH = NJ // 2
        Ttot = sum(g[0].T for g in groups)
        nmax = max(g[0].TT for g in groups)
        deep = not any(g[0].prompt for g in groups)
        hT = A.bf16(8, Ttot)
        sq, t_sq = PSQ, None
        rs, t_rs = PRS, t_PRS
        actb = A.bf16(NH, Ttot)
        WI = Rot([A.bf16(8, 256) for _ in range(7 if deep else 2)])
        WI.items.insert(0, (PWA[:, :, 0:256], t_PWA))
        ffn_pro = {}
        P.nobar = True
        cfg0 = groups[0][0]
        rmsnorm(cfg0, 0, 2 + layer, PHT[:, :, :cfg0.TT], t_PHT, PSQ, None, PRS, t_PRS)
        w0, tw0 = WI.next()
        ffn_pro["tw"] = (tw0, t_PWA2)
        for g_ in range(2):
            c0_ = g_ * D_FF
            P.dma("pool", w0[:, :, g_ * 128:(g_ + 1) * 128],
                  D["wffi"][layer, :, c0_:c0_ + 128].rearrange("(kc p) c -> p kc c", p=128), W=[ffn_pro["tw"][g_]])
        P.nobar = False
        WO = Rot([A.bf16(NH, 128) for _ in range(6 if deep else 2)])
        cbuf = Rot([A.f32(nmax) for _ in range(6 if deep else 2)])
        gbuf = Rot([A.f32(nmax) for _ in range(6 if deep else 2)])
        wi_tok = {}
        abc_tok = {}
        gen["banks"] = [0, 1, 2, 3, 4, 5, 6, 7]
        units = []
        G = []
        col = 0
        stg, t_stg = A.f32(NJ * 128), Tok()
        for cfg, carry_in_rows, conv_out in groups:
            nseq, L, n = cfg.nseq, cfg.L, cfg.TT
            g = Cfg()
            g.cfg, g.conv_out, g.c0 = cfg, conv_out, col
            g.carry, g.t_carry = A.f32(NJ, nseq * 2), [Tok() for _ in range(NJ)]
            g.abuf = Rot([A.f32(nseq * (L + 2)) for _ in range(6 if deep else 3)])
            g.t_h = [Tok() for _ in range(cfg.NT)]
            g.t_act = [[Tok() for _ in range(cfg.NT)] for _ in range(NH)]
            if carry_in_rows is None:
                for j in range(NJ):
                    memset(g.carry[:, j, :], 0.0, [g.t_carry[j]])
            else:
                R_ = nseq * 2
                stage, t_s = stg, t_stg
                P.dma("sp", stage[:R_, :], carry_in_rows, W=[t_s])
                bank, tb = gb()
                for j in range(NJ):
                    tr(bank[:, j * R_:(j + 1) * R_], stage[:R_, j * 128:(j + 1) * 128], ident_f[:R_, :R_],
                       [t_s, t_idf], [tb])
                cp(g.carry, bank[:, 0:NJ * R_].rearrange("p (a b) -> p a b", b=R_), [tb], g.t_carry)
            g.hT = []
            for tti in range(cfg.NT):
                if not units:
                    g.hT.append(PHT[:, :, :n])
                    g.t_h[tti] = t_PHT
                else:
                    g.hT.append(hT[:, :, col + tti * n:col + (tti + 1) * n])
                    rmsnorm(cfg, tti, 2 + layer, g.hT[tti], g.t_h[tti], sq, t_sq, rs, t_rs)
                units.append((g, tti))
            col += cfg.T
            G.append(g)
        wd = lambda k, j: dw[:, j, layer * 3 + k:layer * 3 + k + 1]
        bd = lambda j: dw[:, j, 6 + layer:7 + layer]
        for jh in range(2):
            for jj in range(NH):
                j = jh * NH + jj
                if j == 0:
                    w, tw = w0, ffn_pro["tw"]
                    wi_tok[id(tw0)] = tw
                else:
                    w, tw = WI.next()
                    tw = wi_tok.setdefault(id(tw), (tw, Tok()))
                    for g_ in range(2):
                        c0 = g_ * D_FF + j * 128
                        P.dma("pool", w[:, :, g_ * 128:(g_ + 1) * 128],
                              D["wffi"][layer, :, c0:c0 + 128].rearrange("(kc p) c -> p kc c", p=128), W=[tw[g_]])
                for g, tti in units:
                    cfg = g.cfg
                    nseq, L, n = cfg.nseq, cfg.L, cfg.TT
                    cols = slice(g.c0 + tti * n, g.c0 + (tti + 1) * n)
                    ba, tba = gb()
                    for kc in range(8):
                        mm(ba[:, :n], w[:, kc, 0:128], g.hT[tti][:, kc, :], [tw[0], g.t_h[tti]], [tba], st=(kc == 0), sp=(kc == 7))
                    bu, tbu = gb()
                    for kc in range(8):
                        mm(bu[:, :n], w[:, kc, 128:256], g.hT[tti][:, kc, :], [tw[1], g.t_h[tti]], [tbu], st=(kc == 0), sp=(kc == 7))
                    ab, tab_ = g.abuf.next()
                    tabc = abc_tok.setdefault(id(tab_), Tok())
                    ab3 = ab.rearrange("p (s l) -> p s l", l=L + 2)
                    cr = g.carry[:, j, :].rearrange("p (s r) -> p s r", r=2)
                    cp(ab3[:, :, 0:2], cr, [g.t_carry[j]], [tabc])
                    act(ab3[:, :, 2:L + 2], ba[:, :n].rearrange("p (s l) -> p s l", l=L), AF.Copy, [tba], [tab_])
                    cb, tcb = cbuf.next()
                    cb3 = cb[:, :n].rearrange("p (s l) -> p s l", l=L)
                    act(cb[:, :n], ba[:, :n], AF.Identity, [tba, t_dw], [tcb], scale=wd(2, j), bias=bd(j))
                    stt(cb3, ab3[:, :, 1:L + 1], wd(1, j), cb3, MUL, ADD, [tab_, tabc, tcb, t_dw], [tcb])
                    stt(cb3, ab3[:, :, 0:L], wd(0, j), cb3, MUL, ADD, [tab_, tabc, tcb, t_dw], [tcb])
                    cp(cr, ab3[:, :, L:L + 2], [tab_], [g.t_carry[j]])
                    ge, tge = gbuf.next()
                    act(ge[:, :n], cb[:, :n], AF.Gelu, [tcb], [tge])
                    tt(actb[:, jj, cols], ge[:, :n], bu[:, :n], MUL, [tge, tbu], [g.t_act[jj][tti]])
            for oc in range(8):
                w, tw = WO.next()
                P.dma("pool", w, D["wffo"][layer, jh * NH * 128:(jh + 1) * NH * 128, oc * 128:(oc + 1) * 128]
                      .rearrange("(j p) c -> p j c", p=128), W=[tw])
                for g, tti in units:
                    cfg = g.cfg
                    n = cfg.TT
                    cols = slice(g.c0 + tti * n, g.c0 + (tti + 1) * n)
                    xc = slice(cfg.x0 + tti * n, cfg.x0 + (tti + 1) * n)
                    t_xt = t_x[cfg.xt0 + tti]
                    bank, tb = gb()
                    for jj in range(NH):
                        mm(bank[:, :n], w[:, jj, :], actb[:, jj, cols], [tw, g.t_act[jj][tti]], [tb],
                           st=(jj == 0), sp=(jj == NH - 1))
                    tt(xT[:, oc, xc], xT[:, oc, xc], bank[:, :n], ADD, [t_xt, tb], [t_xt])
        for g in G:
            R_ = g.cfg.nseq * 2
            cstage, t_cs = stg, t_stg
            for q4 in range((NJ + 3) // 4):
                j0, j1 = q4 * 4, min(NJ, q4 * 4 + 4)
                bank, tb = gb()
                for j in range(j0, j1):
                    tr(bank[:R_, (j - j0) * 128:(j - j0 + 1) * 128], g.carry[:, j, :], ident_f, [g.t_carry[j], t_idf], [tb])
                cp(cstage[:R_, j0 * 128:j1 * 128], bank[:R_, 0:(j1 - j0) * 128], [tb], [t_cs])
            P.dma("sp", g.conv_out.rearrange("s r c -> (s r) c"), cstage[:R_, :], R=[t_cs])
        P.barrier()
        A.reset(m)

    SCALE = 192.0 ** -0.5

    def phase_mla(cfg, seq, ckv_out, kr_out, cache_ckv=None, cache_kr=None):
        m = A.mark()
        n, C, NCH, T = cfg.TT, cfg.C, cfg.NCH, cfg.T
        KB = 128 if cfg.prompt else 16
        NB = T // KB
        cqnT, t_cqn = A.bf16(3, T), [Tok() for _ in range(cfg.NT)]
        ckvT, t_ckvT = A.bf16(4, T), [Tok() for _ in range(cfg.NT)]
        krT2, t_krT = A.bf16(T), [Tok() for _ in range(cfg.NT)]
        ckvn_b = None
        if not cfg.prompt:
            ckvn_b, t_cnb = A.bf16(NB, 512), [Tok() for _ in range(NB)]
        m1 = A.mark()
        P.nobar = True
        rmsnorm(cfg, 0, 1, PHT, t_PHT, PSQ, None, PRS, t_PRS)
        P.dma("pool", PWA[:, :, 0:384], D["w_in_c"][:, 0:384].rearrange("(kc p) c -> p kc c", p=128), W=[t_PWA, t_PWA2])
        P.nobar = False
        Wc = A.bf16(8, 960)
        t_wc = [t_PWA] + [load(Wc[:, :, c0:c1], D["w_in_c"][:, c0:c1].rearrange("(kc p) c -> p kc c", p=128), "pool")
                          for c0, c1 in ((384, 896), (896, 960))]
        krc, krs_ = A.f32(17, 64), A.f32(17, 64)
        gkvb = A.f32(512)
        t_t = [load(krc, D["krc"]), load(krs_, D["krs_"]), load(gkvb, D["gkv"].to_broadcast([128, 512]))]
        sq, t_sq = PSQ, None
        rs, t_rs = PRS, t_PRS
        hT2c, t_h2c = [PHT, A.bf16(8, n)], [t_PHT, Tok()]
        sq3, t_sq3 = A.bf16(3, n), Tok()
        rq, t_rq = A.f32(n), Tok()
        ckvn = Rot([A.f32(512) for _ in range(3)])
        cb16 = Rot([A.bf16(512) for _ in range(3)])
        krr = Rot([A.f32(64) for _ in range(3)])
        kt1, kt2, t_kt = A.f32(64), A.f32(64), Tok()
        kb16 = Rot([A.bf16(128) for _ in range(3)])
        st1, t_st1 = A.f32(4), Tok()
        junk, t_junk = A.bf16(512), Tok()
        gen["banks"] = [0, 1, 2, 3, 4, 5, 6, 7]
        for tti in range(cfg.NT):
            cols = slice(tti * n, (tti + 1) * n)
            hT, t_h = hT2c[tti % 2], t_h2c[tti % 2]
            cqb = []
            for j in range(3):
                bank, tb = gb()
                for kc in range(8):
                    mm(bank[:, :n], PWA[:, kc, j * 128:(j + 1) * 128], hT[:, kc, :n], [t_wc[0], t_h], [tb],
                       st=(kc == 0), sp=(kc == 7))
                act(sq3[:, j, :], bank[:, :n], AF.Square, [tb], [t_sq3])
                cqb.append((bank, tb))
            bank, tb = gb()
            for j in range(3):
                mm(bank[:, :n], ones_b, sq3[:, j, :], [t_sq3, t_one], [tb], st=(j == 0), sp=(j == 2))
            act(rq, bank[:, :n], AF.Ln, [tb], [t_rq], scale=1.0 / 384, bias=EPS)
            act(rq, rq, AF.Exp, [t_rq], [t_rq], scale=-0.5)
            for j in range(3):
                stt(cqnT[:, j, cols], cqb[j][0][:, :n], gqc[:, j:j + 1], rq, MUL, MUL,
                    [cqb[j][1], t_rq, t_gq], [t_cqn[tti]])
            if tti + 1 < cfg.NT:
                rmsnorm(cfg, tti + 1, 1, hT2c[(tti + 1) % 2], t_h2c[(tti + 1) % 2], sq, t_sq, rs, t_rs)
            def c1_proj(bi):
                blk = tti * (n // KB) + bi
                tcs = slice(bi * KB, (bi + 1) * KB)
                gcs = slice(blk * KB, (blk + 1) * KB)
                bank, tb = gb()
                for kc in range(8):
                    mm(bank[:KB, :], hT[:, kc, tcs], Wc[:, kc, 384:896], [t_h, t_wc[1]], [tb], st=(kc == 0), sp=(kc == 7))
                act(junk[:KB, :], bank[:KB, :], AF.Square, [tb, t_st1], [t_junk, t_st1], accum=st1[:KB, 0:1])
                rstd_small(st1[:KB, 0:1], st1[:KB, 0:1], 1.0 / 512, [t_st1], [t_st1])
                cn, tcn = ckvn.next()
                stt(cn[:KB, :], bank[:KB, :], st1[:KB, 0:1], gkvb[:KB, :], MUL, MUL, [tb, t_st1, t_t[2]], [tcn])
                P.dma("sp", ckv_out[gcs, :], cn[:KB, :], R=[tcn])
                if cfg.prompt:
                    c16, tc16 = cb16.next()
                else:
                    c16, tc16 = ckvn_b[:, blk, :], t_cnb[blk]
                cp(c16[:KB, :], cn[:KB, :], [tcn], [tc16], eng="act")
                bank, tb = gb()
                for kc in range(8):
                    mm(bank[:KB, 0:64], hT[:, kc, tcs], Wc[:, kc, 896:960], [t_h, t_wc[2]], [tb], st=(kc == 0), sp=(kc == 7))
                tblk = blk if cfg.prompt else 16
                tt(kt1[:KB, :], bank[:KB, 0:64], krc[:KB, tblk, :], MUL, [tb, t_t[0]], [t_kt])
                tt(kt2[:KB, 0:32], bank[:KB, 32:64], krs_[:KB, tblk, 0:32], MUL, [tb, t_t[1]], [t_kt])
                tt(kt2[:KB, 32:64], bank[:KB, 0:32], krs_[:KB, tblk, 32:64], MUL, [tb, t_t[1]], [t_kt])
                kr_, tkr = krr.next()
                tt(kr_[:KB, :], kt1[:KB, :], kt2[:KB, :], ADD, [t_kt], [tkr])
                P.dma("sp", kr_out[gcs, :], kr_[:KB, :], R=[tkr])
                k16, tk16 = kb16.next()
                cp(k16[:KB, 0:64], kr_[:KB, :], [tkr], [tk16], eng="act")
                cp(k16[:KB, 64:128], kr_[:KB, :], [tkr], [tk16], eng="act")
                return gcs, c16, tc16, k16, tk16

            def c1_trans(item):
                gcs, c16, tc16, k16, tk16 = item
                bank2, tb2 = gb()
                bv = bfv(bank2)
                for kc in range(4):
                    tr(bv[:, kc * 128:kc * 128 + KB], c16[:KB, kc * 128:(kc + 1) * 128], ident_b[:KB, :KB],
                       [tc16, t_idb], [tb2])
                tr(bv[:, 512:512 + KB], k16[:KB, :], ident_b[:KB, :KB], [tk16, t_idb], [tb2])
                cp(ckvT[:, :, gcs], bv[:, 0:512].rearrange("p (a b) -> p a b", b=128)[:, :, :KB], [tb2], [t_ckvT[tti]])
                cp(krT2[:, gcs], bv[:, 512:512 + KB], [tb2], [t_krT[tti]])

            pend = []
            for bi in range(n // KB):
                pend.append(c1_proj(bi))
                if len(pend) > 1:
                    c1_trans(pend.pop(0))
            while pend:
                c1_trans(pend.pop(0))
        P.barrier()
        A.reset(m1)
        if cfg.prompt:
            mla_prompt_c2(cfg, cqnT, t_cqn, ckvT, t_ckvT, krT2, t_krT)
        else:
            mla_sample_c2(cfg, cqnT, t_cqn, ckvT, t_ckvT, krT2, t_krT, ckvn_b, t_cnb, cache_ckv, cache_kr)
        P.barrier()
        A.reset(m)

    def mla_prompt_c2(cfg, cqnT, t_cqn, ckvT, t_ckvT, krT2, t_krT):
        n, T, NT = cfg.TT, cfg.T, cfg.NT
        qn, t_qn = [A.bf16(T), A.bf16(T)], [[Tok() for _ in range(NT)] for _ in range(2)]
        qr, t_qr = A.bf16(T), [Tok() for _ in range(NT)]
        kn, t_kn = [A.bf16(T), A.bf16(T)], [[Tok() for _ in range(NT)] for _ in range(2)]
        Vp, t_V = A.bf16(16, 256), [Tok() for _ in range(16)]
        ao2, t_ao2 = [A.bf16(2, n), A.bf16(2, n)], [Tok(), Tok()]
        pending_out = [None]
        WQ = Rot([A.bf16(3, 256) for _ in range(2)])
        WQR = Rot([A.bf16(3, 128) for _ in range(2)])
        WQS = Rot([A.bf16(3, 128) for _ in range(2)])
        WK = Rot([A.bf16(4, 256) for _ in range(2)])
        WV = Rot([A.bf16(4, 256) for _ in range(2)])
        WOo = Rot([A.bf16(2, 1024) for _ in range(2)])
        cosr = Rot([A.f32(n) for _ in range(2)])
        sinr = Rot([A.f32(n) for _ in range(2)])
        r1, r2, t_r = A.f32(n), A.f32(n), [Tok(), Tok()]
        PT = Rot([A.bf16(512) for _ in range(5)])
        rden, t_rden = A.f32(n), Tok()
        gen["banks"] = [0, 1, 2, 3]
        obk = Rot([PB[4], PB[5]])
        obk.items = [(PB[4], TPB[4]), (PB[5], TPB[5])]
        dbk = Rot([PB[6], PB[7]])
        dbk.items = [(PB[6], TPB[6]), (PB[7], TPB[7])]
        r3 = lambda src: src.rearrange("(kc p) c -> p kc c", p=128)
        for pr in range(4):
            wq, twq = WQ.next()
            P.dma("pool", wq, r3(D["wuq_n"][:, pr * 256:(pr + 1) * 256]), W=[twq])
            wqr, twqr = WQR.next()
            P.dma("pool", wqr, r3(D["wuq_r"][:, pr * 128:(pr + 1) * 128]), W=[twqr])
            wqs, twqs = WQS.next()
            P.dma("pool", wqs, r3(D["wuq_rs"][:, pr * 128:(pr + 1) * 128]), W=[twqs])
            wk, twk = WK.next()
            P.dma("pool", wk, r3(D["wuk"][:, pr * 256:(pr + 1) * 256]), W=[twk])
            wv, twv = WV.next()
            P.dma("pool", wv, r3(D["wuv"][:, pr * 256:(pr + 1) * 256]), W=[twv])
            wo, two = WOo.next()
            P.dma("pool", wo, D["w_out_c"][pr * 256:(pr + 1) * 256, :].rearrange("(h p) c -> p h c", p=128), W=[two])
            for tti in range(NT):
                cols = slice(tti * n, (tti + 1) * n)
                for hh in range(2):
                    bank, tb = gb()
                    for kc in range(3):
                        mm(bank[:, :n], wq[:, kc, hh * 128:(hh + 1) * 128], cqnT[:, kc, cols], [twq, t_cqn[tti]], [tb],
                           st=(kc == 0), sp=(kc == 2))
                    cp(qn[hh][:, cols], bank[:, :n], [tb], [t_qn[hh][tti]], eng="act")
                    bank, tb = gb()
                    for kc in range(4):
                        mm(bank[:, :n], wk[:, kc, hh * 128:(hh + 1) * 128], ckvT[:, kc, cols], [twk, t_ckvT[tti]], [tb],
                           st=(kc == 0), sp=(kc == 3))
                    cp(kn[hh][:, cols], bank[:, :n], [tb], [t_kn[hh][tti]], eng="dve")
                cos_t, t_cos = cosr.next()
                sin_t, t_sin = sinr.next()
                P.dma("sp", cos_t, D["cosF"][:, cols], W=[t_cos])
                P.dma("sp", sin_t, D["sinF"][:, cols], W=[t_sin])
                bank1, tb1 = gb()
                for kc in range(3):
                    mm(bank1[:, :n], wqr[:, kc, :], cqnT[:, kc, cols], [twqr, t_cqn[tti]], [tb1], st=(kc == 0), sp=(kc == 2))
                bank2, tb2 = gb()
                for kc in range(3):
                    mm(bank2[:, :n], wqs[:, kc, :], cqnT[:, kc, cols], [twqs, t_cqn[tti]], [tb2], st=(kc == 0), sp=(kc == 2))
                tt(r1, bank1[:, :n], cos_t, MUL, [tb1, t_cos], [t_r[0]])
                tt(r2, bank2[:, :n], sin_t, MUL, [tb2, t_sin], [t_r[1]])
                tt(qr[:, cols], r1, r2, ADD, t_r, [t_qr[tti]])
                for b4 in range(4):
                    blk = tti * 4 + b4
                    bank, tb = gb()
                    for kc in range(4):
                        mm(bank[:, 0:256], ckvT[:, kc, blk * 128:(blk + 1) * 128], wv[:, kc, :], [twv, t_ckvT[tti]], [tb],
                           st=(kc == 0), sp=(kc == 3))
                    cp(Vp[:, blk, :], bank[:, 0:256], [tb], [t_V[blk]], eng=("act" if b4 % 2 == 0 else "dve"))
            for qt in range(NT):
                qcols0 = qt * n
                ao, t_ao = ao2[qt % 2], t_ao2[qt % 2]
                for hh in range(2):
                    prs = slice(hh * 64, hh * 64 + 64)
                    ob, tob = obk.next()
                    db, tdb = dbk.next()
                    nkb = 4 * qt + 4

                    def scores(kb):
                        i = kb - 4 * qt
                        q0 = 0 if i <= 0 else i * 128
                        N = n - q0
                        qs = slice(qcols0 + q0, qcols0 + n)
                        ks = slice(kb * 128, (kb + 1) * 128)
                        bank, tb = gb()
                        mm(bank[:, :N], kn[hh][:, ks], qn[hh][:, qs], [t_kn[hh][kb // 4], t_qn[hh][qt]], [tb], st=True, sp=False)
                        mm(bank[:, :N], krT2[prs, ks], qr[prs, qs], [t_krT[kb // 4], t_qr[qt]], [tb], st=False, sp=True)
                        pt, tpt = PT.next()
                        act(pt[:, :N], bank[:, :N], AF.Exp, [tb], [tpt], scale=SCALE)
                        if i >= 0:
                            memset(pt[64:128, 0:64], 0.0, [tpt])
                        return kb, q0, N, pt, tpt

                    def pv(item):
                        kb, q0, N, pt, tpt = item
                        mm(ob[:, q0:n], Vp[:, kb, hh * 128:(hh + 1) * 128], pt[:, :N], [t_V[kb], tpt], [tob],
                           st=(kb == 0), sp=(kb == nkb - 1))
                        mm(db[:, q0:n], ones_b, pt[:, :N], [t_one, tpt], [tdb], st=(kb == 0), sp=(kb == nkb - 1))

                    pend = []
                    for kb in range(nkb):
                        pend.append(scores(kb))
                        if len(pend) > 2:
                            pv(pend.pop(0))
                        if hh == 0 and kb == 2 and pending_out[0] is not None:
                            pending_out[0]()
                            pending_out[0] = None
                    while pend:
                        pv(pend.pop(0))
                    P.op("dve", lambda e, db=db: e.reciprocal(out=rden, in_=db[:, :n]), [tdb], [t_rden])
                    tt(ao[:, hh, :], ob[:, :n], rden, MUL, [tob, t_rden], [t_ao])

                def outproj(qt=qt, ao=ao, t_ao=t_ao, wo=wo, two=two):
                    qc = slice(qt * n, qt * n + n)
                    for oc in range(8):
                        bank, tb = gb()
                        for hh in range(2):
                            mm(bank[:, :n], wo[:, hh, oc * 128:(oc + 1) * 128], ao[:, hh, :], [two, t_ao], [tb],
                               st=(hh == 0), sp=(hh == 1))
                        tt(xT[:, oc, qc], xT[:, oc, qc], bank[:, :n], ADD, [t_x[qt], tb], [t_x[qt]])
                pending_out[0] = outproj
        if pending_out[0] is not None:
            pending_out[0]()
            pending_out[0] = None

    def mla_sample_c2(cfg, cqnT, t_cqn, ckvT, t_ckvT, krT2, t_krT, ckvn_b, t_cnb, cache_ckv, cache_kr):
        T = cfg.T
        r3 = lambda src: src.rearrange("(kc p) c -> p kc c", p=128)
        WQ, WQR, WQS = A.bf16(3, 1024), A.bf16(3, 512), A.bf16(3, 512)
        WUKT, WV = A.bf16(8, 512), A.bf16(4, 1024)
        t_w = [load(WQ, r3(D["wuq_n"]), "pool"), load(WQR, r3(D["wuq_r"]), "pool"), load(WQS, r3(D["wuq_rs"]), "pool"),
               load(WUKT, D["wukT"], "pool"), load(WV, r3(D["wuv"]), "pool")]
        WOo = Rot([A.bf16(8, 128) for _ in range(2)])
        cos_t, sin_t = A.f32(T), A.f32(T)
        t_cs = [load(cos_t, D["cosF"][:, 2048:2048 + T]), load(sin_t, D["sinF"][:, 2048:2048 + T])]
        qnS, t_qnS = A.bf16(8, T), Tok()
        qrS, t_qrS = A.bf16(8, T), Tok()
        r1, r2, t_r = A.f32(T), A.f32(T), [Tok(), Tok()]
        qlat, t_ql = [A.bf16(4, 128) for _ in range(4)], [Tok() for _ in range(4)]
        CQ = Rot([A.bf16(8, 512) for _ in range(3)])
        KQ = Rot([A.bf16(8, 128) for _ in range(3)])
        kq_tok = {}
        CT = Rot([A.bf16(4, 1024) for _ in range(2)])
        KT = Rot([A.bf16(1024) for _ in range(2)])
        PT = Rot([A.bf16(128) for _ in range(5)])
        rden, t_rden = A.f32(1), Tok()
        olatn, t_on = A.bf16(512), Tok()
        olatT, t_oT = A.bf16(4, 128), Tok()
        aoS, t_ao = A.bf16(8, T), Tok()
        gen["banks"] = [0, 1, 2, 3, 4, 5]
        olb, t_olb, dnb, t_dnb = PB[6], TPB[6], PB[7], TPB[7]
        for h in range(8):
            bank, tb = gb()
            for kc in range(3):
                mm(bank[:, :T], WQ[:, kc, h * 128:(h + 1) * 128], cqnT[:, kc, :T], [t_w[0], t_cqn[0]], [tb], st=(kc == 0), sp=(kc == 2))
            cp(qnS[:, h, :], bank[:, :T], [tb], [t_qnS], eng=("act" if h % 2 else "dve"))
        for h in range(8):
            b1, tb1 = gb()
            for kc in range(3):
                mm(b1[:64, :T], WQR[:, kc, h * 64:(h + 1) * 64], cqnT[:, kc, :T], [t_w[1], t_cqn[0]], [tb1], st=(kc == 0), sp=(kc == 2))
            b2, tb2 = gb()
            for kc in range(3):
                mm(b2[:64, :T], WQS[:, kc, h * 64:(h + 1) * 64], cqnT[:, kc, :T], [t_w[2], t_cqn[0]], [tb2], st=(kc == 0), sp=(kc == 2))
            tt(r1[:64, :], b1[:64, :T], cos_t[:64, :], MUL, [tb1, t_cs[0]], [t_r[0]])
            tt(r2[:64, :], b2[:64, :T], sin_t[:64, :], MUL, [tb2, t_cs[1]], [t_r[1]])
            tt(qrS[:64, h, :], r1[:64, :], r2[:64, :], ADD, t_r, [t_qrS])
        for b in range(4):
            bank, tb = gb()
            for kc in range(4):
                for h in range(8):
                    mm(bank[:, kc * 128 + h * 16:kc * 128 + (h + 1) * 16], WUKT[:, h, kc * 128:(kc + 1) * 128],
                       qnS[:, h, b * 16:(b + 1) * 16], [t_w[3], t_qnS], [tb])
            cp(qlat[b], bank[:, :].rearrange("p (a b) -> p a b", b=128), [tb], [t_ql[b]], eng="act")
        for b in range(4):
            pend = []

            def scores(K_, lc, lr, vrows, Rk, blk):
                bank, tb = gb()
                for kc in range(4):
                    mm(bank[:K_, 0:128], lc(kc), qlat[b][:, kc, :], Rk + [t_ql[b]], [tb], st=(kc == 0), sp=False)
                mm(bank[:K_, 0:128], lr, qrS[:64, :, b * 16:(b + 1) * 16], Rk + [t_qrS], [tb], st=False, sp=True)
                pt, tpt = PT.next()
                act(pt[:K_, :], bank[:K_, 0:128], AF.Exp, [tb], [tpt], scale=SCALE)
                return K_, pt, tpt, vrows, Rk, blk

            def pv(item):
                K_, pt, tpt, vrows, Rk, blk = item
                mm(olb[:, :], pt[:K_, :], vrows, [tpt] + Rk, [t_olb], st=(blk == 0), sp=(blk == 32))
                mm(dnb[:, 0:1], pt[:K_, :], ones_b[:K_, 0:1], [tpt, t_one], [t_dnb], st=(blk == 0), sp=(blk == 32))

            def push(item):
                pend.append(item)
                if len(pend) > 2:
                    pv(pend.pop(0))

            for q4 in range(4):
                cq, tcq = CQ.next()
                kq, tkq = KQ.next()
                tkq = kq_tok.setdefault(id(tkq), (tkq, Tok()))
                P.dma("pool", cq, cache_ckv[b, q4 * 1024:(q4 + 1) * 1024, :].rearrange("(k p) l -> p k l", p=128), W=[tcq])
                for dup in range(2):
                    P.dma("pool", kq[:, :, dup * 64:(dup + 1) * 64],
                          cache_kr[b, q4 * 1024:(q4 + 1) * 1024, :].rearrange("(k p) r -> p k r", p=128), W=[tkq[dup]])
                ct, tct = CT.next()
                kt, tkt = KT.next()
                for k8 in range(8):
                    bs = slice(k8 * 128, (k8 + 1) * 128)
                    bank, tb = gb()
                    bv = bfv(bank)
                    for kc in range(4):
                        tr(bv[:, kc * 128:(kc + 1) * 128], cq[:, k8, kc * 128:(kc + 1) * 128], ident_b, [tcq, t_idb], [tb])
                    tr(bv[:, 512:640], kq[:, k8, :], ident_b, [tkq[0], tkq[1], t_idb], [tb])
                    cp(ct[:, :, bs], bv[:, 0:512].rearrange("p (a b) -> p a b", b=128), [tb], [tct],
                       eng=("act" if k8 % 2 else "dve"))
                    cp(kt[:, bs], bv[:, 512:640], [tb], [tkt], eng=("dve" if k8 % 2 else "act"))
                for k8 in range(8):
                    bs = slice(k8 * 128, (k8 + 1) * 128)
                    push(scores(128, (lambda kc, bs=bs, ct=ct: ct[:, kc, bs]), kt[0:64, bs], cq[:, k8, :],
                                [tct, tkt, tcq], q4 * 8 + k8))
            bs = slice(b * 16, (b + 1) * 16)
            push(scores(16, (lambda kc, bs=bs: ckvT[:, kc, bs]), krT2[0:64, bs], ckvn_b[:16, b, :],
                        [t_ckvT[0], t_krT[0], t_cnb[b]], 32))
            while pend:
                pv(pend.pop(0))
            P.op("dve", lambda e: e.reciprocal(out=rden, in_=dnb[:, 0:1]), [t_dnb], [t_rden])
            ts(olatn, olb[:, :], rden[:, 0:1], MUL, [t_olb, t_rden], [t_on])
            bank, tb = gb()
            bv = bfv(bank)
            for kc in range(4):
                tr(bv[:, kc * 128:(kc + 1) * 128], olatn[:, kc * 128:(kc + 1) * 128], ident_b, [t_on, t_idb], [tb])
            cp(olatT, bv[:, 0:512].rearrange("p (a b) -> p a b", b=128), [tb], [t_oT], eng="act")
            bank, tb = gb()
            for h in range(8):
                for kc in range(4):
                    mm(bank[:, h * 16:(h + 1) * 16], WV[:, kc, h * 128:(h + 1) * 128], olatT[:, kc, h * 16:(h + 1) * 16],
                       [t_w[4], t_oT], [tb], st=(kc == 0), sp=(kc == 3))
            cp(aoS[:, :, b * 16:(b + 1) * 16], bank[:, 0:128].rearrange("p (h i) -> p h i", i=16), [tb], [t_ao])
        for oc in range(8):
            wo, two = WOo.next()
            P.dma("pool", wo, D["w_out_c"][:, oc * 128:(oc + 1) * 128].rearrange("(h p) c -> p h c", p=128), W=[two])
            bank, tb = gb()
            for h in range(8):
                mm(bank[:, :T], wo[:, h, :], aoS[:, h, :], [two, t_ao], [tb], st=(h == 0), sp=(h == 7))
            tt(xT[:, oc, cfg.x0:cfg.x0 + T], xT[:, oc, cfg.x0:cfg.x0 + T], bank[:, :T], ADD, [t_x[cfg.xt0], tb], [t_x[cfg.xt0]])

    def phase_final(cfg, y_out):
        m = A.mark()
        n = cfg.XB
        gfb = A.f32(1024)
        t_g = load(gfb, D["nrm"][4:5, :].to_broadcast([128, 1024]))
        ybuf = Rot([A.f32(1024) for _ in range(2)])
        st, t_st = A.f32(4), Tok()
        junk, t_junk = A.bf16(512), Tok()
        gen["banks"] = [0, 1, 2, 3, 4, 5, 6, 7]
        for blk in range(cfg.T // n):
            cs_ = slice(blk * n, (blk + 1) * n)
            xs_ = slice(cfg.x0 + blk * n, cfg.x0 + (blk + 1) * n)
            tti = cfg.xt0 + (blk * n) // cfg.TT
            banks = []
            for half in range(2):
                bank, tb = gb()
                for c4 in range(4):
                    tr(bank[:n, c4 * 128:(c4 + 1) * 128], xT[:, half * 4 + c4, xs_], ident_f, [t_x[tti], t_idf], [tb])
                act(junk[:n, :], bank[:n, :], AF.Square, [tb, t_st], [t_junk, t_st], accum=st[:n, half:half + 1])
                banks.append((bank, tb))
            tt(st[:n, 2:3], st[:n, 0:1], st[:n, 1:2], ADD, [t_st], [t_st])
            rstd_small(st[:n, 2:3], st[:n, 2:3], 1.0 / 1024, [t_st], [t_st])
            yb, tyb = ybuf.next()
            for half in range(2):
                stt(yb[:n, half * 512:(half + 1) * 512], banks[half][0][:n, :], st[:n, 2:3],
                    gfb[:n, half * 512:(half + 1) * 512], MUL, MUL, [banks[half][1], t_st, t_g], [tyb])
            P.dma("sp", y_out[cs_, :], yb[:n, :], R=[tyb])
        P.barrier()
        A.reset(m)

    pc, sc = make_cfgs()
    MERGE = STAGE >= 99 and (NSEQ_P == 2 or os.environ.get("MK_MERGE") == "1")
    for seq in range(NSEQ_P):
        merged = MERGE and seq == NSEQ_P - 1
        load_x(pc, D["xp"][seq])
        if merged:
            load_x(sc, D["xs"])
        if STAGE >= 1:
            try:
                phase_mixer_ab(pc, seq, None, None, O["glap"], O["retp"])
            except _Stop:
                P.barrier()
                A.reset(base_mark)
            if merged:
                phase_mixer_ab(sc, 0, D["sgla"], D["sret"], O["glas"], O["rets"])
        if STAGE >= 2:
            grp = [(pc, None, O["convp"][0, seq:seq + 1])]
            if merged:
                grp.append((sc, D["sconv"][0:8, :], O["convs"][0]))
            phase_ffn(grp, 0)
        if STAGE >= 3:
            phase_mla(pc, seq, O["ckvp"][seq], O["krp"][seq])
            if merged:
                phase_mla(sc, 0, O["ckvs"], O["krs"], D["cckv"], D["ckr"])
        if STAGE >= 4:
            grp = [(pc, None, O["convp"][1, seq:seq + 1])]
            if merged:
                grp.append((sc, D["sconv"][8:16, :], O["convs"][1]))
            phase_ffn(grp, 1)
        phase_final(pc, O["yp"][seq])
        if merged:
            phase_final(sc, O["ys"])
    if STAGE >= 5 and not MERGE:
        load_x(sc, D["xs"])
        phase_mixer_ab(sc, 0, D["sgla"], D["sret"], O["glas"], O["rets"])
        phase_ffn([(sc, D["sconv"][0:8, :], O["convs"][0])], 0)
        phase_mla(sc, 0, O["ckvs"], O["krs"], D["cckv"], D["ckr"])
        phase_ffn([(sc, D["sconv"][8:16, :], O["convs"][1])], 1)
        phase_final(sc, O["ys"])
    P.emit()
    P.close()
    print("arena peak words", A.peak, "instrs", {k: len(v) for k, v in P.streams.items()}, "signals", P.sigcount)
    return nc


_CACHE = {}


def kernel(x_prompt, x_sample, state_gla, state_ret, cache_ckv, cache_krope, state_conv,
           norm_mix, norm_ffn, norm_final, w_in_ab, w_gate_up, b_gate, g_gla, g_ret, w_out_ab,
           w_in_c, g_q, g_kv, w_uq, w_uk, w_uv, w_out_c, w_ffn_in, w_dwconv, b_dwconv, w_ffn_out):
    f = lambda a: np.ascontiguousarray(np.asarray(a, dtype=np.float32))
    ncores = int(os.environ.get('MK_CORES', '8'))
    if "nc" not in _CACHE:
        _CACHE["nc"] = build_program()
        _CACHE["tabs"] = const_tables()
    nc = _CACHE["nc"]
    tabs = _CACHE["tabs"]
    w_in_ab0 = f(w_in_ab)[0]
    sw = w_in_ab0[:, 1552:2064].reshape(1024, 8, 2, 32)[:, :, ::-1, :].reshape(1024, 512)
    wuq = f(w_uq)[0].reshape(384, 8, 192)
    wuk0 = f(w_uk)[0]
    shared = dict(
        nrm=np.stack([f(norm_mix)[0], f(norm_mix)[1], f(norm_ffn)[0], f(norm_ffn)[1], f(norm_final)]),
        w_in_ab=w_in_ab0, w_ab_sw=f(sw), wgu=f(w_gate_up)[0], bgate=f(b_gate)[0][None, :],
        ggla=f(g_gla)[0][None, :], gret=f(g_ret)[0][None, :], w_out_ab=f(w_out_ab)[0], w_in_c=f(w_in_c)[0],
        gq=f(g_q)[0][None, :], gkv=f(g_kv)[0][None, :],
        wuq_n=f(wuq[:, :, :128].reshape(384, 1024)), wuq_r=f(wuq[:, :, 128:].reshape(384, 512)),
        wuq_rs=f(wuq[:, :, 128:].reshape(384, 8, 2, 32)[:, :, ::-1, :].reshape(384, 512)),
        wuk=f(wuk0.reshape(512, 1024)), wukT=f(wuk0.transpose(2, 1, 0)), wuv=f(w_uv)[0].reshape(512, 1024),
        w_out_c=f(w_out_c)[0], wffi=f(w_ffn_in),
        dwc=f(np.concatenate([f(w_dwconv).reshape(6, 2816), f(b_dwconv)], axis=0)), wffo=f(w_ffn_out),
    )
    shared.update(tabs)
    xpv, xsv = f(x_prompt), f(x_sample)
    sg, sr = f(state_gla)[0], f(state_ret)[0]
    cc, ck, scv = f(cache_ckv)[0], f(cache_krope)[0], f(state_conv)
    in_maps = []
    for c in range(ncores):
        d = dict(shared)
        d["xp"] = xpv[2 * c:2 * c + 2]
        d["xs"] = xsv[4 * c:4 * c + 4].reshape(64, 1024)
        d["sgla"] = sg[4 * c:4 * c + 4].reshape(4, 256, 128)
        d["sret"] = sr[4 * c:4 * c + 4].reshape(4, 256, 128)
        d["cckv"] = cc[4 * c:4 * c + 4]
        d["ckr"] = ck[4 * c:4 * c + 4]
        d["sconv"] = f(scv[:, 4 * c:4 * c + 4].reshape(16, 2816))
        in_maps.append({k: np.ascontiguousarray(v) for k, v in d.items()})
    res = run_bass_kernel_spmd(nc, in_maps, core_ids=list(range(ncores)))
    R = res.results
    cat = lambda k, shp: np.concatenate([R[c][k].reshape(shp) for c in range(ncores)], axis=0)
    y_prompt = cat("yp", (2, 2048, 1024))
    y_sample = cat("ys", (4, 16, 1024))
    gla_p = cat("glap", (2, 4, 64, 128))[None]
    gla_s = cat("glas", (4, 4, 64, 128))[None]
    ret_p = cat("retp", (2, 4, 64, 128))[None]
    ret_s = cat("rets", (4, 4, 64, 128))[None]
    ckv_p = cat("ckvp", (2, 2048, 512))[None]
    ckv_s = cat("ckvs", (4, 16, 512))[None]
    kr_p = cat("krp", (2, 2048, 64))[None]
    kr_s = cat("krs", (4, 16, 64))[None]
    conv_p = np.concatenate([R[c]["convp"] for c in range(ncores)], axis=1)
    conv_s = np.concatenate([R[c]["convs"] for c in range(ncores)], axis=1)
    outs = (y_prompt, y_sample, gla_p, gla_s, ret_p, ret_s, ckv_p, ckv_s, kr_p, kr_s, conv_p, conv_s)
    return tuple(np.ascontiguousarray(o, dtype=np.float32) for o in outs)
```

```python
import contextlib
import math
import os

import numpy as np
import concourse.bass as bass
import concourse.mybir as mybir
from concourse.bass_utils import run_bass_kernel_spmd

F32 = mybir.dt.float32
BF16 = mybir.dt.bfloat16
AF = mybir.ActivationFunctionType
ALU = mybir.AluOpType
AX = mybir.AxisListType

EPS = 1e-6
D_FF = 2816
NJ = 22
STAGE = int(os.environ.get("MK_STAGE", "99"))
NSEQ_P = int(os.environ.get("MK_NSEQ", "2"))
STOPAT = int(os.environ.get("MK_STOP", "0"))


class _Stop(Exception):
    pass


def ck(k):
    if STOPAT == k:
        raise _Stop()

COMPUTE = ("pe", "act", "dve", "pool")
DMA_K = 8
EPOCH = 6000


class Tok:
    __slots__ = ("w", "rs", "excl")

    def __init__(self, excl=False):
        self.w = None
        self.rs = []
        self.excl = excl


class Ins:
    __slots__ = ("eng", "fn", "deps", "dma", "signal", "ev", "prev_ev")

    def __init__(self, eng, fn, dma):
        self.eng = eng
        self.fn = fn
        self.dma = dma
        self.deps = []
        self.signal = False
        self.ev = None
        self.prev_ev = None


class Prog:
    def __init__(self, nc):
        self.nc = nc
        self.es = contextlib.ExitStack()
        self.streams = {e: [] for e in ("pe", "act", "dve", "pool", "sp")}
        self.pending = {e: [] for e in self.streams}
        self.dmas = []
        self.n = 0
        self.nobar = False

    def sb(self, shape, dt, name="t"):
        self.n += 1
        return self.es.enter_context(self.nc.sbuf_tensor(f"{name}{self.n}", list(shape), dt))

    def ps(self, shape, dt, name="p"):
        self.n += 1
        return self.es.enter_context(self.nc.psum_tensor(f"{name}{self.n}", list(shape), dt))

    def op(self, eng, fn, R=(), W=(), dma=False, force=()):
        ins = Ins(eng, fn, dma)
        deps = {id(d): d for d in force}

        def same(d):
            return (not d.dma) and (not dma) and d.eng == eng

        def readers(t):
            seen = set()
            for r in reversed(t.rs):
                if r.dma:
                    yield r
                elif r.eng not in seen:
                    seen.add(r.eng)
                    if not (r.eng == eng and eng == "pe" and not dma):
                        yield r

        for t in R:
            d = t.w
            if d is not None and not (same(d) and eng == "pe"):
                deps[id(d)] = d
            if t.excl:
                for r in readers(t):
                    if not same(r):
                        deps[id(r)] = r
        for t in W:
            d = t.w
            if d is not None and not (same(d) and eng == "pe"):
                deps[id(d)] = d
            for r in readers(t):
                deps[id(r)] = r
        for t in R:
            t.rs.append(ins)
        for t in W:
            t.w = ins
            t.rs = []
        if self.pending[eng] and not self.nobar:
            for d in self.pending[eng]:
                deps[id(d)] = d
            self.pending[eng] = []
        ins.deps = list(deps.values())
        for d in ins.deps:
            d.signal = True
        if dma:
            ins.signal = True
            self.dmas.append(ins)
        self.streams[eng].append(ins)
        return ins

    def dma(self, q, out, in_, R=(), W=(), **kw):
        return self.op(q, lambda e: e.dma_start(out=out, in_=in_, **kw), R=R, W=W, dma=True)

    def barrier(self):
        deps = [st[-1] for st in self.streams.values() if st] + self.dmas
        for d in deps:
            d.signal = True
        for e in self.pending:
            self.pending[e] = self.pending[e] + deps
        self.dmas = []

    def emit(self):
        nc = self.nc
        es = self.es
        sem_dma = {q: [es.enter_context(nc.semaphore(f"semd_{q}{i}")) for i in range(DMA_K)]
                   for q in ("sp", "act", "pool")}
        for e, st in self.streams.items():
            cnt = 0
            nd = 0
            sem = None
            for ins in st:
                if ins.dma:
                    j = nd % DMA_K
                    rnd = nd // DMA_K
                    ins.ev = (sem_dma[e][j], 16 * (rnd + 1))
                    ins.prev_ev = (sem_dma[e][j], 16 * rnd) if rnd > 0 else None
                    nd += 1
                elif ins.signal:
                    if cnt % EPOCH == 0:
                        sem = es.enter_context(nc.semaphore(f"sem_{e}{cnt // EPOCH}"))
                    cnt += 1
                    ins.ev = (sem, (cnt - 1) % EPOCH + 1)
        self.sigcount = {e: sum(1 for i in st if (not i.dma) and i.signal) for e, st in self.streams.items()}
        final_dma = []
        for q in ("sp", "act", "pool"):
            last = {}
            for ins in self.streams[q]:
                if ins.dma:
                    last[id(ins.ev[0])] = ins.ev
            final_dma += list(last.values())

        def run(engobj, st, is_sp):
            waited = {}

            def wait(ev):
                sem, val = ev
                k = id(sem)
                if waited.get(k, 0) < val:
                    engobj.wait_ge(sem, val)
                    waited[k] = val

            for ins in st:
                for d in ins.deps:
                    wait(d.ev)
                if ins.dma and ins.prev_ev is not None:
                    wait(ins.prev_ev)
                bi = ins.fn(engobj)
                if ins.dma:
                    bi.then_inc(ins.ev[0], 16)
                elif ins.signal:
                    bi.then_inc(ins.ev[0], 1)
            if is_sp:
                for ev in final_dma:
                    wait(ev)

        block = es.enter_context(nc.Block())
        S = self.streams

        @block.tensor
        def _(e):
            run(e, S["pe"], False)

        @block.scalar
        def _(e):
            run(e, S["act"], False)

        @block.vector
        def _(e):
            run(e, S["dve"], False)

        @block.gpsimd
        def _(e):
            run(e, S["pool"], False)

        @block.sync
        def _(e):
            run(e, S["sp"], True)

    def close(self):
        self.es.close()


class Arena:
    def __init__(self, P, nwords):
        self.t = P.sb([128, nwords], F32, "arena")
        self.n = nwords
        self.off = 0
        self.peak = 0

    def _take(self, words):
        o = self.off
        self.off += words
        self.peak = max(self.peak, self.off)
        assert self.off <= self.n, f"arena overflow {self.off} > {self.n}"
        return o

    def f32(self, *shape):
        n = int(np.prod(shape))
        o = self._take(n)
        ap = self.t[:, o:o + n]
        return self._shape(ap, shape)

    def bf16(self, *shape):
        n = int(np.prod(shape))
        w = (n + 1) // 2
        o = self._take(w)
        ap = self.t[:, o:o + w].bitcast(BF16)
        if 2 * w != n:
            ap = ap[:, 0:n]
        return self._shape(ap, shape)

    @staticmethod
    def _shape(ap, shape):
        if len(shape) == 1:
            return ap
        if len(shape) == 2:
            return ap.rearrange("p (a b) -> p a b", b=shape[1])
        if len(shape) == 3:
            return ap.rearrange("p (a b c) -> p a b c", b=shape[1], c=shape[2])
        raise ValueError(shape)

    def mark(self):
        return self.off

    def reset(self, m):
        self.off = m


class Rot:
    def __init__(self, items):
        self.items = [(it, Tok()) for it in items]
        self.i = 0

    def next(self):
        r = self.items[self.i % len(self.items)]
        self.i += 1
        return r


class Cfg:
    pass


def make_cfgs():
    p = Cfg()
    p.T, p.TT, p.NT, p.C, p.NCH, p.nseq, p.L, p.tab0, p.XB, p.krblk0 = 2048, 512, 4, 128, 4, 1, 512, 0, 128, 0
    p.prompt = True
    p.x0, p.xt0 = 0, 0
    s = Cfg()
    s.T, s.TT, s.NT, s.C, s.NCH, s.nseq, s.L, s.tab0, s.XB, s.krblk0 = 64, 64, 1, 16, 4, 4, 16, 2048, 64, 16
    s.prompt = False
    s.x0, s.xt0 = 2048, 4
    return p, s


def const_tables():
    half = 32
    freqs = 10000.0 ** (-np.arange(half, dtype=np.float64) / half)
    pos = np.concatenate([np.arange(2048), 4096 + (np.arange(64) % 16)]).astype(np.float64)
    ang = pos[None, :] * freqs[:, None]
    cos = np.cos(ang)
    sin = np.sin(ang)
    p = np.arange(128)
    d = p % 64
    cosF = cos[d % 32, :].astype(np.float32)
    sinF = np.where((d < 32)[:, None], -sin[d % 32, :], sin[d % 32, :]).astype(np.float32)
    kc = np.zeros((128, 17, 64), np.float32)
    ks = np.zeros((128, 17, 64), np.float32)
    for blk in range(17):
        if blk < 16:
            pp = (blk * 128 + p).astype(np.float64)
        else:
            pp = (4096 + (p % 16)).astype(np.float64)
        a = pp[:, None] * freqs[None, :]
        c, s = np.cos(a), np.sin(a)
        kc[:, blk, :32] = c
        kc[:, blk, 32:] = c
        ks[:, blk, :32] = -s
        ks[:, blk, 32:] = s
    def ret_tabs(C, TT):
        nref = (C - 1) // 2 + 1
        E = np.zeros((128, 2, 2, TT), np.float32)
        cst = np.zeros((128, 2, 3), np.float32)
        i = np.arange(TT) % C
        for kcx in range(2):
            h = 2 * kcx + p // 64
            lg = np.log1p(-np.exp2(-5.0 - h.astype(np.float64)))
            E[:, kcx, 0, :] = np.exp((i[None, :] + 1 - nref) * lg[:, None])
            E[:, kcx, 1, :] = np.exp((nref - i[None, :] - 1) * lg[:, None]) * (64 ** -0.5)
            cst[:, kcx, 0] = np.exp(nref * lg)
            cst[:, kcx, 1] = np.exp((C - nref) * lg)
            cst[:, kcx, 2] = np.exp(C * lg)
        return E, cst
    Ep, cp = ret_tabs(128, 512)
    Es, cs = ret_tabs(16, 64)
    mask = (np.arange(128)[:, None] <= np.arange(128)[None, :]).astype(np.float32)
    mask4 = np.tile(mask, (1, 4))
    resetp = np.ones((128, 512), np.float32)
    resetp[:, ::128] = 0.0
    resets = np.ones((128, 64), np.float32)
    resets[:, ::16] = 0.0
    ident = np.eye(128, dtype=np.float32)
    return dict(cosF=cosF, sinF=sinF, krc=kc, krs_=ks, retEp=Ep, retcp=cp, retEs=Es, retcs=cs,
                mask4=mask4, resetp=resetp, resets=resets, ident=ident)


_IN_SHAPES = dict(
    xp=[2, 2048, 1024], xs=[64, 1024], sgla=[4, 256, 128], sret=[4, 256, 128],
    cckv=[4, 4096, 512], ckr=[4, 4096, 64], sconv=[16, 2816], nrm=[5, 1024],
    w_in_ab=[1024, 3088], w_ab_sw=[1024, 512], wgu=[16, 256], bgate=[1, 256], ggla=[1, 512],
    gret=[1, 512], w_out_ab=[1024, 1024], w_in_c=[1024, 960], gq=[1, 384], gkv=[1, 512],
    wuq_n=[384, 1024], wuq_r=[384, 512], wuq_rs=[384, 512], wuk=[512, 1024], wukT=[128, 8, 512],
    wuv=[512, 1024], w_out_c=[1024, 1024], wffi=[2, 1024, 5632], dwc=[8, 2816],
    wffo=[2, 2816, 1024],
    cosF=[128, 2112], sinF=[128, 2112], krc=[128, 17, 64], krs_=[128, 17, 64],
    retEp=[128, 2, 2, 512], retcp=[128, 2, 3], retEs=[128, 2, 2, 64], retcs=[128, 2, 3],
    mask4=[128, 512], resetp=[128, 512], resets=[128, 64], ident=[128, 128],
)
_OUT_SHAPES = dict(
    yp=[2, 2048, 1024], ys=[64, 1024], glap=[2, 256, 128], glas=[4, 256, 128],
    retp=[2, 256, 128], rets=[4, 256, 128], ckvp=[2, 2048, 512], ckvs=[64, 512],
    krp=[2, 2048, 64], krs=[64, 64], convp=[2, 2, 2, 2816], convs=[2, 4, 2, 2816],
)


def build_program():
    nc = bass.Bass("TRN2", target_bir_lowering=False)
    D = {k: nc.dram_tensor(k, list(v), F32, kind="ExternalInput").ap() for k, v in _IN_SHAPES.items()}
    O = {k: nc.dram_tensor(k, list(v), F32, kind="ExternalOutput").ap() for k, v in _OUT_SHAPES.items()}
    P = Prog(nc)
    A = Arena(P, 52900)
    PB = [P.ps([128, 512], F32, "bank") for _ in range(8)]
    TPB = [Tok(excl=True) for _ in range(8)]
    gen = {"banks": [0, 1, 2, 3, 4, 5], "i": 0}

    def gb():
        b = gen["banks"][gen["i"] % len(gen["banks"])]
        gen["i"] += 1
        return PB[b], TPB[b]

    def bfv(bank):
        return bank[:].bitcast(BF16)

    def mm(out, lhsT, rhs, R, W, st=True, sp=True, force=()):
        return P.op("pe", lambda e: e.matmul(out, lhsT, rhs, start=st, stop=sp), R, W, force=force)

    def tr(out, in_, idn, R, W):
        P.op("pe", lambda e: e.transpose(out, in_, idn), R, W)

    def act(out, in_, func, R, W, bias=None, scale=None, accum=None):
        kw = {}
        if bias is not None:
            kw["bias"] = bias
        if scale is not None:
            kw["scale"] = scale
        if accum is not None:
            kw["accum_out"] = accum
        P.op("act", lambda e: e.activation(out=out, in_=in_, func=func, **kw), R, W)

    def tt(out, a, b, op, R, W, eng="dve"):
        P.op(eng, lambda e: e.tensor_tensor(out=out, in0=a, in1=b, op=op), R, W)

    def ts(out, a, s1, op0, R, W, s2=None, op1=None, eng="dve"):
        if op1 is None:
            P.op(eng, lambda e: e.tensor_scalar(out, a, s1, None, op0=op0), R, W)
        else:
            P.op(eng, lambda e: e.tensor_scalar(out, a, s1, s2, op0=op0, op1=op1), R, W)

    def stt(out, in0, scalar, in1, op0, op1, R, W, eng="dve"):
        P.op(eng, lambda e: e.scalar_tensor_tensor(out=out, in0=in0, scalar=scalar, in1=in1,
                                                    op0=op0, op1=op1), R, W)

    def cp(out, in_, R, W, eng="dve"):
        if eng == "act":
            act(out, in_, AF.Copy, R, W)
        else:
            P.op(eng, lambda e: e.tensor_copy(out=out, in_=in_), R, W)

    def memset(ap, val, W, eng="dve"):
        P.op(eng, lambda e: e.memset(ap, val), (), W)

    def load(dst, src, q="sp"):
        t = Tok()
        P.dma(q, dst, src, W=[t])
        return t

    MUL, ADD, SUB = ALU.mult, ALU.add, ALU.subtract

    xT = A.f32(8, 2048 + 64)
    t_x = [Tok() for _ in range(5)]
    ident_f = A.f32(128)
    ident_b = A.bf16(128)
    ones_b = A.bf16(128)
    gn = A.f32(8, 5)
    dw = A.f32(NJ, 8)
    gqc = A.f32(3)
    nbg = A.f32(2)
    t_idf = load(ident_f, D["ident"])
    t_idb = load(ident_b, D["ident"], "pool")
    t_one = Tok()
    memset(ones_b, 1.0, [t_one])
    t_const = [t_idf, t_idb, t_one]
    PSQ = Rot([A.bf16(512) for _ in range(2)])
    PRS, t_PRS = A.f32(512), Tok()
    PHT, t_PHT = A.bf16(8, 512), Tok()
    PWA, t_PWA, t_PWA2 = A.bf16(8, 512), Tok(), Tok()
    base_mark = A.mark()

    def rows_to_cols(src_rows, R_, n, dst, t_dst):
        m = A.mark()
        stage = A.f32(n * 128)
        t_s = load(stage[:R_, :], src_rows)
        bank, tb = gb()
        for c in range(n):
            tr(bank[:, c * R_:(c + 1) * R_], stage[:R_, c * 128:(c + 1) * 128], ident_f[:R_, :R_],
               [t_s, t_idf], [tb])
        if len(dst.shape) == 3:
            cp(dst, bank[:, 0:n * R_].rearrange("p (a b) -> p a b", b=R_), [tb], [t_dst])
        else:
            cp(dst, bank[:, 0:n * R_], [tb], [t_dst])
        P.barrier()
        A.reset(m)

    t_gn, t_dw, t_gq, t_nbg = Tok(), Tok(), Tok(), Tok()
    rows_to_cols(D["nrm"], 5, 8, gn, t_gn)
    rows_to_cols(D["dwc"], 8, NJ, dw, t_dw)
    rows_to_cols(D["gq"], 1, 3, gqc, t_gq)
    rows_to_cols(D["bgate"], 1, 2, nbg, t_nbg)
    ts(nbg, nbg, -1.0, MUL, [t_nbg], [t_nbg])

    def load_x(cfg, xd):
        m = A.mark()
        xin = Rot([A.f32(1024) for _ in range(2)])
        n = cfg.XB
        for blk in range(cfg.T // n):
            xi, txi = xin.next()
            P.dma("sp", xi[:n, :], xd[blk * n:(blk + 1) * n, :], W=[txi])
            ttile = (blk * n) // cfg.TT
            for half in range(2):
                bank, tb = gb()
                for c4 in range(4):
                    c = half * 4 + c4
                    tr(bank[:, c4 * n:(c4 + 1) * n], xi[:n, c * 128:(c + 1) * 128], ident_f[:n, :n],
                       [txi, t_idf], [tb])
                src = bank[:, 0:4 * n].rearrange("p (a b) -> p a b", b=n)
                dst = xT[:, half * 4:(half + 1) * 4, cfg.x0 + blk * n:cfg.x0 + (blk + 1) * n]
                cp(dst, src, [tb], [t_x[cfg.xt0 + ttile]], eng=("act" if half == 0 else "dve"))
        P.barrier()
        A.reset(m)

    def rmsnorm(cfg, tti, grow, hdst, t_h, sqrot, _unused, rs, t_rs):
        n = cfg.TT
        cols = slice(cfg.x0 + tti * n, cfg.x0 + (tti + 1) * n)
        t_xt = t_x[cfg.xt0 + tti]
        bank, tb = gb()
        for c in range(8):
            sqb, t_sq = sqrot.next()
            act(sqb[:, :n], xT[:, c, cols], AF.Square, [t_xt], [t_sq])
            mm(bank[:, :n], ones_b, sqb[:, :n], [t_sq, t_one], [tb], st=(c == 0), sp=(c == 7))
        act(rs[:, :n], bank[:, :n], AF.Ln, [tb], [t_rs], scale=1.0 / 1024, bias=EPS)
        act(rs[:, :n], rs[:, :n], AF.Exp, [t_rs], [t_rs], scale=-0.5)
        for c in range(8):
            stt(hdst[:, c, :n], xT[:, c, cols], gn[:, c, grow:grow + 1], rs[:, :n], MUL, MUL,
                [t_xt, t_rs, t_gn], [t_h])

    def rstd_small(dst, src, inv_n, R, W):
        act(dst, src, AF.Ln, R, W, scale=inv_n, bias=EPS)
        act(dst, dst, AF.Exp, W, W, scale=-0.5)

    def phase_mixer_ab(cfg, seq, s0_gla, s0_ret, out_gla, out_ret):
        m = A.mark()
        n, C, NCH = cfg.TT, cfg.C, cfg.NCH
        refi = (C - 1) // 2
        P.nobar = True
        rmsnorm(cfg, 0, 0, PHT, t_PHT, PSQ, None, PRS, t_PRS)
        P.dma("pool", PWA, D["w_in_ab"][:, 512:1024].rearrange("(kc p) c -> p kc c", p=128), W=[t_PWA, t_PWA2])
        P.nobar = False
        retE = A.f32(2, 2, n)
        retc = A.f32(2, 3)
        mask4 = A.f32(512)
        reset = A.f32(n)
        gglab = A.f32(512)
        gretb = A.f32(512)
        wgu = A.f32(256)
        Wlo = A.bf16(8, 16)
        t_tab = [load(retE, D["retEp"] if cfg.prompt else D["retEs"]),
                 load(retc, D["retcp"] if cfg.prompt else D["retcs"]),
                 load(mask4, D["mask4"]),
                 load(reset, D["resetp"] if cfg.prompt else D["resets"]),
                 load(gglab, D["ggla"].to_broadcast([128, 512])),
                 load(gretb, D["gret"].to_broadcast([128, 512])),
                 load(wgu[:16, :], D["wgu"]),
                 load(Wlo, D["w_in_ab"][:, 1536:1552].rearrange("(kc p) c -> p kc c", p=128), "pool")]
        cosr = Rot([A.f32(n) for _ in range(1)])
        sinr = Rot([A.f32(n) for _ in range(1)])
        sq, t_sq = PSQ, None
        rs, t_rs = PRS, t_PRS
        hT2, t_h2 = [PHT, A.bf16(8, n)], [t_PHT, Tok()]
        WA = Rot([A.bf16(8, 512) for _ in range(2 if cfg.prompt else 5)])
        WA.items.insert(0, (PWA, t_PWA))
        vtok = [A.bf16(NCH, 512), A.bf16(NCH, 512)]
        t_v = [[Tok() for _ in range(NCH)] for _ in range(2)]
        gs = [A.bf16(NCH, 512), A.bf16(NCH, 512)]
        t_gs = [[Tok() for _ in range(NCH)] for _ in range(2)]
        srt = Rot([A.f32(512) for _ in range(1)])
        loT, t_lo = A.f32(n), Tok()
        ebuf, t_e = A.f32(n), Tok()
        spb, t_sp = A.f32(2, n), Tok()
        gate_region = (loT, ebuf, spb)
        cum, t_cum = A.f32(2, n), Tok()
        E1, E2 = A.f32(2, n), A.f32(2, n)
        t_E = [Tok(), Tok()]
        pbias, nbias, dd = A.f32(2, NCH), A.f32(2, NCH), A.f32(2, NCH)
        eref, elr, dec = A.f32(2, NCH), A.f32(2, NCH), A.f32(2, NCH)
        t_col = Tok()
        qrel, krel = A.bf16(4, n), A.bf16(4, n)
        t_q = [Tok() for _ in range(4)]
        t_k = [Tok() for _ in range(4)]
        r1, r2, r3 = ebuf, spb[:, 0, :], spb[:, 1, :]
        t_r = [t_e, t_sp, t_sp]
        kreltok, t_kt = A.bf16(4, 128), Tok()
        Sm2, t_sm2 = [A.bf16(8, 128), A.bf16(8, 128)], [[Tok(), Tok()], [Tok(), Tok()]]
        S, t_S = A.f32(4, 128), [Tok() for _ in range(4)]
        Sp2, t_Sp2 = [A.bf16(4, 128), A.bf16(4, 128)], [[Tok() for _ in range(4)] for _ in range(2)]
        tmpkv, t_tmp = A.f32(4, 128), [Tok() for _ in range(4)]
        mix2, t_mix2 = [A.bf16(1024), A.bf16(1024)], [Tok(), Tok()]
        tmpo, t_tmpo = A.f32(512), Tok()
        mixT, t_mixT = A.bf16(8, n), Tok()
        gen["banks"] = [0, 1, 2, 3]
        obank2 = [[(PB[6], TPB[6]), (PB[7], TPB[7])], [(PB[4], TPB[4]), (PB[5], TPB[5])]]
        stat2, t_ss2, t_st2 = [A.f32(16), A.f32(16)], [[Tok() for _ in range(12)] for _ in range(2)], [Tok(), Tok()]
        junk8 = A.bf16(8, 128)

        def wpiece(src):
            w, tw = WA.next()
            ncol = src.shape[1]
            P.dma("pool", w[:, :, :ncol], src.rearrange("(kc p) c -> p kc c", p=128),
                  W=([tw, t_PWA2] if tw is t_PWA else [tw]))
            return w, tw

        if cfg.prompt:
            for u in range(4):
                memset(S[:, u, :], 0.0, [t_S[u]])

        for tti in range(cfg.NT):
            cols = slice(tti * n, (tti + 1) * n)
            tcol = slice(cfg.tab0 + tti * n, cfg.tab0 + (tti + 1) * n)
            cos_t, t_cos = cosr.next()
            sin_t, t_sin = sinr.next()
            P.dma("sp", cos_t, D["cosF"][:, tcol], W=[t_cos])
            P.dma("sp", sin_t, D["sinF"][:, tcol], W=[t_sin])
            hT, t_h = hT2[tti % 2], t_h2[tti % 2]
            ck(1)
            bank, tb = gb()
            for kc in range(8):
                mm(bank[:16, :n], Wlo[:, kc, :], hT[:, kc, :n], [t_h, t_tab[7]], [tb], st=(kc == 0), sp=(kc == 7))
            cp(loT[:16, :], bank[:16, :n], [tb], [t_lo], eng="act")
            for kc in range(2):
                bank, tb = gb()
                mm(bank[:, :n], wgu[:16, kc * 128:(kc + 1) * 128], loT[:16, :], [t_lo, t_tab[6]], [tb])
                act(ebuf, bank[:, :n], AF.Exp, [tb, t_nbg], [t_e], scale=-1.0, bias=nbg[:, kc:kc + 1])
                act(spb[:, kc, :], ebuf, AF.Ln, [t_e], [t_sp], bias=1.0)
                P.op("dve", lambda e, kc=kc: e.tensor_tensor_scan(out=cum[:, kc, :], data0=reset, data1=spb[:, kc, :],
                                                                   initial=0.0, op0=MUL, op1=ADD),
                     [t_sp, t_tab[3]], [t_cum])
            ck(3)
            for pi, c0 in enumerate((512, 1024, 2064, 2576)):
                if tti == 0 and pi == 0:
                    w, tw = WA.next()
                else:
                    w, tw = wpiece(D["w_in_ab"][:, c0:c0 + 512])
                grp = pi // 2
                for ci in range(NCH):
                    bank, tb = gb()
                    for kc in range(8):
                        mm(bank[:C, :], hT[:, kc, ci * C:(ci + 1) * C], w[:, kc, :], [t_h, tw], [tb],
                           st=(kc == 0), sp=(kc == 7))
                    if pi % 2 == 0:
                        cp(vtok[grp][:C, ci, :], bank[:C, :], [tb], [t_v[grp][ci]], eng="act")
                    else:
                        sr, tsr = srt.next()
                        act(sr[:C, :], bank[:C, :], AF.Silu, [tb], [tsr])
                        tt(gs[grp][:C, ci, :], sr[:C, :], (gglab if grp == 0 else gretb)[:C, :], MUL,
                           [tsr, t_tab[4 + grp]], [t_gs[grp][ci]])
            ck(2)
            cview = cum.rearrange("p k (c i) -> p k c i", i=C)
            cref = cview[:, :, :, refi]
            clast = cview[:, :, :, C - 1]
            ts(pbias, cref, 1.0 / 16, MUL, [t_cum], [t_col])
            ts(nbias, cref, -1.0 / 16, MUL, [t_cum], [t_col])
            tt(dd, clast, cref, SUB, [t_cum], [t_col])
            act(eref, cref, AF.Exp, [t_cum], [t_col], scale=-1.0 / 16)
            act(elr, dd, AF.Exp, [t_col], [t_col], scale=-1.0 / 16)
            act(dec, clast, AF.Exp, [t_cum], [t_col], scale=-1.0 / 16)
            for kc in range(2):
                for ci in range(NCH):
                    cs_ = slice(ci * C, (ci + 1) * C)
                    act(E1[:, kc, cs_], cum[:, kc, cs_], AF.Exp, [t_cum, t_col], [t_E[0]],
                        scale=-1.0 / 16, bias=pbias[:, kc, ci:ci + 1])
                    act(E2[:, kc, cs_], cum[:, kc, cs_], AF.Exp, [t_cum, t_col], [t_E[1]],
                        scale=1.0 / 16, bias=nbias[:, kc, ci:ci + 1])
            ck(4)
            w, tw = wpiece(D["w_in_ab"][:, 0:512])
            for j in range(4):
                bank, tb = gb()
                for kc in range(8):
                    mm(bank[:, :n], w[:, kc, j * 128:(j + 1) * 128], hT[:, kc, :n], [t_h, tw], [tb],
                       st=(kc == 0), sp=(kc == 7))
                kcx = j % 2
                if j < 2:
                    stt(qrel[:, kcx, :], bank[:, :n], 0.125, E1[:, kcx, :], MUL, MUL, [tb, t_E[0]], [t_q[kcx]])
                else:
                    tt(krel[:, kcx, :], bank[:, :n], E2[:, kcx, :], MUL, [tb, t_E[1]], [t_k[kcx]])
            w1, tw1 = wpiece(D["w_in_ab"][:, 1552:2064])
            w2, tw2 = wpiece(D["w_ab_sw"])
            for j in range(4):
                bank1, tb1 = gb()
                for kc in range(8):
                    mm(bank1[:, :n], w1[:, kc, j * 128:(j + 1) * 128], hT[:, kc, :n], [t_h, tw1], [tb1],
                       st=(kc == 0), sp=(kc == 7))
                bank2, tb2 = gb()
                for kc in range(8):
                    mm(bank2[:, :n], w2[:, kc, j * 128:(j + 1) * 128], hT[:, kc, :n], [t_h, tw2], [tb2],
                       st=(kc == 0), sp=(kc == 7))
                kcx = j % 2
                tt(r1, bank1[:, :n], cos_t, MUL, [tb1, t_cos], [t_r[0]])
                tt(r2, bank2[:, :n], sin_t, MUL, [tb2, t_sin], [t_r[1]])
                tt(r3, r1, r2, ADD, [t_r[0], t_r[1]], [t_r[2]])
                if j < 2:
                    tt(qrel[:, 2 + kcx, :], r3, retE[:, kcx, 0, :], MUL, [t_r[2], t_tab[0]], [t_q[2 + kcx]])
                else:
                    tt(krel[:, 2 + kcx, :], r3, retE[:, kcx, 1, :], MUL, [t_r[2], t_tab[0]], [t_k[2 + kcx]])
            ck(5)
            if tti + 1 < cfg.NT:
                rmsnorm(cfg, tti + 1, 0, hT2[(tti + 1) % 2], t_h2[(tti + 1) % 2], sq, t_sq, rs, t_rs)

            def chunk_front(ci):
                    Sm, t_sm, Sp, t_Sp = Sm2[ci % 2], t_sm2[ci % 2], Sp2[ci % 2], t_Sp2[ci % 2]
                    g = tti * NCH + ci
                    cs_ = slice(ci * C, (ci + 1) * C)
                    first = (not cfg.prompt) or g == 0
                    last = (not cfg.prompt) or g == cfg.NT * NCH - 1
                    if not cfg.prompt:
                        for u in range(4):
                            src = (s0_gla if u < 2 else s0_ret)[ci, (u % 2) * 128:(u % 2 + 1) * 128, :]
                            P.dma("sp", S[:, u, :], src, W=[t_S[u]])
                    bank, tb = gb()
                    bv = bfv(bank)
                    for u in range(4):
                        tr(bv[:C, u * 128:(u + 1) * 128], krel[:, u, cs_], ident_b, [t_k[u], t_idb], [tb])
                    cp(kreltok[:C, :, :], bv[:C, 0:512].rearrange("p (a b) -> p a b", b=128), [tb], [t_kt], eng="act")
                    for u in range(4):
                        sc = eref[:, u, ci:ci + 1] if u < 2 else retc[:, u - 2, 0:1]
                        act(Sp[:, u, :], S[:, u, :], AF.Copy, [t_S[u], t_col, t_tab[1]], [t_Sp[u]], scale=sc)
                    for half in range(2):
                        bank, tb = gb()
                        for uu in range(2):
                            u = half * 2 + uu
                            grp, kcx = u // 2, u % 2
                            mm(bank[:, uu * 256:(uu + 1) * 256], kreltok[:C, u, :],
                               vtok[grp][:C, ci, kcx * 256:(kcx + 1) * 256], [t_kt, t_v[grp][ci]], [tb])
                        for uu in range(2):
                            u = half * 2 + uu
                            kcx = u % 2
                            e_lr = elr[:, kcx, ci:ci + 1] if u < 2 else retc[:, kcx, 1:2]
                            e_dc = dec[:, kcx, ci:ci + 1] if u < 2 else retc[:, kcx, 2:3]
                            ts(tmpkv[0:64, u, :], bank[0:64, uu * 256:uu * 256 + 128], e_lr[0:64, :], MUL,
                               [tb, t_col, t_tab[1]], [t_tmp[u]])
                            ts(tmpkv[64:128, u, :], bank[64:128, uu * 256 + 128:uu * 256 + 256], e_lr[64:128, :], MUL,
                               [tb, t_col, t_tab[1]], [t_tmp[u]])
                            stt(S[:, u, :], S[:, u, :], e_dc, tmpkv[:, u, :], MUL, ADD,
                                [t_S[u], t_tmp[u], t_col, t_tab[1]], [t_S[u]])
                            if last:
                                dst = (out_gla if u < 2 else out_ret)
                                dsti = seq if cfg.prompt else ci
                                P.dma("sp", dst[dsti, kcx * 128:(kcx + 1) * 128, :], S[:, u, :], R=[t_S[u]])
                    v3 = lambda ap: ap.rearrange("p (h c) -> p h c", c=128)[:C, :, :C]
                    for par in range(2):
                        bank, tb = gb()
                        pr = slice(par * 64, par * 64 + 64)
                        for slot in range(4):
                            grp = slot // 2
                            u = grp * 2 + slot % 2
                            mm(bank[:C, slot * 128:slot * 128 + C], krel[pr, u, cs_], qrel[pr, u, cs_],
                               [t_k[u], t_q[u]], [tb])
                        tt(Sm[:C, par * 4:(par + 1) * 4, :C], v3(bank[:, :]), v3(mask4[:, :]), MUL,
                           [tb, t_tab[2]], [t_sm[par]])
            def chunk_back(ci):
                    Sm, t_sm, Sp, t_Sp = Sm2[ci % 2], t_sm2[ci % 2], Sp2[ci % 2], t_Sp2[ci % 2]
                    obank = obank2[ci % 2]
                    (oA, t_oA), (oB, t_oB) = obank
                    mix, t_mix = mix2[ci % 2], t_mix2[ci % 2]
                    cs_ = slice(ci * C, (ci + 1) * C)
                    for grp in range(2):
                        ob, tob = obank[grp]
                        for hh in range(4):
                            u = grp * 2 + hh // 2
                            par = hh % 2
                            pr = slice(par * 64, par * 64 + 64)
                            smi = par * 4 + grp * 2 + hh // 2
                            i1 = mm(ob[:C, hh * 128:(hh + 1) * 128], Sm[:C, smi, :C],
                                    vtok[grp][:C, ci, hh * 128:(hh + 1) * 128], [t_sm[par], t_v[grp][ci]], [tob],
                                    st=True, sp=False)
                            mm(ob[:C, hh * 128:(hh + 1) * 128], qrel[pr, u, cs_], Sp[pr, u, :],
                               [t_q[u], t_Sp[u]], [tob], st=False, sp=True, force=([i1] if C < 64 else ()))
                    stat, t_ss, t_st = stat2[ci % 2], t_ss2[ci % 2], t_st2[ci % 2]
                    for grp in range(2):
                        ob, tob = obank[grp]
                        for hh in range(4):
                            k8 = grp * 4 + hh
                            act(junk8[:C, k8, :], ob[:C, hh * 128:(hh + 1) * 128], AF.Square, [tob], [t_ss[k8]],
                                accum=stat[:C, k8:k8 + 1])
                    for hh in range(4):
                        act(junk8[:C, hh, :], oB[:C, hh * 128:(hh + 1) * 128], AF.Copy, [t_oB, t_ss[hh]], [t_ss[hh], t_ss[8 + hh]],
                            accum=stat[:C, 8 + hh:9 + hh])
                    stt(stat[:C, 12:16], stat[:C, 8:12], -1.0 / 128, stat[:C, 8:12], MUL, MUL, [t_st] + t_ss[8:12], [t_st])
                    tt(stat[:C, 4:8], stat[:C, 4:8], stat[:C, 12:16], ADD, [t_st] + t_ss[4:8], [t_st])
                    ts(stat[:C, 8:12], stat[:C, 8:12], 1.0 / 128, MUL, [t_st] + t_ss[8:12], [t_st] + t_ss[8:12])
            def chunk_back2(ci):
                    obank = obank2[ci % 2]
                    (oA, t_oA), (oB, t_oB) = obank
                    mix, t_mix = mix2[ci % 2], t_mix2[ci % 2]
                    stat, t_ss, t_st = stat2[ci % 2], t_ss2[ci % 2], t_st2[ci % 2]
                    rstd_small(stat[:C, 0:8], stat[:C, 0:8], 1.0 / 128, [t_st] + t_ss[0:4], [t_st])
                    for hh in range(4):
                        hs = slice(hh * 128, (hh + 1) * 128)
                        stt(mix[:C, hs], oA[:C, hs], stat[:C, hh:hh + 1], gs[0][:C, ci, hs], MUL, MUL,
                            [t_oA, t_st, t_gs[0][ci]], [t_mix])
                        ts(tmpo[:C, hs], oB[:C, hs], stat[:C, 8 + hh:9 + hh], SUB, [t_oB, t_st], [t_tmpo],
                           s2=stat[:C, 4 + hh:5 + hh], op1=MUL)
                    tt(mix[:C, 512:1024], tmpo[:C, :], gs[1][:C, ci, :], MUL, [t_tmpo, t_gs[1][ci]], [t_mix])
            def chunk_trans(ci):
                    mix, t_mix = mix2[ci % 2], t_mix2[ci % 2]
                    cs_ = slice(ci * C, (ci + 1) * C)
                    bank, tb = gb()
                    bv = bfv(bank)
                    for c in range(8):
                        tr(bv[:, c * 128:c * 128 + C], mix[:C, c * 128:(c + 1) * 128], ident_b[:C, :C],
                           [t_mix, t_idb], [tb])
                    cp(mixT[:, :, cs_], bv[:, :].rearrange("p (a b) -> p a b", b=128)[:, :, :C], [tb], [t_mixT], eng="act")
            for st_ in range(NCH + 3):
                if st_ < NCH:
                    chunk_front(st_)
                if 1 <= st_ <= NCH:
                    chunk_back(st_ - 1)
                if 2 <= st_ <= NCH + 1:
                    chunk_back2(st_ - 2)
                if st_ >= 3:
                    chunk_trans(st_ - 3)
            ck(10)
            for half in range(2):
                w, tw = wpiece(D["w_out_ab"][:, half * 512:(half + 1) * 512])
                for o4 in range(4):
                    oc = half * 4 + o4
                    bank, tb = gb()
                    for kc in range(8):
                        mm(bank[:, :n], w[:, kc, o4 * 128:(o4 + 1) * 128], mixT[:, kc, :n], [tw, t_mixT], [tb],
                           st=(kc == 0), sp=(kc == 7))
                    xc = slice(cfg.x0 + tti * n, cfg.x0 + (tti + 1) * n)
                    tt(xT[:, oc, xc], xT[:, oc, xc], bank[:, :n], ADD, [t_x[cfg.xt0 + tti], tb], [t_x[cfg.xt0 + tti]])
        P.barrier()
        A.reset(m)

    def phase_ffn(groups, layer):
        m = A.mark()
        NH = NJ // 2
        Ttot = sum(g[0].T for g in groups)
        nmax = max(g[0].TT for g in groups)
        deep = not any(g[0].prompt for g in groups)
        hT = A.bf16(8, Ttot)
        sq, t_sq = PSQ, None
        rs, t_rs = PRS, t_PRS
        actb = A.bf16(NH, Ttot)
        WI = Rot([A.bf16(8, 256) for _ in range(7 if deep else 2)])
        WI.items.insert(0, (PWA[:, :, 0:256], t_PWA))
        ffn_pro = {}
        P.nobar = True
        cfg0 = groups[0][0]
        rmsnorm(cfg0, 0, 2 + layer, PHT[:, :, :cfg0.TT], t_PHT, PSQ, None, PRS, t_PRS)
        w0, tw0 = WI.next()
        ffn_pro["tw"] = (tw0, t_PWA2)
        for g_ in range(2):
            c0_ = g_ * D_FF
            P.dma("pool", w0[:, :, g_ * 128:(g_ + 1) * 128],
                  D["wffi"][layer, :, c0_:c0_ + 128].rearrange("(kc p) c -> p kc c", p=128), W=[ffn_pro["tw"][g_]])
        P.nobar = False
        WO = Rot([A.bf16(NH, 128) for _ in range(6 if deep else 2)])
        cbuf = Rot([A.f32(nmax) for _ in range(6 if deep else 2)])
        gbuf = Rot([A.f32(nmax) for _ in range(6 if deep else 2)])
        wi_tok = {}
        abc_tok = {}
        gen["banks"] = [0, 1, 2, 3, 4, 5, 6, 7]
        units = []
        G = []
        col = 0
        stg, t_stg = A.f32(NJ * 128), Tok()
        for cfg, carry_in_rows, conv_out in groups:
            nseq, L, n = cfg.nseq, cfg.L, cfg.TT
            g = Cfg()
            g.cfg, g.conv_out, g.c0 = cfg, conv_out, col
            g.carry, g.t_carry = A.f32(NJ, nseq * 2), [Tok() for _ in range(NJ)]
            g.abuf = Rot([A.f32(nseq * (L + 2)) for _ in range(6 if deep else 3)])
            g.t_h = [Tok() for _ in range(cfg.NT)]
            g.t_act = [[Tok() for _ in range(cfg.NT)] for _ in range(NH)]
            if carry_in_rows is None:
                for j in range(NJ):
                    memset(g.carry[:, j, :], 0.0, [g.t_carry[j]])
            else:
                R_ = nseq * 2
                stage, t_s = stg, t_stg
                P.dma("sp", stage[:R_, :], carry_in_rows, W=[t_s])
                bank, tb = gb()
                for j in range(NJ):
                    tr(bank[:, j * R_:(j + 1) * R_], stage[:R_, j * 128:(j + 1) * 128], ident_f[:R_, :R_],
                       [t_s, t_idf], [tb])
                cp(g.carry, bank[:, 0:NJ * R_].rearrange("p (a b) -> p a b", b=R_), [tb], g.t_carry)
            g.hT = []
            for tti in range(cfg.NT):
                if not units:
                    g.hT.append(PHT[:, :, :n])
                    g.t_h[tti] = t_PHT
                else:
                    g.hT.append(hT[:, :, col + tti * n:col + (tti + 1) * n])
                    rmsnorm(cfg, tti, 2 + layer, g.hT[tti], g.t_h[tti], sq, t_sq, rs, t_rs)
                units.append((g, tti))
            col += cfg.T
            G.append(g)
        wd = lambda k, j: dw[:, j, layer * 3 + k:layer * 3 + k + 1]
        bd = lambda j: dw[:, j, 6 + layer:7 + layer]
        for jh in range(2):
            for jj in range(NH):
                j = jh * NH + jj
                if j == 0:
                    w, tw = w0, ffn_pro["tw"]
                    wi_tok[id(tw0)] = tw
                else:
                    w, tw = WI.next()
                    tw = wi_tok.setdefault(id(tw), (tw, Tok()))
                    for g_ in range(2):
                        c0 = g_ * D_FF + j * 128
                        P.dma("pool", w[:, :, g_ * 128:(g_ + 1) * 128],
                              D["wffi"][layer, :, c0:c0 + 128].rearrange("(kc p) c -> p kc c", p=128), W=[tw[g_]])
                for g, tti in units:
                    cfg = g.cfg
                    nseq, L, n = cfg.nseq, cfg.L, cfg.TT
                    cols = slice(g.c0 + tti * n, g.c0 + (tti + 1) * n)
                    ba, tba = gb()
                    for kc in range(8):
                        mm(ba[:, :n], w[:, kc, 0:128], g.hT[tti][:, kc, :], [tw[0], g.t_h[tti]], [tba], st=(kc == 0), sp=(kc == 7))
                    bu, tbu = gb()
                    for kc in range(8):
                        mm(bu[:, :n], w[:, kc, 128:256], g.hT[tti][:, kc, :], [tw[1], g.t_h[tti]], [tbu], st=(kc == 0), sp=(kc == 7))
                    ab, tab_ = g.abuf.next()
                    tabc = abc_tok.setdefault(id(tab_), Tok())
                    ab3 = ab.rearrange("p (s l) -> p s l", l=L + 2)
                    cr = g.carry[:, j, :].rearrange("p (s r) -> p s r", r=2)
                    cp(ab3[:, :, 0:2], cr, [g.t_carry[j]], [tabc])
                    act(ab3[:, :, 2:L + 2], ba[:, :n].rearrange("p (s l) -> p s l", l=L), AF.Copy, [tba], [tab_])
                    cb, tcb = cbuf.next()
                    cb3 = cb[:, :n].rearrange("p (s l) -> p s l", l=L)
                    act(cb[:, :n], ba[:, :n], AF.Identity, [tba, t_dw], [tcb], scale=wd(2, j), bias=bd(j))
                    stt(cb3, ab3[:, :, 1:L + 1], wd(1, j), cb3, MUL, ADD, [tab_, tabc, tcb, t_dw], [tcb])
                    stt(cb3, ab3[:, :, 0:L], wd(0, j), cb3, MUL, ADD, [tab_, tabc, tcb, t_dw], [tcb])
                    cp(cr, ab3[:, :, L:L + 2], [tab_], [g.t_carry[j]])
                    ge, tge = gbuf.next()
                    act(ge[:, :n], cb[:, :n], AF.Gelu, [tcb], [tge])
                    tt(actb[:, jj, cols], ge[:, :n], bu[:, :n], MUL, [tge, tbu], [g.t_act[jj][tti]])
            for oc in range(8):
                w, tw = WO.next()
                P.dma("pool", w, D["wffo"][layer, jh * NH * 128:(jh + 1) * NH * 128, oc * 128:(oc + 1) * 128]
                      .rearrange("(j p) c -> p j c", p=128), W=[tw])
                for g, tti in units:
                    cfg = g.cfg
                    n = cfg.TT
                    cols = slice(g.c0 + tti * n, g.c0 + (tti + 1) * n)
                    xc = slice(cfg.x0 + tti * n, cfg.x0 + (tti + 1) * n)
                    t_xt = t_x[cfg.xt0 + tti]
                    bank, tb = gb()
                    for jj in range(NH):
                        mm(bank[:, :n], w[:, jj, :], actb[:, jj, cols], [tw, g.t_act[jj][tti]], [tb],
                           st=(jj == 0), sp=(jj == NH - 1))
                    tt(xT[:, oc, xc], xT[:, oc, xc], bank[:, :n], ADD, [t_xt, tb], [t_xt])
        for g in G:
            R_ = g.cfg.nseq * 2
            cstage, t_cs = stg, t_stg
            for q4 in range((NJ + 3) // 4):
                j0, j1 = q4 * 4, min(NJ, q4 * 4 + 4)
                bank, tb = gb()
                for j in range(j0, j1):
                    tr(bank[:R_, (j - j0) * 128:(j - j0 + 1) * 128], g.carry[:, j, :], ident_f, [g.t_carry[j], t_idf], [tb])
                cp(cstage[:R_, j0 * 128:j1 * 128], bank[:R_, 0:(j1 - j0) * 128], [tb], [t_cs])
            P.dma("sp", g.conv_out.rearrange("s r c -> (s r) c"), cstage[:R_, :], R=[t_cs])
        P.barrier()
        A.reset(m)

    SCALE = 192.0 ** -0.5

    def phase_mla(cfg, seq, ckv_out, kr_out, cache_ckv=None, cache_kr=None):
        m = A.mark()
        n, C, NCH, T = cfg.TT, cfg.C, cfg.NCH, cfg.T
        KB = 128 if cfg.prompt else 16
        NB = T // KB
        cqnT, t_cqn = A.bf16(3, T), [Tok() for _ in range(cfg.NT)]
        ckvT, t_ckvT = A.bf16(4, T), [Tok() for _ in range(cfg.NT)]
        krT2, t_krT = A.bf16(T), [Tok() for _ in range(cfg.NT)]
        ckvn_b = None
        if not cfg.prompt:
            ckvn_b, t_cnb = A.bf16(NB, 512), [Tok() for _ in range(NB)]
        m1 = A.mark()
        P.nobar = True
        rmsnorm(cfg, 0, 1, PHT, t_PHT, PSQ, None, PRS, t_PRS)
        P.dma("pool", PWA[:, :, 0:384], D["w_in_c"][:, 0:384].rearrange("(kc p) c -> p kc c", p=128), W=[t_PWA, t_PWA2])
        P.nobar = False
        Wc = A.bf16(8, 960)
        t_wc = [t_PWA] + [load(Wc[:, :, c0:c1], D["w_in_c"][:, c0:c1].rearrange("(kc p) c -> p kc c", p=128), "pool")
                          for c0, c1 in ((384, 896), (896, 960))]
        krc, krs_ = A.f32(17, 64), A.f32(17, 64)
        gkvb = A.f32(512)
        t_t = [load(krc, D["krc"]), load(krs_, D["krs_"]), load(gkvb, D["gkv"].to_broadcast([128, 512]))]
        sq, t_sq = PSQ, None
        rs, t_rs = PRS, t_PRS
        hT2c, t_h2c = [PHT, A.bf16(8, n)], [t_PHT, Tok()]
        sq3, t_sq3 = A.bf16(3, n), Tok()
        rq, t_rq = A.f32(n), Tok()
        ckvn = Rot([A.f32(512) for _ in range(3)])
        cb16 = Rot([A.bf16(512) for _ in range(3)])
        krr = Rot([A.f32(64) for _ in range(3)])
        kt1, kt2, t_kt = A.f32(64), A.f32(64), Tok()
        kb16 = Rot([A.bf16(128) for _ in range(3)])
        st1, t_st1 = A.f32(4), Tok()
        junk, t_junk = A.bf16(512), Tok()
        gen["banks"] = [0, 1, 2, 3, 4, 5, 6, 7]
        for tti in range(cfg.NT):
            cols = slice(tti * n, (tti + 1) * n)
            hT, t_h = hT2c[tti % 2], t_h2c[tti % 2]
            cqb = []
            for j in range(3):
                bank, tb = gb()
                for kc in range(8):
                    mm(bank[:, :n], PWA[:, kc, j * 128:(j + 1) * 128], hT[:, kc, :n], [t_wc[0], t_h], [tb],
                       st=(kc == 0), sp=(kc == 7))
                act(sq3[:, j, :], bank[:, :n], AF.Square, [tb], [t_sq3])
                cqb.append((bank, tb))
            bank, tb = gb()
            for j in range(3):
                mm(bank[:, :n], ones_b, sq3[:, j, :], [t_sq3, t_one], [tb], st=(j == 0), sp=(j == 2))
            act(rq, bank[:, :n], AF.Ln, [tb], [t_rq], scale=1.0 / 384, bias=EPS)
            act(rq, rq, AF.Exp, [t_rq], [t_rq], scale=-0.5)
            for j in range(3):
                stt(cqnT[:, j, cols], cqb[j][0][:, :n], gqc[:, j:j + 1], rq, MUL, MUL,
                    [cqb[j][1], t_rq, t_gq], [t_cqn[tti]])
            if tti + 1 < cfg.NT:
                rmsnorm(cfg, tti + 1, 1, hT2c[(tti + 1) % 2], t_h2c[(tti + 1) % 2], sq, t_sq, rs, t_rs)
            def c1_proj(bi):
                blk = tti * (n // KB) + bi
                tcs = slice(bi * KB, (bi + 1) * KB)
                gcs = slice(blk * KB, (blk + 1) * KB)
                bank, tb = gb()
                for kc in range(8):
                    mm(bank[:KB, :], hT[:, kc, tcs], Wc[:, kc, 384:896], [t_h, t_wc[1]], [tb], st=(kc == 0), sp=(kc == 7))
                act(junk[:KB, :], bank[:KB, :], AF.Square, [tb, t_st1], [t_junk, t_st1], accum=st1[:KB, 0:1])
                rstd_small(st1[:KB, 0:1], st1[:KB, 0:1], 1.0 / 512, [t_st1], [t_st1])
                cn, tcn = ckvn.next()
                stt(cn[:KB, :], bank[:KB, :], st1[:KB, 0:1], gkvb[:KB, :], MUL, MUL, [tb, t_st1, t_t[2]], [tcn])
                P.dma("sp", ckv_out[gcs, :], cn[:KB, :], R=[tcn])
                if cfg.prompt:
                    c16, tc16 = cb16.next()
                else:
                    c16, tc16 = ckvn_b[:, blk, :], t_cnb[blk]
                cp(c16[:KB, :], cn[:KB, :], [tcn], [tc16], eng="act")
                bank, tb = gb()
                for kc in range(8):
                    mm(bank[:KB, 0:64], hT[:, kc, tcs], Wc[:, kc, 896:960], [t_h, t_wc[2]], [tb], st=(kc == 0), sp=(kc == 7))
                tblk = blk if cfg.prompt else 16
                tt(kt1[:KB, :], bank[:KB, 0:64], krc[:KB, tblk, :], MUL, [tb, t_t[0]], [t_kt])
                tt(kt2[:KB, 0:32], bank[:KB, 32:64], krs_[:KB, tblk, 0:32], MUL, [tb, t_t[1]], [t_kt])
                tt(kt2[:KB, 32:64], bank[:KB, 0:32], krs_[:KB, tblk, 32:64], MUL, [tb, t_t[1]], [t_kt])
                kr_, tkr = krr.next()
                tt(kr_[:KB, :], kt1[:KB, :], kt2[:KB, :], ADD, [t_kt], [tkr])
                P.dma("sp", kr_out[gcs, :], kr_[:KB, :], R=[tkr])
                k16, tk16 = kb16.next()
                cp(k16[:KB, 0:64], kr_[:KB, :], [tkr], [tk16], eng="act")
                cp(k16[:KB, 64:128], kr_[:KB, :], [tkr], [tk16], eng="act")
                return gcs, c16, tc16, k16, tk16

            def c1_trans(item):
                gcs, c16, tc16, k16, tk16 = item
                bank2, tb2 = gb()
                bv = bfv(bank2)
                for kc in range(4):
                    tr(bv[:, kc * 128:kc * 128 + KB], c16[:KB, kc * 128:(kc + 1) * 128], ident_b[:KB, :KB],
                       [tc16, t_idb], [tb2])
                tr(bv[:, 512:512 + KB], k16[:KB, :], ident_b[:KB, :KB], [tk16, t_idb], [tb2])
                cp(ckvT[:, :, gcs], bv[:, 0:512].rearrange("p (a b) -> p a b", b=128)[:, :, :KB], [tb2], [t_ckvT[tti]])
                cp(krT2[:, gcs], bv[:, 512:512 + KB], [tb2], [t_krT[tti]])

            pend = []
            for bi in range(n // KB):
                pend.append(c1_proj(bi))
                if len(pend) > 1:
                    c1_trans(pend.pop(0))
            while pend:
                c1_trans(pend.pop(0))
        P.barrier()
        A.reset(m1)
        if cfg.prompt:
            mla_prompt_c2(cfg, cqnT, t_cqn, ckvT, t_ckvT, krT2, t_krT)
        else:
            mla_sample_c2(cfg, cqnT, t_cqn, ckvT, t_ckvT, krT2, t_krT, ckvn_b, t_cnb, cache_ckv, cache_kr)
        P.barrier()
        A.reset(m)

    def mla_prompt_c2(cfg, cqnT, t_cqn, ckvT, t_ckvT, krT2, t_krT):
        n, T, NT = cfg.TT, cfg.T, cfg.NT
        qn, t_qn = [A.bf16(T), A.bf16(T)], [[Tok() for _ in range(NT)] for _ in range(2)]
        qr, t_qr = A.bf16(T), [Tok() for _ in range(NT)]
        kn, t_kn = [A.bf16(T), A.bf16(T)], [[Tok() for _ in range(NT)] for _ in range(2)]
        Vp, t_V = A.bf16(16, 256), [Tok() for _ in range(16)]
        ao2, t_ao2 = [A.bf16(2, n), A.bf16(2, n)], [Tok(), Tok()]
        pending_out = [None]
        WQ = Rot([A.bf16(3, 256) for _ in range(2)])
        WQR = Rot([A.bf16(3, 128) for _ in range(2)])
        WQS = Rot([A.bf16(3, 128) for _ in range(2)])
        WK = Rot([A.bf16(4, 256) for _ in range(2)])
        WV = Rot([A.bf16(4, 256) for _ in range(2)])
        WOo = Rot([A.bf16(2, 1024) for _ in range(2)])
        cosr = Rot([A.f32(n) for _ in range(2)])
        sinr = Rot([A.f32(n) for _ in range(2)])
        r1, r2, t_r = A.f32(n), A.f32(n), [Tok(), Tok()]
        PT = Rot([A.bf16(512) for _ in range(5)])
        rden, t_rden = A.f32(n), Tok()
        gen["banks"] = [0, 1, 2, 3]
        obk = Rot([PB[4], PB[5]])
        obk.items = [(PB[4], TPB[4]), (PB[5], TPB[5])]
        dbk = Rot([PB[6], PB[7]])
        dbk.items = [(PB[6], TPB[6]), (PB[7], TPB[7])]
        r3 = lambda src: src.rearrange("(kc p) c -> p kc c", p=128)
        for pr in range(4):
            wq, twq = WQ.next()
            P.dma("pool", wq, r3(D["wuq_n"][:, pr * 256:(pr + 1) * 256]), W=[twq])
            wqr, twqr = WQR.next()
            P.dma("pool", wqr, r3(D["wuq_r"][:, pr * 128:(pr + 1) * 128]), W=[twqr])
            wqs, twqs = WQS.next()
            P.dma("pool", wqs, r3(D["wuq_rs"][:, pr * 128:(pr + 1) * 128]), W=[twqs])
            wk, twk = WK.next()
            P.dma("pool", wk, r3(D["wuk"][:, pr * 256:(pr + 1) * 256]), W=[twk])
            wv, twv = WV.next()
            P.dma("pool", wv, r3(D["wuv"][:, pr * 256:(pr + 1) * 256]), W=[twv])
            wo, two = WOo.next()
            P.dma("pool", wo, D["w_out_c"][pr * 256:(pr + 1) * 256, :].rearrange("(h p) c -> p h c", p=128), W=[two])
            for tti in range(NT):
                cols = slice(tti * n, (tti + 1) * n)
                for hh in range(2):
                    bank, tb = gb()
                    for kc in range(3):
                        mm(bank[:, :n], wq[:, kc, hh * 128:(hh + 1) * 128], cqnT[:, kc, cols], [twq, t_cqn[tti]], [tb],
                           st=(kc == 0), sp=(kc == 2))
                    cp(qn[hh][:, cols], bank[:, :n], [tb], [t_qn[hh][tti]], eng="act")
                    bank, tb = gb()
                    for kc in range(4):
                        mm(bank[:, :n], wk[:, kc, hh * 128:(hh + 1) * 128], ckvT[:, kc, cols], [twk, t_ckvT[tti]], [tb],
                           st=(kc == 0), sp=(kc == 3))
                    cp(kn[hh][:, cols], bank[:, :n], [tb], [t_kn[hh][tti]], eng="dve")
                cos_t, t_cos = cosr.next()
                sin_t, t_sin = sinr.next()
                P.dma("sp", cos_t, D["cosF"][:, cols], W=[t_cos])
                P.dma("sp", sin_t, D["sinF"][:, cols], W=[t_sin])
                bank1, tb1 = gb()
                for kc in range(3):
                    mm(bank1[:, :n], wqr[:, kc, :], cqnT[:, kc, cols], [twqr, t_cqn[tti]], [tb1], st=(kc == 0), sp=(kc == 2))
                bank2, tb2 = gb()
                for kc in range(3):
                    mm(bank2[:, :n], wqs[:, kc, :], cqnT[:, kc, cols], [twqs, t_cqn[tti]], [tb2], st=(kc == 0), sp=(kc == 2))
                tt(r1, bank1[:, :n], cos_t, MUL, [tb1, t_cos], [t_r[0]])
                tt(r2, bank2[:, :n], sin_t, MUL, [tb2, t_sin], [t_r[1]])
                tt(qr[:, cols], r1, r2, ADD, t_r, [t_qr[tti]])
                for b4 in range(4):
                    blk = tti * 4 + b4
                    bank, tb = gb()
                    for kc in range(4):
                        mm(bank[:, 0:256], ckvT[:, kc, blk * 128:(blk + 1) * 128], wv[:, kc, :], [twv, t_ckvT[tti]], [tb],
                           st=(kc == 0), sp=(kc == 3))
                    cp(Vp[:, blk, :], bank[:, 0:256], [tb], [t_V[blk]], eng=("act" if b4 % 2 == 0 else "dve"))
            for qt in range(NT):
                qcols0 = qt * n
                ao, t_ao = ao2[qt % 2], t_ao2[qt % 2]
                for hh in range(2):
                    prs = slice(hh * 64, hh * 64 + 64)
                    ob, tob = obk.next()
                    db, tdb = dbk.next()
                    nkb = 4 * qt + 4

                    def scores(kb):
                        i = kb - 4 * qt
                        q0 = 0 if i <= 0 else i * 128
                        N = n - q0
                        qs = slice(qcols0 + q0, qcols0 + n)
                        ks = slice(kb * 128, (kb + 1) * 128)
                        bank, tb = gb()
                        mm(bank[:, :N], kn[hh][:, ks], qn[hh][:, qs], [t_kn[hh][kb // 4], t_qn[hh][qt]], [tb], st=True, sp=False)
                        mm(bank[:, :N], krT2[prs, ks], qr[prs, qs], [t_krT[kb // 4], t_qr[qt]], [tb], st=False, sp=True)
                        pt, tpt = PT.next()
                        act(pt[:, :N], bank[:, :N], AF.Exp, [tb], [tpt], scale=SCALE)
                        if i >= 0:
                            memset(pt[64:128, 0:64], 0.0, [tpt])
                        return kb, q0, N, pt, tpt

                    def pv(item):
                        kb, q0, N, pt, tpt = item
                        mm(ob[:, q0:n], Vp[:, kb, hh * 128:(hh + 1) * 128], pt[:, :N], [t_V[kb], tpt], [tob],
                           st=(kb == 0), sp=(kb == nkb - 1))
                        mm(db[:, q0:n], ones_b, pt[:, :N], [t_one, tpt], [tdb], st=(kb == 0), sp=(kb == nkb - 1))

                    pend = []
                    for kb in range(nkb):
                        pend.append(scores(kb))
                        if len(pend) > 2:
                            pv(pend.pop(0))
                        if hh == 0 and kb == 2 and pending_out[0] is not None:
                            pending_out[0]()
                            pending_out[0] = None
                    while pend:
                        pv(pend.pop(0))
                    P.op("dve", lambda e, db=db: e.reciprocal(out=rden, in_=db[:, :n]), [tdb], [t_rden])
                    tt(ao[:, hh, :], ob[:, :n], rden, MUL, [tob, t_rden], [t_ao])

                def outproj(qt=qt, ao=ao, t_ao=t_ao, wo=wo, two=two):
                    qc = slice(qt * n, qt * n + n)
                    for oc in range(8):
                        bank, tb = gb()
                        for hh in range(2):
                            mm(bank[:, :n], wo[:, hh, oc * 128:(oc + 1) * 128], ao[:, hh, :], [two, t_ao], [tb],
                               st=(hh == 0), sp=(hh == 1))
                        tt(xT[:, oc, qc], xT[:, oc, qc], bank[:, :n], ADD, [t_x[qt], tb], [t_x[qt]])
                pending_out[0] = outproj
        if pending_out[0] is not None:
            pending_out[0]()
            pending_out[0] = None

    def mla_sample_c2(cfg, cqnT, t_cqn, ckvT, t_ckvT, krT2, t_krT, ckvn_b, t_cnb, cache_ckv, cache_kr):
        T = cfg.T
        r3 = lambda src: src.rearrange("(kc p) c -> p kc c", p=128)
        WQ, WQR, WQS = A.bf16(3, 1024), A.bf16(3, 512), A.bf16(3, 512)
        WUKT, WV = A.bf16(8, 512), A.bf16(4, 1024)
        t_w = [load(WQ, r3(D["wuq_n"]), "pool"), load(WQR, r3(D["wuq_r"]), "pool"), load(WQS, r3(D["wuq_rs"]), "pool"),
               load(WUKT, D["wukT"], "pool"), load(WV, r3(D["wuv"]), "pool")]
        WOo = Rot([A.bf16(8, 128) for _ in range(2)])
        cos_t, sin_t = A.f32(T), A.f32(T)
        t_cs = [load(cos_t, D["cosF"][:, 2048:2048 + T]), load(sin_t, D["sinF"][:, 2048:2048 + T])]
        qnS, t_qnS = A.bf16(8, T), Tok()
        qrS, t_qrS = A.bf16(8, T), Tok()
        r1, r2, t_r = A.f32(T), A.f32(T), [Tok(), Tok()]
        qlat, t_ql = [A.bf16(4, 128) for _ in range(4)], [Tok() for _ in range(4)]
        CQ = Rot([A.bf16(8, 512) for _ in range(3)])
        KQ = Rot([A.bf16(8, 128) for _ in range(3)])
        kq_tok = {}
        CT = Rot([A.bf16(4, 1024) for _ in range(2)])
        KT = Rot([A.bf16(1024) for _ in range(2)])
        PT = Rot([A.bf16(128) for _ in range(5)])
        rden, t_rden = A.f32(1), Tok()
        olatn, t_on = A.bf16(512), Tok()
        olatT, t_oT = A.bf16(4, 128), Tok()
        aoS, t_ao = A.bf16(8, T), Tok()
        gen["banks"] = [0, 1, 2, 3, 4, 5]
        olb, t_olb, dnb, t_dnb = PB[6], TPB[6], PB[7], TPB[7]
        for h in range(8):
            bank, tb = gb()
            for kc in range(3):
                mm(bank[:, :T], WQ[:, kc, h * 128:(h + 1) * 128], cqnT[:, kc, :T], [t_w[0], t_cqn[0]], [tb], st=(kc == 0), sp=(kc == 2))
            cp(qnS[:, h, :], bank[:, :T], [tb], [t_qnS], eng=("act" if h % 2 else "dve"))
        for h in range(8):
            b1, tb1 = gb()
            for kc in range(3):
                mm(b1[:64, :T], WQR[:, kc, h * 64:(h + 1) * 64], cqnT[:, kc, :T], [t_w[1], t_cqn[0]], [tb1], st=(kc == 0), sp=(kc == 2))
            b2, tb2 = gb()
            for kc in range(3):
                mm(b2[:64, :T], WQS[:, kc, h * 64:(h + 1) * 64], cqnT[:, kc, :T], [t_w[2], t_cqn[0]], [tb2], st=(kc == 0), sp=(kc == 2))
            tt(r1[:64, :], b1[:64, :T], cos_t[:64, :], MUL, [tb1, t_cs[0]], [t_r[0]])
            tt(r2[:64, :], b2[:64, :T], sin_t[:64, :], MUL, [tb2, t_cs[1]], [t_r[1]])
            tt(qrS[:64, h, :], r1[:64, :], r2[:64, :], ADD, t_r, [t_qrS])
        for b in range(4):
            bank, tb = gb()
            for kc in range(4):
                for h in range(8):
                    mm(bank[:, kc * 128 + h * 16:kc * 128 + (h + 1) * 16], WUKT[:, h, kc * 128:(kc + 1) * 128],
                       qnS[:, h, b * 16:(b + 1) * 16], [t_w[3], t_qnS], [tb])
            cp(qlat[b], bank[:, :].rearrange("p (a b) -> p a b", b=128), [tb], [t_ql[b]], eng="act")
        for b in range(4):
            pend = []

            def scores(K_, lc, lr, vrows, Rk, blk):
                bank, tb = gb()
                for kc in range(4):
                    mm(bank[:K_, 0:128], lc(kc), qlat[b][:, kc, :], Rk + [t_ql[b]], [tb], st=(kc == 0), sp=False)
                mm(bank[:K_, 0:128], lr, qrS[:64, :, b * 16:(b + 1) * 16], Rk + [t_qrS], [tb], st=False, sp=True)
                pt, tpt = PT.next()
                act(pt[:K_, :], bank[:K_, 0:128], AF.Exp, [tb], [tpt], scale=SCALE)
                return K_, pt, tpt, vrows, Rk, blk

            def pv(item):
                K_, pt, tpt, vrows, Rk, blk = item
                mm(olb[:, :], pt[:K_, :], vrows, [tpt] + Rk, [t_olb], st=(blk == 0), sp=(blk == 32))
                mm(dnb[:, 0:1], pt[:K_, :], ones_b[:K_, 0:1], [tpt, t_one], [t_dnb], st=(blk == 0), sp=(blk == 32))

            def push(item):
                pend.append(item)
                if len(pend) > 2:
                    pv(pend.pop(0))

            for q4 in range(4):
                cq, tcq = CQ.next()
                kq, tkq = KQ.next()
                tkq = kq_tok.setdefault(id(tkq), (tkq, Tok()))
                P.dma("pool", cq, cache_ckv[b, q4 * 1024:(q4 + 1) * 1024, :].rearrange("(k p) l -> p k l", p=128), W=[tcq])
                for dup in range(2):
                    P.dma("pool", kq[:, :, dup * 64:(dup + 1) * 64],
                          cache_kr[b, q4 * 1024:(q4 + 1) * 1024, :].rearrange("(k p) r -> p k r", p=128), W=[tkq[dup]])
                ct, tct = CT.next()
                kt, tkt = KT.next()
                for k8 in range(8):
                    bs = slice(k8 * 128, (k8 + 1) * 128)
                    bank, tb = gb()
                    bv = bfv(bank)
                    for kc in range(4):
                        tr(bv[:, kc * 128:(kc + 1) * 128], cq[:, k8, kc * 128:(kc + 1) * 128], ident_b, [tcq, t_idb], [tb])
                    tr(bv[:, 512:640], kq[:, k8, :], ident_b, [tkq[0], tkq[1], t_idb], [tb])
                    cp(ct[:, :, bs], bv[:, 0:512].rearrange("p (a b) -> p a b", b=128), [tb], [tct],
                       eng=("act" if k8 % 2 else "dve"))
                    cp(kt[:, bs], bv[:, 512:640], [tb], [tkt], eng=("dve" if k8 % 2 else "act"))
                for k8 in range(8):
                    bs = slice(k8 * 128, (k8 + 1) * 128)
                    push(scores(128, (lambda kc, bs=bs, ct=ct: ct[:, kc, bs]), kt[0:64, bs], cq[:, k8, :],
                                [tct, tkt, tcq], q4 * 8 + k8))
            bs = slice(b * 16, (b + 1) * 16)
            push(scores(16, (lambda kc, bs=bs: ckvT[:, kc, bs]), krT2[0:64, bs], ckvn_b[:16, b, :],
                        [t_ckvT[0], t_krT[0], t_cnb[b]], 32))
            while pend:
                pv(pend.pop(0))
            P.op("dve", lambda e: e.reciprocal(out=rden, in_=dnb[:, 0:1]), [t_dnb], [t_rden])
            ts(olatn, olb[:, :], rden[:, 0:1], MUL, [t_olb, t_rden], [t_on])
            bank, tb = gb()
            bv = bfv(bank)
            for kc in range(4):
                tr(bv[:, kc * 128:(kc + 1) * 128], olatn[:, kc * 128:(kc + 1) * 128], ident_b, [t_on, t_idb], [tb])
            cp(olatT, bv[:, 0:512].rearrange("p (a b) -> p a b", b=128), [tb], [t_oT], eng="act")
            bank, tb = gb()
            for h in range(8):
                for kc in range(4):
                    mm(bank[:, h * 16:(h + 1) * 16], WV[:, kc, h * 128:(h + 1) * 128], olatT[:, kc, h * 16:(h + 1) * 16],
                       [t_w[4], t_oT], [tb], st=(kc == 0), sp=(kc == 3))
            cp(aoS[:, :, b * 16:(b + 1) * 16], bank[:, 0:128].rearrange("p (h i) -> p h i", i=16), [tb], [t_ao])
        for oc in range(8):
            wo, two = WOo.next()
            P.dma("pool", wo, D["w_out_c"][:, oc * 128:(oc + 1) * 128].rearrange("(h p) c -> p h c", p=128), W=[two])
            bank, tb = gb()
            for h in range(8):
                mm(bank[:, :T], wo[:, h, :], aoS[:, h, :], [two, t_ao], [tb], st=(h == 0), sp=(h == 7))
            tt(xT[:, oc, cfg.x0:cfg.x0 + T], xT[:, oc, cfg.x0:cfg.x0 + T], bank[:, :T], ADD, [t_x[cfg.xt0], tb], [t_x[cfg.xt0]])

    def phase_final(cfg, y_out):
        m = A.mark()
        n = cfg.XB
        gfb = A.f32(1024)
        t_g = load(gfb, D["nrm"][4:5, :].to_broadcast([128, 1024]))
        ybuf = Rot([A.f32(1024) for _ in range(2)])
        st, t_st = A.f32(4), Tok()
        junk, t_junk = A.bf16(512), Tok()
        gen["banks"] = [0, 1, 2, 3, 4, 5, 6, 7]
        for blk in range(cfg.T // n):
            cs_ = slice(blk * n, (blk + 1) * n)
            xs_ = slice(cfg.x0 + blk * n, cfg.x0 + (blk + 1) * n)
            tti = cfg.xt0 + (blk * n) // cfg.TT
            banks = []
            for half in range(2):
                bank, tb = gb()
                for c4 in range(4):
                    tr(bank[:n, c4 * 128:(c4 + 1) * 128], xT[:, half * 4 + c4, xs_], ident_f, [t_x[tti], t_idf], [tb])
                act(junk[:n, :], bank[:n, :], AF.Square, [tb, t_st], [t_junk, t_st], accum=st[:n, half:half + 1])
                banks.append((bank, tb))
            tt(st[:n, 2:3], st[:n, 0:1], st[:n, 1:2], ADD, [t_st], [t_st])
            rstd_small(st[:n, 2:3], st[:n, 2:3], 1.0 / 1024, [t_st], [t_st])
            yb, tyb = ybuf.next()
            for half in range(2):
                stt(yb[:n, half * 512:(half + 1) * 512], banks[half][0][:n, :], st[:n, 2:3],
                    gfb[:n, half * 512:(half + 1) * 512], MUL, MUL, [banks[half][1], t_st, t_g], [tyb])
            P.dma("sp", y_out[cs_, :], yb[:n, :], R=[tyb])
        P.barrier()
        A.reset(m)

    pc, sc = make_cfgs()
    MERGE = STAGE >= 99 and (NSEQ_P == 2 or os.environ.get("MK_MERGE") == "1")
    for seq in range(NSEQ_P):
        merged = MERGE and seq == NSEQ_P - 1
        load_x(pc, D["xp"][seq])
        if merged:
            load_x(sc, D["xs"])
        if STAGE >= 1:
            try:
                phase_mixer_ab(pc, seq, None, None, O["glap"], O["retp"])
            except _Stop:
                P.barrier()
                A.reset(base_mark)
            if merged:
                phase_mixer_ab(sc, 0, D["sgla"], D["sret"], O["glas"], O["rets"])
        if STAGE >= 2:
            grp = [(pc, None, O["convp"][0, seq:seq + 1])]
            if merged:
                grp.append((sc, D["sconv"][0:8, :], O["convs"][0]))
            phase_ffn(grp, 0)
        if STAGE >= 3:
            phase_mla(pc, seq, O["ckvp"][seq], O["krp"][seq])
            if merged:
                phase_mla(sc, 0, O["ckvs"], O["krs"], D["cckv"], D["ckr"])
        if STAGE >= 4:
            grp = [(pc, None, O["convp"][1, seq:seq + 1])]
            if merged:
                grp.append((sc, D["sconv"][8:16, :], O["convs"][1]))
            phase_ffn(grp, 1)
        phase_final(pc, O["yp"][seq])
        if merged:
            phase_final(sc, O["ys"])
    if STAGE >= 5 and not MERGE:
        load_x(sc, D["xs"])
        phase_mixer_ab(sc, 0, D["sgla"], D["sret"], O["glas"], O["rets"])
        phase_ffn([(sc, D["sconv"][0:8, :], O["convs"][0])], 0)
        phase_mla(sc, 0, O["ckvs"], O["krs"], D["cckv"], D["ckr"])
        phase_ffn([(sc, D["sconv"][8:16, :], O["convs"][1])], 1)
        phase_final(sc, O["ys"])
    P.emit()
    P.close()
    print("arena peak words", A.peak, "instrs", {k: len(v) for k, v in P.streams.items()}, "signals", P.sigcount)
    return nc


_CACHE = {}


def kernel(x_prompt, x_sample, state_gla, state_ret, cache_ckv, cache_krope, state_conv,
           norm_mix, norm_ffn, norm_final, w_in_ab, w_gate_up, b_gate, g_gla, g_ret, w_out_ab,
           w_in_c, g_q, g_kv, w_uq, w_uk, w_uv, w_out_c, w_ffn_in, w_dwconv, b_dwconv, w_ffn_out):
    f = lambda a: np.ascontiguousarray(np.asarray(a, dtype=np.float32))
    ncores = int(os.environ.get('MK_CORES', '8'))
    if "nc" not in _CACHE:
        _CACHE["nc"] = build_program()
        _CACHE["tabs"] = const_tables()
    nc = _CACHE["nc"]
    tabs = _CACHE["tabs"]
    w_in_ab0 = f(w_in_ab)[0]
    sw = w_in_ab0[:, 1552:2064].reshape(1024, 8, 2, 32)[:, :, ::-1, :].reshape(1024, 512)
    wuq = f(w_uq)[0].reshape(384, 8, 192)
    wuk0 = f(w_uk)[0]
    shared = dict(
        nrm=np.stack([f(norm_mix)[0], f(norm_mix)[1], f(norm_ffn)[0], f(norm_ffn)[1], f(norm_final)]),
        w_in_ab=w_in_ab0, w_ab_sw=f(sw), wgu=f(w_gate_up)[0], bgate=f(b_gate)[0][None, :],
        ggla=f(g_gla)[0][None, :], gret=f(g_ret)[0][None, :], w_out_ab=f(w_out_ab)[0], w_in_c=f(w_in_c)[0],
        gq=f(g_q)[0][None, :], gkv=f(g_kv)[0][None, :],
        wuq_n=f(wuq[:, :, :128].reshape(384, 1024)), wuq_r=f(wuq[:, :, 128:].reshape(384, 512)),
        wuq_rs=f(wuq[:, :, 128:].reshape(384, 8, 2, 32)[:, :, ::-1, :].reshape(384, 512)),
        wuk=f(wuk0.reshape(512, 1024)), wukT=f(wuk0.transpose(2, 1, 0)), wuv=f(w_uv)[0].reshape(512, 1024),
        w_out_c=f(w_out_c)[0], wffi=f(w_ffn_in),
        dwc=f(np.concatenate([f(w_dwconv).reshape(6, 2816), f(b_dwconv)], axis=0)), wffo=f(w_ffn_out),
    )
    shared.update(tabs)
    xpv, xsv = f(x_prompt), f(x_sample)
    sg, sr = f(state_gla)[0], f(state_ret)[0]
    cc, ck, scv = f(cache_ckv)[0], f(cache_krope)[0], f(state_conv)
    in_maps = []
    for c in range(ncores):
        d = dict(shared)
        d["xp"] = xpv[2 * c:2 * c + 2]
        d["xs"] = xsv[4 * c:4 * c + 4].reshape(64, 1024)
        d["sgla"] = sg[4 * c:4 * c + 4].reshape(4, 256, 128)
        d["sret"] = sr[4 * c:4 * c + 4].reshape(4, 256, 128)
        d["cckv"] = cc[4 * c:4 * c + 4]
        d["ckr"] = ck[4 * c:4 * c + 4]
        d["sconv"] = f(scv[:, 4 * c:4 * c + 4].reshape(16, 2816))
        in_maps.append({k: np.ascontiguousarray(v) for k, v in d.items()})
    res = run_bass_kernel_spmd(nc, in_maps, core_ids=list(range(ncores)))
    R = res.results
    cat = lambda k, shp: np.concatenate([R[c][k].reshape(shp) for c in range(ncores)], axis=0)
    y_prompt = cat("yp", (2, 2048, 1024))
    y_sample = cat("ys", (4, 16, 1024))
    gla_p = cat("glap", (2, 4, 64, 128))[None]
    gla_s = cat("glas", (4, 4, 64, 128))[None]
    ret_p = cat("retp", (2, 4, 64, 128))[None]
    ret_s = cat("rets", (4, 4, 64, 128))[None]
    ckv_p = cat("ckvp", (2, 2048, 512))[None]
    ckv_s = cat("ckvs", (4, 16, 512))[None]
    kr_p = cat("krp", (2, 2048, 64))[None]
    kr_s = cat("krs", (4, 16, 64))[None]
    conv_p = np.concatenate([R[c]["convp"] for c in range(ncores)], axis=1)
    conv_s = np.concatenate([R[c]["convs"] for c in range(ncores)], axis=1)
    outs = (y_prompt, y_sample, gla_p, gla_s, ret_p, ret_s, ckv_p, ckv_s, kr_p, kr_s, conv_p, conv_s)
    return tuple(np.ascontiguousarray(o, dtype=np.float32) for o in outs)
```

```python
import contextlib
import math
import os

import numpy as np
import concourse.bass as bass
import concourse.mybir as mybir
from concourse.bass_utils import run_bass_kernel_spmd

F32 = mybir.dt.float32
BF16 = mybir.dt.bfloat16
AF = mybir.ActivationFunctionType
ALU = mybir.AluOpType
AX = mybir.AxisListType

EPS = 1e-6
D_FF = 2816
NJ = 22
STAGE = int(os.environ.get("MK_STAGE", "99"))
NSEQ_P = int(os.environ.get("MK_NSEQ", "2"))
STOPAT = int(os.environ.get("MK_STOP", "0"))


class _Stop(Exception):
    pass


def ck(k):
    if STOPAT == k:
        raise _Stop()

COMPUTE = ("pe", "act", "dve", "pool")
DMA_K = 8
EPOCH = 6000


class Tok:
    __slots__ = ("w", "rs", "excl")

    def __init__(self, excl=False):
        self.w = None
        self.rs = []
        self.excl = excl


class Ins:
    __slots__ = ("eng", "fn", "deps", "dma", "signal", "ev", "prev_ev")

    def __init__(self, eng, fn, dma):
        self.eng = eng
        self.fn = fn
        self.dma = dma
        self.deps = []
        self.signal = False
        self.ev = None
        self.prev_ev = None


class Prog:
    def __init__(self, nc):
        self.nc = nc
        self.es = contextlib.ExitStack()
        self.streams = {e: [] for e in ("pe", "act", "dve", "pool", "sp")}
        self.pending = {e: [] for e in self.streams}
        self.dmas = []
        self.n = 0
        self.nobar = False

    def sb(self, shape, dt, name="t"):
        self.n += 1
        return self.es.enter_context(self.nc.sbuf_tensor(f"{name}{self.n}", list(shape), dt))

    def ps(self, shape, dt, name="p"):
        self.n += 1
        return self.es.enter_context(self.nc.psum_tensor(f"{name}{self.n}", list(shape), dt))

    def op(self, eng, fn, R=(), W=(), dma=False, force=()):
        ins = Ins(eng, fn, dma)
        deps = {id(d): d for d in force}

        def same(d):
            return (not d.dma) and (not dma) and d.eng == eng

        def readers(t):
            seen = set()
            for r in reversed(t.rs):
                if r.dma:
                    yield r
                elif r.eng not in seen:
                    seen.add(r.eng)
                    if not (r.eng == eng and eng == "pe" and not dma):
                        yield r

        for t in R:
            d = t.w
            if d is not None and not (same(d) and eng == "pe"):
                deps[id(d)] = d
            if t.excl:
                for r in readers(t):
                    if not same(r):
                        deps[id(r)] = r
        for t in W:
            d = t.w
            if d is not None and not (same(d) and eng == "pe"):
                deps[id(d)] = d
            for r in readers(t):
                deps[id(r)] = r
        for t in R:
            t.rs.append(ins)
        for t in W:
            t.w = ins
            t.rs = []
        if self.pending[eng] and not self.nobar:
            for d in self.pending[eng]:
                deps[id(d)] = d
            self.pending[eng] = []
        ins.deps = list(deps.values())
        for d in ins.deps:
            d.signal = True
        if dma:
            ins.signal = True
            self.dmas.append(ins)
        self.streams[eng].append(ins)
        return ins

    def dma(self, q, out, in_, R=(), W=(), **kw):
        return self.op(q, lambda e: e.dma_start(out=out, in_=in_, **kw), R=R, W=W, dma=True)

    def barrier(self):
        deps = [st[-1] for st in self.streams.values() if st] + self.dmas
        for d in deps:
            d.signal = True
        for e in self.pending:
            self.pending[e] = self.pending[e] + deps
        self.dmas = []

    def emit(self):
        nc = self.nc
        es = self.es
        sem_dma = {q: [es.enter_context(nc.semaphore(f"semd_{q}{i}")) for i in range(DMA_K)]
                   for q in ("sp", "act", "pool")}
        for e, st in self.streams.items():
            cnt = 0
            nd = 0
            sem = None
            for ins in st:
                if ins.dma:
                    j = nd % DMA_K
                    rnd = nd // DMA_K
                    ins.ev = (sem_dma[e][j], 16 * (rnd + 1))
                    ins.prev_ev = (sem_dma[e][j], 16 * rnd) if rnd > 0 else None
                    nd += 1
                elif ins.signal:
                    if cnt % EPOCH == 0:
                        sem = es.enter_context(nc.semaphore(f"sem_{e}{cnt // EPOCH}"))
                    cnt += 1
                    ins.ev = (sem, (cnt - 1) % EPOCH + 1)
        self.sigcount = {e: sum(1 for i in st if (not i.dma) and i.signal) for e, st in self.streams.items()}
        final_dma = []
        for q in ("sp", "act", "pool"):
            last = {}
            for ins in self.streams[q]:
                if ins.dma:
                    last[id(ins.ev[0])] = ins.ev
            final_dma += list(last.values())

        def run(engobj, st, is_sp):
            waited = {}

            def wait(ev):
                sem, val = ev
                k = id(sem)
                if waited.get(k, 0) < val:
                    engobj.wait_ge(sem, val)
                    waited[k] = val

            for ins in st:
                for d in ins.deps:
                    wait(d.ev)
                if ins.dma and ins.prev_ev is not None:
                    wait(ins.prev_ev)
                bi = ins.fn(engobj)
                if ins.dma:
                    bi.then_inc(ins.ev[0], 16)
                elif ins.signal:
                    bi.then_inc(ins.ev[0], 1)
            if is_sp:
                for ev in final_dma:
                    wait(ev)

        block = es.enter_context(nc.Block())
        S = self.streams

        @block.tensor
        def _(e):
            run(e, S["pe"], False)

        @block.scalar
        def _(e):
            run(e, S["act"], False)

        @block.vector
        def _(e):
            run(e, S["dve"], False)

        @block.gpsimd
        def _(e):
            run(e, S["pool"], False)

        @block.sync
        def _(e):
            run(e, S["sp"], True)

    def close(self):
        self.es.close()


class Arena:
    def __init__(self, P, nwords):
        self.t = P.sb([128, nwords], F32, "arena")
        self.n = nwords
        self.off = 0
        self.peak = 0

    def _take(self, words):
        o = self.off
        self.off += words
        self.peak = max(self.peak, self.off)
        assert self.off <= self.n, f"arena overflow {self.off} > {self.n}"
        return o

    def f32(self, *shape):
        n = int(np.prod(shape))
        o = self._take(n)
        ap = self.t[:, o:o + n]
        return self._shape(ap, shape)

    def bf16(self, *shape):
        n = int(np.prod(shape))
        w = (n + 1) // 2
        o = self._take(w)
        ap = self.t[:, o:o + w].bitcast(BF16)
        if 2 * w != n:
            ap = ap[:, 0:n]
        return self._shape(ap, shape)

    @staticmethod
    def _shape(ap, shape):
        if len(shape) == 1:
            return ap
        if len(shape) == 2:
            return ap.rearrange("p (a b) -> p a b", b=shape[1])
        if len(shape) == 3:
            return ap.rearrange("p (a b c) -> p a b c", b=shape[1], c=shape[2])
        raise ValueError(shape)

    def mark(self):
        return self.off

    def reset(self, m):
        self.off = m


class Rot:
    def __init__(self, items):
        self.items = [(it, Tok()) for it in items]
        self.i = 0

    def next(self):
        r = self.items[self.i % len(self.items)]
        self.i += 1
        return r


class Cfg:
    pass


def make_cfgs():
    p = Cfg()
    p.T, p.TT, p.NT, p.C, p.NCH, p.nseq, p.L, p.tab0, p.XB, p.krblk0 = 2048, 512, 4, 128, 4, 1, 512, 0, 128, 0
    p.prompt = True
    p.x0, p.xt0 = 0, 0
    s = Cfg()
    s.T, s.TT, s.NT, s.C, s.NCH, s.nseq, s.L, s.tab0, s.XB, s.krblk0 = 64, 64, 1, 16, 4, 4, 16, 2048, 64, 16
    s.prompt = False
    s.x0, s.xt0 = 2048, 4
    return p, s


def const_tables():
    half = 32
    freqs = 10000.0 ** (-np.arange(half, dtype=np.float64) / half)
    pos = np.concatenate([np.arange(2048), 4096 + (np.arange(64) % 16)]).astype(np.float64)
    ang = pos[None, :] * freqs[:, None]
    cos = np.cos(ang)
    sin = np.sin(ang)
    p = np.arange(128)
    d = p % 64
    cosF = cos[d % 32, :].astype(np.float32)
    sinF = np.where((d < 32)[:, None], -sin[d % 32, :], sin[d % 32, :]).astype(np.float32)
    kc = np.zeros((128, 17, 64), np.float32)
    ks = np.zeros((128, 17, 64), np.float32)
    for blk in range(17):
        if blk < 16:
            pp = (blk * 128 + p).astype(np.float64)
        else:
            pp = (4096 + (p % 16)).astype(np.float64)
        a = pp[:, None] * freqs[None, :]
        c, s = np.cos(a), np.sin(a)
        kc[:, blk, :32] = c
        kc[:, blk, 32:] = c
        ks[:, blk, :32] = -s
        ks[:, blk, 32:] = s
    def ret_tabs(C, TT):
        nref = (C - 1) // 2 + 1
        E = np.zeros((128, 2, 2, TT), np.float32)
        cst = np.zeros((128, 2, 3), np.float32)
        i = np.arange(TT) % C
        for kcx in range(2):
            h = 2 * kcx + p // 64
            lg = np.log1p(-np.exp2(-5.0 - h.astype(np.float64)))
            E[:, kcx, 0, :] = np.exp((i[None, :] + 1 - nref) * lg[:, None])
            E[:, kcx, 1, :] = np.exp((nref - i[None, :] - 1) * lg[:, None]) * (64 ** -0.5)
            cst[:, kcx, 0] = np.exp(nref * lg)
            cst[:, kcx, 1] = np.exp((C - nref) * lg)
            cst[:, kcx, 2] = np.exp(C * lg)
        return E, cst
    Ep, cp = ret_tabs(128, 512)
    Es, cs = ret_tabs(16, 64)
    mask = (np.arange(128)[:, None] <= np.arange(128)[None, :]).astype(np.float32)
    mask4 = np.tile(mask, (1, 4))
    resetp = np.ones((128, 512), np.float32)
    resetp[:, ::128] = 0.0
    resets = np.ones((128, 64), np.float32)
    resets[:, ::16] = 0.0
    ident = np.eye(128, dtype=np.float32)
    return dict(cosF=cosF, sinF=sinF, krc=kc, krs_=ks, retEp=Ep, retcp=cp, retEs=Es, retcs=cs,
                mask4=mask4, resetp=resetp, resets=resets, ident=ident)


_IN_SHAPES = dict(
    xp=[2, 2048, 1024], xs=[64, 1024], sgla=[4, 256, 128], sret=[4, 256, 128],
    cckv=[4, 4096, 512], ckr=[4, 4096, 64], sconv=[16, 2816], nrm=[5, 1024],
    w_in_ab=[1024, 3088], w_ab_sw=[1024, 512], wgu=[16, 256], bgate=[1, 256], ggla=[1, 512],
    gret=[1, 512], w_out_ab=[1024, 1024], w_in_c=[1024, 960], gq=[1, 384], gkv=[1, 512],
    wuq_n=[384, 1024], wuq_r=[384, 512], wuq_rs=[384, 512], wuk=[512, 1024], wukT=[128, 8, 512],
    wuv=[512, 1024], w_out_c=[1024, 1024], wffi=[2, 1024, 5632], dwc=[8, 2816],
    wffo=[2, 2816, 1024],
    cosF=[128, 2112], sinF=[128, 2112], krc=[128, 17, 64], krs_=[128, 17, 64],
    retEp=[128, 2, 2, 512], retcp=[128, 2, 3], retEs=[128, 2, 2, 64], retcs=[128, 2, 3],
    mask4=[128, 512], resetp=[128, 512], resets=[128, 64], ident=[128, 128],
)
_OUT_SHAPES = dict(
    yp=[2, 2048, 1024], ys=[64, 1024], glap=[2, 256, 128], glas=[4, 256, 128],
    retp=[2, 256, 128], rets=[4, 256, 128], ckvp=[2, 2048, 512], ckvs=[64, 512],
    krp=[2, 2048, 64], krs=[64, 64], convp=[2, 2, 2, 2816], convs=[2, 4, 2, 2816],
)


def build_program():
    nc = bass.Bass("TRN2", target_bir_lowering=False)
    D = {k: nc.dram_tensor(k, list(v), F32, kind="ExternalInput").ap() for k, v in _IN_SHAPES.items()}
    O = {k: nc.dram_tensor(k, list(v), F32, kind="ExternalOutput").ap() for k, v in _OUT_SHAPES.items()}
    P = Prog(nc)
    A = Arena(P, 52900)
    PB = [P.ps([128, 512], F32, "bank") for _ in range(8)]
    TPB = [Tok(excl=True) for _ in range(8)]
    gen = {"banks": [0, 1, 2, 3, 4, 5], "i": 0}

    def gb():
        b = gen["banks"][gen["i"] % len(gen["banks"])]
        gen["i"] += 1
        return PB[b], TPB[b]

    def bfv(bank):
        return bank[:].bitcast(BF16)

    def mm(out, lhsT, rhs, R, W, st=True, sp=True, force=()):
        return P.op("pe", lambda e: e.matmul(out, lhsT, rhs, start=st, stop=sp), R, W, force=force)

    def tr(out, in_, idn, R, W):
        P.op("pe", lambda e: e.transpose(out, in_, idn), R, W)

    def act(out, in_, func, R, W, bias=None, scale=None, accum=None):
        kw = {}
        if bias is not None:
            kw["bias"] = bias
        if scale is not None:
            kw["scale"] = scale
        if accum is not None:
            kw["accum_out"] = accum
        P.op("act", lambda e: e.activation(out=out, in_=in_, func=func, **kw), R, W)

    def tt(out, a, b, op, R, W, eng="dve"):
        P.op(eng, lambda e: e.tensor_tensor(out=out, in0=a, in1=b, op=op), R, W)

    def ts(out, a, s1, op0, R, W, s2=None, op1=None, eng="dve"):
        if op1 is None:
            P.op(eng, lambda e: e.tensor_scalar(out, a, s1, None, op0=op0), R, W)
        else:
            P.op(eng, lambda e: e.tensor_scalar(out, a, s1, s2, op0=op0, op1=op1), R, W)

    def stt(out, in0, scalar, in1, op0, op1, R, W, eng="dve"):
        P.op(eng, lambda e: e.scalar_tensor_tensor(out=out, in0=in0, scalar=scalar, in1=in1,
                                                    op0=op0, op1=op1), R, W)

    def cp(out, in_, R, W, eng="dve"):
        if eng == "act":
            act(out, in_, AF.Copy, R, W)
        else:
            P.op(eng, lambda e: e.tensor_copy(out=out, in_=in_), R, W)

    def memset(ap, val, W, eng="dve"):
        P.op(eng, lambda e: e.memset(ap, val), (), W)

    def load(dst, src, q="sp"):
        t = Tok()
        P.dma(q, dst, src, W=[t])
        return t

    MUL, ADD, SUB = ALU.mult, ALU.add, ALU.subtract

    xT = A.f32(8, 2048 + 64)
    t_x = [Tok() for _ in range(5)]
    ident_f = A.f32(128)
    ident_b = A.bf16(128)
    ones_b = A.bf16(128)
    gn = A.f32(8, 5)
    dw = A.f32(NJ, 8)
    gqc = A.f32(3)
    nbg = A.f32(2)
    t_idf = load(ident_f, D["ident"])
    t_idb = load(ident_b, D["ident"], "pool")
    t_one = Tok()
    memset(ones_b, 1.0, [t_one])
    t_const = [t_idf, t_idb, t_one]
    PSQ = Rot([A.bf16(512) for _ in range(2)])
    PRS, t_PRS = A.f32(512), Tok()
    PHT, t_PHT = A.bf16(8, 512), Tok()
    PWA, t_PWA, t_PWA2 = A.bf16(8, 512), Tok(), Tok()
    base_mark = A.mark()

    def rows_to_cols(src_rows, R_, n, dst, t_dst):
        m = A.mark()
        stage = A.f32(n * 128)
        t_s = load(stage[:R_, :], src_rows)
        bank, tb = gb()
        for c in range(n):
            tr(bank[:, c * R_:(c + 1) * R_], stage[:R_, c * 128:(c + 1) * 128], ident_f[:R_, :R_],
               [t_s, t_idf], [tb])
        if len(dst.shape) == 3:
            cp(dst, bank[:, 0:n * R_].rearrange("p (a b) -> p a b", b=R_), [tb], [t_dst])
        else:
            cp(dst, bank[:, 0:n * R_], [tb], [t_dst])
        P.barrier()
        A.reset(m)

    t_gn, t_dw, t_gq, t_nbg = Tok(), Tok(), Tok(), Tok()
    rows_to_cols(D["nrm"], 5, 8, gn, t_gn)
    rows_to_cols(D["dwc"], 8, NJ, dw, t_dw)
    rows_to_cols(D["gq"], 1, 3, gqc, t_gq)
    rows_to_cols(D["bgate"], 1, 2, nbg, t_nbg)
    ts(nbg, nbg, -1.0, MUL, [t_nbg], [t_nbg])

    def load_x(cfg, xd):
        m = A.mark()
        xin = Rot([A.f32(1024) for _ in range(2)])
        n = cfg.XB
        for blk in range(cfg.T // n):
            xi, txi = xin.next()
            P.dma("sp", xi[:n, :], xd[blk * n:(blk + 1) * n, :], W=[txi])
            ttile = (blk * n) // cfg.TT
            for half in range(2):
                bank, tb = gb()
                for c4 in range(4):
                    c = half * 4 + c4
                    tr(bank[:, c4 * n:(c4 + 1) * n], xi[:n, c * 128:(c + 1) * 128], ident_f[:n, :n],
                       [txi, t_idf], [tb])
                src = bank[:, 0:4 * n].rearrange("p (a b) -> p a b", b=n)
                dst = xT[:, half * 4:(half + 1) * 4, cfg.x0 + blk * n:cfg.x0 + (blk + 1) * n]
                cp(dst, src, [tb], [t_x[cfg.xt0 + ttile]], eng=("act" if half == 0 else "dve"))
        P.barrier()
        A.reset(m)

    def rmsnorm(cfg, tti, grow, hdst, t_h, sqrot, _unused, rs, t_rs):
        n = cfg.TT
        cols = slice(cfg.x0 + tti * n, cfg.x0 + (tti + 1) * n)
        t_xt = t_x[cfg.xt0 + tti]
        bank, tb = gb()
        for c in range(8):
            sqb, t_sq = sqrot.next()
            act(sqb[:, :n], xT[:, c, cols], AF.Square, [t_xt], [t_sq])
            mm(bank[:, :n], ones_b, sqb[:, :n], [t_sq, t_one], [tb], st=(c == 0), sp=(c == 7))
        act(rs[:, :n], bank[:, :n], AF.Ln, [tb], [t_rs], scale=1.0 / 1024, bias=EPS)
        act(rs[:, :n], rs[:, :n], AF.Exp, [t_rs], [t_rs], scale=-0.5)
        for c in range(8):
            stt(hdst[:, c, :n], xT[:, c, cols], gn[:, c, grow:grow + 1], rs[:, :n], MUL, MUL,
                [t_xt, t_rs, t_gn], [t_h])

    def rstd_small(dst, src, inv_n, R, W):
        act(dst, src, AF.Ln, R, W, scale=inv_n, bias=EPS)
        act(dst, dst, AF.Exp, W, W, scale=-0.5)

    def phase_mixer_ab(cfg, seq, s0_gla, s0_ret, out_gla, out_ret):
        m = A.mark()
        n, C, NCH = cfg.TT, cfg.C, cfg.NCH
        refi = (C - 1) // 2
        P.nobar = True
        rmsnorm(cfg, 0, 0, PHT, t_PHT, PSQ, None, PRS, t_PRS)
        P.dma("pool", PWA, D["w_in_ab"][:, 512:1024].rearrange("(kc p) c -> p kc c", p=128), W=[t_PWA, t_PWA2])
        P.nobar = False
        retE = A.f32(2, 2, n)
        retc = A.f32(2, 3)
        mask4 = A.f32(512)
        reset = A.f32(n)
        gglab = A.f32(512)
        gretb = A.f32(512)
        wgu = A.f32(256)
        Wlo = A.bf16(8, 16)
        t_tab = [load(retE, D["retEp"] if cfg.prompt else D["retEs"]),
                 load(retc, D["retcp"] if cfg.prompt else D["retcs"]),
                 load(mask4, D["mask4"]),
                 load(reset, D["resetp"] if cfg.prompt else D["resets"]),
                 load(gglab, D["ggla"].to_broadcast([128, 512])),
                 load(gretb, D["gret"].to_broadcast([128, 512])),
                 load(wgu[:16, :], D["wgu"]),
                 load(Wlo, D["w_in_ab"][:, 1536:1552].rearrange("(kc p) c -> p kc c", p=128), "pool")]
        cosr = Rot([A.f32(n) for _ in range(1)])
        sinr = Rot([A.f32(n) for _ in range(1)])
        sq, t_sq = PSQ, None
        rs, t_rs = PRS, t_PRS
        hT2, t_h2 = [PHT, A.bf16(8, n)], [t_PHT, Tok()]
        WA = Rot([A.bf16(8, 512) for _ in range(2 if cfg.prompt else 5)])
        WA.items.insert(0, (PWA, t_PWA))
        vtok = [A.bf16(NCH, 512), A.bf16(NCH, 512)]
        t_v = [[Tok() for _ in range(NCH)] for _ in range(2)]
        gs = [A.bf16(NCH, 512), A.bf16(NCH, 512)]
        t_gs = [[Tok() for _ in range(NCH)] for _ in range(2)]
        srt = Rot([A.f32(512) for _ in range(1)])
        loT, t_lo = A.f32(n), Tok()
        ebuf, t_e = A.f32(n), Tok()
        spb, t_sp = A.f32(2, n), Tok()
        gate_region = (loT, ebuf, spb)
        cum, t_cum = A.f32(2, n), Tok()
        E1, E2 = A.f32(2, n), A.f32(2, n)
        t_E = [Tok(), Tok()]
        pbias, nbias, dd = A.f32(2, NCH), A.f32(2, NCH), A.f32(2, NCH)
        eref, elr, dec = A.f32(2, NCH), A.f32(2, NCH), A.f32(2, NCH)
        t_col = Tok()
        qrel, krel = A.bf16(4, n), A.bf16(4, n)
        t_q = [Tok() for _ in range(4)]
        t_k = [Tok() for _ in range(4)]
        r1, r2, r3 = ebuf, spb[:, 0, :], spb[:, 1, :]
        t_r = [t_e, t_sp, t_sp]
        kreltok, t_kt = A.bf16(4, 128), Tok()
        Sm2, t_sm2 = [A.bf16(8, 128), A.bf16(8, 128)], [[Tok(), Tok()], [Tok(), Tok()]]
        S, t_S = A.f32(4, 128), [Tok() for _ in range(4)]
        Sp2, t_Sp2 = [A.bf16(4, 128), A.bf16(4, 128)], [[Tok() for _ in range(4)] for _ in range(2)]
        tmpkv, t_tmp = A.f32(4, 128), [Tok() for _ in range(4)]
        mix2, t_mix2 = [A.bf16(1024), A.bf16(1024)], [Tok(), Tok()]
        tmpo, t_tmpo = A.f32(512), Tok()
        mixT, t_mixT = A.bf16(8, n), Tok()
        gen["banks"] = [0, 1, 2, 3]
        obank2 = [[(PB[6], TPB[6]), (PB[7], TPB[7])], [(PB[4], TPB[4]), (PB[5], TPB[5])]]
        stat2, t_ss2, t_st2 = [A.f32(16), A.f32(16)], [[Tok() for _ in range(12)] for _ in range(2)], [Tok(), Tok()]
        junk8 = A.bf16(8, 128)

        def wpiece(src):
            w, tw = WA.next()
            ncol = src.shape[1]
            P.dma("pool", w[:, :, :ncol], src.rearrange("(kc p) c -> p kc c", p=128),
                  W=([tw, t_PWA2] if tw is t_PWA else [tw]))
            return w, tw

        if cfg.prompt:
            for u in range(4):
                memset(S[:, u, :], 0.0, [t_S[u]])

        for tti in range(cfg.NT):
            cols = slice(tti * n, (tti + 1) * n)
            tcol = slice(cfg.tab0 + tti * n, cfg.tab0 + (tti + 1) * n)
            cos_t, t_cos = cosr.next()
            sin_t, t_sin = sinr.next()
            P.dma("sp", cos_t, D["cosF"][:, tcol], W=[t_cos])
            P.dma("sp", sin_t, D["sinF"][:, tcol], W=[t_sin])
            hT, t_h = hT2[tti % 2], t_h2[tti % 2]
            ck(1)
            bank, tb = gb()
            for kc in range(8):
                mm(bank[:16, :n], Wlo[:, kc, :], hT[:, kc, :n], [t_h, t_tab[7]], [tb], st=(kc == 0), sp=(kc == 7))
            cp(loT[:16, :], bank[:16, :n], [tb], [t_lo], eng="act")
            for kc in range(2):
                bank, tb = gb()
                mm(bank[:, :n], wgu[:16, kc * 128:(kc + 1) * 128], loT[:16, :], [t_lo, t_tab[6]], [tb])
                act(ebuf, bank[:, :n], AF.Exp, [tb, t_nbg], [t_e], scale=-1.0, bias=nbg[:, kc:kc + 1])
                act(spb[:, kc, :], ebuf, AF.Ln, [t_e], [t_sp], bias=1.0)
                P.op("dve", lambda e, kc=kc: e.tensor_tensor_scan(out=cum[:, kc, :], data0=reset, data1=spb[:, kc, :],
                                                                   initial=0.0, op0=MUL, op1=ADD),
                     [t_sp, t_tab[3]], [t_cum])
            ck(3)
            for pi, c0 in enumerate((512, 1024, 2064, 2576)):
                if tti == 0 and pi == 0:
                    w, tw = WA.next()
                else:
                    w, tw = wpiece(D["w_in_ab"][:, c0:c0 + 512])
                grp = pi // 2
                for ci in range(NCH):
                    bank, tb = gb()
                    for kc in range(8):
                        mm(bank[:C, :], hT[:, kc, ci * C:(ci + 1) * C], w[:, kc, :], [t_h, tw], [tb],
                           st=(kc == 0), sp=(kc == 7))
                    if pi % 2 == 0:
                        cp(vtok[grp][:C, ci, :], bank[:C, :], [tb], [t_v[grp][ci]], eng="act")
                    else:
                        sr, tsr = srt.next()
                        act(sr[:C, :], bank[:C, :], AF.Silu, [tb], [tsr])
                        tt(gs[grp][:C, ci, :], sr[:C, :], (gglab if grp == 0 else gretb)[:C, :], MUL,
                           [tsr, t_tab[4 + grp]], [t_gs[grp][ci]])
            ck(2)
            cview = cum.rearrange("p k (c i) -> p k c i", i=C)
            cref = cview[:, :, :, refi]
            clast = cview[:, :, :, C - 1]
            ts(pbias, cref, 1.0 / 16, MUL, [t_cum], [t_col])
            ts(nbias, cref, -1.0 / 16, MUL, [t_cum], [t_col])
            tt(dd, clast, cref, SUB, [t_cum], [t_col])
            act(eref, cref, AF.Exp, [t_cum], [t_col], scale=-1.0 / 16)
            act(elr, dd, AF.Exp, [t_col], [t_col], scale=-1.0 / 16)
            act(dec, clast, AF.Exp, [t_cum], [t_col], scale=-1.0 / 16)
            for kc in range(2):
                for ci in range(NCH):
                    cs_ = slice(ci * C, (ci + 1) * C)
                    act(E1[:, kc, cs_], cum[:, kc, cs_], AF.Exp, [t_cum, t_col], [t_E[0]],
                        scale=-1.0 / 16, bias=pbias[:, kc, ci:ci + 1])
                    act(E2[:, kc, cs_], cum[:, kc, cs_], AF.Exp, [t_cum, t_col], [t_E[1]],
                        scale=1.0 / 16, bias=nbias[:, kc, ci:ci + 1])
            ck(4)
            w, tw = wpiece(D["w_in_ab"][:, 0:512])
            for j in range(4):
                bank, tb = gb()
                for kc in range(8):
                    mm(bank[:, :n], w[:, kc, j * 128:(j + 1) * 128], hT[:, kc, :n], [t_h, tw], [tb],
                       st=(kc == 0), sp=(kc == 7))
                kcx = j % 2
                if j < 2:
                    stt(qrel[:, kcx, :], bank[:, :n], 0.125, E1[:, kcx, :], MUL, MUL, [tb, t_E[0]], [t_q[kcx]])
                else:
                    tt(krel[:, kcx, :], bank[:, :n], E2[:, kcx, :], MUL, [tb, t_E[1]], [t_k[kcx]])
            w1, tw1 = wpiece(D["w_in_ab"][:, 1552:2064])
            w2, tw2 = wpiece(D["w_ab_sw"])
            for j in range(4):
                bank1, tb1 = gb()
                for kc in range(8):
                    mm(bank1[:, :n], w1[:, kc, j * 128:(j + 1) * 128], hT[:, kc, :n], [t_h, tw1], [tb1],
                       st=(kc == 0), sp=(kc == 7))
                bank2, tb2 = gb()
                for kc in range(8):
                    mm(bank2[:, :n], w2[:, kc, j * 128:(j + 1) * 128], hT[:, kc, :n], [t_h, tw2], [tb2],
                       st=(kc == 0), sp=(kc == 7))
                kcx = j % 2
                tt(r1, bank1[:, :n], cos_t, MUL, [tb1, t_cos], [t_r[0]])
                tt(r2, bank2[:, :n], sin_t, MUL, [tb2, t_sin], [t_r[1]])
                tt(r3, r1, r2, ADD, [t_r[0], t_r[1]], [t_r[2]])
                if j < 2:
                    tt(qrel[:, 2 + kcx, :], r3, retE[:, kcx, 0, :], MUL, [t_r[2], t_tab[0]], [t_q[2 + kcx]])
                else:
                    tt(krel[:, 2 + kcx, :], r3, retE[:, kcx, 1, :], MUL, [t_r[2], t_tab[0]], [t_k[2 + kcx]])
            ck(5)
            if tti + 1 < cfg.NT:
                rmsnorm(cfg, tti + 1, 0, hT2[(tti + 1) % 2], t_h2[(tti + 1) % 2], sq, t_sq, rs, t_rs)

            def chunk_front(ci):
                    Sm, t_sm, Sp, t_Sp = Sm2[ci % 2], t_sm2[ci % 2], Sp2[ci % 2], t_Sp2[ci % 2]
                    g = tti * NCH + ci
                    cs_ = slice(ci * C, (ci + 1) * C)
                    first = (not cfg.prompt) or g == 0
                    last = (not cfg.prompt) or g == cfg.NT * NCH - 1
                    if not cfg.prompt:
                        for u in range(4):
                            src = (s0_gla if u < 2 else s0_ret)[ci, (u % 2) * 128:(u % 2 + 1) * 128, :]
                            P.dma("sp", S[:, u, :], src, W=[t_S[u]])
                    bank, tb = gb()
                    bv = bfv(bank)
                    for u in range(4):
                        tr(bv[:C, u * 128:(u + 1) * 128], krel[:, u, cs_], ident_b, [t_k[u], t_idb], [tb])
                    cp(kreltok[:C, :, :], bv[:C, 0:512].rearrange("p (a b) -> p a b", b=128), [tb], [t_kt], eng="act")
                    for u in range(4):
                        sc = eref[:, u, ci:ci + 1] if u < 2 else retc[:, u - 2, 0:1]
                        act(Sp[:, u, :], S[:, u, :], AF.Copy, [t_S[u], t_col, t_tab[1]], [t_Sp[u]], scale=sc)
                    for half in range(2):
                        bank, tb = gb()
                        for uu in range(2):
                            u = half * 2 + uu
                            grp, kcx = u // 2, u % 2
                            mm(bank[:, uu * 256:(uu + 1) * 256], kreltok[:C, u, :],
                               vtok[grp][:C, ci, kcx * 256:(kcx + 1) * 256], [t_kt, t_v[grp][ci]], [tb])
                        for uu in range(2):
                            u = half * 2 + uu
                            kcx = u % 2
                            e_lr = elr[:, kcx, ci:ci + 1] if u < 2 else retc[:, kcx, 1:2]
                            e_dc = dec[:, kcx, ci:ci + 1] if u < 2 else retc[:, kcx, 2:3]
                            ts(tmpkv[0:64, u, :], bank[0:64, uu * 256:uu * 256 + 128], e_lr[0:64, :], MUL,
                               [tb, t_col, t_tab[1]], [t_tmp[u]])
                            ts(tmpkv[64:128, u, :], bank[64:128, uu * 256 + 128:uu * 256 + 256], e_lr[64:128, :], MUL,
                               [tb, t_col, t_tab[1]], [t_tmp[u]])
                            stt(S[:, u, :], S[:, u, :], e_dc, tmpkv[:, u, :], MUL, ADD,
                                [t_S[u], t_tmp[u], t_col, t_tab[1]], [t_S[u]])
                            if last:
                                dst = (out_gla if u < 2 else out_ret)
                                dsti = seq if cfg.prompt else ci
                                P.dma("sp", dst[dsti, kcx * 128:(kcx + 1) * 128, :], S[:, u, :], R=[t_S[u]])
                    v3 = lambda ap: ap.rearrange("p (h c) -> p h c", c=128)[:C, :, :C]
                    for par in range(2):
                        bank, tb = gb()
                        pr = slice(par * 64, par * 64 + 64)
                        for slot in range(4):
                            grp = slot // 2
                            u = grp * 2 + slot % 2
                            mm(bank[:C, slot * 128:slot * 128 + C], krel[pr, u, cs_], qrel[pr, u, cs_],
                               [t_k[u], t_q[u]], [tb])
                        tt(Sm[:C, par * 4:(par + 1) * 4, :C], v3(bank[:, :]), v3(mask4[:, :]), MUL,
                           [tb, t_tab[2]], [t_sm[par]])
            def chunk_back(ci):
                    Sm, t_sm, Sp, t_Sp = Sm2[ci % 2], t_sm2[ci % 2], Sp2[ci % 2], t_Sp2[ci % 2]
                    obank = obank2[ci % 2]
                    (oA, t_oA), (oB, t_oB) = obank
                    mix, t_mix = mix2[ci % 2], t_mix2[ci % 2]
                    cs_ = slice(ci * C, (ci + 1) * C)
                    for grp in range(2):
                        ob, tob = obank[grp]
                        for hh in range(4):
                            u = grp * 2 + hh // 2
                            par = hh % 2
                            pr = slice(par * 64, par * 64 + 64)
                            smi = par * 4 + grp * 2 + hh // 2
                            i1 = mm(ob[:C, hh * 128:(hh + 1) * 128], Sm[:C, smi, :C],
                                    vtok[grp][:C, ci, hh * 128:(hh + 1) * 128], [t_sm[par], t_v[grp][ci]], [tob],
                                    st=True, sp=False)
                            mm(ob[:C, hh * 128:(hh + 1) * 128], qrel[pr, u, cs_], Sp[pr, u, :],
                               [t_q[u], t_Sp[u]], [tob], st=False, sp=True, force=([i1] if C < 64 else ()))
                    stat, t_ss, t_st = stat2[ci % 2], t_ss2[ci % 2], t_st2[ci % 2]
                    for grp in range(2):
                        ob, tob = obank[grp]
                        for hh in range(4):
                            k8 = grp * 4 + hh
                            act(junk8[:C, k8, :], ob[:C, hh * 128:(hh + 1) * 128], AF.Square, [tob], [t_ss[k8]],
                                accum=stat[:C, k8:k8 + 1])
                    for hh in range(4):
                        act(junk8[:C, hh, :], oB[:C, hh * 128:(hh + 1) * 128], AF.Copy, [t_oB, t_ss[hh]], [t_ss[hh], t_ss[8 + hh]],
                            accum=stat[:C, 8 + hh:9 + hh])
                    stt(stat[:C, 12:16], stat[:C, 8:12], -1.0 / 128, stat[:C, 8:12], MUL, MUL, [t_st] + t_ss[8:12], [t_st])
                    tt(stat[:C, 4:8], stat[:C, 4:8], stat[:C, 12:16], ADD, [t_st] + t_ss[4:8], [t_st])
                    ts(stat[:C, 8:12], stat[:C, 8:12], 1.0 / 128, MUL, [t_st] + t_ss[8:12], [t_st] + t_ss[8:12])
            def chunk_back2(ci):
                    obank = obank2[ci % 2]
                    (oA, t_oA), (oB, t_oB) = obank
                    mix, t_mix = mix2[ci % 2], t_mix2[ci % 2]
                    stat, t_ss, t_st = stat2[ci % 2], t_ss2[ci % 2], t_st2[ci % 2]
                    rstd_small(stat[:C, 0:8], stat[:C, 0:8], 1.0 / 128, [t_st] + t_ss[0:4], [t_st])
                    for hh in range(4):
                        hs = slice(hh * 128, (hh + 1) * 128)
                        stt(mix[:C, hs], oA[:C, hs], stat[:C, hh:hh + 1], gs[0][:C, ci, hs], MUL, MUL,
                            [t_oA, t_st, t_gs[0][ci]], [t_mix])
                        ts(tmpo[:C, hs], oB[:C, hs], stat[:C, 8 + hh:9 + hh], SUB, [t_oB, t_st], [t_tmpo],
                           s2=stat[:C, 4 + hh:5 + hh], op1=MUL)
                    tt(mix[:C, 512:1024], tmpo[:C, :], gs[1][:C, ci, :], MUL, [t_tmpo, t_gs[1][ci]], [t_mix])
            def chunk_trans(ci):
                    mix, t_mix = mix2[ci % 2], t_mix2[ci % 2]
                    cs_ = slice(ci * C, (ci + 1) * C)
                    bank, tb = gb()
                    bv = bfv(bank)
                    for c in range(8):
                        tr(bv[:, c * 128:c * 128 + C], mix[:C, c * 128:(c + 1) * 128], ident_b[:C, :C],
                           [t_mix, t_idb], [tb])
                    cp(mixT[:, :, cs_], bv[:, :].rearrange("p (a b) -> p a b", b=128)[:, :, :C], [tb], [t_mixT], eng="act")
            for st_ in range(NCH + 3):
                if st_ < NCH:
                    chunk_front(st_)
                if 1 <= st_ <= NCH:
                    chunk_back(st_ - 1)
                if 2 <= st_ <= NCH + 1:
                    chunk_back2(st_ - 2)
                if st_ >= 3:
                    chunk_trans(st_ - 3)
            ck(10)
            for half in range(2):
                w, tw = wpiece(D["w_out_ab"][:, half * 512:(half + 1) * 512])
                for o4 in range(4):
                    oc = half * 4 + o4
                    bank, tb = gb()
                    for kc in range(8):
                        mm(bank[:, :n], w[:, kc, o4 * 128:(o4 + 1) * 128], mixT[:, kc, :n], [tw, t_mixT], [tb],
                           st=(kc == 0), sp=(kc == 7))
                    xc = slice(cfg.x0 + tti * n, cfg.x0 + (tti + 1) * n)
                    tt(xT[:, oc, xc], xT[:, oc, xc], bank[:, :n], ADD, [t_x[cfg.xt0 + tti], tb], [t_x[cfg.xt0 + tti]])
        P.barrier()
        A.reset(m)

    def phase_ffn(groups, layer):
        m = A.mark()
        NH = NJ // 2
        Ttot = sum(g[0].T for g in groups)
        nmax = max(g[0].TT for g in groups)
        deep = not any(g[0].prompt for g in groups)
        hT = A.bf16(8, Ttot)
        sq, t_sq = PSQ, None
        rs, t_rs = PRS, t_PRS
        actb = A.bf16(NH, Ttot)
        WI = Rot([A.bf16(8, 256) for _ in range(7 if deep else 2)])
        WI.items.insert(0, (PWA[:, :, 0:256], t_PWA))
        ffn_pro = {}
        P.nobar = True
        cfg0 = groups[0][0]
        rmsnorm(cfg0, 0, 2 + layer, PHT[:, :, :cfg0.TT], t_PHT, PSQ, None, PRS, t_PRS)
        w0, tw0 = WI.next()
        ffn_pro["tw"] = (tw0, t_PWA2)
        for g_ in range(2):
            c0_ = g_ * D_FF
            P.dma("pool", w0[:, :, g_ * 128:(g_ + 1) * 128],
                  D["wffi"][layer, :, c0_:c0_ + 128].rearrange("(kc p) c -> p kc c", p=128), W=[ffn_pro["tw"][g_]])
        P.nobar = False
        WO = Rot([A.bf16(NH, 128) for _ in range(6 if deep else 2)])
        cbuf = Rot([A.f32(nmax) for _ in range(6 if deep else 2)])
        gbuf = Rot([A.f32(nmax) for _ in range(6 if deep else 2)])
        wi_tok = {}
        abc_tok = {}
        gen["banks"] = [0, 1, 2, 3, 4, 5, 6, 7]
        units = []
        G = []
        col = 0
        stg, t_stg = A.f32(NJ * 128), Tok()
        for cfg, carry_in_rows, conv_out in groups:
            nseq, L, n = cfg.nseq, cfg.L, cfg.TT
            g = Cfg()
            g.cfg, g.conv_out, g.c0 = cfg, conv_out, col
            g.carry, g.t_carry = A.f32(NJ, nseq * 2), [Tok() for _ in range(NJ)]
            g.abuf = Rot([A.f32(nseq * (L + 2)) for _ in range(6 if deep else 3)])
            g.t_h = [Tok() for _ in range(cfg.NT)]
            g.t_act = [[Tok() for _ in range(cfg.NT)] for _ in range(NH)]
            if carry_in_rows is None:
                for j in range(NJ):
                    memset(g.carry[:, j, :], 0.0, [g.t_carry[j]])
            else:
                R_ = nseq * 2
                stage, t_s = stg, t_stg
                P.dma("sp", stage[:R_, :], carry_in_rows, W=[t_s])
                bank, tb = gb()
                for j in range(NJ):
                    tr(bank[:, j * R_:(j + 1) * R_], stage[:R_, j * 128:(j + 1) * 128], ident_f[:R_, :R_],
                       [t_s, t_idf], [tb])
                cp(g.carry, bank[:, 0:NJ * R_].rearrange("p (a b) -> p a b", b=R_), [tb], g.t_carry)
            g.hT = []
            for tti in range(cfg.NT):
                if not units:
                    g.hT.append(PHT[:, :, :n])
                    g.t_h[tti] = t_PHT
                else:
                    g.hT.append(hT[:, :, col + tti * n:col + (tti + 1) * n])
                    rmsnorm(cfg, tti, 2 + layer, g.hT[tti], g.t_h[tti], sq, t_sq, rs, t_rs)
                units.append((g, tti))
            col += cfg.T
            G.append(g)
        wd = lambda k, j: dw[:, j, layer * 3 + k:layer * 3 + k + 1]
        bd = lambda j: dw[:, j, 6 + layer:7 + layer]
        for jh in range(2):
            for jj in range(NH):
                j = jh * NH + jj
                if j == 0:
                    w, tw = w0, ffn_pro["tw"]
                    wi_tok[id(tw0)] = tw
                else:
                    w, tw = WI.next()
                    tw = wi_tok.setdefault(id(tw), (tw, Tok()))
                    for g_ in range(2):
                        c0 = g_ * D_FF + j * 128
                        P.dma("pool", w[:, :, g_ * 128:(g_ + 1) * 128],
                              D["wffi"][layer, :, c0:c0 + 128].rearrange("(kc p) c -> p kc c", p=128), W=[tw[g_]])
                for g, tti in units:
                    cfg = g.cfg
                    nseq, L, n = cfg.nseq, cfg.L, cfg.TT
                    cols = slice(g.c0 + tti * n, g.c0 + (tti + 1) * n)
                    ba, tba = gb()
                    for kc in range(8):
                        mm(ba[:, :n], w[:, kc, 0:128], g.hT[tti][:, kc, :], [tw[0], g.t_h[tti]], [tba], st=(kc == 0), sp=(kc == 7))
                    bu, tbu = gb()
                    for kc in range(8):
                        mm(bu[:, :n], w[:, kc, 128:256], g.hT[tti][:, kc, :], [tw[1], g.t_h[tti]], [tbu], st=(kc == 0), sp=(kc == 7))
                    ab, tab_ = g.abuf.next()
                    tabc = abc_tok.setdefault(id(tab_), Tok())
                    ab3 = ab.rearrange("p (s l) -> p s l", l=L + 2)
                    cr = g.carry[:, j, :].rearrange("p (s r) -> p s r", r=2)
                    cp(ab3[:, :, 0:2], cr, [g.t_carry[j]], [tabc])
                    act(ab3[:, :, 2:L + 2], ba[:, :n].rearrange("p (s l) -> p s l", l=L), AF.Copy, [tba], [tab_])
                    cb, tcb = cbuf.next()
                    cb3 = cb[:, :n].rearrange("p (s l) -> p s l", l=L)
                    act(cb[:, :n], ba[:, :n], AF.Identity, [tba, t_dw], [tcb], scale=wd(2, j), bias=bd(j))
                    stt(cb3, ab3[:, :, 1:L + 1], wd(1, j), cb3, MUL, ADD, [tab_, tabc, tcb, t_dw], [tcb])
                    stt(cb3, ab3[:, :, 0:L], wd(0, j), cb3, MUL, ADD, [tab_, tabc, tcb, t_dw], [tcb])
                    cp(cr, ab3[:, :, L:L + 2], [tab_], [g.t_carry[j]])
                    ge, tge = gbuf.next()
                    act(ge[:, :n], cb[:, :n], AF.Gelu, [tcb], [tge])
                    tt(actb[:, jj, cols], ge[:, :n], bu[:, :n], MUL, [tge, tbu], [g.t_act[jj][tti]])
            for oc in range(8):
                w, tw = WO.next()
                P.dma("pool", w, D["wffo"][layer, jh * NH * 128:(jh + 1) * NH * 128, oc * 128:(oc + 1) * 128]
                      .rearrange("(j p) c -> p j c", p=128), W=[tw])
                for g, tti in units:
                    cfg = g.cfg
                    n = cfg.TT
                    cols = slice(g.c0 + tti * n, g.c0 + (tti + 1) * n)
                    xc = slice(cfg.x0 + tti * n, cfg.x0 + (tti + 1) * n)
                    t_xt = t_x[cfg.xt0 + tti]
                    bank, tb = gb()
                    for jj in range(NH):
                        mm(bank[:, :n], w[:, jj, :], actb[:, jj, cols], [tw, g.t_act[jj][tti]], [tb],
                           st=(jj == 0), sp=(jj == NH - 1))
                    tt(xT[:, oc, xc], xT[:, oc, xc], bank[:, :n], ADD, [t_xt, tb], [t_xt])
        for g in G:
            R_ = g.cfg.nseq * 2
            cstage, t_cs = stg, t_stg
            for q4 in range((NJ + 3) // 4):
                j0, j1 = q4 * 4, min(NJ, q4 * 4 + 4)
                bank, tb = gb()
                for j in range(j0, j1):
                    tr(bank[:R_, (j - j0) * 128:(j - j0 + 1) * 128], g.carry[:, j, :], ident_f, [g.t_carry[j], t_idf], [tb])
                cp(cstage[:R_, j0 * 128:j1 * 128], bank[:R_, 0:(j1 - j0) * 128], [tb], [t_cs])
            P.dma("sp", g.conv_out.rearrange("s r c -> (s r) c"), cstage[:R_, :], R=[t_cs])
        P.barrier()
        A.reset(m)

    SCALE = 192.0 ** -0.5

    def phase_mla(cfg, seq, ckv_out, kr_out, cache_ckv=None, cache_kr=None):
        m = A.mark()
        n, C, NCH, T = cfg.TT, cfg.C, cfg.NCH, cfg.T
        KB = 128 if cfg.prompt else 16
        NB = T // KB
        cqnT, t_cqn = A.bf16(3, T), [Tok() for _ in range(cfg.NT)]
        ckvT, t_ckvT = A.bf16(4, T), [Tok() for _ in range(cfg.NT)]
        krT2, t_krT = A.bf16(T), [Tok() for _ in range(cfg.NT)]
        ckvn_b = None
        if not cfg.prompt:
            ckvn_b, t_cnb = A.bf16(NB, 512), [Tok() for _ in range(NB)]
        m1 = A.mark()
        P.nobar = True
        rmsnorm(cfg, 0, 1, PHT, t_PHT, PSQ, None, PRS, t_PRS)
        P.dma("pool", PWA[:, :, 0:384], D["w_in_c"][:, 0:384].rearrange("(kc p) c -> p kc c", p=128), W=[t_PWA, t_PWA2])
        P.nobar = False
        Wc = A.bf16(8, 960)
        t_wc = [t_PWA] + [load(Wc[:, :, c0:c1], D["w_in_c"][:, c0:c1].rearrange("(kc p) c -> p kc c", p=128), "pool")
                          for c0, c1 in ((384, 896), (896, 960))]
        krc, krs_ = A.f32(17, 64), A.f32(17, 64)
        gkvb = A.f32(512)
        t_t = [load(krc, D["krc"]), load(krs_, D["krs_"]), load(gkvb, D["gkv"].to_broadcast([128, 512]))]
        sq, t_sq = PSQ, None
        rs, t_rs = PRS, t_PRS
        hT2c, t_h2c = [PHT, A.bf16(8, n)], [t_PHT, Tok()]
        sq3, t_sq3 = A.bf16(3, n), Tok()
        rq, t_rq = A.f32(n), Tok()
        ckvn = Rot([A.f32(512) for _ in range(3)])
        cb16 = Rot([A.bf16(512) for _ in range(3)])
        krr = Rot([A.f32(64) for _ in range(3)])
        kt1, kt2, t_kt = A.f32(64), A.f32(64), Tok()
        kb16 = Rot([A.bf16(128) for _ in range(3)])
        st1, t_st1 = A.f32(4), Tok()
        junk, t_junk = A.bf16(512), Tok()
        gen["banks"] = [0, 1, 2, 3, 4, 5, 6, 7]
        for tti in range(cfg.NT):
            cols = slice(tti * n, (tti + 1) * n)
            hT, t_h = hT2c[tti % 2], t_h2c[tti % 2]
            cqb = []
            for j in range(3):
                bank, tb = gb()
                for kc in range(8):
                    mm(bank[:, :n], PWA[:, kc, j * 128:(j + 1) * 128], hT[:, kc, :n], [t_wc[0], t_h], [tb],
                       st=(kc == 0), sp=(kc == 7))
                act(sq3[:, j, :], bank[:, :n], AF.Square, [tb], [t_sq3])
                cqb.append((bank, tb))
            bank, tb = gb()
            for j in range(3):
                mm(bank[:, :n], ones_b, sq3[:, j, :], [t_sq3, t_one], [tb], st=(j == 0), sp=(j == 2))
            act(rq, bank[:, :n], AF.Ln, [tb], [t_rq], scale=1.0 / 384, bias=EPS)
            act(rq, rq, AF.Exp, [t_rq], [t_rq], scale=-0.5)
            for j in range(3):
                stt(cqnT[:, j, cols], cqb[j][0][:, :n], gqc[:, j:j + 1], rq, MUL, MUL,
                    [cqb[j][1], t_rq, t_gq], [t_cqn[tti]])
            if tti + 1 < cfg.NT:
                rmsnorm(cfg, tti + 1, 1, hT2c[(tti + 1) % 2], t_h2c[(tti + 1) % 2], sq, t_sq, rs, t_rs)
            def c1_proj(bi):
                blk = tti * (n // KB) + bi
                tcs = slice(bi * KB, (bi + 1) * KB)
                gcs = slice(blk * KB, (blk + 1) * KB)
                bank, tb = gb()
                for kc in range(8):
                    mm(bank[:KB, :], hT[:, kc, tcs], Wc[:, kc, 384:896], [t_h, t_wc[1]], [tb], st=(kc == 0), sp=(kc == 7))
                act(junk[:KB, :], bank[:KB, :], AF.Square, [tb, t_st1], [t_junk, t_st1], accum=st1[:KB, 0:1])
                rstd_small(st1[:KB, 0:1], st1[:KB, 0:1], 1.0 / 512, [t_st1], [t_st1])
                cn, tcn = ckvn.next()
                stt(cn[:KB, :], bank[:KB, :], st1[:KB, 0:1], gkvb[:KB, :], MUL, MUL, [tb, t_st1, t_t[2]], [tcn])
                P.dma("sp", ckv_out[gcs, :], cn[:KB, :], R=[tcn])
                if cfg.prompt:
                    c16, tc16 = cb16.next()
                else:
                    c16, tc16 = ckvn_b[:, blk, :], t_cnb[blk]
                cp(c16[:KB, :], cn[:KB, :], [tcn], [tc16], eng="act")
                bank, tb = gb()
                for kc in range(8):
                    mm(bank[:KB, 0:64], hT[:, kc, tcs], Wc[:, kc, 896:960], [t_h, t_wc[2]], [tb], st=(kc == 0), sp=(kc == 7))
                tblk = blk if cfg.prompt else 16
                tt(kt1[:KB, :], bank[:KB, 0:64], krc[:KB, tblk, :], MUL, [tb, t_t[0]], [t_kt])
                tt(kt2[:KB, 0:32], bank[:KB, 32:64], krs_[:KB, tblk, 0:32], MUL, [tb, t_t[1]], [t_kt])
                tt(kt2[:KB, 32:64], bank[:KB, 0:32], krs_[:KB, tblk, 32:64], MUL, [tb, t_t[1]], [t_kt])
                kr_, tkr = krr.next()
                tt(kr_[:KB, :], kt1[:KB, :], kt2[:KB, :], ADD, [t_kt], [tkr])
                P.dma("sp", kr_out[gcs, :], kr_[:KB, :], R=[tkr])
                k16, tk16 = kb16.next()
                cp(k16[:KB, 0:64], kr_[:KB, :], [tkr], [tk16], eng="act")
                cp(k16[:KB, 64:128], kr_[:KB, :], [tkr], [tk16], eng="act")
                return gcs, c16, tc16, k16, tk16

            def c1_trans(item):
                gcs, c16, tc16, k16, tk16 = item
                bank2, tb2 = gb()
                bv = bfv(bank2)
                for kc in range(4):
                    tr(bv[:, kc * 128:kc * 128 + KB], c16[:KB, kc * 128:(kc + 1) * 128], ident_b[:KB, :KB],
                       [tc16, t_idb], [tb2])
                tr(bv[:, 512:512 + KB], k16[:KB, :], ident_b[:KB, :KB], [tk16, t_idb], [tb2])
                cp(ckvT[:, :, gcs], bv[:, 0:512].rearrange("p (a b) -> p a b", b=128)[:, :, :KB], [tb2], [t_ckvT[tti]])
                cp(krT2[:, gcs], bv[:, 512:512 + KB], [tb2], [t_krT[tti]])

            pend = []
            for bi in range(n // KB):
                pend.append(c1_proj(bi))
                if len(pend) > 1:
                    c1_trans(pend.pop(0))
            while pend:
                c1_trans(pend.pop(0))
        P.barrier()
        A.reset(m1)
        if cfg.prompt:
            mla_prompt_c2(cfg, cqnT, t_cqn, ckvT, t_ckvT, krT2, t_krT)
        else:
            mla_sample_c2(cfg, cqnT, t_cqn, ckvT, t_ckvT, krT2, t_krT, ckvn_b, t_cnb, cache_ckv, cache_kr)
        P.barrier()
        A.reset(m)

    def mla_prompt_c2(cfg, cqnT, t_cqn, ckvT, t_ckvT, krT2, t_krT):
        n, T, NT = cfg.TT, cfg.T, cfg.NT
        qn, t_qn = [A.bf16(T), A.bf16(T)], [[Tok() for _ in range(NT)] for _ in range(2)]
        qr, t_qr = A.bf16(T), [Tok() for _ in range(NT)]
        kn, t_kn = [A.bf16(T), A.bf16(T)], [[Tok() for _ in range(NT)] for _ in range(2)]
        Vp, t_V = A.bf16(16, 256), [Tok() for _ in range(16)]
        ao2, t_ao2 = [A.bf16(2, n), A.bf16(2, n)], [Tok(), Tok()]
        pending_out = [None]
        WQ = Rot([A.bf16(3, 256) for _ in range(2)])
        WQR = Rot([A.bf16(3, 128) for _ in range(2)])
        WQS = Rot([A.bf16(3, 128) for _ in range(2)])
        WK = Rot([A.bf16(4, 256) for _ in range(2)])
        WV = Rot([A.bf16(4, 256) for _ in range(2)])
        WOo = Rot([A.bf16(2, 1024) for _ in range(2)])
        cosr = Rot([A.f32(n) for _ in range(2)])
        sinr = Rot([A.f32(n) for _ in range(2)])
        r1, r2, t_r = A.f32(n), A.f32(n), [Tok(), Tok()]
        PT = Rot([A.bf16(512) for _ in range(6)])
        rden, t_rden = A.f32(n), Tok()
        gen["banks"] = [0, 1, 2, 3]
        obk = Rot([PB[4], PB[5]])
        obk.items = [(PB[4], TPB[4]), (PB[5], TPB[5])]
        dbk = Rot([PB[6], PB[7]])
        dbk.items = [(PB[6], TPB[6]), (PB[7], TPB[7])]
        r3 = lambda src: src.rearrange("(kc p) c -> p kc c", p=128)
        for pr in range(4):
            wq, twq = WQ.next()
            P.dma("pool", wq, r3(D["wuq_n"][:, pr * 256:(pr + 1) * 256]), W=[twq])
            wqr, twqr = WQR.next()
            P.dma("pool", wqr, r3(D["wuq_r"][:, pr * 128:(pr + 1) * 128]), W=[twqr])
            wqs, twqs = WQS.next()
            P.dma("pool", wqs, r3(D["wuq_rs"][:, pr * 128:(pr + 1) * 128]), W=[twqs])
            wk, twk = WK.next()
            P.dma("pool", wk, r3(D["wuk"][:, pr * 256:(pr + 1) * 256]), W=[twk])
            wv, twv = WV.next()
            P.dma("pool", wv, r3(D["wuv"][:, pr * 256:(pr + 1) * 256]), W=[twv])
            wo, two = WOo.next()
            P.dma("pool", wo, D["w_out_c"][pr * 256:(pr + 1) * 256, :].rearrange("(h p) c -> p h c", p=128), W=[two])
            for tti in range(NT):
                cols = slice(tti * n, (tti + 1) * n)
                for hh in range(2):
                    bank, tb = gb()
                    for kc in range(3):
                        mm(bank[:, :n], wq[:, kc, hh * 128:(hh + 1) * 128], cqnT[:, kc, cols], [twq, t_cqn[tti]], [tb],
                           st=(kc == 0), sp=(kc == 2))
                    cp(qn[hh][:, cols], bank[:, :n], [tb], [t_qn[hh][tti]], eng="act")
                    bank, tb = gb()
                    for kc in range(4):
                        mm(bank[:, :n], wk[:, kc, hh * 128:(hh + 1) * 128], ckvT[:, kc, cols], [twk, t_ckvT[tti]], [tb],
                           st=(kc == 0), sp=(kc == 3))
                    cp(kn[hh][:, cols], bank[:, :n], [tb], [t_kn[hh][tti]], eng="dve")
                cos_t, t_cos = cosr.next()
                sin_t, t_sin = sinr.next()
                P.dma("sp", cos_t, D["cosF"][:, cols], W=[t_cos])
                P.dma("sp", sin_t, D["sinF"][:, cols], W=[t_sin])
                bank1, tb1 = gb()
                for kc in range(3):
                    mm(bank1[:, :n], wqr[:, kc, :], cqnT[:, kc, cols], [twqr, t_cqn[tti]], [tb1], st=(kc == 0), sp=(kc == 2))
                bank2, tb2 = gb()
                for kc in range(3):
                    mm(bank2[:, :n], wqs[:, kc, :], cqnT[:, kc, cols], [twqs, t_cqn[tti]], [tb2], st=(kc == 0), sp=(kc == 2))
                tt(r1, bank1[:, :n], cos_t, MUL, [tb1, t_cos], [t_r[0]])
                tt(r2, bank2[:, :n], sin_t, MUL, [tb2, t_sin], [t_r[1]])
                tt(qr[:, cols], r1, r2, ADD, t_r, [t_qr[tti]])
                for b4 in range(4):
                    blk = tti * 4 + b4
                    bank, tb = gb()
                    for kc in range(4):
                        mm(bank[:, 0:256], ckvT[:, kc, blk * 128:(blk + 1) * 128], wv[:, kc, :], [twv, t_ckvT[tti]], [tb],
                           st=(kc == 0), sp=(kc == 3))
                    cp(Vp[:, blk, :], bank[:, 0:256], [tb], [t_V[blk]], eng=("act" if b4 % 2 == 0 else "dve"))
            for qt in range(NT):
                qcols0 = qt * n
                ao, t_ao = ao2[qt % 2], t_ao2[qt % 2]
                for hh in range(2):
                    prs = slice(hh * 64, hh * 64 + 64)
                    ob, tob = obk.next()
                    db, tdb = dbk.next()
                    nkb = 4 * qt + 4

                    def scores(kb):
                        i = kb - 4 * qt
                        q0 = 0 if i <= 0 else i * 128
                        N = n - q0
                        qs = slice(qcols0 + q0, qcols0 + n)
                        ks = slice(kb * 128, (kb + 1) * 128)
                        bank, tb = gb()
                        mm(bank[:, :N], kn[hh][:, ks], qn[hh][:, qs], [t_kn[hh][kb // 4], t_qn[hh][qt]], [tb], st=True, sp=False)
                        mm(bank[:, :N], krT2[prs, ks], qr[prs, qs], [t_krT[kb // 4], t_qr[qt]], [tb], st=False, sp=True)
                        pt, tpt = PT.next()
                        act(pt[:, :N], bank[:, :N], AF.Exp, [tb], [tpt], scale=SCALE)
                        if i >= 0:
                            memset(pt[64:128, 0:64], 0.0, [tpt])
                        return kb, q0, N, pt, tpt

                    def pv(item):
                        kb, q0, N, pt, tpt = item
                        mm(ob[:, q0:n], Vp[:, kb, hh * 128:(hh + 1) * 128], pt[:, :N], [t_V[kb], tpt], [tob],
                           st=(kb == 0), sp=(kb == nkb - 1))
                        mm(db[:, q0:n], ones_b, pt[:, :N], [t_one, tpt], [tdb], st=(kb == 0), sp=(kb == nkb - 1))

                    pend = []
                    for kb in range(nkb):
                        pend.append(scores(kb))
                        if len(pend) > 3:
                            pv(pend.pop(0))
                        if hh == 0 and kb == 2 and pending_out[0] is not None:
                            pending_out[0]()
                            pending_out[0] = None
                    while pend:
                        pv(pend.pop(0))
                    P.op("dve", lambda e, db=db: e.reciprocal(out=rden, in_=db[:, :n]), [tdb], [t_rden])
                    tt(ao[:, hh, :], ob[:, :n], rden, MUL, [tob, t_rden], [t_ao])

                def outproj(qt=qt, ao=ao, t_ao=t_ao, wo=wo, two=two):
                    qc = slice(qt * n, qt * n + n)
                    for oc in range(8):
                        bank, tb = gb()
                        for hh in range(2):
                            mm(bank[:, :n], wo[:, hh, oc * 128:(oc + 1) * 128], ao[:, hh, :], [two, t_ao], [tb],
                               st=(hh == 0), sp=(hh == 1))
                        tt(xT[:, oc, qc], xT[:, oc, qc], bank[:, :n], ADD, [t_x[qt], tb], [t_x[qt]])
                pending_out[0] = outproj
        if pending_out[0] is not None:
            pending_out[0]()
            pending_out[0] = None

    def mla_sample_c2(cfg, cqnT, t_cqn, ckvT, t_ckvT, krT2, t_krT, ckvn_b, t_cnb, cache_ckv, cache_kr):
        T = cfg.T
        r3 = lambda src: src.rearrange("(kc p) c -> p kc c", p=128)
        WQ, WQR, WQS = A.bf16(3, 1024), A.bf16(3, 512), A.bf16(3, 512)
        WUKT, WV = A.bf16(8, 512), A.bf16(4, 1024)
        t_w = [load(WQ, r3(D["wuq_n"]), "pool"), load(WQR, r3(D["wuq_r"]), "pool"), load(WQS, r3(D["wuq_rs"]), "pool"),
               load(WUKT, D["wukT"], "pool"), load(WV, r3(D["wuv"]), "pool")]
        WOo = Rot([A.bf16(8, 128) for _ in range(2)])
        cos_t, sin_t = A.f32(T), A.f32(T)
        t_cs = [load(cos_t, D["cosF"][:, 2048:2048 + T]), load(sin_t, D["sinF"][:, 2048:2048 + T])]
        qnS, t_qnS = A.bf16(8, T), Tok()
        qrS, t_qrS = A.bf16(8, T), Tok()
        r1, r2, t_r = A.f32(T), A.f32(T), [Tok(), Tok()]
        qlat, t_ql = [A.bf16(4, 128) for _ in range(4)], [Tok() for _ in range(4)]
        CQ = Rot([A.bf16(8, 512) for _ in range(3)])
        KQ = Rot([A.bf16(8, 128) for _ in range(3)])
        kq_tok = {}
        CT = Rot([A.bf16(4, 1024) for _ in range(2)])
        KT = Rot([A.bf16(1024) for _ in range(2)])
        PT = Rot([A.bf16(128) for _ in range(5)])
        rden, t_rden = A.f32(1), Tok()
        olatn, t_on = A.bf16(512), Tok()
        olatT, t_oT = A.bf16(4, 128), Tok()
        aoS, t_ao = A.bf16(8, T), Tok()
        gen["banks"] = [0, 1, 2, 3, 4, 5]
        olb, t_olb, dnb, t_dnb = PB[6], TPB[6], PB[7], TPB[7]
        for h in range(8):
            bank, tb = gb()
            for kc in range(3):
                mm(bank[:, :T], WQ[:, kc, h * 128:(h + 1) * 128], cqnT[:, kc, :T], [t_w[0], t_cqn[0]], [tb], st=(kc == 0), sp=(kc == 2))
            cp(qnS[:, h, :], bank[:, :T], [tb], [t_qnS], eng=("act" if h % 2 else "dve"))
        for h in range(8):
            b1, tb1 = gb()
            for kc in range(3):
                mm(b1[:64, :T], WQR[:, kc, h * 64:(h + 1) * 64], cqnT[:, kc, :T], [t_w[1], t_cqn[0]], [tb1], st=(kc == 0), sp=(kc == 2))
            b2, tb2 = gb()
            for kc in range(3):
                mm(b2[:64, :T], WQS[:, kc, h * 64:(h + 1) * 64], cqnT[:, kc, :T], [t_w[2], t_cqn[0]], [tb2], st=(kc == 0), sp=(kc == 2))
            tt(r1[:64, :], b1[:64, :T], cos_t[:64, :], MUL, [tb1, t_cs[0]], [t_r[0]])
            tt(r2[:64, :], b2[:64, :T], sin_t[:64, :], MUL, [tb2, t_cs[1]], [t_r[1]])
            tt(qrS[:64, h, :], r1[:64, :], r2[:64, :], ADD, t_r, [t_qrS])
        for b in range(4):
            bank, tb = gb()
            for kc in range(4):
                for h in range(8):
                    mm(bank[:, kc * 128 + h * 16:kc * 128 + (h + 1) * 16], WUKT[:, h, kc * 128:(kc + 1) * 128],
                       qnS[:, h, b * 16:(b + 1) * 16], [t_w[3], t_qnS], [tb])
            cp(qlat[b], bank[:, :].rearrange("p (a b) -> p a b", b=128), [tb], [t_ql[b]], eng="act")
        for b in range(4):
            pend = []

            def scores(K_, lc, lr, vrows, Rk, blk):
                bank, tb = gb()
                for kc in range(4):
                    mm(bank[:K_, 0:128], lc(kc), qlat[b][:, kc, :], Rk + [t_ql[b]], [tb], st=(kc == 0), sp=False)
                mm(bank[:K_, 0:128], lr, qrS[:64, :, b * 16:(b + 1) * 16], Rk + [t_qrS], [tb], st=False, sp=True)
                pt, tpt = PT.next()
                act(pt[:K_, :], bank[:K_, 0:128], AF.Exp, [tb], [tpt], scale=SCALE)
                return K_, pt, tpt, vrows, Rk, blk

            def pv(item):
                K_, pt, tpt, vrows, Rk, blk = item
                mm(olb[:, :], pt[:K_, :], vrows, [tpt] + Rk, [t_olb], st=(blk == 0), sp=(blk == 32))
                mm(dnb[:, 0:1], pt[:K_, :], ones_b[:K_, 0:1], [tpt, t_one], [t_dnb], st=(blk == 0), sp=(blk == 32))

            def push(item):
                pend.append(item)
                if len(pend) > 2:
                    pv(pend.pop(0))

            for q4 in range(4):
                cq, tcq = CQ.next()
                kq, tkq = KQ.next()
                tkq = kq_tok.setdefault(id(tkq), (tkq, Tok()))
                P.dma("pool", cq, cache_ckv[b, q4 * 1024:(q4 + 1) * 1024, :].rearrange("(k p) l -> p k l", p=128), W=[tcq])
                for dup in range(2):
                    P.dma("pool", kq[:, :, dup * 64:(dup + 1) * 64],
                          cache_kr[b, q4 * 1024:(q4 + 1) * 1024, :].rearrange("(k p) r -> p k r", p=128), W=[tkq[dup]])
                ct, tct = CT.next()
                kt, tkt = KT.next()
                for k8 in range(8):
                    bs = slice(k8 * 128, (k8 + 1) * 128)
                    bank, tb = gb()
                    bv = bfv(bank)
                    for kc in range(4):
                        tr(bv[:, kc * 128:(kc + 1) * 128], cq[:, k8, kc * 128:(kc + 1) * 128], ident_b, [tcq, t_idb], [tb])
                    tr(bv[:, 512:640], kq[:, k8, :], ident_b, [tkq[0], tkq[1], t_idb], [tb])
                    cp(ct[:, :, bs], bv[:, 0:512].rearrange("p (a b) -> p a b", b=128), [tb], [tct],
                       eng=("act" if k8 % 2 else "dve"))
                    cp(kt[:, bs], bv[:, 512:640], [tb], [tkt], eng=("dve" if k8 % 2 else "act"))
                for k8 in range(8):
                    bs = slice(k8 * 128, (k8 + 1) * 128)
                    push(scores(128, (lambda kc, bs=bs, ct=ct: ct[:, kc, bs]), kt[0:64, bs], cq[:, k8, :],
                                [tct, tkt, tcq], q4 * 8 + k8))
            bs = slice(b * 16, (b + 1) * 16)
            push(scores(16, (lambda kc, bs=bs: ckvT[:, kc, bs]), krT2[0:64, bs], ckvn_b[:16, b, :],
                        [t_ckvT[0], t_krT[0], t_cnb[b]], 32))
            while pend:
                pv(pend.pop(0))
            P.op("dve", lambda e: e.reciprocal(out=rden, in_=dnb[:, 0:1]), [t_dnb], [t_rden])
            ts(olatn, olb[:, :], rden[:, 0:1], MUL, [t_olb, t_rden], [t_on])
            bank, tb = gb()
            bv = bfv(bank)
            for kc in range(4):
                tr(bv[:, kc * 128:(kc + 1) * 128], olatn[:, kc * 128:(kc + 1) * 128], ident_b, [t_on, t_idb], [tb])
            cp(olatT, bv[:, 0:512].rearrange("p (a b) -> p a b", b=128), [tb], [t_oT], eng="act")
            bank, tb = gb()
            for h in range(8):
                for kc in range(4):
                    mm(bank[:, h * 16:(h + 1) * 16], WV[:, kc, h * 128:(h + 1) * 128], olatT[:, kc, h * 16:(h + 1) * 16],
                       [t_w[4], t_oT], [tb], st=(kc == 0), sp=(kc == 3))
            cp(aoS[:, :, b * 16:(b + 1) * 16], bank[:, 0:128].rearrange("p (h i) -> p h i", i=16), [tb], [t_ao])
        for oc in range(8):
            wo, two = WOo.next()
            P.dma("pool", wo, D["w_out_c"][:, oc * 128:(oc + 1) * 128].rearrange("(h p) c -> p h c", p=128), W=[two])
            bank, tb = gb()
            for h in range(8):
                mm(bank[:, :T], wo[:, h, :], aoS[:, h, :], [two, t_ao], [tb], st=(h == 0), sp=(h == 7))
            tt(xT[:, oc, cfg.x0:cfg.x0 + T], xT[:, oc, cfg.x0:cfg.x0 + T], bank[:, :T], ADD, [t_x[cfg.xt0], tb], [t_x[cfg.xt0]])

    def phase_final(cfg, y_out):
        m = A.mark()
        n = cfg.XB
        gfb = A.f32(1024)
        t_g = load(gfb, D["nrm"][4:5, :].to_broadcast([128, 1024]))
        ybuf = Rot([A.f32(1024) for _ in range(2)])
        st, t_st = A.f32(4), Tok()
        junk, t_junk = A.bf16(512), Tok()
        gen["banks"] = [0, 1, 2, 3, 4, 5, 6, 7]
        for blk in range(cfg.T // n):
            cs_ = slice(blk * n, (blk + 1) * n)
            xs_ = slice(cfg.x0 + blk * n, cfg.x0 + (blk + 1) * n)
            tti = cfg.xt0 + (blk * n) // cfg.TT
            banks = []
            for half in range(2):
                bank, tb = gb()
                for c4 in range(4):
                    tr(bank[:n, c4 * 128:(c4 + 1) * 128], xT[:, half * 4 + c4, xs_], ident_f, [t_x[tti], t_idf], [tb])
                act(junk[:n, :], bank[:n, :], AF.Square, [tb, t_st], [t_junk, t_st], accum=st[:n, half:half + 1])
                banks.append((bank, tb))
            tt(st[:n, 2:3], st[:n, 0:1], st[:n, 1:2], ADD, [t_st], [t_st])
            rstd_small(st[:n, 2:3], st[:n, 2:3], 1.0 / 1024, [t_st], [t_st])
            yb, tyb = ybuf.next()
            for half in range(2):
                stt(yb[:n, half * 512:(half + 1) * 512], banks[half][0][:n, :], st[:n, 2:3],
                    gfb[:n, half * 512:(half + 1) * 512], MUL, MUL, [banks[half][1], t_st, t_g], [tyb])
            P.dma("sp", y_out[cs_, :], yb[:n, :], R=[tyb])
        P.barrier()
        A.reset(m)

    pc, sc = make_cfgs()
    MERGE = STAGE >= 99 and (NSEQ_P == 2 or os.environ.get("MK_MERGE") == "1")
    for seq in range(NSEQ_P):
        merged = MERGE and seq == NSEQ_P - 1
        load_x(pc, D["xp"][seq])
        if merged:
            load_x(sc, D["xs"])
        if STAGE >= 1:
            try:
                phase_mixer_ab(pc, seq, None, None, O["glap"], O["retp"])
            except _Stop:
                P.barrier()
                A.reset(base_mark)
            if merged:
                phase_mixer_ab(sc, 0, D["sgla"], D["sret"], O["glas"], O["rets"])
        if STAGE >= 2:
            grp = [(pc, None, O["convp"][0, seq:seq + 1])]
            if merged:
                grp.append((sc, D["sconv"][0:8, :], O["convs"][0]))
            phase_ffn(grp, 0)
        if STAGE >= 3:
            phase_mla(pc, seq, O["ckvp"][seq], O["krp"][seq])
            if merged:
                phase_mla(sc, 0, O["ckvs"], O["krs"], D["cckv"], D["ckr"])
        if STAGE >= 4:
            grp = [(pc, None, O["convp"][1, seq:seq + 1])]
            if merged:
                grp.append((sc, D["sconv"][8:16, :], O["convs"][1]))
            phase_ffn(grp, 1)
        phase_final(pc, O["yp"][seq])
        if merged:
            phase_final(sc, O["ys"])
    if STAGE >= 5 and not MERGE:
        load_x(sc, D["xs"])
        phase_mixer_ab(sc, 0, D["sgla"], D["sret"], O["glas"], O["rets"])
        phase_ffn([(sc, D["sconv"][0:8, :], O["convs"][0])], 0)
        phase_mla(sc, 0, O["ckvs"], O["krs"], D["cckv"], D["ckr"])
        phase_ffn([(sc, D["sconv"][8:16, :], O["convs"][1])], 1)
        phase_final(sc, O["ys"])
    P.emit()
    P.close()
    print("arena peak words", A.peak, "instrs", {k: len(v) for k, v in P.streams.items()}, "signals", P.sigcount)
    return nc


_CACHE = {}


def kernel(x_prompt, x_sample, state_gla, state_ret, cache_ckv, cache_krope, state_conv,
           norm_mix, norm_ffn, norm_final, w_in_ab, w_gate_up, b_gate, g_gla, g_ret, w_out_ab,
           w_in_c, g_q, g_kv, w_uq, w_uk, w_uv, w_out_c, w_ffn_in, w_dwconv, b_dwconv, w_ffn_out):
    f = lambda a: np.ascontiguousarray(np.asarray(a, dtype=np.float32))
    ncores = int(os.environ.get('MK_CORES', '8'))
    if "nc" not in _CACHE:
        _CACHE["nc"] = build_program()
        _CACHE["tabs"] = const_tables()
    nc = _CACHE["nc"]
    tabs = _CACHE["tabs"]
    w_in_ab0 = f(w_in_ab)[0]
    sw = w_in_ab0[:, 1552:2064].reshape(1024, 8, 2, 32)[:, :, ::-1, :].reshape(1024, 512)
    wuq = f(w_uq)[0].reshape(384, 8, 192)
    wuk0 = f(w_uk)[0]
    shared = dict(
        nrm=np.stack([f(norm_mix)[0], f(norm_mix)[1], f(norm_ffn)[0], f(norm_ffn)[1], f(norm_final)]),
        w_in_ab=w_in_ab0, w_ab_sw=f(sw), wgu=f(w_gate_up)[0], bgate=f(b_gate)[0][None, :],
        ggla=f(g_gla)[0][None, :], gret=f(g_ret)[0][None, :], w_out_ab=f(w_out_ab)[0], w_in_c=f(w_in_c)[0],
        gq=f(g_q)[0][None, :], gkv=f(g_kv)[0][None, :],
        wuq_n=f(wuq[:, :, :128].reshape(384, 1024)), wuq_r=f(wuq[:, :, 128:].reshape(384, 512)),
        wuq_rs=f(wuq[:, :, 128:].reshape(384, 8, 2, 32)[:, :, ::-1, :].reshape(384, 512)),
        wuk=f(wuk0.reshape(512, 1024)), wukT=f(wuk0.transpose(2, 1, 0)), wuv=f(w_uv)[0].reshape(512, 1024),
        w_out_c=f(w_out_c)[0], wffi=f(w_ffn_in),
        dwc=f(np.concatenate([f(w_dwconv).reshape(6, 2816), f(b_dwconv)], axis=0)), wffo=f(w_ffn_out),
    )
    shared.update(tabs)
    xpv, xsv = f(x_prompt), f(x_sample)
    sg, sr = f(state_gla)[0], f(state_ret)[0]
    cc, ck, scv = f(cache_ckv)[0], f(cache_krope)[0], f(state_conv)
    in_maps = []
    for c in range(ncores):
        d = dict(shared)
        d["xp"] = xpv[2 * c:2 * c + 2]
        d["xs"] = xsv[4 * c:4 * c + 4].reshape(64, 1024)
        d["sgla"] = sg[4 * c:4 * c + 4].reshape(4, 256, 128)
        d["sret"] = sr[4 * c:4 * c + 4].reshape(4, 256, 128)
        d["cckv"] = cc[4 * c:4 * c + 4]
        d["ckr"] = ck[4 * c:4 * c + 4]
        d["sconv"] = f(scv[:, 4 * c:4 * c + 4].reshape(16, 2816))
        in_maps.append({k: np.ascontiguousarray(v) for k, v in d.items()})
    res = run_bass_kernel_spmd(nc, in_maps, core_ids=list(range(ncores)))
    R = res.results
    cat = lambda k, shp: np.concatenate([R[c][k].reshape(shp) for c in range(ncores)], axis=0)
    y_prompt = cat("yp", (2, 2048, 1024))
    y_sample = cat("ys", (4, 16, 1024))
    gla_p = cat("glap", (2, 4, 64, 128))[None]
    gla_s = cat("glas", (4, 4, 64, 128))[None]
    ret_p = cat("retp", (2, 4, 64, 128))[None]
    ret_s = cat("rets", (4, 4, 64, 128))[None]
    ckv_p = cat("ckvp", (2, 2048, 512))[None]
    ckv_s = cat("ckvs", (4, 16, 512))[None]
    kr_p = cat("krp", (2, 2048, 64))[None]
    kr_s = cat("krs", (4, 16, 64))[None]
    conv_p = np.concatenate([R[c]["convp"] for c in range(ncores)], axis=1)
    conv_s = np.concatenate([R[c]["convs"] for c in range(ncores)], axis=1)
    outs = (y_prompt, y_sample, gla_p, gla_s, ret_p, ret_s, ckv_p, ckv_s, kr_p, kr_s, conv_p, conv_s)
    return tuple(np.ascontiguousarray(o, dtype=np.float32) for o in outs)
```
